# Optimizing a Trainium2 kernel written in Bass

```python
import math
import jax, jax.numpy as jnp
from jax import lax
import numpy as np

D_MODEL = 1024
BATCH = 32
SEQ = 256
DEPTH = 2
DEC_BATCH = 8
DEC_SEQ = 1024
PAST_LEN = 512

GRID_W = 64
HEAD_DIM = 64
N_BRANCH = 4
BRANCH_W = D_MODEL // N_BRANCH
ATT_HEADS = BRANCH_W // HEAD_DIM
ATT_KV_HEADS = ATT_HEADS // 2
GDN_HEADS = BRANCH_W // HEAD_DIM
RET_HEADS = BRANCH_W // HEAD_DIM
NA_HEADS = BRANCH_W // HEAD_DIM
SHORT_CONV = 5
CHUNK = 64
Q_BLOCK = 128
NA_ROWS = 8
NA_COLS = 16
NA_KCOLS = 2 * NA_COLS
ROPE_THETA = 10000.0
RET_DECAY_BASE_FWD = 5.0
RET_DECAY_BASE_BWD = 5.5
EPS = 1e-6
IN_SPLITS = (
    ATT_HEADS * HEAD_DIM, ATT_KV_HEADS * HEAD_DIM, ATT_KV_HEADS * HEAD_DIM, BRANCH_W,
    3 * GDN_HEADS * HEAD_DIM, 4 * GDN_HEADS, BRANCH_W,
    RET_HEADS * HEAD_DIM, RET_HEADS * HEAD_DIM, RET_HEADS * HEAD_DIM, BRANCH_W,
    NA_HEADS * HEAD_DIM, NA_HEADS * HEAD_DIM, NA_HEADS * HEAD_DIM, BRANCH_W,
    N_BRANCH * D_MODEL,
)
IN_W = sum(IN_SPLITS)

kernel_name = "hybrid_diffusion_parallel_mixer_step"


def _rmsnorm(x, g):
    xf = x.astype(jnp.float32)
    y = xf * lax.rsqrt(jnp.mean(xf * xf, axis=-1, keepdims=True) + EPS)
    return (y * g.astype(jnp.float32)).astype(x.dtype)


def _l2norm(x):
    return x * lax.rsqrt(jnp.sum(x * x, axis=-1, keepdims=True) + EPS)


def _heads(x, h):
    return x.reshape(x.shape[0], x.shape[1], h, HEAD_DIM)


def _split_proj(p):
    offs = np.cumsum(IN_SPLITS)[:-1].tolist()
    return jnp.split(p, offs, axis=-1)


def _axial_rope(x):
    s = x.shape[1]
    t = jnp.arange(s)
    row = (t // GRID_W).astype(jnp.float32)
    col = (t % GRID_W).astype(jnp.float32)
    half = HEAD_DIM // 2
    quarter = half // 2
    inv = ROPE_THETA ** (-jnp.arange(quarter, dtype=jnp.float32) / quarter)

    def rot(xp, pos):
        ang = pos[:, None] * inv
        cos = jnp.cos(ang)[None, :, None, :]
        sin = jnp.sin(ang)[None, :, None, :]
        x1, x2 = xp[..., :quarter], xp[..., quarter:]
        return jnp.concatenate([x1 * cos - x2 * sin, x2 * cos + x1 * sin], axis=-1)

    xf = x.astype(jnp.float32)
    out = jnp.concatenate([rot(xf[..., :half], row), rot(xf[..., half:], col)], axis=-1)
    return out.astype(x.dtype)


def _block_attention(q, k, v):
    b, s, hq, d = q.shape
    hkv = k.shape[2]
    rep = hq // hkv
    nblk = s // Q_BLOCK
    qb = q.reshape(b, nblk, Q_BLOCK, hkv, rep, d).transpose(1, 0, 2, 3, 4, 5)
    scale = d ** -0.5

    def one(qi):
        sc = jnp.einsum('bqgrd,bkgd->bgrqk', qi, k, preferred_element_type=jnp.float32) * scale
        p = jax.nn.softmax(sc, axis=-1).astype(v.dtype)
        return jnp.einsum('bgrqk,bkgd->bqgrd', p, v)

    o = lax.map(one, qb)
    return o.transpose(1, 0, 2, 3, 4, 5).reshape(b, s, hq * d)


def _neighbourhood_attention(q, k, v, ctx_k, ctx_v, bias_table):
    b, n, h, d = q.shape
    rows = n // GRID_W
    wr = min(NA_ROWS, rows)
    ncb = GRID_W // NA_COLS
    r = jnp.arange(rows)
    rs = jnp.clip(r - wr // 2, 0, rows - wr)
    row_idx = rs[:, None] + jnp.arange(wr)[None, :]
    j = jnp.arange(ncb)
    bs = jnp.clip(j * NA_COLS - NA_COLS // 2, 0, GRID_W - NA_KCOLS)
    col_idx = bs[:, None] + jnp.arange(NA_KCOLS)[None, :]
    kg = k.reshape(b, rows, GRID_W, h, d)[:, row_idx][:, :, :, col_idx]
    vg = v.reshape(b, rows, GRID_W, h, d)[:, row_idx][:, :, :, col_idx]
    qg = q.reshape(b, rows, ncb, NA_COLS, h, d)
    scale = d ** -0.5
    s_loc = jnp.einsum('brjqhd,brwjkhd->brjhqwk', qg, kg, preferred_element_type=jnp.float32) * scale
    qc = j[:, None] * NA_COLS + jnp.arange(NA_COLS)[None, :]
    cs = jnp.clip(qc - NA_COLS // 2, 0, GRID_W - NA_COLS)
    kc = col_idx[:, None, :]
    col_ok = (kc >= cs[:, :, None]) & (kc < cs[:, :, None] + NA_COLS)
    dc_i = jnp.clip(kc - qc[:, :, None] + NA_COLS - 1, 0, 2 * NA_COLS - 2)
    dr_i = row_idx - r[:, None] + NA_ROWS - 1
    bias = bias_table[:, dr_i[:, None, None, :, None], dc_i[None, :, :, None, :]].astype(jnp.float32)
    bias = bias.transpose(1, 2, 0, 3, 4, 5)
    s_loc = jnp.where(col_ok[None, None, :, None, :, None, :], s_loc + bias[None], -jnp.inf)
    s_loc = s_loc.reshape(b, rows, ncb, h, NA_COLS, wr * NA_KCOLS)
    s_ctx = jnp.einsum('brjqhd,blhd->brjhql', qg, ctx_k.astype(q.dtype), preferred_element_type=jnp.float32) * scale
    p = jax.nn.softmax(jnp.concatenate([s_loc, s_ctx], axis=-1), axis=-1).astype(v.dtype)
    p_loc = p[..., :wr * NA_KCOLS].reshape(b, rows, ncb, h, NA_COLS, wr, NA_KCOLS)
    p_ctx = p[..., wr * NA_KCOLS:]
    o = (jnp.einsum('brjhqwk,brwjkhd->brjqhd', p_loc, vg)
         + jnp.einsum('brjhql,blhd->brjqhd', p_ctx, ctx_v.astype(v.dtype)))
    return o.reshape(b, n, h * d)


def _short_conv(x, w):
    ch = x.shape[-1]
    pad = SHORT_CONV // 2
    y = lax.conv_general_dilated(x, w[:, None, :].astype(x.dtype), window_strides=(1,),
                                 padding=((pad, pad),), dimension_numbers=('NWC', 'WIO', 'NWC'),
                                 feature_group_count=ch)
    return jax.nn.silu(y)


def _to_chunks(x):
    b, s = x.shape[:2]
    x = x.reshape((b, s // CHUNK, CHUNK) + x.shape[2:])
    return x.transpose((1, 0, 3, 2) + tuple(range(4, x.ndim)))


def _from_chunks(o, b, s):
    return o.transpose(1, 0, 3, 2, 4).reshape(b, s, o.shape[2], o.shape[4])


def _gated_delta_chunked(q, k, v, beta, log_a, s0):
    b, s = q.shape[:2]
    qc, kc, vc = _to_chunks(q), _to_chunks(k), _to_chunks(v)
    bc = _to_chunks(beta)
    gc = jnp.cumsum(_to_chunks(log_a), axis=-1)
    idx = jnp.arange(CHUNK)
    causal = idx[:, None] >= idx[None, :]
    strict = idx[:, None] > idx[None, :]
    gdiff = gc[..., :, None] - gc[..., None, :]
    decay_incl = jnp.exp(jnp.where(causal, gdiff, -jnp.inf))
    decay_strict = jnp.where(strict, decay_incl, 0.0)
    kb = kc * bc[..., None]
    a_mat = jnp.einsum('nbhid,nbhjd->nbhij', kb, kc) * decay_strict
    u = lax.linalg.triangular_solve(a_mat, vc * bc[..., None], left_side=True, lower=True, unit_diagonal=True)
    w = lax.linalg.triangular_solve(a_mat, kb * jnp.exp(gc)[..., None], left_side=True, lower=True, unit_diagonal=True)
    qk = jnp.einsum('nbhid,nbhjd->nbhij', qc, kc) * decay_incl
    q_dec = qc * jnp.exp(gc)[..., None]
    g_last = gc[..., -1]
    k_dec = kc * jnp.exp(g_last[..., None] - gc)[..., None]

    def step(state, xs):
        u_i, w_i, qk_i, qd_i, kd_i, gl_i = xs
        v_new = u_i - jnp.einsum('bhck,bhkv->bhcv', w_i, state)
        o = jnp.einsum('bhck,bhkv->bhcv', qd_i, state) + jnp.einsum('bhij,bhjv->bhiv', qk_i, v_new)
        state = state * jnp.exp(gl_i)[..., None, None] + jnp.einsum('bhck,bhcv->bhkv', kd_i, v_new)
        return state, o

    s_fin, o = lax.scan(step, s0, (u, w, qk, q_dec, k_dec, g_last))
    return _from_chunks(o, b, s), s_fin


def _retention_chunked(q, k, v, log_gamma, s0):
    b, s = q.shape[:2]
    qc, kc, vc = _to_chunks(q), _to_chunks(k), _to_chunks(v)
    idx = jnp.arange(CHUNK, dtype=jnp.float32)
    diff = idx[:, None] - idx[None, :]
    decay = jnp.where(diff >= 0, jnp.exp(jnp.maximum(diff, 0.0)[None] * log_gamma[:, None, None]), 0.0)
    q_dec = jnp.exp((idx + 1.0)[None, :] * log_gamma[:, None])
    k_dec = jnp.exp((CHUNK - 1.0 - idx)[None, :] * log_gamma[:, None])
    c_dec = jnp.exp(CHUNK * log_gamma)
    intra = jnp.einsum('nbhij,nbhjv->nbhiv', jnp.einsum('nbhid,nbhjd->nbhij', qc, kc) * decay, vc)
    qd = qc * q_dec[:, :, None]
    kd = kc * k_dec[:, :, None]

    def step(state, xs):
        intra_i, qd_i, kd_i, v_i = xs
        o = intra_i + jnp.einsum('bhck,bhkv->bhcv', qd_i, state)
        state = state * c_dec[:, None, None] + jnp.einsum('bhck,bhcv->bhkv', kd_i, v_i)
        return state, o

    s_fin, o = lax.scan(step, s0, (intra, qd, kd, vc))
    return _from_chunks(o, b, s), s_fin


def _flip(x):
    return jnp.flip(x, axis=1)


def _gdn_branch(qkv_raw, ab_raw, z, conv_w, a_log, dt_bias, norm_g, s0):
    f32 = jnp.float32
    b, s, _ = qkv_raw.shape
    qkv = _short_conv(qkv_raw, conv_w).astype(f32)
    q, k, v = jnp.split(qkv, 3, axis=-1)
    q = _l2norm(_heads(q, GDN_HEADS)) * HEAD_DIM ** -0.5
    k = _l2norm(_heads(k, GDN_HEADS))
    v = _heads(v, GDN_HEADS)
    ab = ab_raw.astype(f32).reshape(b, s, 4, GDN_HEADS)
    beta = jax.nn.sigmoid(ab[:, :, 0:2])
    log_a = -jnp.exp(a_log.astype(f32)) * jax.nn.softplus(ab[:, :, 2:4] + dt_bias.astype(f32))
    s0 = s0.astype(f32)
    o_f, st_f = _gated_delta_chunked(q, k, v, beta[:, :, 0], log_a[:, :, 0], s0[:, 0])
    o_b, st_b = _gated_delta_chunked(_flip(q), _flip(k), _flip(v), _flip(beta[:, :, 1]),
                                     _flip(log_a[:, :, 1]), s0[:, 1])
    o = _rmsnorm(o_f + _flip(o_b), norm_g).reshape(b, s, BRANCH_W).astype(z.dtype)
    return o * jax.nn.silu(z), jnp.stack([st_f, st_b], axis=1)


def _ret_log_gamma(base):
    return jnp.log1p(-jnp.exp2(-(base + jnp.arange(RET_HEADS, dtype=jnp.float32))))


def _ret_branch(q_raw, k_raw, v_raw, z, norm_g, s0):
    f32 = jnp.float32
    b, s, _ = q_raw.shape
    q = _heads(q_raw.astype(f32), RET_HEADS) * HEAD_DIM ** -0.5
    k = _heads(k_raw.astype(f32), RET_HEADS)
    v = _heads(v_raw.astype(f32), RET_HEADS)
    s0 = s0.astype(f32)
    o_f, st_f = _retention_chunked(q, k, v, _ret_log_gamma(RET_DECAY_BASE_FWD), s0[:, 0])
    o_b, st_b = _retention_chunked(_flip(q), _flip(k), _flip(v), _ret_log_gamma(RET_DECAY_BASE_BWD), s0[:, 1])
    o = _rmsnorm(o_f + _flip(o_b), norm_g).reshape(b, s, BRANCH_W).astype(z.dtype)
    return o * jax.nn.silu(z), jnp.stack([st_f, st_b], axis=1)


def _layer(h, cond, ctx, w_ada, b_ada, norm_g, w_in, conv_w, gdn_a_log, gdn_dt_bias, gdn_norm,
           attn_q_norm, attn_k_norm, ret_norm, na_bias, w_branch, w_out):
    b, s, _ = h.shape
    mod = jax.nn.silu(cond) @ w_ada + b_ada
    shift, scale, gate = jnp.split(mod, 3, axis=-1)
    hn = _rmsnorm(h, norm_g) * (1 + scale[:, None]) + shift[:, None]
    (aq, ak, av, az, gqkv, gab, gz, rq, rk, rv, rz, nq, nk, nv, nz, mg) = _split_proj(hn @ w_in)
    qa = _rmsnorm(_heads(aq, ATT_HEADS), attn_q_norm)
    ka = _rmsnorm(_heads(ak, ATT_KV_HEADS), attn_k_norm)
    va = _heads(av, ATT_KV_HEADS)
    qn, kn, vn = _heads(nq, NA_HEADS), _heads(nk, NA_HEADS), _heads(nv, NA_HEADS)
    if ctx is None:
        oa = _block_attention(qa, ka, va)
        od = _block_attention(qn, kn, vn)
        sg0 = jnp.zeros((b, 2, GDN_HEADS, HEAD_DIM, HEAD_DIM), jnp.float32)
        sr0 = jnp.zeros((b, 2, RET_HEADS, HEAD_DIM, HEAD_DIM), jnp.float32)
    else:
        ctx_ka, ctx_va, ctx_kn, ctx_vn, sg0, sr0 = ctx
        keys = jnp.concatenate([_axial_rope(ka), ctx_ka.astype(ka.dtype)], axis=1)
        vals = jnp.concatenate([va, ctx_va.astype(va.dtype)], axis=1)
        oa = _block_attention(_axial_rope(qa), keys, vals)
        od = _neighbourhood_attention(qn, kn, vn, ctx_kn, ctx_vn, na_bias)
    oa = oa * jax.nn.silu(az)
    od = od * jax.nn.silu(nz)
    ob, sg = _gdn_branch(gqkv, gab, gz, conv_w, gdn_a_log, gdn_dt_bias, gdn_norm, sg0)
    oc, sr = _ret_branch(rq, rk, rv, rz, ret_norm, sr0)
    gates = jax.nn.sigmoid(mg.astype(jnp.float32)).astype(h.dtype).reshape(b, s, N_BRANCH, D_MODEL)
    branches = jnp.stack([oa, ob, oc, od], axis=2)
    up = jnp.einsum('bsnw,nwd->bsnd', branches, w_branch)
    out = jnp.einsum('bsnd,bsnd->bsd', gates, up) @ w_out
    return h + gate[:, None] * out, (ka, va, kn, vn, sg, sr)


def setup_inputs(seed: int = 0) -> dict:
    key = jax.random.key(seed)
    ks = jax.random.split(key, 26)
    f32 = jnp.float32

    def nrm(k, shape, sd):
        return jax.random.normal(k, shape, f32) * sd

    dt = jnp.exp(jax.random.uniform(ks[16], (DEPTH, 2, GDN_HEADS), f32, math.log(1e-3), math.log(1e-1)))
    return {
        "x_prompt": nrm(ks[0], (BATCH, SEQ, D_MODEL), 1.0),
        "x_sample": nrm(ks[1], (DEC_BATCH, DEC_SEQ, D_MODEL), 1.0),
        "cache_attn_k": nrm(ks[2], (DEC_BATCH, DEPTH, PAST_LEN, ATT_KV_HEADS, HEAD_DIM), 1.0),
        "cache_attn_v": nrm(ks[3], (DEC_BATCH, DEPTH, PAST_LEN, ATT_KV_HEADS, HEAD_DIM), 1.0),
        "cache_na_k": nrm(ks[4], (DEC_BATCH, DEPTH, PAST_LEN, NA_HEADS, HEAD_DIM), 1.0),
        "cache_na_v": nrm(ks[5], (DEC_BATCH, DEPTH, PAST_LEN, NA_HEADS, HEAD_DIM), 1.0),
        "state_gdn": nrm(ks[6], (DEC_BATCH, DEPTH, 2, GDN_HEADS, HEAD_DIM, HEAD_DIM), 0.5),
        "state_ret": nrm(ks[7], (DEC_BATCH, DEPTH, 2, RET_HEADS, HEAD_DIM, HEAD_DIM), 1.0),
        "c": nrm(ks[8], (DEC_BATCH, D_MODEL), 1.0),
        "c_ctx": nrm(ks[9], (D_MODEL,), 1.0),
        "w_ada": nrm(ks[10], (DEPTH, D_MODEL, 3 * D_MODEL), 0.5 * D_MODEL ** -0.5),
        "b_ada": nrm(ks[11], (DEPTH, 3 * D_MODEL), 0.01),
        "norm_g": 1.0 + nrm(ks[12], (DEPTH, D_MODEL), 0.02),
        "w_in": nrm(ks[13], (DEPTH, D_MODEL, IN_W), D_MODEL ** -0.5),
        "conv_w": nrm(ks[14], (DEPTH, SHORT_CONV, 3 * GDN_HEADS * HEAD_DIM), SHORT_CONV ** -0.5),
        "gdn_a_log": jnp.log(jax.random.uniform(ks[15], (DEPTH, 2, GDN_HEADS), f32, 1.0, 16.0)),
        "gdn_dt_bias": dt + jnp.log(-jnp.expm1(-dt)),
        "gdn_norm": 1.0 + nrm(ks[17], (DEPTH, HEAD_DIM), 0.02),
        "attn_q_norm": 1.0 + nrm(ks[18], (DEPTH, HEAD_DIM), 0.02),
        "attn_k_norm": 1.0 + nrm(ks[19], (DEPTH, HEAD_DIM), 0.02),
        "ret_norm": 1.0 + nrm(ks[20], (DEPTH, HEAD_DIM), 0.02),
        "na_bias": nrm(ks[21], (DEPTH, NA_HEADS, 2 * NA_ROWS - 1, 2 * NA_COLS - 1), 0.1),
        "w_branch": nrm(ks[22], (DEPTH, N_BRANCH, BRANCH_W, D_MODEL), BRANCH_W ** -0.5),
        "w_out": nrm(ks[23], (DEPTH, D_MODEL, D_MODEL), D_MODEL ** -0.5),
        "final_norm": 1.0 + nrm(ks[24], (D_MODEL,), 0.02),
    }


def reference(x_prompt, x_sample, cache_attn_k, cache_attn_v, cache_na_k, cache_na_v, state_gdn, state_ret,
              c, c_ctx, w_ada, b_ada, norm_g, w_in, conv_w, gdn_a_log, gdn_dt_bias, gdn_norm,
              attn_q_norm, attn_k_norm, ret_norm, na_bias, w_branch, w_out, final_norm):
    def params(l):
        return dict(w_ada=w_ada[l], b_ada=b_ada[l], norm_g=norm_g[l], w_in=w_in[l], conv_w=conv_w[l],
                    gdn_a_log=gdn_a_log[l], gdn_dt_bias=gdn_dt_bias[l], gdn_norm=gdn_norm[l],
                    attn_q_norm=attn_q_norm[l], attn_k_norm=attn_k_norm[l], ret_norm=ret_norm[l],
                    na_bias=na_bias[l], w_branch=w_branch[l], w_out=w_out[l])

    h = x_prompt
    cond_ctx = jnp.broadcast_to(c_ctx, (x_prompt.shape[0], D_MODEL))
    ka_l, va_l, kn_l, vn_l, sg_l, sr_l = [], [], [], [], [], []
    for l in range(DEPTH):
        h, (ka, va, kn, vn, sg, sr) = _layer(h, cond_ctx, None, **params(l))
        ka_l.append(ka); va_l.append(va); kn_l.append(kn); vn_l.append(vn); sg_l.append(sg); sr_l.append(sr)
    y_prompt = _rmsnorm(h, final_norm)
    dt = x_prompt.dtype
    new_attn_k = jnp.stack(ka_l, axis=1).astype(dt)
    new_attn_v = jnp.stack(va_l, axis=1).astype(dt)
    new_na_k = jnp.stack(kn_l, axis=1).astype(dt)
    new_na_v = jnp.stack(vn_l, axis=1).astype(dt)
    new_state_gdn = jnp.stack(sg_l, axis=1).astype(dt)
    new_state_ret = jnp.stack(sr_l, axis=1).astype(dt)

    h = x_sample
    for l in range(DEPTH):
        ctx = (cache_attn_k[:, l], cache_attn_v[:, l], cache_na_k[:, l], cache_na_v[:, l],
               state_gdn[:, l], state_ret[:, l])
        h, _ = _layer(h, c, ctx, **params(l))
    y_sample = _rmsnorm(h, final_norm)
    return (y_prompt, y_sample, new_attn_k, new_attn_v, new_na_k, new_na_v, new_state_gdn, new_state_ret)
```

```python
import numpy as np
import concourse.bass as bass
import concourse.mybir as mybir
from concourse.bass_utils import run_bass_kernel_spmd
from contextlib import ExitStack

F32 = mybir.dt.float32
BF16 = mybir.dt.bfloat16
ALU = mybir.AluOpType
AF = mybir.ActivationFunctionType
AX = mybir.AxisListType
EPS = 1e-6
NEG = -30000.0
BIG = 1.0e5


import types
import os
_CL = int(os.environ.get('CLEVEL', '9'))
_BL = int(os.environ.get('BLEVEL', '9'))
_GB = int(os.environ.get('GB', '9'))
_GBQ = int(os.environ.get('GBQ', '99'))
_GBS = int(os.environ.get('GBS', '9'))
_STRICT = bool(int(os.environ.get('STRICT', '0')))


def _freeze(fn, _depth=0):
    if fn is None or fn.__closure__ is None:
        return fn
    cells = []
    for c in fn.__closure__:
        try:
            v = c.cell_contents
        except ValueError:
            cells.append(c)
            continue
        if isinstance(v, types.FunctionType) and v.__closure__ is not None and _depth < 3:
            v = _freeze(v, _depth + 1)
        cells.append(types.CellType(v))
    return types.FunctionType(fn.__code__, fn.__globals__, fn.__name__, fn.__defaults__, tuple(cells))


class _Op:
    __slots__ = ("fn", "waits", "dma", "dma_n")

    def __init__(self, fn, waits, dma, dma_n):
        self.fn, self.waits, self.dma, self.dma_n = fn, waits, dma, dma_n


class Sched:
    KD = 8
    BLK = {"pe": "tensor", "act": "scalar", "dve": "vector", "pool": "gpsimd", "sp": "sync"}

    def __init__(self, nc, es):
        self.nc = nc
        self.ops = {e: [] for e in self.BLK}
        self.state = {}
        self.ndma = {e: 0 for e in self.BLK}
        self.seen_c = {e: {} for e in self.BLK}
        self.seen_d = {e: set() for e in self.BLK}
        self.last = {}
        self.csem = {e: es.enter_context(nc.semaphore("c_" + e)) for e in ("pe", "act", "dve", "pool")}
        self.dsem = {e: [es.enter_context(nc.semaphore("d_%s%d" % (e, i))) for i in range(self.KD)]
                     for e in ("sp", "pool")}

    def _split(self, key):
        if isinstance(key, tuple):
            return key[0], key[1:]
        return key, None

    def _recs(self, key):
        name, sub = self._split(key)
        d = self.state.get(name)
        if not d:
            return []
        if sub is None:
            return list(d.values())
        out = []
        if sub in d:
            out.append(d[sub])
        if None in d:
            out.append(d[None])
        return out

    def _filter(self, eng, raw, other, dma):
        waits = []
        for d in sorted(raw | other):
            if d[0] == "c":
                if d[1] == eng and not dma:
                    if eng == "pe" or (d not in raw and not _STRICT):
                        continue
                if self.seen_c[eng].get(d[1], -1) >= d[2]:
                    continue
                self.seen_c[eng][d[1]] = d[2]
                waits.append(d)
            else:
                if d in self.seen_d[eng]:
                    continue
                self.seen_d[eng].add(d)
                waits.append(d)
        best, fin = {}, []
        for w in waits:
            if w[0] == "c":
                if w[1] not in best or best[w[1]][2] < w[2]:
                    best[w[1]] = w
            else:
                fin.append(w)
        fin.extend(best.values())
        return fin

    mute = False

    def op(self, eng, fn, reads=(), writes=(), dma=False):
        if self.mute:
            return None
        fn = _freeze(fn)
        idx = len(self.ops[eng])
        raw, other = set(), set()
        for key in reads:
            for rec in self._recs(key):
                if rec[0] is not None:
                    raw.add(rec[0])
        for key in writes:
            for rec in self._recs(key):
                if rec[0] is not None:
                    other.add(rec[0])
                other.update(rec[1])
        if dma:
            n = self.ndma[eng]
            self.ndma[eng] += 1
            ev = ("d", eng, n)
            self.last[("d", eng, n % self.KD)] = ev
        else:
            n = None
            ev = ("c", eng, idx)
            self.last[("c", eng)] = ev
        fin = self._filter(eng, raw, other, dma)
        self.ops[eng].append(_Op(fn, fin, dma, n))
        for key in reads:
            name, sub = self._split(key)
            self.state.setdefault(name, {}).setdefault(sub, [None, []])[1].append(ev)
        for key in writes:
            name, sub = self._split(key)
            d = self.state.setdefault(name, {})
            if sub is None:
                d.clear()
            d[sub] = [ev, []]
        return ev

    def barrier(self):
        evs = set(self.last.values())
        for e in self.BLK:
            fin = self._filter(e, set(evs), set(), True)
            if fin:
                self.ops[e].append(_Op(None, fin, False, None))
        self.state = {}

    def emit(self):
        for e in ("sp", "pool"):
            n = self.ndma[e]
            if n:
                self.ops[e].append(_Op(None, [("d", e, i) for i in range(max(0, n - self.KD), n)], False, None))
        waited = {e: set() for e in self.BLK}
        for e, ops in self.ops.items():
            for o in ops:
                for w in o.waits:
                    if w[0] == "c":
                        waited[w[1]].add(w[2])
        val = {}
        for e in self.BLK:
            for rank, idx in enumerate(sorted(waited[e])):
                val[(e, idx)] = rank + 1
        KD = self.KD
        self.maxval = {e: len(waited[e]) for e in self.BLK}
        self.maxdma = {e: 16 * ((self.ndma[e] - 1) // KD + 1) for e in ("sp", "pool")}
        if os.environ.get("SEMDBG"):
            print("SEM max values", self.maxval, self.maxdma, flush=True)
        with self.nc.Block() as block:
            for e in self.BLK:
                def body(engine, e=e):
                    for idx, o in enumerate(self.ops[e]):
                        for w in o.waits:
                            if w[0] == "c":
                                engine.wait_ge(self.csem[w[1]], val[(w[1], w[2])])
                            else:
                                engine.wait_ge(self.dsem[w[1]][w[2] % KD], 16 * (w[2] // KD + 1))
                        if o.fn is None:
                            continue
                        if o.dma:
                            n = o.dma_n
                            if n >= KD:
                                engine.wait_ge(self.dsem[e][n % KD], 16 * (n // KD))
                            o.fn(engine).then_inc(self.dsem[e][n % KD], 16)
                        else:
                            ins = o.fn(engine)
                            if idx in waited[e]:
                                ins.then_inc(self.csem[e], 1)
                getattr(block, self.BLK[e])(body)
        return {e: len(self.ops[e]) for e in self.BLK}


def _na_pairs():
    t = np.arange(1024)
    r, c = t // 64, t % 64
    rs = np.clip(r - 4, 0, 8)
    cs = np.clip(c - 8, 0, 48)
    valid = ((r[None, :] >= rs[:, None]) & (r[None, :] < rs[:, None] + 8) &
             (c[None, :] >= cs[:, None]) & (c[None, :] < cs[:, None] + 16))
    dr = r[None, :] - r[:, None] + 7
    dc = np.clip(c[None, :] - c[:, None] + 15, 0, 30)
    pairs = []
    for qt in range(8):
        for kt in range(8):
            if valid[qt * 128:(qt + 1) * 128, kt * 128:(kt + 1) * 128].any():
                pairs.append((qt, kt))
    return valid, dr, dc, pairs


_NA = _na_pairs()
NPAIR = len(_NA[3])


def _consts(na_bias):
    c = {}
    c["c_ident"] = np.eye(128, dtype=np.float32)
    p = np.arange(128)[:, None]
    f = np.arange(128)[None, :]
    m = np.zeros((6, 128, 128), np.float32)
    m[0] = (p <= f)
    m[1] = (p >= f)
    m[2] = np.where(f < p, 0.0, BIG)
    m[3] = np.where(f > p, 0.0, BIG)
    m[4] = np.where(f >= p, 0.0, -BIG)
    m[5] = np.where(f <= p, 0.0, -BIG)
    c["c_masks"] = m
    h = np.arange(4, dtype=np.float64)
    lgf = np.log1p(-np.exp2(-(5.0 + h)))
    lgb = np.log1p(-np.exp2(-(5.5 + h)))
    j = np.arange(128, dtype=np.float64)
    dct = np.zeros((4, 128, 128), np.float64)
    for hh in range(4):
        d = j[None, :] - j[:, None]
        dct[hh] = np.where(d > 0, np.exp(np.maximum(d, 0) * lgf[hh]), 0.0) + \
            np.where(d < 0, np.exp(np.maximum(-d, 0) * lgb[hh]), 0.0) + np.where(d == 0, 2.0, 0.0)
    c["c_dct"] = (dct * 0.125).astype(np.float32)
    rqt = np.zeros((2, 4, 128), np.float64)
    kdt = np.zeros((128, 4, 2, 64), np.float64)
    cdt = np.zeros((128, 4, 64), np.float64)
    for hh in range(4):
        rqt[0, hh] = np.exp((j + 1.0) * lgf[hh]) * 0.125
        rqt[1, hh] = np.exp((128.0 - j) * lgb[hh]) * 0.125
        kdt[:, hh, 0, :] = np.exp((127.0 - j) * lgf[hh])[:, None]
        kdt[:, hh, 1, :] = np.exp(j * lgb[hh])[:, None]
        hf_, pr_ = hh % 2, hh // 2
        cdt[hf_ * 64:(hf_ + 1) * 64, 0 * 2 + pr_, :] = np.exp(128.0 * lgf[hh])
        cdt[hf_ * 64:(hf_ + 1) * 64, 1 * 2 + pr_, :] = np.exp(128.0 * lgb[hh])
    c["c_rqt"] = rqt.astype(np.float32)
    c["c_kdt"] = kdt.astype(np.float32)
    c["c_cdt"] = cdt.astype(np.float32)
    t = np.arange(1024)
    row = (t // 64).astype(np.float32)
    col = (t % 64).astype(np.float32)
    inv = (10000.0 ** (-np.arange(16, dtype=np.float32) / 16)).astype(np.float32)
    ar = row[:, None] * inv[None, :]
    ac = col[:, None] * inv[None, :]
    cc = np.concatenate([np.cos(ar), np.cos(ar), np.cos(ac), np.cos(ac)], axis=1)
    ss = np.concatenate([-np.sin(ar), np.sin(ar), -np.sin(ac), np.sin(ac)], axis=1)
    c["c_rope"] = np.stack([np.tile(cc, (1, 6)), np.tile(ss, (1, 6))]).astype(np.float32)
    valid, dr, dc, pairs = _NA
    nab = np.empty((2, NPAIR, 4, 128, 128), np.float32)
    for pi, (qt, kt) in enumerate(pairs):
        qs = slice(qt * 128, (qt + 1) * 128)
        ks = slice(kt * 128, (kt + 1) * 128)
        v = valid[qs, ks].T
        g = na_bias[:, :, dr[qs, ks].T, dc[qs, ks].T]
        nab[:, pi] = np.where(v[None, None], g, np.float32(NEG))
    c["c_nab"] = nab
    return c


IN_SHAPES = dict(
    xp=[1024, 1024], xs=[1024, 1024], cak=[2, 512, 128], cav=[2, 512, 128], cnk=[2, 512, 256], cnv=[2, 512, 256],
    sg=[2, 2, 4, 64, 64], sr=[2, 2, 4, 64, 64], cond=[2, 1024],
    w_ada=[2, 1024, 3072], b_ada=[2, 3072], norm_g=[2, 1024], w_in=[2, 1024, 7952], conv_w=[2, 5, 768],
    a_log=[2, 8], dt_bias=[2, 8], gdn_norm=[2, 64], q_norm=[2, 64], k_norm=[2, 64], ret_norm=[2, 64],
    w_branch=[2, 4, 256, 1024], w_out=[2, 1024, 1024], final_norm=[1, 1024],
    c_ident=[128, 128], c_masks=[6, 128, 128], c_dct=[4, 128, 128], c_rqt=[2, 4, 128], c_kdt=[128, 4, 2, 64],
    c_cdt=[128, 4, 64], c_rope=[2, 1024, 384], c_nab=[2, NPAIR, 4, 128, 128])
OUT_SHAPES = dict(yp=[1024, 1024], ys=[1024, 1024], nak=[4, 2, 256, 128], nav=[4, 2, 256, 128],
                  nnk=[4, 2, 256, 256], nnv=[4, 2, 256, 256], nsg=[4, 2, 2, 4, 64, 64], nsr=[4, 2, 2, 4, 64, 64])


class _Stop(Exception):
    pass


def build(taps=None, groups=(0, 1), layers=(0, 1), stop_after=None):
    taps = taps or {}
    nc = bass.Bass("TRN2", target_bir_lowering=False)
    D = {}
    for n, s in IN_SHAPES.items():
        D[n] = nc.dram_tensor(n, list(s), F32, kind="ExternalInput").ap()
    for n, s in OUT_SHAPES.items():
        D[n] = nc.dram_tensor(n, list(s), F32, kind="ExternalOutput").ap()
    for n, s in taps.items():
        D["tap_" + n] = nc.dram_tensor("tap_" + n, list(s), F32, kind="ExternalOutput").ap()

    with ExitStack() as es:
        S = Sched(nc, es)

        uid = [0]

        def sb(name, shape, dt=F32, st=es):
            uid[0] += 1
            return st.enter_context(nc.sbuf_tensor("%s_%d" % (name, uid[0]), list(shape), dt))

        PS = [es.enter_context(nc.psum_tensor("ps%d" % i, [128, 512], F32)) for i in range(8)]
        rr = [0]

        def bank():
            i = 4 + rr[0] % 4
            rr[0] += 1
            return PS[i], ("ps", i)

        def V(fn, r, w): return S.op("dve", fn, r, w)
        def A(fn, r, w): return S.op("act", fn, r, w)
        def G(fn, r, w): return S.op("pool", fn, r, w)
        def P(fn, r, w): return S.op("pe", fn, r, w)
        def DS(fn, r, w): return S.op("sp", fn, r, w, dma=True)
        def DG(fn, r, w): return S.op("pool", fn, r, w, dma=True)

        def mm(out, lhsT, rhs, start, stop, r, w):
            P(lambda e: e.matmul(out, lhsT=lhsT, rhs=rhs, start=start, stop=stop), r, w)

        def tr(out, in_, idt, r, w):
            P(lambda e: e.transpose(out, in_, idt), r, w)

        def act(out, in_, func, r, w, bias=0.0, scale=1.0):
            A(lambda e: e.activation(out=out, in_=in_, func=func, bias=bias, scale=scale), r, w)

        def tap(name, ap, reads):
            if name in taps and name not in os.environ.get("NOTAP", "").split(","):
                DG(lambda e: e.dma_start(out=D["tap_" + name], in_=ap), reads, [])

        ident = sb("ident", [128, 128])
        identb = sb("identb", [128, 128], BF16)
        ones = sb("ones", [128, 128])
        onesblk = sb("onesblk", [128, 128])
        onespad = sb("onespad", [128, 2, 128], BF16)
        masks = sb("masks", [128, 6, 128])
        dct = sb("dct", [128, 4, 128])
        rqt = sb("rqt", [128, 2, 4, 128])
        kdt = sb("kdt", [128, 4, 2, 64])
        cdt = sb("cdt", [128, 4, 64])
        ngT = sb("ngT", [128, 2, 8])
        fnT = sb("fnT", [128, 8])
        baT = sb("baT", [128, 2, 24])
        cwT = sb("cwT", [128, 2, 6, 5])
        gqk = sb("gqk", [128, 2, 6, 64])
        gkn = sb("gkn", [128, 2, 2, 64])
        ggn = sb("ggn", [128, 2, 4, 64])
        grn = sb("grn", [128, 2, 4, 64])
        alb = sb("alb", [128, 2, 8])
        dtb = sb("dtb", [128, 2, 8])
        nega = sb("nega", [128, 2, 8])
        condT = sb("condT", [128, 8, 2])
        scond = sb("scond", [128, 8, 2])
        modT = sb("modT", [128, 2, 24, 2])
        gmul = sb("gmul", [128, 2, 8, 2])

        DS(lambda e: e.dma_start(out=ident[:], in_=D["c_ident"]), [], ["ident"])
        DG(lambda e: e.dma_start(out=identb[:], in_=D["c_ident"]), [], ["identb"])
        V(lambda e: e.memset(ones[:], 1.0), [], ["ones"])
        V(lambda e: e.memset(onesblk[:], 0.0), [], ["onesblk"])
        V(lambda e: e.memset(onesblk[0:64, 0:64], 1.0), [], ["onesblk"])
        V(lambda e: e.memset(onesblk[64:128, 64:128], 1.0), [], ["onesblk"])
        V(lambda e: e.memset(onespad[:], 0.0), [], ["onespad"])
        V(lambda e: e.memset(onespad[:, 0, 0:64], 1.0), [], ["onespad"])
        V(lambda e: e.memset(onespad[:, 1, 64:128], 1.0), [], ["onespad"])
        DS(lambda e: e.dma_start(out=masks[:], in_=D["c_masks"].rearrange("m p f -> p m f")), [], ["masks"])
        DS(lambda e: e.dma_start(out=dct[:], in_=D["c_dct"].rearrange("m p f -> p m f")), [], ["dct"])
        DS(lambda e: e.dma_start(out=rqt[:].rearrange("p a b c -> p (a b c)"),
                                 in_=D["c_rqt"].rearrange("a b c -> (a b c)").partition_broadcast(128)), [], ["rqt"])
        DS(lambda e: e.dma_start(out=kdt[:], in_=D["c_kdt"]), [], ["kdt"])
        DS(lambda e: e.dma_start(out=cdt[:], in_=D["c_cdt"]), [], ["cdt"])
        DS(lambda e: e.dma_start(out=ngT[:], in_=D["norm_g"].rearrange("l (c p) -> p l c", p=128)), [], ["ngT"])
        DS(lambda e: e.dma_start(out=fnT[:], in_=D["final_norm"].rearrange("o (c p) -> p (o c)", p=128)), [], ["fnT"])
        DS(lambda e: e.dma_start(out=baT[:], in_=D["b_ada"].rearrange("l (c p) -> p l c", p=128)), [], ["baT"])
        for l in range(2):
            for c6 in range(6):
                DS(lambda e, l=l, c6=c6: e.dma_start(
                    out=cwT[:, l, c6, :], in_=D["conv_w"][l, :, c6 * 128:(c6 + 1) * 128].rearrange("j p -> p j")),
                   [], ["cwT"])
        for jj in range(2):
            DS(lambda e, jj=jj: e.dma_start(out=condT[:, :, jj], in_=D["cond"][jj].rearrange("(c p) -> p c", p=128)), [], ["condT"])
        for l in range(2):
            for hh in range(6):
                src = "q_norm" if hh < 4 else "k_norm"
                DS(lambda e, l=l, hh=hh, src=src: e.dma_start(out=gqk[:, l, hh, :], in_=D[src][l].partition_broadcast(128)),
                   [], ["gqk"])
            for hh in range(2):
                DS(lambda e, l=l, hh=hh: e.dma_start(out=gkn[:, l, hh, :], in_=D["k_norm"][l].partition_broadcast(128)),
                   [], ["gkn"])
            for hh in range(4):
                DS(lambda e, l=l, hh=hh: e.dma_start(out=ggn[:, l, hh, :], in_=D["gdn_norm"][l].partition_broadcast(128)),
                   [], ["ggn"])
                DS(lambda e, l=l, hh=hh: e.dma_start(out=grn[:, l, hh, :], in_=D["ret_norm"][l].partition_broadcast(128)),
                   [], ["grn"])
            DS(lambda e, l=l: e.dma_start(out=alb[:, l, :], in_=D["a_log"][l].partition_broadcast(128)), [], ["alb"])
            DS(lambda e, l=l: e.dma_start(out=dtb[:, l, :], in_=D["dt_bias"][l].partition_broadcast(128)), [], ["dtb"])
        for l in range(2):
            V(lambda e, l=l: e.tensor_scalar(out=gqk[:, l, 0:4, :], in0=gqk[:, l, 0:4, :], scalar1=0.125, scalar2=None,
                                             op0=ALU.mult), ["gqk"], ["gqk"])
        act(nega[:], alb[:], AF.Exp, ["alb"], ["nega"])
        V(lambda e: e.tensor_scalar(out=nega[:], in0=nega[:], scalar1=-1.0, scalar2=None, op0=ALU.mult), ["nega"], ["nega"])
        act(scond[:], condT[:], AF.Silu, ["condT"], ["scond"])

        with ExitStack() as ph:
            wa = [sb("wa%d" % i, [128, 8, 512], F32, ph) for i in range(2)]
            cnt = 0
            for l in range(2):
                pb, pk = PS[0], ("ps", 0)
                for ch in range(6):
                    w_t, wk = wa[cnt % 2], "wa%d" % (cnt % 2)
                    cnt += 1
                    DS(lambda e, l=l, ch=ch, w_t=w_t: e.dma_start(
                        out=w_t[:], in_=D["w_ada"][l, :, ch * 512:(ch + 1) * 512].rearrange("(k p) n -> p k n", p=128)),
                       [], [wk])
                    for oc in range(4):
                        col = (ch * 4 + oc) * 2
                        for k in range(8):
                            mm(pb[:, col:col + 2], w_t[:, k, oc * 128:(oc + 1) * 128], scond[:, k, :], k == 0, k == 7,
                               [wk, "scond"], [pk])
                for j in range(2):
                    V(lambda e, l=l, j=j, pb=pb: e.tensor_tensor(
                        out=modT[:, l, :, j], in0=pb[:, 0:48].rearrange("p (a b) -> p a b", b=2)[:, :, j],
                        in1=baT[:, l, :], op=ALU.add), [pk, "baT"], ["modT"])
                    V(lambda e, l=l, j=j: e.scalar_tensor_tensor(
                        out=gmul[:, l, :, j], in0=modT[:, l, 8:16, j], scalar=1.0, in1=ngT[:, l, :],
                        op0=ALU.add, op1=ALU.mult), ["modT", "ngT"], ["gmul"])
            tap("modT", modT[:].rearrange("p a b c -> p (a b c)"), ["modT"])
        S.barrier()

        hT = sb("hT", [128, 8, 1024])
        hnT = sb("hnT", [128, 8, 1024], BF16)
        brT = sb("brT", [128, 8, 1024], BF16)
        wbs = [sb("wb%d" % i, [128, 8, 512], BF16) for i in range(3)]
        stg = [sb("stg%d" % i, [128, 1024]) for i in range(2)]
        wcnt = [0]
        scnt = [0]

        def load_w(l, c0, c1):
            i = wcnt[0] % 3
            wcnt[0] += 1
            t, k = wbs[i], "wb%d" % i
            DG(lambda e: e.dma_start(out=t[:, :, 0:c1 - c0],
                                     in_=D["w_in"][l, :, c0:c1].rearrange("(k p) n -> p k n", p=128)), [], [k])
            return t, k

        def proj_T(wt, wk, a, b, tt, pb, pk):
            for k in range(8):
                mm(pb[:, 0:b - a], hnT[:, k, tt * 128:(tt + 1) * 128], wt[:, k, a:b], k == 0, k == 7, [wk, "hnT"], [pk])

        def proj_F(wt, wk, a, m, tb, pb, pk):
            for k in range(8):
                mm(pb[0:m, :], wt[:, k, a:a + m], hnT[:, k, tb * 512:(tb + 1) * 512], k == 0, k == 7, [wk, "hnT"], [pk])

        def rstd_from(ss_ap, out_ap, n, r, w):
            act(out_ap, ss_ap, AF.Sqrt, r, w, bias=EPS, scale=1.0 / n)
            V(lambda e: e.reciprocal(out=out_ap, in_=out_ap), w, w)

        def rmsnorm_T(st, gcol_fn, scol_fn, out_fn, outkey):
            sq = sb("rn_sq", [128, 8, 512], F32, st)
            rs = sb("rn_rs", [128, 512], F32, st)
            tmp = sb("rn_tmp", [128, 512], F32, st)
            for tb in range(2):
                blk = slice(tb * 512, (tb + 1) * 512)
                act(sq[:], hT[:, :, blk], AF.Square, ["hT"], ["rn_sq"])
                pb, pk = bank()
                for k in range(8):
                    mm(pb[:, :], ones[:, :], sq[:, k, :], k == 0, k == 7, ["ones", "rn_sq"], [pk])
                rstd_from(pb[:, :], rs[:], 1024.0, [pk], ["rn_rs"])
                for k in range(8):
                    V(lambda e, k=k, blk=blk: e.tensor_tensor(out=tmp[:], in0=hT[:, k, blk], in1=rs[:], op=ALU.mult),
                      ["hT", "rn_rs"], ["rn_tmp"])
                    sc = scol_fn(k)
                    if sc is None:
                        V(lambda e, k=k, tb=tb: e.tensor_scalar(out=out_fn(k, tb), in0=tmp[:], scalar1=gcol_fn(k),
                                                                scalar2=None, op0=ALU.mult), ["rn_tmp", "gmul", "fnT"], [outkey])
                    else:
                        V(lambda e, k=k, tb=tb, sc=sc: e.tensor_scalar(out=out_fn(k, tb), in0=tmp[:], scalar1=gcol_fn(k),
                                                                       scalar2=sc, op0=ALU.mult, op1=ALU.add),
                          ["rn_tmp", "gmul", "modT"], [outkey])

        def norm_gate_out(st, tag, o_acc, okey, gtile, gkey, l, zT, zkey, br):
            sq = sb(tag + "_sq", [128, 256], F32, st)
            ss = sb(tag + "_ss", [128, 4], F32, st)
            on = sb(tag + "_on", [128, 256], F32, st)
            for tt in range(8):
                act(sq[:], o_acc[:, tt, :], AF.Square, [(okey, tt)], [tag + "_sq"])
                V(lambda e: e.tensor_reduce(out=ss[:], in_=sq[:].rearrange("p (h d) -> p h d", h=4), axis=AX.X, op=ALU.add),
                  [tag + "_sq"], [tag + "_ss"])
                rstd_from(ss[:], ss[:], 64.0, [tag + "_ss"], [tag + "_ss"])
                V(lambda e, tt=tt: e.tensor_tensor(out=on[:].rearrange("p (h d) -> p h d", h=4),
                                                   in0=o_acc[:, tt, :].rearrange("p (h d) -> p h d", h=4),
                                                   in1=ss[:].unsqueeze(2).to_broadcast([128, 4, 64]), op=ALU.mult),
                  [(okey, tt), tag + "_ss"], [tag + "_on"])
                G(lambda e: e.tensor_tensor(out=on[:].rearrange("p (h d) -> p h d", h=4),
                                            in0=on[:].rearrange("p (h d) -> p h d", h=4), in1=gtile[:, l, :, :], op=ALU.mult),
                  [tag + "_on", gkey], [tag + "_on"])
                pb, pk = bank()
                for c in range(2):
                    tr(pb[:, c * 128:(c + 1) * 128], on[:, c * 128:(c + 1) * 128], ident[:], [tag + "_on", "ident"], [pk])
                V(lambda e, tt=tt, pb=pb: e.tensor_tensor(out=brT[:, br * 2:br * 2 + 2, tt * 128:(tt + 1) * 128],
                                                          in0=pb[:, 0:256].rearrange("p (c t) -> p c t", c=2),
                                                          in1=zT[:, :, tt * 128:(tt + 1) * 128], op=ALU.mult),
                  [pk, zkey], [("brT", br)])

        def attention(st, tag, qT, kT, vpad, kvmap, vslot, qblocks, keys_fn, zT, zkey, br):
            pts = [sb(tag + "_p%d" % i, [128, 512], BF16, st) for i in range(3)]
            rden = sb(tag + "_rd", [128, 512], F32, st)
            osb = sb(tag + "_o", [128, 512], F32, st)
            pc = 0
            it = 0
            for (q0, qn) in qblocks:
                keys = keys_fn(q0)
                for pr in range(2):
                    bo, bd = (0, 1) if it % 2 == 0 else (2, 3)
                    it += 1
                    psO, psD = PS[bo], PS[bd]
                    ko, kd = ("ps", bo), ("ps", bd)
                    nk = len(keys) * 2
                    ci = 0
                    for (kidx, bias_fn) in keys:
                        for hh in range(2):
                            h = 2 * pr + hh
                            pb, pk = bank()
                            mm(pb[:, 0:qn], kT[:, kvmap(h), kidx * 128:(kidx + 1) * 128], qT[:, h, q0:q0 + qn],
                               True, bias_fn is None, [tag + "_kT", tag + "_qT"], [pk])
                            if bias_fn is not None:
                                bap, bkey = bias_fn(h)
                                mm(pb[:, 0:qn], identb[:, :], bap, False, True, ["identb", bkey], [pk])
                            pt, ptk = pts[pc % 3], tag + "_p%d" % (pc % 3)
                            pc += 1
                            act(pt[:, 0:qn], pb[:, 0:qn], AF.Exp, [pk], [ptk])
                            mm(psO[:, 0:qn], vpad[:, kidx, vslot(h), :], pt[:, 0:qn], ci == 0, ci == nk - 1,
                               [tag + "_vp", ptk], [ko])
                            mm(psD[:, 0:qn], onespad[:, hh, :], pt[:, 0:qn], ci == 0, ci == nk - 1, ["onespad", ptk], [kd])
                            ci += 1
                    V(lambda e, psD=psD, qn=qn: e.reciprocal(out=rden[:, 0:qn], in_=psD[:, 0:qn]), [kd], [tag + "_rd"])
                    V(lambda e, psO=psO, qn=qn: e.tensor_tensor(out=osb[:, 0:qn], in0=psO[:, 0:qn], in1=rden[:, 0:qn],
                                                                op=ALU.mult), [ko, tag + "_rd"], [tag + "_o"])
                    G(lambda e, pr=pr, q0=q0, qn=qn: e.tensor_tensor(out=brT[:, br * 2 + pr, q0:q0 + qn], in0=osb[:, 0:qn],
                                                                     in1=zT[:, pr, q0:q0 + qn], op=ALU.mult),
                      [tag + "_o", zkey], [("brT", br)])

        def zproj(wt, wk, a, zT, zkey):
            for c in range(2):
                for tb in range(2):
                    pb, pk = bank()
                    proj_F(wt, wk, a + c * 128, 128, tb, pb, pk)
                    act(zT[:, c, tb * 512:(tb + 1) * 512], pb[:, :], AF.Silu, [pk], [zkey])

        def stage(name):
            if stop_after == name or (stop_after == "Dproj" and name == "D"):
                raise _Stop()

        def run_groups():
          for grp in (groups if stop_after != "p0" else ()):
            G(lambda e: e.memset(brT[:], 0.0), [], ["brT"])
            xin = D["xp"] if grp == 0 else D["xs"]
            yout = D["yp"] if grp == 0 else D["ys"]
            for tt in range(8):
                s_t, sk = stg[scnt[0] % 2], "stg%d" % (scnt[0] % 2)
                scnt[0] += 1
                DS(lambda e, tt=tt, s_t=s_t: e.dma_start(out=s_t[:], in_=xin[tt * 128:(tt + 1) * 128, :]), [], [sk])
                for half in range(2):
                    pb, pk = bank()
                    for q in range(4):
                        k = half * 4 + q
                        tr(pb[:, q * 128:(q + 1) * 128], s_t[:, k * 128:(k + 1) * 128], ident[:], [sk, "ident"], [pk])
                    eng = A if half == 0 else V
                    if half == 0:
                        A(lambda e, tt=tt, pb=pb: e.copy(out=hT[:, 0:4, tt * 128:(tt + 1) * 128],
                                                         in_=pb[:, :].rearrange("p (a b) -> p a b", a=4)), [pk], ["hT"])
                    else:
                        V(lambda e, tt=tt, pb=pb: e.tensor_copy(out=hT[:, 4:8, tt * 128:(tt + 1) * 128],
                                                                in_=pb[:, :].rearrange("p (a b) -> p a b", a=4)), [pk], ["hT"])
            for l in layers:
                j = grp
                with ExitStack() as ph:
                    rmsnorm_T(ph, lambda k: gmul[:, l, k, j:j + 1], lambda k: modT[:, l, k, j:j + 1],
                              lambda k, tb: hnT[:, k, tb * 512:(tb + 1) * 512], "hnT")
                S.barrier()
                stage("norm")
                if grp == groups[0] and l == layers[0]:
                    tap("hnT", hnT[:].rearrange("p a b -> p (a b)"), ["hnT"])

                with ExitStack() as ph:
                    nkt = 8 if grp == 0 else 12
                    qT = sb("a_qT", [64, 4, 1024], BF16, ph)
                    kT = sb("a_kT", [64, 2, 128 * nkt], BF16, ph)
                    vpad = sb("a_vp", [128, nkt, 4, 128], BF16, ph)
                    zT = sb("a_zT", [128, 2, 1024], BF16, ph)
                    sq = sb("a_sq", [128, 384], F32, ph)
                    ss = sb("a_ss", [128, 6], F32, ph)
                    qk = sb("a_qk", [128, 384], F32, ph)
                    qk2 = sb("a_qk2", [128, 384], F32, ph)
                    kout = sb("a_ko", [128, 128], F32, ph)
                    vout = sb("a_vo", [128, 128], F32, ph)
                    rp = sb("a_rp", [128, 2, 384], F32, ph)
                    G(lambda e: e.memset(vpad[:], 0.0), [], ["a_vp"])
                    w0, w0k = load_w(l, 0, 512)
                    w1, w1k = load_w(l, 512, 768)
                    for tt in range(8):
                        pb, pk = bank()
                        proj_T(w0, w0k, 0, 512, tt, pb, pk)
                        act(sq[:], pb[:, 0:384], AF.Square, [pk], ["a_sq"])
                        V(lambda e: e.tensor_reduce(out=ss[:], in_=sq[:].rearrange("p (h d) -> p h d", h=6), axis=AX.X,
                                                    op=ALU.add), ["a_sq"], ["a_ss"])
                        rstd_from(ss[:], ss[:], 64.0, ["a_ss"], ["a_ss"])
                        V(lambda e, pb=pb: e.tensor_tensor(out=qk[:].rearrange("p (h d) -> p h d", h=6),
                                                           in0=pb[:, 0:384].rearrange("p (h d) -> p h d", h=6),
                                                           in1=ss[:].unsqueeze(2).to_broadcast([128, 6, 64]), op=ALU.mult),
                          [pk, "a_ss"], ["a_qk"])
                        if grp == 0:
                            b_, s0 = tt // 2, (tt % 2) * 128
                            G(lambda e: e.tensor_tensor(out=kout[:].rearrange("p (h d) -> p h d", h=2),
                                                        in0=qk[:, 256:384].rearrange("p (h d) -> p h d", h=2),
                                                        in1=gkn[:, l, :, :], op=ALU.mult), ["a_qk", "gkn"], ["a_ko"])
                            DS(lambda e, b_=b_, s0=s0: e.dma_start(out=D["nak"][b_, l, s0:s0 + 128, :], in_=kout[:]),
                               ["a_ko"], [])
                            A(lambda e, pb=pb: e.copy(out=vout[:], in_=pb[:, 384:512]), [pk], ["a_vo"])
                            DS(lambda e, b_=b_, s0=s0: e.dma_start(out=D["nav"][b_, l, s0:s0 + 128, :], in_=vout[:]),
                               ["a_vo"], [])
                        G(lambda e: e.tensor_tensor(out=qk[:].rearrange("p (h d) -> p h d", h=6),
                                                    in0=qk[:].rearrange("p (h d) -> p h d", h=6), in1=gqk[:, l, :, :],
                                                    op=ALU.mult), ["a_qk", "gqk"], ["a_qk"])
                        src, srck = qk, "a_qk"
                        if grp == 1:
                            DS(lambda e, tt=tt: e.dma_start(out=rp[:], in_=D["c_rope"][:, tt * 128:(tt + 1) * 128, :]
                                                            .rearrange("a p f -> p a f")), [], ["a_rp"])
                            V(lambda e: e.tensor_tensor(out=qk2[:], in0=qk[:], in1=rp[:, 0, :], op=ALU.mult),
                              ["a_qk", "a_rp"], ["a_qk2"])
                            qv = qk[:].rearrange("p (g s d) -> p g s d", s=2, d=16)
                            sv = rp[:, 1, :].rearrange("p (g s d) -> p g s d", s=2, d=16)
                            G(lambda e, qv=qv, sv=sv: e.tensor_tensor(
                                out=sq[:].rearrange("p (g s d) -> p g s d", s=2, d=16)[:, :, 0, :], in0=qv[:, :, 1, :],
                                in1=sv[:, :, 0, :], op=ALU.mult), ["a_qk", "a_rp"], ["a_sq"])
                            G(lambda e, qv=qv, sv=sv: e.tensor_tensor(
                                out=sq[:].rearrange("p (g s d) -> p g s d", s=2, d=16)[:, :, 1, :], in0=qv[:, :, 0, :],
                                in1=sv[:, :, 1, :], op=ALU.mult), ["a_qk", "a_rp"], ["a_sq"])
                            V(lambda e: e.tensor_tensor(out=qk2[:], in0=qk2[:], in1=sq[:], op=ALU.add),
                              ["a_qk2", "a_sq"], ["a_qk2"])
                            src, srck = qk2, "a_qk2"
                        pq, pqk = bank()
                        for h in range(4):
                            tr(pq[0:64, h * 128:(h + 1) * 128], src[:, h * 64:(h + 1) * 64], ident[:], [srck, "ident"], [pqk])
                        A(lambda e, tt=tt, pq=pq: e.copy(out=qT[:, :, tt * 128:(tt + 1) * 128],
                                                         in_=pq[0:64, :].rearrange("p (a b) -> p a b", a=4)), [pqk], ["a_qT"])
                        pk2, pk2k = bank()
                        for h in range(2):
                            tr(pk2[0:64, h * 128:(h + 1) * 128], src[:, 256 + h * 64:256 + (h + 1) * 64], ident[:],
                               [srck, "ident"], [pk2k])
                        V(lambda e, tt=tt, pk2=pk2: e.tensor_copy(out=kT[:, :, tt * 128:(tt + 1) * 128],
                                                                  in_=pk2[0:64, 0:256].rearrange("p (a b) -> p a b", a=2)),
                          [pk2k], ["a_kT"])
                        for kv in range(2):
                            for pos in range(2):
                                A(lambda e, tt=tt, kv=kv, pos=pos, pb=pb: e.copy(
                                    out=vpad[:, tt, kv * 2 + pos, pos * 64:(pos + 1) * 64],
                                    in_=pb[:, 384 + kv * 64:384 + (kv + 1) * 64]), [pk], ["a_vp"])
                    if grp == 1:
                        for ct in range(4):
                            s_t, sk = stg[scnt[0] % 2], "stg%d" % (scnt[0] % 2)
                            scnt[0] += 1
                            DS(lambda e, ct=ct, s_t=s_t: e.dma_start(out=s_t[:, 0:128], in_=D["cak"][l, ct * 128:(ct + 1) * 128, :]),
                               [], [sk])
                            DS(lambda e, ct=ct, s_t=s_t: e.dma_start(out=s_t[:, 128:256], in_=D["cav"][l, ct * 128:(ct + 1) * 128, :]),
                               [], [sk])
                            pk2, pk2k = bank()
                            for h in range(2):
                                tr(pk2[0:64, h * 128:(h + 1) * 128], s_t[:, h * 64:(h + 1) * 64], ident[:], [sk, "ident"], [pk2k])
                            V(lambda e, ct=ct, pk2=pk2: e.tensor_copy(
                                out=kT[:, :, (8 + ct) * 128:(9 + ct) * 128],
                                in_=pk2[0:64, 0:256].rearrange("p (a b) -> p a b", a=2)), [pk2k], ["a_kT"])
                            for kv in range(2):
                                for pos in range(2):
                                    A(lambda e, ct=ct, kv=kv, pos=pos, s_t=s_t: e.copy(
                                        out=vpad[:, 8 + ct, kv * 2 + pos, pos * 64:(pos + 1) * 64],
                                        in_=s_t[:, 128 + kv * 64:128 + (kv + 1) * 64]), [sk], ["a_vp"])
                    zproj(w1, w1k, 0, zT, "a_zT")
                    if grp == 0:
                        qblocks = [(b_ * 256, 256) for b_ in range(4)]
                        keys_fn = lambda q0: [((q0 // 128) + i, None) for i in range(2)]
                    else:
                        qblocks = [(0, 512), (512, 512)]
                        keys_fn = lambda q0: [(i, None) for i in range(12)]
                    attention(ph, "a", qT, kT, vpad, lambda h: h // 2, lambda h: (h // 2) * 2 + (h % 2), qblocks, keys_fn,
                              zT, "a_zT", 0)
                S.barrier()
                stage("A")
                if grp == groups[0] and l == layers[0]:
                    tap("brA", brT[:, 0:2, :].rearrange("p a b -> p (a b)"), [("brT", 0)])

                with ExitStack() as ph:
                    nkt = 8 if grp == 0 else 12
                    qT = sb("d_qT", [64, 4, 1024], BF16, ph)
                    kT = sb("d_kT", [64, 4, 128 * nkt], BF16, ph)
                    vpad = sb("d_vp", [128, nkt, 4, 128], BF16, ph)
                    zT = sb("d_zT", [128, 2, 1024], BF16, ph)
                    kout = sb("d_ko", [128, 256], F32, ph)
                    vout = sb("d_vo", [128, 256], F32, ph)
                    G(lambda e: e.memset(vpad[:], 0.0), [], ["d_vp"])
                    w0, w0k = load_w(l, 2832, 3344)
                    w1, w1k = load_w(l, 3344, 3856)
                    for c in range(2):
                        for tb in range(2):
                            blk = slice(tb * 512, (tb + 1) * 512)
                            pb, pk = bank()
                            proj_F(w0, w0k, c * 128, 128, tb, pb, pk)
                            for hf in range(2):
                                V(lambda e, c=c, hf=hf, blk=blk, pb=pb: e.tensor_scalar(
                                    out=qT[:, 2 * c + hf, blk], in0=pb[hf * 64:(hf + 1) * 64, :], scalar1=0.125, scalar2=None,
                                    op0=ALU.mult), [pk], ["d_qT"])
                            pb, pk = bank()
                            proj_F(w0, w0k, 256 + c * 128, 128, tb, pb, pk)
                            for hf in range(2):
                                A(lambda e, c=c, hf=hf, blk=blk, pb=pb: e.copy(out=kT[:, 2 * c + hf, blk],
                                                                               in_=pb[hf * 64:(hf + 1) * 64, :]), [pk], ["d_kT"])
                    for tt in range(8):
                        pb, pk = bank()
                        proj_T(w1, w1k, 0, 256, tt, pb, pk)
                        for h in range(4):
                            A(lambda e, tt=tt, h=h, pb=pb: e.copy(out=vpad[:, tt, h, (h % 2) * 64:(h % 2) * 64 + 64],
                                                                  in_=pb[:, h * 64:(h + 1) * 64]), [pk], ["d_vp"])
                        if grp == 0:
                            b_, s0 = tt // 2, (tt % 2) * 128
                            A(lambda e, pb=pb: e.copy(out=vout[:], in_=pb[:, 0:256]), [pk], ["d_vo"])
                            DS(lambda e, b_=b_, s0=s0: e.dma_start(out=D["nnv"][b_, l, s0:s0 + 128, :], in_=vout[:]),
                               ["d_vo"], [])
                            pb2, pk2k = bank()
                            proj_T(w0, w0k, 256, 512, tt, pb2, pk2k)
                            V(lambda e, pb2=pb2: e.tensor_copy(out=kout[:], in_=pb2[:, 0:256]), [pk2k], ["d_ko"])
                            DS(lambda e, b_=b_, s0=s0: e.dma_start(out=D["nnk"][b_, l, s0:s0 + 128, :], in_=kout[:]),
                               ["d_ko"], [])
                    if grp == 1:
                        for ct in range(4):
                            s_t, sk = stg[scnt[0] % 2], "stg%d" % (scnt[0] % 2)
                            scnt[0] += 1
                            DS(lambda e, ct=ct, s_t=s_t: e.dma_start(out=s_t[:, 0:256], in_=D["cnk"][l, ct * 128:(ct + 1) * 128, :]),
                               [], [sk])
                            DS(lambda e, ct=ct, s_t=s_t: e.dma_start(out=s_t[:, 256:512], in_=D["cnv"][l, ct * 128:(ct + 1) * 128, :]),
                               [], [sk])
                            pk2, pk2k = bank()
                            for h in range(4):
                                tr(pk2[0:64, h * 128:(h + 1) * 128], s_t[:, h * 64:(h + 1) * 64], ident[:], [sk, "ident"], [pk2k])
                            V(lambda e, ct=ct, pk2=pk2: e.tensor_copy(
                                out=kT[:, :, (8 + ct) * 128:(9 + ct) * 128],
                                in_=pk2[0:64, :].rearrange("p (a b) -> p a b", a=4)), [pk2k], ["d_kT"])
                            for h in range(4):
                                A(lambda e, ct=ct, h=h, s_t=s_t: e.copy(
                                    out=vpad[:, 8 + ct, h, (h % 2) * 64:(h % 2) * 64 + 64],
                                    in_=s_t[:, 256 + h * 64:256 + (h + 1) * 64]), [sk], ["d_vp"])
                    zproj(w1, w1k, 256, zT, "d_zT")
                    if stop_after == "Dproj":
                        pass
                    elif grp == 0:
                        qblocks = [(b_ * 256, 256) for b_ in range(4)]
                        keys_fn = lambda q0: [((q0 // 128) + i, None) for i in range(2)]
                        attention(ph, "d", qT, kT, vpad, lambda h: h, lambda h: h, qblocks, keys_fn, zT, "d_zT", 3)
                    else:
                        nbt = [sb("d_nb%d" % i, [128, 6, 4, 128], BF16, ph) for i in range(2)]
                        pairs = _NA[3]
                        qblocks = [(qt * 128, 128) for qt in range(8)]
                        cache = {}

                        def keys_fn(q0):
                            qt = q0 // 128
                            pis = [pi for pi, (a, b) in enumerate(pairs) if a == qt]
                            nb, nbk = nbt[qt % 2], "d_nb%d" % (qt % 2)
                            DG(lambda e: e.dma_start(out=nb[:, 0:len(pis), :, :],
                                                     in_=D["c_nab"][l, pis[0]:pis[0] + len(pis)].rearrange("a h k q -> k a h q")),
                               [], [nbk])
                            out = []
                            for ii, pi in enumerate(pis):
                                out.append((pairs[pi][1], (lambda h, ii=ii: (nb[:, ii, h, :], nbk))))
                            out += [(8 + i, None) for i in range(4)]
                            return out
                        attention(ph, "d", qT, kT, vpad, lambda h: h, lambda h: h, qblocks, keys_fn, zT, "d_zT", 3)
                S.barrier()
                stage("D")
                if grp == groups[0] and l == layers[0]:
                    tap("brD", brT[:, 6:8, :].rearrange("p a b -> p (a b)"), [("brT", 3)])

                with ExitStack() as ph:
                    qTm = sb("r_qTm", [128, 2, 2, 1024], BF16, ph)
                    kTr = sb("r_kT", [128, 2, 1024], BF16, ph)
                    kdp = sb("r_kdp", [128, 4, 2, 128], BF16, ph)
                    vtk = sb("r_v", [128, 8, 256], BF16, ph)
                    zT = sb("r_zT", [128, 2, 1024], BF16, ph)
                    U = sb("r_U", [128, 8, 256], F32, ph)
                    Sin = sb("r_Sin", [128, 8, 256], F32, ph)
                    Sfin = sb("r_Sfin", [128, 256], F32, ph)
                    Sinb = sb("r_Sinb", [128, 256], BF16, ph)
                    qkm = sb("r_qkm", [128, 4, 128], BF16, ph)
                    qdm = sb("r_qdm", [128, 4, 2, 128], BF16, ph)
                    oacc = sb("r_oacc", [128, 8, 256], F32, ph)
                    tmpS = sb("r_tmpS", [128, 256], F32, ph)
                    R2 = int(os.environ.get("R2", "511"))
                    if R2 & 1:
                        G(lambda e: e.memset(qTm[:], 0.0), [], ["r_qTm"])
                        G(lambda e: e.memset(kdp[:], 0.0), [], ["r_kdp"])
                    w0, w0k = load_w(l, 1808, 2320)
                    w1, w1k = load_w(l, 2320, 2832)
                    for c in range(2):
                        for tb in range(2):
                            blk = slice(tb * 512, (tb + 1) * 512)
                            pb, pk = bank()
                            if R2 & 2:
                                proj_F(w0, w0k, c * 128, 128, tb, pb, pk)
                            for hf in (range(2) if R2 & 2 else []):
                                rows = slice(hf * 64, (hf + 1) * 64)
                                if hf == 0:
                                    A(lambda e, c=c, hf=hf, blk=blk, rows=rows, pb=pb: e.copy(out=qTm[rows, c, hf, blk], in_=pb[rows, :]),
                                      [pk], ["r_qTm"])
                                else:
                                    V(lambda e, c=c, hf=hf, blk=blk, rows=rows, pb=pb: e.tensor_copy(out=qTm[rows, c, hf, blk], in_=pb[rows, :]),
                                      [pk], ["r_qTm"])
                            pb, pk = bank()
                            if R2 & 4:
                                proj_F(w0, w0k, 256 + c * 128, 128, tb, pb, pk)
                                V(lambda e, c=c, blk=blk, pb=pb: e.tensor_copy(out=kTr[:, c, blk], in_=pb[:, :]), [pk], ["r_kT"])
                    for tt in range(8):
                        pb, pk = bank()
                        if R2 & 8:
                            proj_T(w0, w0k, 256, 512, tt, pb, pk)
                        for d in (range(2) if R2 & 8 else []):
                            for h in range(4):
                                hf = h % 2
                                V(lambda e, d=d, h=h, hf=hf, pb=pb: e.tensor_tensor(
                                    out=kdp[:, h, d, hf * 64:(hf + 1) * 64], in0=pb[:, h * 64:(h + 1) * 64],
                                    in1=kdt[:, h, d, :], op=ALU.mult), [pk, "kdt"], ["r_kdp"])
                        pb, pk = bank()
                        if R2 & 16:
                            proj_T(w1, w1k, 0, 256, tt, pb, pk)
                            A(lambda e, tt=tt, pb=pb: e.copy(out=vtk[:, tt, :], in_=pb[:, 0:256]), [pk], [("r_v", tt)])
                        pb, pk = bank()
                        for d in (range(2) if R2 & 32 else []):
                            for pr in range(2):
                                cs_ = slice((d * 2 + pr) * 64, (d * 2 + pr + 1) * 64)
                                for hf in range(2):
                                    h = 2 * pr + hf
                                    mm(pb[:, cs_], kdp[:, h, d, :], vtk[:, tt, h * 64:(h + 1) * 64], hf == 0, hf == 1,
                                       ["r_kdp", ("r_v", tt)], [pk])
                        if R2 & 32:
                            V(lambda e, tt=tt, pb=pb: e.tensor_copy(out=U[:, tt, :], in_=pb[:, 0:256]), [pk], [("r_U", tt)])
                    if R2 & 64:
                        zproj(w1, w1k, 256, zT, "r_zT")
                    seqs = [(2 * b_, 2 * b_ + 1) for b_ in range(4)] if grp == 0 else [tuple(range(8))]
                    cdv = cdt[:].rearrange("p a d -> p (a d)")
                    FW, BW = slice(0, 128), slice(128, 256)
                    for si, tiles in enumerate(seqs if R2 & 128 else []):
                        first, lastt = tiles[0], tiles[-1]
                        if grp == 0:
                            G(lambda e, first=first: e.memset(Sin[:, first, FW], 0.0), [], [("r_Sin", "f", first)])
                            G(lambda e, lastt=lastt: e.memset(Sin[:, lastt, BW], 0.0), [], [("r_Sin", "b", lastt)])
                        else:
                            DS(lambda e: e.dma_start(out=Sin[:, 0, FW].rearrange("p (a v) -> p a v", a=2),
                                                     in_=D["sr"][l, 0].rearrange("(a b) k v -> (b k) a v", b=2)), [], [("r_Sin", "f", 0)])
                            DS(lambda e: e.dma_start(out=Sin[:, 7, BW].rearrange("p (a v) -> p a v", a=2),
                                                     in_=D["sr"][l, 1].rearrange("(a b) k v -> (b k) a v", b=2)), [], [("r_Sin", "b", 7)])
                        for tt in tiles:
                            dst = Sin[:, tt + 1, FW] if tt != lastt else Sfin[:, FW]
                            dk = ("r_Sin", "f", tt + 1) if tt != lastt else ("r_Sfin", "f")
                            V(lambda e, tt=tt: e.tensor_tensor(out=tmpS[:, FW], in0=Sin[:, tt, FW], in1=cdv[:, FW], op=ALU.mult),
                              [("r_Sin", "f", tt), "cdt"], [("r_tmpS", "f")])
                            V(lambda e, tt=tt, dst=dst: e.tensor_tensor(out=dst, in0=tmpS[:, FW], in1=U[:, tt, FW], op=ALU.add),
                              [("r_tmpS", "f"), ("r_U", tt)], [dk])
                        for tt in reversed(tiles):
                            dst = Sin[:, tt - 1, BW] if tt != first else Sfin[:, BW]
                            dk = ("r_Sin", "b", tt - 1) if tt != first else ("r_Sfin", "b")
                            G(lambda e, tt=tt: e.tensor_tensor(out=tmpS[:, BW], in0=Sin[:, tt, BW], in1=cdv[:, BW], op=ALU.mult),
                              [("r_Sin", "b", tt), "cdt"], [("r_tmpS", "b")])
                            G(lambda e, tt=tt, dst=dst: e.tensor_tensor(out=dst, in0=tmpS[:, BW], in1=U[:, tt, BW], op=ALU.add),
                              [("r_tmpS", "b"), ("r_U", tt)], [dk])
                        if grp == 0 and R2 & 256:
                            b_ = si
                            DS(lambda e, b_=b_: e.dma_start(out=D["nsr"][b_, l, 0].rearrange("(a b) k v -> (b k) a v", b=2),
                                                            in_=Sfin[:, FW].rearrange("p (a v) -> p a v", a=2)),
                               [("r_Sfin", "f")], [])
                            DS(lambda e, b_=b_: e.dma_start(out=D["nsr"][b_, l, 1].rearrange("(a b) k v -> (b k) a v", b=2),
                                                            in_=Sfin[:, BW].rearrange("p (a v) -> p a v", a=2)),
                               [("r_Sfin", "b")], [])
                    _m = int(os.environ.get("L3", "31"))
                    for tt in range(int(os.environ.get("L3N", "8")) if _CL >= 3 else 0):
                        ts_ = slice(tt * 128, (tt + 1) * 128)
                        pb, pk = bank()
                        for h in (range(4) if _m & 1 else []):
                            mm(pb[:, h * 128:(h + 1) * 128], kTr[:, h // 2, ts_], qTm[:, h // 2, h % 2, ts_], True, True,
                               ["r_kT", "r_qTm"], [pk])
                        if _m & 2:
                            V(lambda e, pb=pb: e.tensor_tensor(out=qkm[:].rearrange("p a b -> p (a b)"), in0=pb[:, :],
                                                               in1=dct[:].rearrange("p a b -> p (a b)"), op=ALU.mult),
                              [pk, "dct"], ["r_qkm"])
                        for d in (range(2) if _m & 4 else []):
                            G(lambda e, d=d, ts_=ts_: e.tensor_tensor(
                                out=qdm[:, :, d, :], in0=qTm[:, :, :, ts_].rearrange("p c f t -> p (c f) t"),
                                in1=rqt[:, d, :, :], op=ALU.mult), ["r_qTm", "rqt"], ["r_qdm"])
                        if _m & 8:
                            A(lambda e, tt=tt: e.copy(out=Sinb[:], in_=Sin[:, tt, :]), [("r_Sin", "f", tt), ("r_Sin", "b", tt)], ["r_Sinb"])
                        pb, pk = bank()
                        for h in (range(4) if _m & 16 else []):
                            pr = h // 2
                            hc = slice(h * 64, (h + 1) * 64)
                            mm(pb[:, hc], qkm[:, h, :], vtk[:, tt, hc], True, False, ["r_qkm", ("r_v", tt)], [pk])
                            mm(pb[:, hc], qdm[:, h, 0, :], Sinb[:, pr * 64:(pr + 1) * 64], False, False, ["r_qdm", "r_Sinb"], [pk])
                            mm(pb[:, hc], qdm[:, h, 1, :], Sinb[:, (2 + pr) * 64:(3 + pr) * 64], False, True, ["r_qdm", "r_Sinb"], [pk])
                        if _m & 16:
                            A(lambda e, tt=tt, pb=pb: e.copy(out=oacc[:, tt, :], in_=pb[:, 0:256]), [pk], [("r_oacc", tt)])
                    if _CL >= 4:
                        norm_gate_out(ph, "r", oacc, "r_oacc", grn, "grn", l, zT, "r_zT", 2)
                S.barrier()
                stage("C")
                if grp == groups[0] and l == layers[0]:
                    tap("brC", brT[:, 4:6, :].rearrange("p a b -> p (a b)"), [("brT", 2)])

                with ExitStack() as ph:
                  if _BL >= 1:
                      gx = [sb("g_x%d" % i, [128, 1024], F32, ph) for i in range(1)]
                      gy = [sb("g_y%d" % i, [128, 1024], F32, ph) for i in range(1)]
                      qkT = sb("g_qkT", [128, 4, 1024], F32, ph)
                      ktok = sb("g_ktok", [128, 8, 256], F32, ph)
                      vtok = sb("g_vtok", [128, 8, 256], F32, ph)
                      zT = sb("g_zT", [128, 2, 1024], BF16, ph)
                      ab = sb("g_ab", [128, 8, 16], F32, ph)
                      beta = sb("g_beta", [128, 8, 8], F32, ph)
                      nbeta = sb("g_nbeta", [128, 8, 8], F32, ph)
                      la = sb("g_la", [128, 8, 8], F32, ph)
                      gc = sb("g_gc", [128, 8, 8], F32, ph)
                      ngc = sb("g_ngc", [128, 8, 8], F32, ph)
                      egc = sb("g_egc", [128, 8, 8], F32, ph)
                      bege = sb("g_bege", [128, 8, 8], F32, ph)
                      oacc = sb("g_oacc", [128, 8, 256], F32, ph)
                      dg = sb("g_dg", [128, 4, 128], F32, ph)
                      Dm = sb("g_Dm", [128, 4, 128], F32, ph)
                      DT = sb("g_DT", [128, 4, 128], F32, ph)
                      Bm = [sb("g_B%d" % i, [128, 4, 128], F32, ph) for i in range(2)]
                      BTm = [sb("g_BT%d" % i, [128, 4, 128], F32, ph) for i in range(2)]
                      Ym = [sb("g_Y%d" % i, [128, 4, 128], F32, ph) for i in range(2)]
                      qkd = sb("g_qkd", [128, 4, 128], F32, ph)
                      vb = sb("g_vb", [128, 4, 64], F32, ph)
                      kbgp = sb("g_kbgp", [128, 4, 128], F32, ph)
                      kdcp = sb("g_kdcp", [128, 4, 128], F32, ph)
                      kds = sb("g_kds", [128, 4], F32, ph)
                      eg2 = sb("g_eg2", [128, 2], F32, ph)
                      wTm = sb("g_wTm", [128, 4, 128], F32, ph)

                      Sst = sb("g_S", [128, 2, 2, 64], F32, ph)
                      G(lambda e: e.memset(kbgp[:], 0.0), [], ["g_kbgp"])
                      G(lambda e: e.memset(kdcp[:], 0.0), [], ["g_kdcp"])
                      G(lambda e: e.memset(wTm[:], 0.0), [], ["g_wTm"])
                      wA, wAk = load_w(l, 768, 1280)
                      wB, wBk = load_w(l, 1280, 1552)
                      wC, wCk = load_w(l, 1552, 1808)
                      nseq = 4 if grp == 0 else 1
                      L = 1024 // nseq
                      for c6 in range(6):
                          xt, xk = gx[0], "g_x0"
                          yt, yk = gy[0], "g_y0"
                          wt, wk, a = (wA, wAk, c6 * 128) if c6 < 4 else (wB, wBk, (c6 - 4) * 128)
                          for tb in range(2):
                              pb, pk = bank()
                              proj_F(wt, wk, a, 128, tb, pb, pk)
                              A(lambda e, tb=tb, pb=pb, xt=xt: e.copy(out=xt[:, tb * 512:(tb + 1) * 512], in_=pb[:, :]), [pk], [xk])
                          x3 = xt[:].rearrange("p (s t) -> p s t", s=nseq)
                          y3 = yt[:].rearrange("p (s t) -> p s t", s=nseq)
                          V(lambda e, c6=c6, xt=xt, yt=yt: e.tensor_scalar(out=yt[:], in0=xt[:], scalar1=cwT[:, l, c6, 2:3],
                                                                           scalar2=None, op0=ALU.mult), [xk, "cwT"], [yk])
                          for jj in (0, 1, 3, 4):
                              dsh = jj - 2
                              lo, hi = max(0, -dsh), L - max(0, dsh)
                              V(lambda e, c6=c6, jj=jj, x3=x3, y3=y3, lo=lo, hi=hi, dsh=dsh: e.scalar_tensor_tensor(
                                  out=y3[:, :, lo:hi], in0=x3[:, :, lo + dsh:hi + dsh], scalar=cwT[:, l, c6, jj:jj + 1],
                                  in1=y3[:, :, lo:hi], op0=ALU.mult, op1=ALU.add), [xk, yk, "cwT"], [yk])
                          act(yt[:], yt[:], AF.Silu, [yk], [yk])
                          if c6 < 4:
                              for tb in range(2):
                                  blk = slice(tb * 512, (tb + 1) * 512)
                                  sqv = xt[:, 0:512]
                                  rsv = xt[:, 512:1024]
                                  act(sqv, yt[:, blk], AF.Square, [yk], [xk])
                                  pb, pk = bank()
                                  mm(pb[:, :], onesblk[:, :], sqv, True, True, ["onesblk", xk], [pk])
                                  act(rsv, pb[:, :], AF.Sqrt, [pk], [xk], bias=EPS, scale=1.0)
                                  V(lambda e, rsv=rsv: e.reciprocal(out=rsv, in_=rsv), [xk], [xk])
                                  if c6 < 2:
                                      V(lambda e, c6=c6, blk=blk, yt=yt, rsv=rsv: e.scalar_tensor_tensor(
                                          out=qkT[:, c6, blk], in0=yt[:, blk], scalar=0.125, in1=rsv, op0=ALU.mult,
                                          op1=ALU.mult), [yk, xk], [("g_qkT", c6)])
                                  else:
                                      V(lambda e, c6=c6, blk=blk, yt=yt, rsv=rsv: e.tensor_tensor(out=qkT[:, c6, blk], in0=yt[:, blk], in1=rsv,
                                                                                        op=ALU.mult), [yk, xk], [("g_qkT", c6)])
                          if c6 >= 2:
                              srcT = qkT[:, c6, :] if c6 < 4 else yt[:]
                              srck = ("g_qkT", c6) if c6 < 4 else yk
                              dst = ktok if c6 < 4 else vtok
                              dstk = "g_ktok" if c6 < 4 else "g_vtok"
                              cc = c6 % 2
                              for half in range(2):
                                  pb, pk = bank()
                                  for q in range(4):
                                      tt = half * 4 + q
                                      tr(pb[:, q * 128:(q + 1) * 128], srcT[:, tt * 128:(tt + 1) * 128], ident[:], [srck, "ident"], [pk])
                                  A(lambda e, half=half, pb=pb, dst=dst, cc=cc: e.copy(
                                      out=dst[:, half * 4:half * 4 + 4, cc * 128:(cc + 1) * 128],
                                      in_=pb[:, :].rearrange("p (a b) -> p a b", a=4)), [pk], [dstk])
                      zproj(wC, wCk, 0, zT, "g_zT")
                      if _GB >= 2:
                        pb, pk = bank()
                        for tt in range(8):
                            for k in range(8):
                                mm(pb[:, tt * 16:(tt + 1) * 16], hnT[:, k, tt * 128:(tt + 1) * 128], wB[:, k, 256:272], k == 0, k == 7,
                                   [wBk, "hnT"], [pk])
                        V(lambda e, pb=pb: e.tensor_copy(out=ab[:].rearrange("p a b -> p (a b)"), in_=pb[:, 0:128]), [pk], ["g_ab"])
                        act(beta[:], ab[:, :, 0:8], AF.Sigmoid, ["g_ab"], ["g_beta"])
                        V(lambda e: e.tensor_scalar(out=nbeta[:], in0=beta[:], scalar1=-1.0, scalar2=None, op0=ALU.mult),
                          ["g_beta"], ["g_nbeta"])
                        V(lambda e: e.tensor_tensor(out=la[:], in0=ab[:, :, 8:16], in1=dtb[:, l, :].unsqueeze(1).to_broadcast([128, 8, 8]),
                                                    op=ALU.add), ["g_ab", "dtb"], ["g_la"])
                        V(lambda e: e.tensor_scalar(out=la[:], in0=la[:], scalar1=30.0, scalar2=None, op0=ALU.min), ["g_la"], ["g_la"])
                        act(la[:], la[:], AF.Exp, ["g_la"], ["g_la"])
                        act(la[:], la[:], AF.Ln, ["g_la"], ["g_la"], bias=1.0, scale=1.0)
                        V(lambda e: e.tensor_tensor(out=la[:], in0=la[:], in1=nega[:, l, :].unsqueeze(1).to_broadcast([128, 8, 8]),
                                                    op=ALU.mult), ["g_la", "nega"], ["g_la"])
                        pb, pk = bank()
                        for tt in range(8):
                            for d in range(2):
                                mm(pb[:, tt * 8 + d * 4:tt * 8 + d * 4 + 4], masks[:, d, :], la[:, tt, d * 4:(d + 1) * 4], True, True,
                                   ["masks", "g_la"], [pk])
                        V(lambda e, pb=pb: e.tensor_copy(out=gc[:].rearrange("p a b -> p (a b)"), in_=pb[:, 0:64]), [pk], ["g_gc"])
                        V(lambda e: e.tensor_scalar(out=ngc[:], in0=gc[:], scalar1=-1.0, scalar2=None, op0=ALU.mult), ["g_gc"], ["g_ngc"])
                        act(egc[:], gc[:], AF.Exp, ["g_gc"], ["g_egc"])
                        V(lambda e: e.tensor_tensor(out=bege[:], in0=beta[:], in1=egc[:], op=ALU.mult), ["g_beta", "g_egc"], ["g_bege"])
                        tap("g_gc", gc[:].rearrange("p a b -> p (a b)"), ["g_gc"])
                        tap("g_qkT", qkT[:].rearrange("p a b -> p (a b)"), ["g_qkT"])
                      if _GB >= 3:
                        S.barrier()
                        qkz = gx[0][:].rearrange("p (c f t) -> p c f t", c=4, f=2)
                        usb = gy[0][:, 0:256]
                        vnew = gy[0][:, 256:512]
                        o2s = gy[0][:, 512:768]
                        otmp = gy[0][:, 768:1024]
                        G(lambda e: e.memset(gx[0][:], 0.0), [], ["g_qkz"])
                        seqs = [(2 * b_, 2 * b_ + 1) for b_ in range(4)] if grp == 0 else [tuple(range(8))]
                        qn_ = [0]
                        for d in range(2):
                            last = 127 if d == 0 else 0
                            for si, tiles in enumerate(seqs):
                                order = tiles if d == 0 else tuple(reversed(tiles))
                                if grp == 0:
                                    V(lambda e, d=d: e.memset(Sst[:, d, :, :], 0.0), [], [("g_S", d)])
                                else:
                                    DS(lambda e, d=d: e.dma_start(out=Sst[:, d, :, :],
                                                                  in_=D["sg"][l, d].rearrange("(a b) k v -> (b k) a v", b=2)),
                                       [], [("g_S", d)])
                                for tt in order:
                                    S.mute = False
                                    qn_[0] += 1
                                    if qn_[0] > _GBQ:
                                        continue
                                    ts_ = slice(tt * 128, (tt + 1) * 128)
                                    u0 = d * 4
                                    for h in range(4):
                                        G(lambda e, h=h, tt=tt, u0=u0: e.tensor_scalar(out=dg[:, h, :], in0=ident[:],
                                                                                     scalar1=gc[:, tt, u0 + h:u0 + h + 1], scalar2=None,
                                                                                     op0=ALU.mult), ["ident", "g_gc"], ["g_dg"])
                                    psN, kN = PS[0], ("ps", 0)
                                    psP, kP = PS[1], ("ps", 1)
                                    for h in range(4):
                                        hs = slice(h * 128, (h + 1) * 128)
                                        mm(psN[:, hs], ones[:, :], dg[:, h, :], True, False, ["ones", "g_dg"], [kN])
                                        mm(psN[:, hs], ident[:, :], masks[:, 4 + d, :], False, True, ["ident", "masks"], [kN])
                                        mm(psP[:, hs], ones[:, :], dg[:, h, :], True, False, ["ones", "g_dg"], [kP])
                                        mm(psP[:, hs], ident[:, :], masks[:, 2 + d, :], False, True, ["ident", "masks"], [kP])
                                    for h in range(4):
                                        hs = slice(h * 128, (h + 1) * 128)
                                        act(Dm[:, h, :], psP[:, hs], AF.Exp, [kP, "g_gc"], ["g_Dm"], bias=gc[:, tt, u0 + h:u0 + h + 1], scale=-1.0)
                                        act(DT[:, h, :], psN[:, hs], AF.Exp, [kN, "g_ngc"], ["g_DT"], bias=ngc[:, tt, u0 + h:u0 + h + 1], scale=1.0)
                                        act(kds[:, h:h + 1], psN[:, h * 128 + last:h * 128 + last + 1], AF.Exp, [kN, "g_ngc"], ["g_kds"],
                                            bias=ngc[:, tt, u0 + h:u0 + h + 1], scale=1.0)
                                    for pr in range(2):
                                        for hf in range(2):
                                            h = 2 * pr + hf
                                            A(lambda e, pr=pr, hf=hf, h=h: e.activation(
                                                out=eg2[hf * 64:(hf + 1) * 64, pr:pr + 1],
                                                in_=psN[hf * 64:(hf + 1) * 64, h * 128 + last:h * 128 + last + 1], func=AF.Exp),
                                              [kN], ["g_eg2"])
                                    S.mute = _GBS < 2
                                    psKx = [(PS[2], ("ps", 2)), (PS[3], ("ps", 3))]
                                    psQx = [(PS[0], ("ps", 0)), (PS[1], ("ps", 1))]
                                    for c4 in range(4):
                                        for hf in range(2):
                                            rows = slice(hf * 64, (hf + 1) * 64)
                                            if hf == 0:
                                                G(lambda e, c4=c4, hf=hf, rows=rows, ts_=ts_: e.tensor_copy(out=qkz[rows, c4, hf, :], in_=qkT[rows, c4, ts_]),
                                                  ["g_qkT"], ["g_qkz"])
                                            else:
                                                A(lambda e, c4=c4, hf=hf, rows=rows, ts_=ts_: e.copy(out=qkz[rows, c4, hf, :], in_=qkT[rows, c4, ts_]),
                                                  ["g_qkT"], ["g_qkz"])
                                    for h in range(4):
                                        hs = slice((h // 2) * 128, (h // 2 + 1) * 128)
                                        kfull = qkT[:, 2 + h // 2, ts_]
                                        mm(psKx[h % 2][0][:, hs], kfull, qkz[:, 2 + h // 2, h % 2, :], True, True, ["g_qkT", "g_qkz"], [psKx[h % 2][1]])
                                        mm(psQx[h % 2][0][:, hs], kfull, qkz[:, h // 2, h % 2, :], True, True, ["g_qkT", "g_qkz"], [psQx[h % 2][1]])
                                    for h in range(4):
                                        hs = slice((h // 2) * 128, (h // 2 + 1) * 128)
                                        psK, kK = psKx[h % 2]
                                        V(lambda e, h=h, hs=hs, tt=tt, u0=u0, psK=psK: e.scalar_tensor_tensor(
                                            out=Bm[0][:, h, :], in0=psK[:, hs], scalar=nbeta[:, tt, u0 + h:u0 + h + 1], in1=Dm[:, h, :],
                                            op0=ALU.mult, op1=ALU.mult), [kK, "g_nbeta", "g_Dm"], ["g_B0"])
                                    for h in range(4):
                                        psQ, kQ = psQx[h % 2]
                                        V(lambda e, h=h, psQ=psQ: e.tensor_tensor(
                                            out=qkd[:, h, :], in0=psQ[:, (h // 2) * 128:(h // 2 + 1) * 128],
                                            in1=DT[:, h, :], op=ALU.mult), [kQ, "g_DT"], ["g_qkd"])
                                    S.mute = _GBS < 3
                                    pb, pk = bank()
                                    for h in range(4):
                                        tr(pb[:, h * 128:(h + 1) * 128], Bm[0][:, h, :], ident[:], ["g_B0", "ident"], [pk])
                                    _gx = int(os.environ.get("GBX", "7"))
                                    if _gx & 2:
                                        V(lambda e, pb=pb: e.tensor_copy(out=BTm[0][:].rearrange("p a b -> p (a b)"), in_=pb[:, :]), [pk], ["g_BT0"])
                                    for h in (range(4) if _gx & 4 else []):
                                        V(lambda e, pb=pb, h=h: e.tensor_tensor(out=Ym[0][:, h, :], in0=pb[:, h * 128:(h + 1) * 128], in1=ident[:], op=ALU.add),
                                          [pk, "ident"], ["g_Y0"])
                                    S.mute = _GBS < 4
                                    cur = 0
                                    for lev in range(1, 7):
                                        nxt = 1 - cur
                                        pb, pk = bank()
                                        for h in range(4):
                                            mm(pb[:, h * 128:(h + 1) * 128], BTm[cur][:, h, :], Bm[cur][:, h, :], True, True,
                                               ["g_BT%d" % cur, "g_B%d" % cur], [pk])
                                        if lev < 6:
                                            pb2, pk2 = bank()
                                            for h in range(4):
                                                mm(pb2[:, h * 128:(h + 1) * 128], Bm[cur][:, h, :], BTm[cur][:, h, :], True, True,
                                                   ["g_BT%d" % cur, "g_B%d" % cur], [pk2])
                                        A(lambda e, pb=pb, nxt=nxt: e.copy(out=Bm[nxt][:].rearrange("p a b -> p (a b)"), in_=pb[:, :]),
                                          [pk], ["g_B%d" % nxt])
                                        if lev < 6:
                                            V(lambda e, pb2=pb2, nxt=nxt: e.tensor_copy(out=BTm[nxt][:].rearrange("p a b -> p (a b)"),
                                                                                       in_=pb2[:, :]), [pk2], ["g_BT%d" % nxt])
                                        pb3, pk3 = bank()
                                        for h in range(4):
                                            mm(pb3[:, h * 128:(h + 1) * 128], Bm[nxt][:, h, :], Ym[cur][:, h, :], True, True,
                                               ["g_B%d" % nxt, "g_Y%d" % cur], [pk3])
                                        V(lambda e, pb3=pb3, nxt=nxt, cur=cur: e.tensor_tensor(
                                            out=Ym[nxt][:].rearrange("p a b -> p (a b)"), in0=pb3[:, :],
                                            in1=Ym[cur][:].rearrange("p a b -> p (a b)"), op=ALU.add), [pk3, "g_Y%d" % cur], ["g_Y%d" % nxt])
                                        cur = nxt
                                    S.mute = _GBS < 5
                                    Yf, Yk = Ym[cur], "g_Y%d" % cur
                                    for h in range(4):
                                        hc = slice(h * 64, (h + 1) * 64)
                                        pc_ = slice((h % 2) * 64, (h % 2) * 64 + 64)
                                        G(lambda e, h=h, hc=hc, tt=tt, u0=u0: e.tensor_scalar(
                                            out=vb[:, h, :], in0=vtok[:, tt, hc], scalar1=beta[:, tt, u0 + h:u0 + h + 1], scalar2=None,
                                            op0=ALU.mult), ["g_vtok", "g_beta"], ["g_vb"])
                                        G(lambda e, h=h, hc=hc, pc_=pc_, tt=tt, u0=u0: e.tensor_scalar(
                                            out=kbgp[:, h, pc_], in0=ktok[:, tt, hc], scalar1=bege[:, tt, u0 + h:u0 + h + 1], scalar2=None,
                                            op0=ALU.mult), ["g_ktok", "g_bege"], ["g_kbgp"])
                                        G(lambda e, h=h, hc=hc, pc_=pc_, tt=tt: e.tensor_scalar(
                                            out=kdcp[:, h, pc_], in0=ktok[:, tt, hc], scalar1=kds[:, h:h + 1], scalar2=None,
                                            op0=ALU.mult), ["g_ktok", "g_kds"], ["g_kdcp"])
                                    pb, pk = bank()
                                    for h in range(4):
                                        mm(pb[:, h * 64:(h + 1) * 64], Yf[:, h, :], vb[:, h, :], True, True, [Yk, "g_vb"], [pk])
                                    A(lambda e, pb=pb: e.copy(out=usb[:], in_=pb[:, 0:256]), [pk], ["g_u"])
                                    pb, pk = bank()
                                    for pr in range(2):
                                        for hf in range(2):
                                            h = 2 * pr + hf
                                            mm(pb[:, pr * 128:(pr + 1) * 128], kbgp[:, h, :], Yf[:, h, :], hf == 0, hf == 1,
                                               ["g_kbgp", Yk], [pk])
                                    for pr in range(2):
                                        for hf in range(2):
                                            rows = slice(hf * 64, (hf + 1) * 64)
                                            V(lambda e, pb=pb, pr=pr, hf=hf, rows=rows: e.tensor_copy(
                                                out=wTm[rows, 2 * pr + hf, :], in_=pb[rows, pr * 128:(pr + 1) * 128]), [pk], ["g_wTm"])
                                    S.mute = _GBS < 6
                                    pbvx = [bank(), bank()]
                                    pbox = [bank(), bank()]
                                    for h in range(4):
                                        bs_ = (h % 2) * 64
                                        hc2 = slice((h // 2) * 64, (h // 2 + 1) * 64)
                                        mm(pbvx[h % 2][0][:, hc2], wTm[:, h, :], Sst[:, d, h // 2, :], True, True,
                                           ["g_wTm", ("g_S", d)], [pbvx[h % 2][1]])
                                        mm(pbox[h % 2][0][:, hc2], qkz[:, h // 2, h % 2, :], Sst[:, d, h // 2, :], True, True,
                                           ["g_qkz", ("g_S", d)], [pbox[h % 2][1]])
                                    for h in range(4):
                                        pbv, pkv = pbvx[h % 2]
                                        V(lambda e, pbv=pbv, h=h: e.tensor_tensor(
                                            out=vnew[:, h * 64:(h + 1) * 64], in0=usb[:, h * 64:(h + 1) * 64],
                                            in1=pbv[:, (h // 2) * 64:(h // 2 + 1) * 64], op=ALU.subtract),
                                          ["g_u", pkv], ["g_vnew"])
                                    pb2, pk2 = bank()
                                    for h in range(4):
                                        hc = slice(h * 64, (h + 1) * 64)
                                        mm(pb2[:, hc], qkd[:, h, :], vnew[:, hc], True, True, ["g_qkd", "g_vnew"], [pk2])
                                    A(lambda e, pb2=pb2: e.copy(out=o2s[:], in_=pb2[:, 0:256]), [pk2], ["g_o2s"])
                                    for h in range(4):
                                        hc = slice(h * 64, (h + 1) * 64)
                                        dsto = oacc[:, tt, hc] if d == 0 else otmp[:, hc]
                                        dstk = ("g_oacc", tt) if d == 0 else "g_otmp"
                                        pbo, pko = pbox[h % 2]
                                        hc2 = slice((h // 2) * 64, (h // 2 + 1) * 64)
                                        V(lambda e, h=h, hc=hc, hc2=hc2, tt=tt, u0=u0, pbo=pbo, dsto=dsto: e.scalar_tensor_tensor(
                                            out=dsto, in0=pbo[:, hc2], scalar=egc[:, tt, u0 + h:u0 + h + 1], in1=o2s[:, hc],
                                            op0=ALU.mult, op1=ALU.add), [pko, "g_egc", "g_o2s"], [dstk])
                                    if d == 1:
                                        G(lambda e, tt=tt: e.tensor_tensor(out=oacc[:, tt, :], in0=oacc[:, tt, :], in1=otmp[:], op=ALU.add),
                                          [("g_oacc", tt), "g_otmp"], [("g_oacc", tt)])
                                    pbs, pks = bank()
                                    for pr in range(2):
                                        for hf in range(2):
                                            h = 2 * pr + hf
                                            mm(pbs[:, pr * 64:(pr + 1) * 64], kdcp[:, h, :], vnew[:, h * 64:(h + 1) * 64], hf == 0, hf == 1,
                                               ["g_kdcp", "g_vnew"], [pks])
                                    for pr in range(2):
                                        V(lambda e, pr=pr, d=d, pbs=pbs: e.scalar_tensor_tensor(
                                            out=Sst[:, d, pr, :], in0=Sst[:, d, pr, :], scalar=eg2[:, pr:pr + 1],
                                            in1=pbs[:, pr * 64:(pr + 1) * 64], op0=ALU.mult, op1=ALU.add),
                                          [("g_S", d), "g_eg2", pks], [("g_S", d)])
                                if grp == 0:
                                    DS(lambda e, si=si, d=d: e.dma_start(
                                        out=D["nsg"][si, l, d].rearrange("(a b) k v -> (b k) a v", b=2), in_=Sst[:, d, :, :]),
                                       [("g_S", d)], [])
                      S.mute = False
                      if _GB >= 4:
                        norm_gate_out(ph, "g", oacc, "g_oacc", ggn, "ggn", l, zT, "g_zT", 1)
                S.barrier()
                stage("B")
                if grp == groups[0] and l == layers[0]:
                    tap("brB", brT[:, 2:4, :].rearrange("p a b -> p (a b)"), [("brT", 1)])

                with ExitStack() as ph:
                    mT = sb("m_T", [128, 8, 1024], BF16, ph)
                    wmg = [sb("m_wg%d" % i, [128, 4, 8, 128], BF16, ph) for i in range(2)]
                    wbr = [sb("m_wb%d" % i, [128, 4, 2, 128], BF16, ph) for i in range(2)]
                    gts = [sb("m_gt%d" % i, [128, 512], BF16, ph) for i in range(2)]
                    acc = sb("m_acc", [128, 512], F32, ph)
                    tmpm = sb("m_tmp", [128, 512], F32, ph)
                    gi = 0
                    for dc in range(8):
                        wg_t, wgk = wmg[dc % 2], "m_wg%d" % (dc % 2)
                        wb_t, wbk = wbr[dc % 2], "m_wb%d" % (dc % 2)
                        for n in range(4):
                            c0 = 3856 + n * 1024 + dc * 128
                            DG(lambda e, n=n, c0=c0, wg_t=wg_t: e.dma_start(
                                out=wg_t[:, n, :, :], in_=D["w_in"][l, :, c0:c0 + 128].rearrange("(k p) c -> p k c", p=128)),
                               [], [wgk])
                            DG(lambda e, n=n, dc=dc, wb_t=wb_t: e.dma_start(
                                out=wb_t[:, n, :, :],
                                in_=D["w_branch"][l, n, :, dc * 128:(dc + 1) * 128].rearrange("(k p) c -> p k c", p=128)),
                               [], [wbk])
                        for tb in range(2):
                            blk = slice(tb * 512, (tb + 1) * 512)
                            for n in range(4):
                                pb, pk = bank()
                                for k in range(8):
                                    mm(pb[:, :], wg_t[:, n, k, :], hnT[:, k, blk], k == 0, k == 7, [wgk, "hnT"], [pk])
                                gt, gtk = gts[gi % 2], "m_gt%d" % (gi % 2)
                                gi += 1
                                act(gt[:], pb[:, :], AF.Sigmoid, [pk], [gtk])
                                pb2, pk2 = bank()
                                for kk in range(2):
                                    mm(pb2[:, :], wb_t[:, n, kk, :], brT[:, n * 2 + kk, blk], kk == 0, kk == 1, [wbk, ("brT", n)], [pk2])
                                if n == 0:
                                    V(lambda e, pb2=pb2, gt=gt: e.tensor_tensor(out=acc[:], in0=pb2[:, :], in1=gt[:], op=ALU.mult),
                                      [pk2, gtk], ["m_acc"])
                                else:
                                    V(lambda e, pb2=pb2, gt=gt: e.tensor_tensor(out=tmpm[:], in0=pb2[:, :], in1=gt[:], op=ALU.mult),
                                      [pk2, gtk], ["m_tmp"])
                                    if n < 3:
                                        G(lambda e: e.tensor_tensor(out=acc[:], in0=acc[:], in1=tmpm[:], op=ALU.add),
                                          ["m_acc", "m_tmp"], ["m_acc"])
                                    else:
                                        G(lambda e, dc=dc, blk=blk: e.tensor_tensor(out=mT[:, dc, blk], in0=acc[:], in1=tmpm[:], op=ALU.add),
                                          ["m_acc", "m_tmp"], [("m_T", dc)])
                    for half in range(2):
                        i = wcnt[0] % 3
                        wcnt[0] += 1
                        wo, wok = wbs[i], "wb%d" % i
                        DG(lambda e, half=half, wo=wo: e.dma_start(
                            out=wo[:], in_=D["w_out"][l, :, half * 512:(half + 1) * 512].rearrange("(k p) n -> p k n", p=128)),
                           [], [wok])
                        for q in range(4):
                            oc = half * 4 + q
                            for tb in range(2):
                                blk = slice(tb * 512, (tb + 1) * 512)
                                pb, pk = bank()
                                for k in range(8):
                                    mm(pb[:, :], wo[:, k, q * 128:(q + 1) * 128], mT[:, k, blk], k == 0, k == 7, [wok, ("m_T", k)], [pk])
                                V(lambda e, oc=oc, blk=blk, pb=pb: e.scalar_tensor_tensor(
                                    out=hT[:, oc, blk], in0=pb[:, :], scalar=modT[:, l, 16 + oc, j:j + 1], in1=hT[:, oc, blk],
                                    op0=ALU.mult, op1=ALU.add), [pk, "modT", "hT"], ["hT"])
                S.barrier()
                stage("merge")
                if grp == groups[0] and l == layers[0]:
                    tap("hT1", hT[:].rearrange("p a b -> p (a b)"), ["hT"])

            with ExitStack() as ph:
                ynT = sb("f_yn", [128, 8, 1024], F32, ph)
                rmsnorm_T(ph, lambda k: fnT[:, k:k + 1], lambda k: None,
                          lambda k, tb: ynT[:, k, tb * 512:(tb + 1) * 512], "f_yn")
                for tt in range(8):
                    s_t, sk = stg[scnt[0] % 2], "stg%d" % (scnt[0] % 2)
                    scnt[0] += 1
                    for half in range(2):
                        pb, pk = bank()
                        for q in range(4):
                            k = half * 4 + q
                            tr(pb[:, q * 128:(q + 1) * 128], ynT[:, k, tt * 128:(tt + 1) * 128], ident[:], ["f_yn", "ident"], [pk])
                        if half == 0:
                            A(lambda e, pb=pb, s_t=s_t: e.copy(out=s_t[:, 0:512], in_=pb[:, :]), [pk], [sk])
                        else:
                            V(lambda e, pb=pb, s_t=s_t: e.tensor_copy(out=s_t[:, 512:1024], in_=pb[:, :]), [pk], [sk])
                    DS(lambda e, tt=tt, s_t=s_t: e.dma_start(out=yout[tt * 128:(tt + 1) * 128, :], in_=s_t[:]), [sk], [])
            S.barrier()
        try:
            run_groups()
        except _Stop:
            S.barrier()
        with nc.allow_non_contiguous_dma(reason="small transposed parameter loads"):
            stats = S.emit()
    return nc, stats


_CACHE = {}


def _in_maps(inp):
    f = lambda a: np.ascontiguousarray(np.asarray(a, dtype=np.float32))
    cst = _consts(f(inp["na_bias"]))
    shared = dict(
        w_ada=f(inp["w_ada"]), b_ada=f(inp["b_ada"]), norm_g=f(inp["norm_g"]), w_in=f(inp["w_in"]), conv_w=f(inp["conv_w"]),
        a_log=f(inp["gdn_a_log"]).reshape(2, 8), dt_bias=f(inp["gdn_dt_bias"]).reshape(2, 8), gdn_norm=f(inp["gdn_norm"]),
        q_norm=f(inp["attn_q_norm"]), k_norm=f(inp["attn_k_norm"]), ret_norm=f(inp["ret_norm"]),
        w_branch=f(inp["w_branch"]), w_out=f(inp["w_out"]), final_norm=f(inp["final_norm"]).reshape(1, 1024), **cst)
    xp, xs = f(inp["x_prompt"]), f(inp["x_sample"])
    maps = []
    for c in range(8):
        m = dict(shared)
        m["xp"] = xp[4 * c:4 * c + 4].reshape(1024, 1024)
        m["xs"] = xs[c]
        m["cak"] = f(inp["cache_attn_k"][c]).reshape(2, 512, 128)
        m["cav"] = f(inp["cache_attn_v"][c]).reshape(2, 512, 128)
        m["cnk"] = f(inp["cache_na_k"][c]).reshape(2, 512, 256)
        m["cnv"] = f(inp["cache_na_v"][c]).reshape(2, 512, 256)
        m["sg"] = f(inp["state_gdn"][c])
        m["sr"] = f(inp["state_ret"][c])
        m["cond"] = np.stack([f(inp["c_ctx"]), f(inp["c"][c])])
        maps.append(m)
    return maps


def kernel(**inputs):
    if "nc" not in _CACHE:
        _CACHE["nc"] = build()[0]
    nc = _CACHE["nc"]
    maps = _in_maps(inputs)
    res = run_bass_kernel_spmd(nc, maps, core_ids=list(range(8)))
    R = res.results
    cat = lambda n: np.concatenate([np.asarray(r[n]) for r in R], axis=0)
    y_prompt = cat("yp").reshape(32, 256, 1024)
    y_sample = np.stack([np.asarray(r["ys"]) for r in R])
    nak = cat("nak").reshape(32, 2, 256, 2, 64)
    nav = cat("nav").reshape(32, 2, 256, 2, 64)
    nnk = cat("nnk").reshape(32, 2, 256, 4, 64)
    nnv = cat("nnv").reshape(32, 2, 256, 4, 64)
    nsg = cat("nsg")
    nsr = cat("nsr")
    return tuple(np.ascontiguousarray(a, dtype=np.float32) for a in (y_prompt, y_sample, nak, nav, nnk, nnv, nsg, nsr))
```

```python
import numpy as np
import concourse.bass as bass
import concourse.mybir as mybir
from concourse.bass_utils import run_bass_kernel_spmd
from contextlib import ExitStack

F32 = mybir.dt.float32
BF16 = mybir.dt.bfloat16
ALU = mybir.AluOpType
AF = mybir.ActivationFunctionType
AX = mybir.AxisListType
EPS = 1e-6
NEG = -30000.0
BIG = 1.0e5


import types
import os
_CL = int(os.environ.get('CLEVEL', '9'))
_BL = int(os.environ.get('BLEVEL', '9'))
_GB = int(os.environ.get('GB', '9'))
_GBQ = int(os.environ.get('GBQ', '99'))
_GBS = int(os.environ.get('GBS', '9'))
_STRICT = bool(int(os.environ.get('STRICT', '0')))


def _freeze(fn, _depth=0):
    if fn is None or fn.__closure__ is None:
        return fn
    cells = []
    for c in fn.__closure__:
        try:
            v = c.cell_contents
        except ValueError:
            cells.append(c)
            continue
        if isinstance(v, types.FunctionType) and v.__closure__ is not None and _depth < 3:
            v = _freeze(v, _depth + 1)
        cells.append(types.CellType(v))
    return types.FunctionType(fn.__code__, fn.__globals__, fn.__name__, fn.__defaults__, tuple(cells))


class _Op:
    __slots__ = ("fn", "waits", "dma", "dma_n")

    def __init__(self, fn, waits, dma, dma_n):
        self.fn, self.waits, self.dma, self.dma_n = fn, waits, dma, dma_n


class Sched:
    KD = 8
    BLK = {"pe": "tensor", "act": "scalar", "dve": "vector", "pool": "gpsimd", "sp": "sync"}

    def __init__(self, nc, es):
        self.nc = nc
        self.ops = {e: [] for e in self.BLK}
        self.state = {}
        self.ndma = {e: 0 for e in self.BLK}
        self.seen_c = {e: {} for e in self.BLK}
        self.seen_d = {e: set() for e in self.BLK}
        self.last = {}
        self.csem = {e: es.enter_context(nc.semaphore("c_" + e)) for e in ("pe", "act", "dve", "pool")}
        self.dsem = {e: [es.enter_context(nc.semaphore("d_%s%d" % (e, i))) for i in range(self.KD)]
                     for e in ("sp", "pool")}

    def _split(self, key):
        if isinstance(key, tuple):
            return key[0], key[1:]
        return key, None

    def _recs(self, key):
        name, sub = self._split(key)
        d = self.state.get(name)
        if not d:
            return []
        if sub is None:
            return list(d.values())
        out = []
        if sub in d:
            out.append(d[sub])
        if None in d:
            out.append(d[None])
        return out

    def _filter(self, eng, raw, other, dma):
        waits = []
        for d in sorted(raw | other):
            if d[0] == "c":
                if d[1] == eng and not dma:
                    if eng == "pe" or (d not in raw and not _STRICT):
                        continue
                if self.seen_c[eng].get(d[1], -1) >= d[2]:
                    continue
                self.seen_c[eng][d[1]] = d[2]
                waits.append(d)
            else:
                if d in self.seen_d[eng]:
                    continue
                self.seen_d[eng].add(d)
                waits.append(d)
        best, fin = {}, []
        for w in waits:
            if w[0] == "c":
                if w[1] not in best or best[w[1]][2] < w[2]:
                    best[w[1]] = w
            else:
                fin.append(w)
        fin.extend(best.values())
        return fin

    mute = False

    def op(self, eng, fn, reads=(), writes=(), dma=False):
        if self.mute:
            return None
        fn = _freeze(fn)
        idx = len(self.ops[eng])
        raw, other = set(), set()
        for key in reads:
            for rec in self._recs(key):
                if rec[0] is not None:
                    raw.add(rec[0])
        for key in writes:
            for rec in self._recs(key):
                if rec[0] is not None:
                    other.add(rec[0])
                other.update(rec[1])
        if dma:
            n = self.ndma[eng]
            self.ndma[eng] += 1
            ev = ("d", eng, n)
            self.last[("d", eng, n % self.KD)] = ev
        else:
            n = None
            ev = ("c", eng, idx)
            self.last[("c", eng)] = ev
        fin = self._filter(eng, raw, other, dma)
        self.ops[eng].append(_Op(fn, fin, dma, n))
        for key in reads:
            name, sub = self._split(key)
            self.state.setdefault(name, {}).setdefault(sub, [None, []])[1].append(ev)
        for key in writes:
            name, sub = self._split(key)
            d = self.state.setdefault(name, {})
            if sub is None:
                d.clear()
            d[sub] = [ev, []]
        return ev

    def barrier(self):
        evs = set(self.last.values())
        for e in self.BLK:
            fin = self._filter(e, set(evs), set(), True)
            if fin:
                self.ops[e].append(_Op(None, fin, False, None))
        self.state = {}

    def emit(self):
        for e in ("sp", "pool"):
            n = self.ndma[e]
            if n:
                self.ops[e].append(_Op(None, [("d", e, i) for i in range(max(0, n - self.KD), n)], False, None))
        waited = {e: set() for e in self.BLK}
        for e, ops in self.ops.items():
            for o in ops:
                for w in o.waits:
                    if w[0] == "c":
                        waited[w[1]].add(w[2])
        val = {}
        for e in self.BLK:
            for rank, idx in enumerate(sorted(waited[e])):
                val[(e, idx)] = rank + 1
        KD = self.KD
        self.maxval = {e: len(waited[e]) for e in self.BLK}
        self.maxdma = {e: 16 * ((self.ndma[e] - 1) // KD + 1) for e in ("sp", "pool")}
        if os.environ.get("SEMDBG"):
            print("SEM max values", self.maxval, self.maxdma, flush=True)
        with self.nc.Block() as block:
            for e in self.BLK:
                def body(engine, e=e):
                    for idx, o in enumerate(self.ops[e]):
                        for w in o.waits:
                            if w[0] == "c":
                                engine.wait_ge(self.csem[w[1]], val[(w[1], w[2])])
                            else:
                                engine.wait_ge(self.dsem[w[1]][w[2] % KD], 16 * (w[2] // KD + 1))
                        if o.fn is None:
                            continue
                        if o.dma:
                            n = o.dma_n
                            if n >= KD:
                                engine.wait_ge(self.dsem[e][n % KD], 16 * (n // KD))
                            o.fn(engine).then_inc(self.dsem[e][n % KD], 16)
                        else:
                            ins = o.fn(engine)
                            if idx in waited[e]:
                                ins.then_inc(self.csem[e], 1)
                getattr(block, self.BLK[e])(body)
        return {e: len(self.ops[e]) for e in self.BLK}


def _na_pairs():
    t = np.arange(1024)
    r, c = t // 64, t % 64
    rs = np.clip(r - 4, 0, 8)
    cs = np.clip(c - 8, 0, 48)
    valid = ((r[None, :] >= rs[:, None]) & (r[None, :] < rs[:, None] + 8) &
             (c[None, :] >= cs[:, None]) & (c[None, :] < cs[:, None] + 16))
    dr = r[None, :] - r[:, None] + 7
    dc = np.clip(c[None, :] - c[:, None] + 15, 0, 30)
    pairs = []
    for qt in range(8):
        for kt in range(8):
            if valid[qt * 128:(qt + 1) * 128, kt * 128:(kt + 1) * 128].any():
                pairs.append((qt, kt))
    return valid, dr, dc, pairs


_NA = _na_pairs()
NPAIR = len(_NA[3])


def _consts(na_bias):
    c = {}
    c["c_ident"] = np.eye(128, dtype=np.float32)
    p = np.arange(128)[:, None]
    f = np.arange(128)[None, :]
    m = np.zeros((6, 128, 128), np.float32)
    m[0] = (p <= f)
    m[1] = (p >= f)
    m[2] = np.where(f < p, 0.0, BIG)
    m[3] = np.where(f > p, 0.0, BIG)
    m[4] = np.where(f >= p, 0.0, -BIG)
    m[5] = np.where(f <= p, 0.0, -BIG)
    c["c_masks"] = m
    h = np.arange(4, dtype=np.float64)
    lgf = np.log1p(-np.exp2(-(5.0 + h)))
    lgb = np.log1p(-np.exp2(-(5.5 + h)))
    j = np.arange(128, dtype=np.float64)
    dct = np.zeros((4, 128, 128), np.float64)
    for hh in range(4):
        d = j[None, :] - j[:, None]
        dct[hh] = np.where(d > 0, np.exp(np.maximum(d, 0) * lgf[hh]), 0.0) + \
            np.where(d < 0, np.exp(np.maximum(-d, 0) * lgb[hh]), 0.0) + np.where(d == 0, 2.0, 0.0)
    c["c_dct"] = (dct * 0.125).astype(np.float32)
    rqt = np.zeros((2, 4, 128), np.float64)
    kdt = np.zeros((128, 4, 2, 64), np.float64)
    cdt = np.zeros((128, 4, 64), np.float64)
    for hh in range(4):
        rqt[0, hh] = np.exp((j + 1.0) * lgf[hh]) * 0.125
        rqt[1, hh] = np.exp((128.0 - j) * lgb[hh]) * 0.125
        kdt[:, hh, 0, :] = np.exp((127.0 - j) * lgf[hh])[:, None]
        kdt[:, hh, 1, :] = np.exp(j * lgb[hh])[:, None]
        hf_, pr_ = hh % 2, hh // 2
        cdt[hf_ * 64:(hf_ + 1) * 64, 0 * 2 + pr_, :] = np.exp(128.0 * lgf[hh])
        cdt[hf_ * 64:(hf_ + 1) * 64, 1 * 2 + pr_, :] = np.exp(128.0 * lgb[hh])
    c["c_rqt"] = rqt.astype(np.float32)
    c["c_kdt"] = kdt.astype(np.float32)
    c["c_cdt"] = cdt.astype(np.float32)
    t = np.arange(1024)
    row = (t // 64).astype(np.float32)
    col = (t % 64).astype(np.float32)
    inv = (10000.0 ** (-np.arange(16, dtype=np.float32) / 16)).astype(np.float32)
    ar = row[:, None] * inv[None, :]
    ac = col[:, None] * inv[None, :]
    cc = np.concatenate([np.cos(ar), np.cos(ar), np.cos(ac), np.cos(ac)], axis=1)
    ss = np.concatenate([-np.sin(ar), np.sin(ar), -np.sin(ac), np.sin(ac)], axis=1)
    c["c_rope"] = np.stack([np.tile(cc, (1, 6)), np.tile(ss, (1, 6))]).astype(np.float32)
    valid, dr, dc, pairs = _NA
    nab = np.empty((2, NPAIR, 4, 128, 128), np.float32)
    for pi, (qt, kt) in enumerate(pairs):
        qs = slice(qt * 128, (qt + 1) * 128)
        ks = slice(kt * 128, (kt + 1) * 128)
        v = valid[qs, ks].T
        g = na_bias[:, :, dr[qs, ks].T, dc[qs, ks].T]
        nab[:, pi] = np.where(v[None, None], g, np.float32(NEG))
    c["c_nab"] = nab
    return c


IN_SHAPES = dict(
    xp=[1024, 1024], xs=[1024, 1024], cak=[2, 512, 128], cav=[2, 512, 128], cnk=[2, 512, 256], cnv=[2, 512, 256],
    sg=[2, 2, 4, 64, 64], sr=[2, 2, 4, 64, 64], cond=[2, 1024],
    w_ada=[2, 1024, 3072], b_ada=[2, 3072], norm_g=[2, 1024], w_in=[2, 1024, 7952], conv_w=[2, 5, 768],
    a_log=[2, 8], dt_bias=[2, 8], gdn_norm=[2, 64], q_norm=[2, 64], k_norm=[2, 64], ret_norm=[2, 64],
    w_branch=[2, 4, 256, 1024], w_out=[2, 1024, 1024], final_norm=[1, 1024],
    c_ident=[128, 128], c_masks=[6, 128, 128], c_dct=[4, 128, 128], c_rqt=[2, 4, 128], c_kdt=[128, 4, 2, 64],
    c_cdt=[128, 4, 64], c_rope=[2, 1024, 384], c_nab=[2, NPAIR, 4, 128, 128])
OUT_SHAPES = dict(yp=[1024, 1024], ys=[1024, 1024], nak=[4, 2, 256, 128], nav=[4, 2, 256, 128],
                  nnk=[4, 2, 256, 256], nnv=[4, 2, 256, 256], nsg=[4, 2, 2, 4, 64, 64], nsr=[4, 2, 2, 4, 64, 64])


class _Stop(Exception):
    pass


def build(taps=None, groups=(0, 1), layers=(0, 1), stop_after=None):
    taps = taps or {}
    nc = bass.Bass("TRN2", target_bir_lowering=False)
    D = {}
    for n, s in IN_SHAPES.items():
        D[n] = nc.dram_tensor(n, list(s), F32, kind="ExternalInput").ap()
    for n, s in OUT_SHAPES.items():
        D[n] = nc.dram_tensor(n, list(s), F32, kind="ExternalOutput").ap()
    for n, s in taps.items():
        D["tap_" + n] = nc.dram_tensor("tap_" + n, list(s), F32, kind="ExternalOutput").ap()

    with ExitStack() as es:
        S = Sched(nc, es)

        uid = [0]

        def sb(name, shape, dt=F32, st=es):
            uid[0] += 1
            return st.enter_context(nc.sbuf_tensor("%s_%d" % (name, uid[0]), list(shape), dt))

        PS = [es.enter_context(nc.psum_tensor("ps%d" % i, [128, 512], F32)) for i in range(8)]
        rr = [0]

        def bank():
            i = 4 + rr[0] % 4
            rr[0] += 1
            return PS[i], ("ps", i)

        def V(fn, r, w): return S.op("dve", fn, r, w)
        def A(fn, r, w): return S.op("act", fn, r, w)
        def G(fn, r, w): return S.op("pool", fn, r, w)
        def P(fn, r, w): return S.op("pe", fn, r, w)
        def DS(fn, r, w): return S.op("sp", fn, r, w, dma=True)
        def DG(fn, r, w): return S.op("pool", fn, r, w, dma=True)

        def mm(out, lhsT, rhs, start, stop, r, w):
            P(lambda e: e.matmul(out, lhsT=lhsT, rhs=rhs, start=start, stop=stop), r, w)

        def tr(out, in_, idt, r, w):
            P(lambda e: e.transpose(out, in_, idt), r, w)

        def act(out, in_, func, r, w, bias=0.0, scale=1.0):
            A(lambda e: e.activation(out=out, in_=in_, func=func, bias=bias, scale=scale), r, w)

        def tap(name, ap, reads):
            if name in taps and name not in os.environ.get("NOTAP", "").split(","):
                DG(lambda e: e.dma_start(out=D["tap_" + name], in_=ap), reads, [])

        ident = sb("ident", [128, 128])
        identb = sb("identb", [128, 128], BF16)
        ones = sb("ones", [128, 128])
        onesblk = sb("onesblk", [128, 128])
        onespad = sb("onespad", [128, 2, 128], BF16)
        masks = sb("masks", [128, 6, 128])
        dct = sb("dct", [128, 4, 128])
        rqt = sb("rqt", [128, 2, 4, 128])
        kdt = sb("kdt", [128, 4, 2, 64])
        cdt = sb("cdt", [128, 4, 64])
        ngT = sb("ngT", [128, 2, 8])
        fnT = sb("fnT", [128, 8])
        baT = sb("baT", [128, 2, 24])
        cwT = sb("cwT", [128, 2, 6, 5])
        gqk = sb("gqk", [128, 2, 6, 64])
        gkn = sb("gkn", [128, 2, 2, 64])
        ggn = sb("ggn", [128, 2, 4, 64])
        grn = sb("grn", [128, 2, 4, 64])
        alb = sb("alb", [128, 2, 8])
        dtb = sb("dtb", [128, 2, 8])
        nega = sb("nega", [128, 2, 8])
        condT = sb("condT", [128, 8, 2])
        scond = sb("scond", [128, 8, 2])
        modT = sb("modT", [128, 2, 24, 2])
        gmul = sb("gmul", [128, 2, 8, 2])

        DS(lambda e: e.dma_start(out=ident[:], in_=D["c_ident"]), [], ["ident"])
        DG(lambda e: e.dma_start(out=identb[:], in_=D["c_ident"]), [], ["identb"])
        V(lambda e: e.memset(ones[:], 1.0), [], ["ones"])
        V(lambda e: e.memset(onesblk[:], 0.0), [], ["onesblk"])
        V(lambda e: e.memset(onesblk[0:64, 0:64], 1.0), [], ["onesblk"])
        V(lambda e: e.memset(onesblk[64:128, 64:128], 1.0), [], ["onesblk"])
        V(lambda e: e.memset(onespad[:], 0.0), [], ["onespad"])
        V(lambda e: e.memset(onespad[:, 0, 0:64], 1.0), [], ["onespad"])
        V(lambda e: e.memset(onespad[:, 1, 64:128], 1.0), [], ["onespad"])
        DS(lambda e: e.dma_start(out=masks[:], in_=D["c_masks"].rearrange("m p f -> p m f")), [], ["masks"])
        DS(lambda e: e.dma_start(out=dct[:], in_=D["c_dct"].rearrange("m p f -> p m f")), [], ["dct"])
        DS(lambda e: e.dma_start(out=rqt[:].rearrange("p a b c -> p (a b c)"),
                                 in_=D["c_rqt"].rearrange("a b c -> (a b c)").partition_broadcast(128)), [], ["rqt"])
        DS(lambda e: e.dma_start(out=kdt[:], in_=D["c_kdt"]), [], ["kdt"])
        DS(lambda e: e.dma_start(out=cdt[:], in_=D["c_cdt"]), [], ["cdt"])
        DS(lambda e: e.dma_start(out=ngT[:], in_=D["norm_g"].rearrange("l (c p) -> p l c", p=128)), [], ["ngT"])
        DS(lambda e: e.dma_start(out=fnT[:], in_=D["final_norm"].rearrange("o (c p) -> p (o c)", p=128)), [], ["fnT"])
        DS(lambda e: e.dma_start(out=baT[:], in_=D["b_ada"].rearrange("l (c p) -> p l c", p=128)), [], ["baT"])
        for l in range(2):
            for c6 in range(6):
                DS(lambda e, l=l, c6=c6: e.dma_start(
                    out=cwT[:, l, c6, :], in_=D["conv_w"][l, :, c6 * 128:(c6 + 1) * 128].rearrange("j p -> p j")),
                   [], ["cwT"])
        for jj in range(2):
            DS(lambda e, jj=jj: e.dma_start(out=condT[:, :, jj], in_=D["cond"][jj].rearrange("(c p) -> p c", p=128)), [], ["condT"])
        for l in range(2):
            for hh in range(6):
                src = "q_norm" if hh < 4 else "k_norm"
                DS(lambda e, l=l, hh=hh, src=src: e.dma_start(out=gqk[:, l, hh, :], in_=D[src][l].partition_broadcast(128)),
                   [], ["gqk"])
            for hh in range(2):
                DS(lambda e, l=l, hh=hh: e.dma_start(out=gkn[:, l, hh, :], in_=D["k_norm"][l].partition_broadcast(128)),
                   [], ["gkn"])
            for hh in range(4):
                DS(lambda e, l=l, hh=hh: e.dma_start(out=ggn[:, l, hh, :], in_=D["gdn_norm"][l].partition_broadcast(128)),
                   [], ["ggn"])
                DS(lambda e, l=l, hh=hh: e.dma_start(out=grn[:, l, hh, :], in_=D["ret_norm"][l].partition_broadcast(128)),
                   [], ["grn"])
            DS(lambda e, l=l: e.dma_start(out=alb[:, l, :], in_=D["a_log"][l].partition_broadcast(128)), [], ["alb"])
            DS(lambda e, l=l: e.dma_start(out=dtb[:, l, :], in_=D["dt_bias"][l].partition_broadcast(128)), [], ["dtb"])
        for l in range(2):
            V(lambda e, l=l: e.tensor_scalar(out=gqk[:, l, 0:4, :], in0=gqk[:, l, 0:4, :], scalar1=0.125, scalar2=None,
                                             op0=ALU.mult), ["gqk"], ["gqk"])
        act(nega[:], alb[:], AF.Exp, ["alb"], ["nega"])
        V(lambda e: e.tensor_scalar(out=nega[:], in0=nega[:], scalar1=-1.0, scalar2=None, op0=ALU.mult), ["nega"], ["nega"])
        act(scond[:], condT[:], AF.Silu, ["condT"], ["scond"])

        with ExitStack() as ph:
            wa = [sb("wa%d" % i, [128, 8, 512], F32, ph) for i in range(2)]
            cnt = 0
            for l in range(2):
                pb, pk = PS[0], ("ps", 0)
                for ch in range(6):
                    w_t, wk = wa[cnt % 2], "wa%d" % (cnt % 2)
                    cnt += 1
                    DS(lambda e, l=l, ch=ch, w_t=w_t: e.dma_start(
                        out=w_t[:], in_=D["w_ada"][l, :, ch * 512:(ch + 1) * 512].rearrange("(k p) n -> p k n", p=128)),
                       [], [wk])
                    for oc in range(4):
                        col = (ch * 4 + oc) * 2
                        for k in range(8):
                            mm(pb[:, col:col + 2], w_t[:, k, oc * 128:(oc + 1) * 128], scond[:, k, :], k == 0, k == 7,
                               [wk, "scond"], [pk])
                for j in range(2):
                    V(lambda e, l=l, j=j, pb=pb: e.tensor_tensor(
                        out=modT[:, l, :, j], in0=pb[:, 0:48].rearrange("p (a b) -> p a b", b=2)[:, :, j],
                        in1=baT[:, l, :], op=ALU.add), [pk, "baT"], ["modT"])
                    V(lambda e, l=l, j=j: e.scalar_tensor_tensor(
                        out=gmul[:, l, :, j], in0=modT[:, l, 8:16, j], scalar=1.0, in1=ngT[:, l, :],
                        op0=ALU.add, op1=ALU.mult), ["modT", "ngT"], ["gmul"])
            tap("modT", modT[:].rearrange("p a b c -> p (a b c)"), ["modT"])
        S.barrier()

        hT = sb("hT", [128, 8, 1024])
        hnT = sb("hnT", [128, 8, 1024], BF16)
        brT = sb("brT", [128, 8, 1024], BF16)
        wbs = [sb("wb%d" % i, [128, 8, 512], BF16) for i in range(3)]
        stg = [sb("stg%d" % i, [128, 1024]) for i in range(2)]
        wcnt = [0]
        scnt = [0]

        def load_w(l, c0, c1):
            i = wcnt[0] % 3
            wcnt[0] += 1
            t, k = wbs[i], "wb%d" % i
            DG(lambda e: e.dma_start(out=t[:, :, 0:c1 - c0],
                                     in_=D["w_in"][l, :, c0:c1].rearrange("(k p) n -> p k n", p=128)), [], [k])
            return t, k

        def proj_T(wt, wk, a, b, tt, pb, pk):
            for k in range(8):
                mm(pb[:, 0:b - a], hnT[:, k, tt * 128:(tt + 1) * 128], wt[:, k, a:b], k == 0, k == 7, [wk, "hnT"], [pk])

        def proj_F(wt, wk, a, m, tb, pb, pk):
            for k in range(8):
                mm(pb[0:m, :], wt[:, k, a:a + m], hnT[:, k, tb * 512:(tb + 1) * 512], k == 0, k == 7, [wk, "hnT"], [pk])

        def rstd_from(ss_ap, out_ap, n, r, w):
            act(out_ap, ss_ap, AF.Sqrt, r, w, bias=EPS, scale=1.0 / n)
            V(lambda e: e.reciprocal(out=out_ap, in_=out_ap), w, w)

        def rmsnorm_T(st, gcol_fn, scol_fn, out_fn, outkey):
            sq = sb("rn_sq", [128, 8, 512], F32, st)
            rs = sb("rn_rs", [128, 512], F32, st)
            tmp = sb("rn_tmp", [128, 512], F32, st)
            for tb in range(2):
                blk = slice(tb * 512, (tb + 1) * 512)
                act(sq[:], hT[:, :, blk], AF.Square, ["hT"], ["rn_sq"])
                pb, pk = bank()
                for k in range(8):
                    mm(pb[:, :], ones[:, :], sq[:, k, :], k == 0, k == 7, ["ones", "rn_sq"], [pk])
                rstd_from(pb[:, :], rs[:], 1024.0, [pk], ["rn_rs"])
                for k in range(8):
                    V(lambda e, k=k, blk=blk: e.tensor_tensor(out=tmp[:], in0=hT[:, k, blk], in1=rs[:], op=ALU.mult),
                      ["hT", "rn_rs"], ["rn_tmp"])
                    sc = scol_fn(k)
                    if sc is None:
                        V(lambda e, k=k, tb=tb: e.tensor_scalar(out=out_fn(k, tb), in0=tmp[:], scalar1=gcol_fn(k),
                                                                scalar2=None, op0=ALU.mult), ["rn_tmp", "gmul", "fnT"], [outkey])
                    else:
                        V(lambda e, k=k, tb=tb, sc=sc: e.tensor_scalar(out=out_fn(k, tb), in0=tmp[:], scalar1=gcol_fn(k),
                                                                       scalar2=sc, op0=ALU.mult, op1=ALU.add),
                          ["rn_tmp", "gmul", "modT"], [outkey])

        def norm_gate_out(st, tag, o_acc, okey, gtile, gkey, l, zT, zkey, br):
            sq = sb(tag + "_sq", [128, 256], F32, st)
            ss = sb(tag + "_ss", [128, 4], F32, st)
            on = sb(tag + "_on", [128, 256], F32, st)
            for tt in range(8):
                act(sq[:], o_acc[:, tt, :], AF.Square, [(okey, tt)], [tag + "_sq"])
                V(lambda e: e.tensor_reduce(out=ss[:], in_=sq[:].rearrange("p (h d) -> p h d", h=4), axis=AX.X, op=ALU.add),
                  [tag + "_sq"], [tag + "_ss"])
                rstd_from(ss[:], ss[:], 64.0, [tag + "_ss"], [tag + "_ss"])
                V(lambda e, tt=tt: e.tensor_tensor(out=on[:].rearrange("p (h d) -> p h d", h=4),
                                                   in0=o_acc[:, tt, :].rearrange("p (h d) -> p h d", h=4),
                                                   in1=ss[:].unsqueeze(2).to_broadcast([128, 4, 64]), op=ALU.mult),
                  [(okey, tt), tag + "_ss"], [tag + "_on"])
                G(lambda e: e.tensor_tensor(out=on[:].rearrange("p (h d) -> p h d", h=4),
                                            in0=on[:].rearrange("p (h d) -> p h d", h=4), in1=gtile[:, l, :, :], op=ALU.mult),
                  [tag + "_on", gkey], [tag + "_on"])
                pb, pk = bank()
                for c in range(2):
                    tr(pb[:, c * 128:(c + 1) * 128], on[:, c * 128:(c + 1) * 128], ident[:], [tag + "_on", "ident"], [pk])
                V(lambda e, tt=tt, pb=pb: e.tensor_tensor(out=brT[:, br * 2:br * 2 + 2, tt * 128:(tt + 1) * 128],
                                                          in0=pb[:, 0:256].rearrange("p (c t) -> p c t", c=2),
                                                          in1=zT[:, :, tt * 128:(tt + 1) * 128], op=ALU.mult),
                  [pk, zkey], [("brT", br)])

        def attention(st, tag, qT, kT, vpad, kvmap, vslot, qblocks, keys_fn, zT, zkey, br):
            pts = [sb(tag + "_p%d" % i, [128, 512], BF16, st) for i in range(3)]
            rden = sb(tag + "_rd", [128, 512], F32, st)
            osb = sb(tag + "_o", [128, 512], F32, st)
            pc = 0
            it = 0
            for (q0, qn) in qblocks:
                keys = keys_fn(q0)
                for pr in range(2):
                    bo, bd = (0, 1) if it % 2 == 0 else (2, 3)
                    it += 1
                    psO, psD = PS[bo], PS[bd]
                    ko, kd = ("ps", bo), ("ps", bd)
                    nk = len(keys) * 2
                    ci = 0
                    for (kidx, bias_fn) in keys:
                        for hh in range(2):
                            h = 2 * pr + hh
                            pb, pk = bank()
                            mm(pb[:, 0:qn], kT[:, kvmap(h), kidx * 128:(kidx + 1) * 128], qT[:, h, q0:q0 + qn],
                               True, bias_fn is None, [tag + "_kT", tag + "_qT"], [pk])
                            if bias_fn is not None:
                                bap, bkey = bias_fn(h)
                                mm(pb[:, 0:qn], identb[:, :], bap, False, True, ["identb", bkey], [pk])
                            pt, ptk = pts[pc % 3], tag + "_p%d" % (pc % 3)
                            pc += 1
                            act(pt[:, 0:qn], pb[:, 0:qn], AF.Exp, [pk], [ptk])
                            mm(psO[:, 0:qn], vpad[:, kidx, vslot(h), :], pt[:, 0:qn], ci == 0, ci == nk - 1,
                               [tag + "_vp", ptk], [ko])
                            mm(psD[:, 0:qn], onespad[:, hh, :], pt[:, 0:qn], ci == 0, ci == nk - 1, ["onespad", ptk], [kd])
                            ci += 1
                    V(lambda e, psD=psD, qn=qn: e.reciprocal(out=rden[:, 0:qn], in_=psD[:, 0:qn]), [kd], [tag + "_rd"])
                    V(lambda e, psO=psO, qn=qn: e.tensor_tensor(out=osb[:, 0:qn], in0=psO[:, 0:qn], in1=rden[:, 0:qn],
                                                                op=ALU.mult), [ko, tag + "_rd"], [tag + "_o"])
                    G(lambda e, pr=pr, q0=q0, qn=qn: e.tensor_tensor(out=brT[:, br * 2 + pr, q0:q0 + qn], in0=osb[:, 0:qn],
                                                                     in1=zT[:, pr, q0:q0 + qn], op=ALU.mult),
                      [tag + "_o", zkey], [("brT", br)])

        def zproj(wt, wk, a, zT, zkey):
            for c in range(2):
                for tb in range(2):
                    pb, pk = bank()
                    proj_F(wt, wk, a + c * 128, 128, tb, pb, pk)
                    act(zT[:, c, tb * 512:(tb + 1) * 512], pb[:, :], AF.Silu, [pk], [zkey])

        def stage(name):
            if stop_after == name or (stop_after == "Dproj" and name == "D"):
                raise _Stop()

        def run_groups():
          for grp in (groups if stop_after != "p0" else ()):
            G(lambda e: e.memset(brT[:], 0.0), [], ["brT"])
            xin = D["xp"] if grp == 0 else D["xs"]
            yout = D["yp"] if grp == 0 else D["ys"]
            for tt in range(8):
                s_t, sk = stg[scnt[0] % 2], "stg%d" % (scnt[0] % 2)
                scnt[0] += 1
                DS(lambda e, tt=tt, s_t=s_t: e.dma_start(out=s_t[:], in_=xin[tt * 128:(tt + 1) * 128, :]), [], [sk])
                for half in range(2):
                    pb, pk = bank()
                    for q in range(4):
                        k = half * 4 + q
                        tr(pb[:, q * 128:(q + 1) * 128], s_t[:, k * 128:(k + 1) * 128], ident[:], [sk, "ident"], [pk])
                    eng = A if half == 0 else V
                    if half == 0:
                        A(lambda e, tt=tt, pb=pb: e.copy(out=hT[:, 0:4, tt * 128:(tt + 1) * 128],
                                                         in_=pb[:, :].rearrange("p (a b) -> p a b", a=4)), [pk], ["hT"])
                    else:
                        V(lambda e, tt=tt, pb=pb: e.tensor_copy(out=hT[:, 4:8, tt * 128:(tt + 1) * 128],
                                                                in_=pb[:, :].rearrange("p (a b) -> p a b", a=4)), [pk], ["hT"])
            for l in layers:
                j = grp
                with ExitStack() as ph:
                    rmsnorm_T(ph, lambda k: gmul[:, l, k, j:j + 1], lambda k: modT[:, l, k, j:j + 1],
                              lambda k, tb: hnT[:, k, tb * 512:(tb + 1) * 512], "hnT")
                S.barrier()
                stage("norm")
                if grp == groups[0] and l == layers[0]:
                    tap("hnT", hnT[:].rearrange("p a b -> p (a b)"), ["hnT"])

                with ExitStack() as ph:
                    nkt = 8 if grp == 0 else 12
                    qT = sb("a_qT", [64, 4, 1024], BF16, ph)
                    kT = sb("a_kT", [64, 2, 128 * nkt], BF16, ph)
                    vpad = sb("a_vp", [128, nkt, 4, 128], BF16, ph)
                    zT = sb("a_zT", [128, 2, 1024], BF16, ph)
                    sq = sb("a_sq", [128, 384], F32, ph)
                    ss = sb("a_ss", [128, 6], F32, ph)
                    qk = sb("a_qk", [128, 384], F32, ph)
                    qk2 = sb("a_qk2", [128, 384], F32, ph)
                    kout = sb("a_ko", [128, 128], F32, ph)
                    vout = sb("a_vo", [128, 128], F32, ph)
                    rp = sb("a_rp", [128, 2, 384], F32, ph)
                    G(lambda e: e.memset(vpad[:], 0.0), [], ["a_vp"])
                    w0, w0k = load_w(l, 0, 512)
                    w1, w1k = load_w(l, 512, 768)
                    for tt in range(8):
                        pb, pk = bank()
                        proj_T(w0, w0k, 0, 512, tt, pb, pk)
                        act(sq[:], pb[:, 0:384], AF.Square, [pk], ["a_sq"])
                        V(lambda e: e.tensor_reduce(out=ss[:], in_=sq[:].rearrange("p (h d) -> p h d", h=6), axis=AX.X,
                                                    op=ALU.add), ["a_sq"], ["a_ss"])
                        rstd_from(ss[:], ss[:], 64.0, ["a_ss"], ["a_ss"])
                        V(lambda e, pb=pb: e.tensor_tensor(out=qk[:].rearrange("p (h d) -> p h d", h=6),
                                                           in0=pb[:, 0:384].rearrange("p (h d) -> p h d", h=6),
                                                           in1=ss[:].unsqueeze(2).to_broadcast([128, 6, 64]), op=ALU.mult),
                          [pk, "a_ss"], ["a_qk"])
                        if grp == 0:
                            b_, s0 = tt // 2, (tt % 2) * 128
                            G(lambda e: e.tensor_tensor(out=kout[:].rearrange("p (h d) -> p h d", h=2),
                                                        in0=qk[:, 256:384].rearrange("p (h d) -> p h d", h=2),
                                                        in1=gkn[:, l, :, :], op=ALU.mult), ["a_qk", "gkn"], ["a_ko"])
                            DS(lambda e, b_=b_, s0=s0: e.dma_start(out=D["nak"][b_, l, s0:s0 + 128, :], in_=kout[:]),
                               ["a_ko"], [])
                            A(lambda e, pb=pb: e.copy(out=vout[:], in_=pb[:, 384:512]), [pk], ["a_vo"])
                            DS(lambda e, b_=b_, s0=s0: e.dma_start(out=D["nav"][b_, l, s0:s0 + 128, :], in_=vout[:]),
                               ["a_vo"], [])
                        G(lambda e: e.tensor_tensor(out=qk[:].rearrange("p (h d) -> p h d", h=6),
                                                    in0=qk[:].rearrange("p (h d) -> p h d", h=6), in1=gqk[:, l, :, :],
                                                    op=ALU.mult), ["a_qk", "gqk"], ["a_qk"])
                        src, srck = qk, "a_qk"
                        if grp == 1:
                            DS(lambda e, tt=tt: e.dma_start(out=rp[:], in_=D["c_rope"][:, tt * 128:(tt + 1) * 128, :]
                                                            .rearrange("a p f -> p a f")), [], ["a_rp"])
                            V(lambda e: e.tensor_tensor(out=qk2[:], in0=qk[:], in1=rp[:, 0, :], op=ALU.mult),
                              ["a_qk", "a_rp"], ["a_qk2"])
                            qv = qk[:].rearrange("p (g s d) -> p g s d", s=2, d=16)
                            sv = rp[:, 1, :].rearrange("p (g s d) -> p g s d", s=2, d=16)
                            G(lambda e, qv=qv, sv=sv: e.tensor_tensor(
                                out=sq[:].rearrange("p (g s d) -> p g s d", s=2, d=16)[:, :, 0, :], in0=qv[:, :, 1, :],
                                in1=sv[:, :, 0, :], op=ALU.mult), ["a_qk", "a_rp"], ["a_sq"])
                            G(lambda e, qv=qv, sv=sv: e.tensor_tensor(
                                out=sq[:].rearrange("p (g s d) -> p g s d", s=2, d=16)[:, :, 1, :], in0=qv[:, :, 0, :],
                                in1=sv[:, :, 1, :], op=ALU.mult), ["a_qk", "a_rp"], ["a_sq"])
                            V(lambda e: e.tensor_tensor(out=qk2[:], in0=qk2[:], in1=sq[:], op=ALU.add),
                              ["a_qk2", "a_sq"], ["a_qk2"])
                            src, srck = qk2, "a_qk2"
                        pq, pqk = bank()
                        for h in range(4):
                            tr(pq[0:64, h * 128:(h + 1) * 128], src[:, h * 64:(h + 1) * 64], ident[:], [srck, "ident"], [pqk])
                        A(lambda e, tt=tt, pq=pq: e.copy(out=qT[:, :, tt * 128:(tt + 1) * 128],
                                                         in_=pq[0:64, :].rearrange("p (a b) -> p a b", a=4)), [pqk], ["a_qT"])
                        pk2, pk2k = bank()
                        for h in range(2):
                            tr(pk2[0:64, h * 128:(h + 1) * 128], src[:, 256 + h * 64:256 + (h + 1) * 64], ident[:],
                               [srck, "ident"], [pk2k])
                        V(lambda e, tt=tt, pk2=pk2: e.tensor_copy(out=kT[:, :, tt * 128:(tt + 1) * 128],
                                                                  in_=pk2[0:64, 0:256].rearrange("p (a b) -> p a b", a=2)),
                          [pk2k], ["a_kT"])
                        for kv in range(2):
                            for pos in range(2):
                                A(lambda e, tt=tt, kv=kv, pos=pos, pb=pb: e.copy(
                                    out=vpad[:, tt, kv * 2 + pos, pos * 64:(pos + 1) * 64],
                                    in_=pb[:, 384 + kv * 64:384 + (kv + 1) * 64]), [pk], ["a_vp"])
                    if grp == 1:
                        for ct in range(4):
                            s_t, sk = stg[scnt[0] % 2], "stg%d" % (scnt[0] % 2)
                            scnt[0] += 1
                            DS(lambda e, ct=ct, s_t=s_t: e.dma_start(out=s_t[:, 0:128], in_=D["cak"][l, ct * 128:(ct + 1) * 128, :]),
                               [], [sk])
                            DS(lambda e, ct=ct, s_t=s_t: e.dma_start(out=s_t[:, 128:256], in_=D["cav"][l, ct * 128:(ct + 1) * 128, :]),
                               [], [sk])
                            pk2, pk2k = bank()
                            for h in range(2):
                                tr(pk2[0:64, h * 128:(h + 1) * 128], s_t[:, h * 64:(h + 1) * 64], ident[:], [sk, "ident"], [pk2k])
                            V(lambda e, ct=ct, pk2=pk2: e.tensor_copy(
                                out=kT[:, :, (8 + ct) * 128:(9 + ct) * 128],
                                in_=pk2[0:64, 0:256].rearrange("p (a b) -> p a b", a=2)), [pk2k], ["a_kT"])
                            for kv in range(2):
                                for pos in range(2):
                                    A(lambda e, ct=ct, kv=kv, pos=pos, s_t=s_t: e.copy(
                                        out=vpad[:, 8 + ct, kv * 2 + pos, pos * 64:(pos + 1) * 64],
                                        in_=s_t[:, 128 + kv * 64:128 + (kv + 1) * 64]), [sk], ["a_vp"])
                    zproj(w1, w1k, 0, zT, "a_zT")
                    if grp == 0:
                        qblocks = [(b_ * 256, 256) for b_ in range(4)]
                        keys_fn = lambda q0: [((q0 // 128) + i, None) for i in range(2)]
                    else:
                        qblocks = [(0, 512), (512, 512)]
                        keys_fn = lambda q0: [(i, None) for i in range(12)]
                    attention(ph, "a", qT, kT, vpad, lambda h: h // 2, lambda h: (h // 2) * 2 + (h % 2), qblocks, keys_fn,
                              zT, "a_zT", 0)
                S.barrier()
                stage("A")
                if grp == groups[0] and l == layers[0]:
                    tap("brA", brT[:, 0:2, :].rearrange("p a b -> p (a b)"), [("brT", 0)])

                with ExitStack() as ph:
                    nkt = 8 if grp == 0 else 12
                    qT = sb("d_qT", [64, 4, 1024], BF16, ph)
                    kT = sb("d_kT", [64, 4, 128 * nkt], BF16, ph)
                    vpad = sb("d_vp", [128, nkt, 4, 128], BF16, ph)
                    zT = sb("d_zT", [128, 2, 1024], BF16, ph)
                    kout = sb("d_ko", [128, 256], F32, ph)
                    vout = sb("d_vo", [128, 256], F32, ph)
                    G(lambda e: e.memset(vpad[:], 0.0), [], ["d_vp"])
                    w0, w0k = load_w(l, 2832, 3344)
                    w1, w1k = load_w(l, 3344, 3856)
                    for c in range(2):
                        for tb in range(2):
                            blk = slice(tb * 512, (tb + 1) * 512)
                            pb, pk = bank()
                            proj_F(w0, w0k, c * 128, 128, tb, pb, pk)
                            for hf in range(2):
                                V(lambda e, c=c, hf=hf, blk=blk, pb=pb: e.tensor_scalar(
                                    out=qT[:, 2 * c + hf, blk], in0=pb[hf * 64:(hf + 1) * 64, :], scalar1=0.125, scalar2=None,
                                    op0=ALU.mult), [pk], ["d_qT"])
                            pb, pk = bank()
                            proj_F(w0, w0k, 256 + c * 128, 128, tb, pb, pk)
                            for hf in range(2):
                                A(lambda e, c=c, hf=hf, blk=blk, pb=pb: e.copy(out=kT[:, 2 * c + hf, blk],
                                                                               in_=pb[hf * 64:(hf + 1) * 64, :]), [pk], ["d_kT"])
                    for tt in range(8):
                        pb, pk = bank()
                        proj_T(w1, w1k, 0, 256, tt, pb, pk)
                        for h in range(4):
                            A(lambda e, tt=tt, h=h, pb=pb: e.copy(out=vpad[:, tt, h, (h % 2) * 64:(h % 2) * 64 + 64],
                                                                  in_=pb[:, h * 64:(h + 1) * 64]), [pk], ["d_vp"])
                        if grp == 0:
                            b_, s0 = tt // 2, (tt % 2) * 128
                            A(lambda e, pb=pb: e.copy(out=vout[:], in_=pb[:, 0:256]), [pk], ["d_vo"])
                            DS(lambda e, b_=b_, s0=s0: e.dma_start(out=D["nnv"][b_, l, s0:s0 + 128, :], in_=vout[:]),
                               ["d_vo"], [])
                            pb2, pk2k = bank()
                            proj_T(w0, w0k, 256, 512, tt, pb2, pk2k)
                            V(lambda e, pb2=pb2: e.tensor_copy(out=kout[:], in_=pb2[:, 0:256]), [pk2k], ["d_ko"])
                            DS(lambda e, b_=b_, s0=s0: e.dma_start(out=D["nnk"][b_, l, s0:s0 + 128, :], in_=kout[:]),
                               ["d_ko"], [])
                    if grp == 1:
                        for ct in range(4):
                            s_t, sk = stg[scnt[0] % 2], "stg%d" % (scnt[0] % 2)
                            scnt[0] += 1
                            DS(lambda e, ct=ct, s_t=s_t: e.dma_start(out=s_t[:, 0:256], in_=D["cnk"][l, ct * 128:(ct + 1) * 128, :]),
                               [], [sk])
                            DS(lambda e, ct=ct, s_t=s_t: e.dma_start(out=s_t[:, 256:512], in_=D["cnv"][l, ct * 128:(ct + 1) * 128, :]),
                               [], [sk])
                            pk2, pk2k = bank()
                            for h in range(4):
                                tr(pk2[0:64, h * 128:(h + 1) * 128], s_t[:, h * 64:(h + 1) * 64], ident[:], [sk, "ident"], [pk2k])
                            V(lambda e, ct=ct, pk2=pk2: e.tensor_copy(
                                out=kT[:, :, (8 + ct) * 128:(9 + ct) * 128],
                                in_=pk2[0:64, :].rearrange("p (a b) -> p a b", a=4)), [pk2k], ["d_kT"])
                            for h in range(4):
                                A(lambda e, ct=ct, h=h, s_t=s_t: e.copy(
                                    out=vpad[:, 8 + ct, h, (h % 2) * 64:(h % 2) * 64 + 64],
                                    in_=s_t[:, 256 + h * 64:256 + (h + 1) * 64]), [sk], ["d_vp"])
                    zproj(w1, w1k, 256, zT, "d_zT")
                    if stop_after == "Dproj":
                        pass
                    elif grp == 0:
                        qblocks = [(b_ * 256, 256) for b_ in range(4)]
                        keys_fn = lambda q0: [((q0 // 128) + i, None) for i in range(2)]
                        attention(ph, "d", qT, kT, vpad, lambda h: h, lambda h: h, qblocks, keys_fn, zT, "d_zT", 3)
                    else:
                        nbt = [sb("d_nb%d" % i, [128, 6, 4, 128], BF16, ph) for i in range(2)]
                        pairs = _NA[3]
                        qblocks = [(qt * 128, 128) for qt in range(8)]
                        cache = {}

                        def keys_fn(q0):
                            qt = q0 // 128
                            pis = [pi for pi, (a, b) in enumerate(pairs) if a == qt]
                            nb, nbk = nbt[qt % 2], "d_nb%d" % (qt % 2)
                            DG(lambda e: e.dma_start(out=nb[:, 0:len(pis), :, :],
                                                     in_=D["c_nab"][l, pis[0]:pis[0] + len(pis)].rearrange("a h k q -> k a h q")),
                               [], [nbk])
                            out = []
                            for ii, pi in enumerate(pis):
                                out.append((pairs[pi][1], (lambda h, ii=ii: (nb[:, ii, h, :], nbk))))
                            out += [(8 + i, None) for i in range(4)]
                            return out
                        attention(ph, "d", qT, kT, vpad, lambda h: h, lambda h: h, qblocks, keys_fn, zT, "d_zT", 3)
                S.barrier()
                stage("D")
                if grp == groups[0] and l == layers[0]:
                    tap("brD", brT[:, 6:8, :].rearrange("p a b -> p (a b)"), [("brT", 3)])

                with ExitStack() as ph:
                    qTm = sb("r_qTm", [128, 2, 2, 1024], BF16, ph)
                    kTr = sb("r_kT", [128, 2, 1024], BF16, ph)
                    kdp = sb("r_kdp", [128, 4, 2, 128], BF16, ph)
                    vtk = sb("r_v", [128, 8, 256], BF16, ph)
                    zT = sb("r_zT", [128, 2, 1024], BF16, ph)
                    U = sb("r_U", [128, 8, 256], F32, ph)
                    Sin = sb("r_Sin", [128, 8, 256], F32, ph)
                    Sfin = sb("r_Sfin", [128, 256], F32, ph)
                    Sinb = sb("r_Sinb", [128, 256], BF16, ph)
                    qkm = sb("r_qkm", [128, 4, 128], BF16, ph)
                    qdm = sb("r_qdm", [128, 4, 2, 128], BF16, ph)
                    oacc = sb("r_oacc", [128, 8, 256], F32, ph)
                    tmpS = sb("r_tmpS", [128, 256], F32, ph)
                    R2 = int(os.environ.get("R2", "511"))
                    if R2 & 1:
                        G(lambda e: e.memset(qTm[:], 0.0), [], ["r_qTm"])
                        G(lambda e: e.memset(kdp[:], 0.0), [], ["r_kdp"])
                    w0, w0k = load_w(l, 1808, 2320)
                    w1, w1k = load_w(l, 2320, 2832)
                    for c in range(2):
                        for tb in range(2):
                            blk = slice(tb * 512, (tb + 1) * 512)
                            pb, pk = bank()
                            if R2 & 2:
                                proj_F(w0, w0k, c * 128, 128, tb, pb, pk)
                            for hf in (range(2) if R2 & 2 else []):
                                rows = slice(hf * 64, (hf + 1) * 64)
                                if hf == 0:
                                    A(lambda e, c=c, hf=hf, blk=blk, rows=rows, pb=pb: e.copy(out=qTm[rows, c, hf, blk], in_=pb[rows, :]),
                                      [pk], ["r_qTm"])
                                else:
                                    V(lambda e, c=c, hf=hf, blk=blk, rows=rows, pb=pb: e.tensor_copy(out=qTm[rows, c, hf, blk], in_=pb[rows, :]),
                                      [pk], ["r_qTm"])
                            pb, pk = bank()
                            if R2 & 4:
                                proj_F(w0, w0k, 256 + c * 128, 128, tb, pb, pk)
                                V(lambda e, c=c, blk=blk, pb=pb: e.tensor_copy(out=kTr[:, c, blk], in_=pb[:, :]), [pk], ["r_kT"])
                    for tt in range(8):
                        pb, pk = bank()
                        if R2 & 8:
                            proj_T(w0, w0k, 256, 512, tt, pb, pk)
                        for d in (range(2) if R2 & 8 else []):
                            for h in range(4):
                                hf = h % 2
                                V(lambda e, d=d, h=h, hf=hf, pb=pb: e.tensor_tensor(
                                    out=kdp[:, h, d, hf * 64:(hf + 1) * 64], in0=pb[:, h * 64:(h + 1) * 64],
                                    in1=kdt[:, h, d, :], op=ALU.mult), [pk, "kdt"], ["r_kdp"])
                        pb, pk = bank()
                        if R2 & 16:
                            proj_T(w1, w1k, 0, 256, tt, pb, pk)
                            A(lambda e, tt=tt, pb=pb: e.copy(out=vtk[:, tt, :], in_=pb[:, 0:256]), [pk], [("r_v", tt)])
                        pb, pk = bank()
                        for d in (range(2) if R2 & 32 else []):
                            for pr in range(2):
                                cs_ = slice((d * 2 + pr) * 64, (d * 2 + pr + 1) * 64)
                                for hf in range(2):
                                    h = 2 * pr + hf
                                    mm(pb[:, cs_], kdp[:, h, d, :], vtk[:, tt, h * 64:(h + 1) * 64], hf == 0, hf == 1,
                                       ["r_kdp", ("r_v", tt)], [pk])
                        if R2 & 32:
                            V(lambda e, tt=tt, pb=pb: e.tensor_copy(out=U[:, tt, :], in_=pb[:, 0:256]), [pk], [("r_U", tt)])
                    if R2 & 64:
                        zproj(w1, w1k, 256, zT, "r_zT")
                    seqs = [(2 * b_, 2 * b_ + 1) for b_ in range(4)] if grp == 0 else [tuple(range(8))]
                    cdv = cdt[:].rearrange("p a d -> p (a d)")
                    FW, BW = slice(0, 128), slice(128, 256)
                    for si, tiles in enumerate(seqs if R2 & 128 else []):
                        first, lastt = tiles[0], tiles[-1]
                        if grp == 0:
                            G(lambda e, first=first: e.memset(Sin[:, first, FW], 0.0), [], [("r_Sin", "f", first)])
                            G(lambda e, lastt=lastt: e.memset(Sin[:, lastt, BW], 0.0), [], [("r_Sin", "b", lastt)])
                        else:
                            DS(lambda e: e.dma_start(out=Sin[:, 0, FW].rearrange("p (a v) -> p a v", a=2),
                                                     in_=D["sr"][l, 0].rearrange("(a b) k v -> (b k) a v", b=2)), [], [("r_Sin", "f", 0)])
                            DS(lambda e: e.dma_start(out=Sin[:, 7, BW].rearrange("p (a v) -> p a v", a=2),
                                                     in_=D["sr"][l, 1].rearrange("(a b) k v -> (b k) a v", b=2)), [], [("r_Sin", "b", 7)])
                        for tt in tiles:
                            dst = Sin[:, tt + 1, FW] if tt != lastt else Sfin[:, FW]
                            dk = ("r_Sin", "f", tt + 1) if tt != lastt else ("r_Sfin", "f")
                            V(lambda e, tt=tt: e.tensor_tensor(out=tmpS[:, FW], in0=Sin[:, tt, FW], in1=cdv[:, FW], op=ALU.mult),
                              [("r_Sin", "f", tt), "cdt"], [("r_tmpS", "f")])
                            V(lambda e, tt=tt, dst=dst: e.tensor_tensor(out=dst, in0=tmpS[:, FW], in1=U[:, tt, FW], op=ALU.add),
                              [("r_tmpS", "f"), ("r_U", tt)], [dk])
                        for tt in reversed(tiles):
                            dst = Sin[:, tt - 1, BW] if tt != first else Sfin[:, BW]
                            dk = ("r_Sin", "b", tt - 1) if tt != first else ("r_Sfin", "b")
                            G(lambda e, tt=tt: e.tensor_tensor(out=tmpS[:, BW], in0=Sin[:, tt, BW], in1=cdv[:, BW], op=ALU.mult),
                              [("r_Sin", "b", tt), "cdt"], [("r_tmpS", "b")])
                            G(lambda e, tt=tt, dst=dst: e.tensor_tensor(out=dst, in0=tmpS[:, BW], in1=U[:, tt, BW], op=ALU.add),
                              [("r_tmpS", "b"), ("r_U", tt)], [dk])
                        if grp == 0 and R2 & 256:
                            b_ = si
                            DS(lambda e, b_=b_: e.dma_start(out=D["nsr"][b_, l, 0].rearrange("(a b) k v -> (b k) a v", b=2),
                                                            in_=Sfin[:, FW].rearrange("p (a v) -> p a v", a=2)),
                               [("r_Sfin", "f")], [])
                            DS(lambda e, b_=b_: e.dma_start(out=D["nsr"][b_, l, 1].rearrange("(a b) k v -> (b k) a v", b=2),
                                                            in_=Sfin[:, BW].rearrange("p (a v) -> p a v", a=2)),
                               [("r_Sfin", "b")], [])
                    _m = int(os.environ.get("L3", "31"))
                    for tt in range(int(os.environ.get("L3N", "8")) if _CL >= 3 else 0):
                        ts_ = slice(tt * 128, (tt + 1) * 128)
                        pb, pk = bank()
                        for h in (range(4) if _m & 1 else []):
                            mm(pb[:, h * 128:(h + 1) * 128], kTr[:, h // 2, ts_], qTm[:, h // 2, h % 2, ts_], True, True,
                               ["r_kT", "r_qTm"], [pk])
                        if _m & 2:
                            V(lambda e, pb=pb: e.tensor_tensor(out=qkm[:].rearrange("p a b -> p (a b)"), in0=pb[:, :],
                                                               in1=dct[:].rearrange("p a b -> p (a b)"), op=ALU.mult),
                              [pk, "dct"], ["r_qkm"])
                        for d in (range(2) if _m & 4 else []):
                            G(lambda e, d=d, ts_=ts_: e.tensor_tensor(
                                out=qdm[:, :, d, :], in0=qTm[:, :, :, ts_].rearrange("p c f t -> p (c f) t"),
                                in1=rqt[:, d, :, :], op=ALU.mult), ["r_qTm", "rqt"], ["r_qdm"])
                        if _m & 8:
                            A(lambda e, tt=tt: e.copy(out=Sinb[:], in_=Sin[:, tt, :]), [("r_Sin", "f", tt), ("r_Sin", "b", tt)], ["r_Sinb"])
                        pb, pk = bank()
                        for h in (range(4) if _m & 16 else []):
                            pr = h // 2
                            hc = slice(h * 64, (h + 1) * 64)
                            mm(pb[:, hc], qkm[:, h, :], vtk[:, tt, hc], True, False, ["r_qkm", ("r_v", tt)], [pk])
                            mm(pb[:, hc], qdm[:, h, 0, :], Sinb[:, pr * 64:(pr + 1) * 64], False, False, ["r_qdm", "r_Sinb"], [pk])
                            mm(pb[:, hc], qdm[:, h, 1, :], Sinb[:, (2 + pr) * 64:(3 + pr) * 64], False, True, ["r_qdm", "r_Sinb"], [pk])
                        if _m & 16:
                            A(lambda e, tt=tt, pb=pb: e.copy(out=oacc[:, tt, :], in_=pb[:, 0:256]), [pk], [("r_oacc", tt)])
                    if _CL >= 4:
                        norm_gate_out(ph, "r", oacc, "r_oacc", grn, "grn", l, zT, "r_zT", 2)
                S.barrier()
                stage("C")
                if grp == groups[0] and l == layers[0]:
                    tap("brC", brT[:, 4:6, :].rearrange("p a b -> p (a b)"), [("brT", 2)])

                with ExitStack() as ph:
                  if _BL >= 1:
                      gx = [sb("g_x%d" % i, [128, 1024], F32, ph) for i in range(1)]
                      gy = [sb("g_y%d" % i, [128, 1024], F32, ph) for i in range(1)]
                      qkT = sb("g_qkT", [128, 4, 1024], F32, ph)
                      ktok = sb("g_ktok", [128, 8, 256], F32, ph)
                      vtok = sb("g_vtok", [128, 8, 256], F32, ph)
                      zT = sb("g_zT", [128, 2, 1024], BF16, ph)
                      ab = sb("g_ab", [128, 8, 16], F32, ph)
                      beta = sb("g_beta", [128, 8, 8], F32, ph)
                      nbeta = sb("g_nbeta", [128, 8, 8], F32, ph)
                      la = sb("g_la", [128, 8, 8], F32, ph)
                      gc = sb("g_gc", [128, 8, 8], F32, ph)
                      ngc = sb("g_ngc", [128, 8, 8], F32, ph)
                      egc = sb("g_egc", [128, 8, 8], F32, ph)
                      bege = sb("g_bege", [128, 8, 8], F32, ph)
                      oacc = sb("g_oacc", [128, 8, 256], F32, ph)
                      dg = sb("g_dg", [128, 4, 128], F32, ph)
                      Dm = sb("g_Dm", [128, 4, 128], F32, ph)
                      DT = sb("g_DT", [128, 4, 128], F32, ph)
                      Bm = [sb("g_B%d" % i, [128, 4, 128], F32, ph) for i in range(2)]
                      BTm = [sb("g_BT%d" % i, [128, 4, 128], F32, ph) for i in range(2)]
                      Ym = [sb("g_Y%d" % i, [128, 4, 128], F32, ph) for i in range(2)]
                      qkd = sb("g_qkd", [128, 4, 128], F32, ph)
                      vb = sb("g_vb", [128, 4, 64], F32, ph)
                      kbgp = sb("g_kbgp", [128, 4, 128], F32, ph)
                      kdcp = sb("g_kdcp", [128, 4, 128], F32, ph)
                      kds = sb("g_kds", [128, 4], F32, ph)
                      eg2 = sb("g_eg2", [128, 2], F32, ph)
                      wTm = sb("g_wTm", [128, 4, 128], F32, ph)

                      Sst = sb("g_S", [128, 2, 2, 64], F32, ph)
                      G(lambda e: e.memset(kbgp[:], 0.0), [], ["g_kbgp"])
                      G(lambda e: e.memset(kdcp[:], 0.0), [], ["g_kdcp"])
                      G(lambda e: e.memset(wTm[:], 0.0), [], ["g_wTm"])
                      wA, wAk = load_w(l, 768, 1280)
                      wB, wBk = load_w(l, 1280, 1552)
                      wC, wCk = load_w(l, 1552, 1808)
                      nseq = 4 if grp == 0 else 1
                      L = 1024 // nseq
                      for c6 in range(6):
                          xt, xk = gx[0], "g_x0"
                          yt, yk = gy[0], "g_y0"
                          wt, wk, a = (wA, wAk, c6 * 128) if c6 < 4 else (wB, wBk, (c6 - 4) * 128)
                          for tb in range(2):
                              pb, pk = bank()
                              proj_F(wt, wk, a, 128, tb, pb, pk)
                              A(lambda e, tb=tb, pb=pb, xt=xt: e.copy(out=xt[:, tb * 512:(tb + 1) * 512], in_=pb[:, :]), [pk], [xk])
                          x3 = xt[:].rearrange("p (s t) -> p s t", s=nseq)
                          y3 = yt[:].rearrange("p (s t) -> p s t", s=nseq)
                          V(lambda e, c6=c6, xt=xt, yt=yt: e.tensor_scalar(out=yt[:], in0=xt[:], scalar1=cwT[:, l, c6, 2:3],
                                                                           scalar2=None, op0=ALU.mult), [xk, "cwT"], [yk])
                          for jj in (0, 1, 3, 4):
                              dsh = jj - 2
                              lo, hi = max(0, -dsh), L - max(0, dsh)
                              V(lambda e, c6=c6, jj=jj, x3=x3, y3=y3, lo=lo, hi=hi, dsh=dsh: e.scalar_tensor_tensor(
                                  out=y3[:, :, lo:hi], in0=x3[:, :, lo + dsh:hi + dsh], scalar=cwT[:, l, c6, jj:jj + 1],
                                  in1=y3[:, :, lo:hi], op0=ALU.mult, op1=ALU.add), [xk, yk, "cwT"], [yk])
                          act(yt[:], yt[:], AF.Silu, [yk], [yk])
                          if c6 < 4:
                              for tb in range(2):
                                  blk = slice(tb * 512, (tb + 1) * 512)
                                  sqv = xt[:, 0:512]
                                  rsv = xt[:, 512:1024]
                                  act(sqv, yt[:, blk], AF.Square, [yk], [xk])
                                  pb, pk = bank()
                                  mm(pb[:, :], onesblk[:, :], sqv, True, True, ["onesblk", xk], [pk])
                                  act(rsv, pb[:, :], AF.Sqrt, [pk], [xk], bias=EPS, scale=1.0)
                                  V(lambda e, rsv=rsv: e.reciprocal(out=rsv, in_=rsv), [xk], [xk])
                                  if c6 < 2:
                                      V(lambda e, c6=c6, blk=blk, yt=yt, rsv=rsv: e.scalar_tensor_tensor(
                                          out=qkT[:, c6, blk], in0=yt[:, blk], scalar=0.125, in1=rsv, op0=ALU.mult,
                                          op1=ALU.mult), [yk, xk], [("g_qkT", c6)])
                                  else:
                                      V(lambda e, c6=c6, blk=blk, yt=yt, rsv=rsv: e.tensor_tensor(out=qkT[:, c6, blk], in0=yt[:, blk], in1=rsv,
                                                                                        op=ALU.mult), [yk, xk], [("g_qkT", c6)])
                          if c6 >= 2:
                              srcT = qkT[:, c6, :] if c6 < 4 else yt[:]
                              srck = ("g_qkT", c6) if c6 < 4 else yk
                              dst = ktok if c6 < 4 else vtok
                              dstk = "g_ktok" if c6 < 4 else "g_vtok"
                              cc = c6 % 2
                              for half in range(2):
                                  pb, pk = bank()
                                  for q in range(4):
                                      tt = half * 4 + q
                                      tr(pb[:, q * 128:(q + 1) * 128], srcT[:, tt * 128:(tt + 1) * 128], ident[:], [srck, "ident"], [pk])
                                  A(lambda e, half=half, pb=pb, dst=dst, cc=cc: e.copy(
                                      out=dst[:, half * 4:half * 4 + 4, cc * 128:(cc + 1) * 128],
                                      in_=pb[:, :].rearrange("p (a b) -> p a b", a=4)), [pk], [dstk])
                      zproj(wC, wCk, 0, zT, "g_zT")
                      if _GB >= 2:
                        pb, pk = bank()
                        for tt in range(8):
                            for k in range(8):
                                mm(pb[:, tt * 16:(tt + 1) * 16], hnT[:, k, tt * 128:(tt + 1) * 128], wB[:, k, 256:272], k == 0, k == 7,
                                   [wBk, "hnT"], [pk])
                        V(lambda e, pb=pb: e.tensor_copy(out=ab[:].rearrange("p a b -> p (a b)"), in_=pb[:, 0:128]), [pk], ["g_ab"])
                        act(beta[:], ab[:, :, 0:8], AF.Sigmoid, ["g_ab"], ["g_beta"])
                        V(lambda e: e.tensor_scalar(out=nbeta[:], in0=beta[:], scalar1=-1.0, scalar2=None, op0=ALU.mult),
                          ["g_beta"], ["g_nbeta"])
                        V(lambda e: e.tensor_tensor(out=la[:], in0=ab[:, :, 8:16], in1=dtb[:, l, :].unsqueeze(1).to_broadcast([128, 8, 8]),
                                                    op=ALU.add), ["g_ab", "dtb"], ["g_la"])
                        V(lambda e: e.tensor_scalar(out=la[:], in0=la[:], scalar1=30.0, scalar2=None, op0=ALU.min), ["g_la"], ["g_la"])
                        act(la[:], la[:], AF.Exp, ["g_la"], ["g_la"])
                        act(la[:], la[:], AF.Ln, ["g_la"], ["g_la"], bias=1.0, scale=1.0)
                        V(lambda e: e.tensor_tensor(out=la[:], in0=la[:], in1=nega[:, l, :].unsqueeze(1).to_broadcast([128, 8, 8]),
                                                    op=ALU.mult), ["g_la", "nega"], ["g_la"])
                        pb, pk = bank()
                        for tt in range(8):
                            for d in range(2):
                                mm(pb[:, tt * 8 + d * 4:tt * 8 + d * 4 + 4], masks[:, d, :], la[:, tt, d * 4:(d + 1) * 4], True, True,
                                   ["masks", "g_la"], [pk])
                        V(lambda e, pb=pb: e.tensor_copy(out=gc[:].rearrange("p a b -> p (a b)"), in_=pb[:, 0:64]), [pk], ["g_gc"])
                        V(lambda e: e.tensor_scalar(out=ngc[:], in0=gc[:], scalar1=-1.0, scalar2=None, op0=ALU.mult), ["g_gc"], ["g_ngc"])
                        act(egc[:], gc[:], AF.Exp, ["g_gc"], ["g_egc"])
                        V(lambda e: e.tensor_tensor(out=bege[:], in0=beta[:], in1=egc[:], op=ALU.mult), ["g_beta", "g_egc"], ["g_bege"])
                        tap("g_gc", gc[:].rearrange("p a b -> p (a b)"), ["g_gc"])
                        tap("g_qkT", qkT[:].rearrange("p a b -> p (a b)"), ["g_qkT"])
                      if _GB >= 3:
                        S.barrier()
                        wv = [w_[:].rearrange("p a b -> p (a b)").bitcast(F32) for w_ in wbs]

                        def v4(ap):
                            return ap.rearrange("p (h f) -> p h f", h=4)
                        sets = []
                        sets.append(dict(
                            dg=dg[:], Dm=Dm[:], DT=DT[:], B=[Bm[0][:], Bm[1][:]], BT=[BTm[0][:], BTm[1][:]], Y=[Ym[0][:], Ym[1][:]],
                            qkd=qkd[:], vb=vb[:], kbgp=kbgp[:], kdcp=kdcp[:], kds=kds[:], eg2=eg2[:], wTm=wTm[:],
                            qkz=gx[0][:].rearrange("p (c f t) -> p c f t", c=4, f=2),
                            usb=gy[0][:, 0:256], vnew=gy[0][:, 256:512], o2s=gy[0][:, 512:768], otmp=gy[0][:, 768:1024]))
                        kds1 = sb("g_kds1", [128, 4], F32, ph)
                        eg21 = sb("g_eg21", [128, 2], F32, ph)
                        vb1 = sb("g_vb1", [128, 4, 64], F32, ph)
                        DT1 = sb("g_DT1", [128, 4, 128], F32, ph)
                        sets.append(dict(
                            dg=v4(stg[1][:, 0:512]), Dm=v4(stg[1][:, 512:1024]), DT=DT1[:],
                            B=[v4(wv[0][:, 0:512]), v4(wv[0][:, 512:1024])], BT=[v4(wv[0][:, 1024:1536]), v4(wv[0][:, 1536:2048])],
                            Y=[v4(wv[1][:, 0:512]), v4(wv[1][:, 512:1024])],
                            qkz=wv[1][:, 1024:2048].rearrange("p (c f t) -> p c f t", c=4, f=2),
                            qkd=v4(wv[2][:, 0:512]), kbgp=v4(wv[2][:, 512:1024]), kdcp=v4(wv[2][:, 1024:1536]), wTm=v4(wv[2][:, 1536:2048]),
                            vb=vb1[:], kds=kds1[:], eg2=eg21[:],
                            usb=stg[0][:, 0:256], vnew=stg[0][:, 256:512], o2s=stg[0][:, 512:768], otmp=stg[0][:, 768:1024]))
                        G(lambda e: e.memset(gx[0][:], 0.0), [], ["q0_qkz"])
                        G(lambda e: e.memset(wv[1][:, 1024:2048], 0.0), [], ["q1_qkz"])
                        G(lambda e: e.memset(wv[2][:, 512:2048], 0.0), [], ["q1_kbgp", "q1_kdcp", "q1_wTm"])
                        G(lambda e: e.memset(oacc[:], 0.0), [], ["g_oacc"])
                        seqs = [(2 * b_, 2 * b_ + 1) for b_ in range(4)] if grp == 0 else [tuple(range(8))]

                        def quad(si_, d, tt, init, fin, sidx):
                            T_ = sets[si_]
                            kp = "q%d_" % si_
                            if si_ == 0:
                                kq = dict(kbgp="g_kbgp", kdcp="g_kdcp", wTm="g_wTm")
                            else:
                                kq = dict(kbgp="q1_kbgp", kdcp="q1_kdcp", wTm="q1_wTm")
                            kq = {**{n: kp + n for n in ("dg", "Dm", "DT", "B0", "B1", "BT0", "BT1", "Y0", "Y1", "qkd", "vb", "kds", "eg2",
                                                         "qkz", "usb", "vnew", "o2s", "otmp")}, **kq}
                            qrr = [0]

                            def qbank():
                                i = 4 * si_ + qrr[0] % 4
                                qrr[0] += 1
                                return PS[i], ("ps", i)
                            last = 127 if d == 0 else 0
                            ts_ = slice(tt * 128, (tt + 1) * 128)
                            u0 = d * 4
                            dgt, Dmt, DTt, Bt, BTt, Yt = T_["dg"], T_["Dm"], T_["DT"], T_["B"], T_["BT"], T_["Y"]
                            qkdt, vbt, kbgpt, kdcpt, kdst, eg2t, wTmt, qkzt = (T_["qkd"], T_["vb"], T_["kbgp"], T_["kdcp"], T_["kds"],
                                                                               T_["eg2"], T_["wTm"], T_["qkz"])
                            usbt, vnewt, o2st, otmpt = T_["usb"], T_["vnew"], T_["o2s"], T_["otmp"]
                            if init:
                                if grp == 0:
                                    V(lambda e: e.memset(Sst[:, d, :, :], 0.0), [], [("g_S", d)])
                                else:
                                    DS(lambda e: e.dma_start(out=Sst[:, d, :, :],
                                                             in_=D["sg"][l, d].rearrange("(a b) k v -> (b k) a v", b=2)), [], [("g_S", d)])
                            for h in range(4):
                                G(lambda e, h=h: e.tensor_scalar(out=dgt[:, h, :], in0=ident[:], scalar1=gc[:, tt, u0 + h:u0 + h + 1],
                                                                 scalar2=None, op0=ALU.mult), ["ident", "g_gc"], [kq["dg"]])
                            psN, kN = qbank()
                            psP, kP = qbank()
                            for h in range(4):
                                hs = slice(h * 128, (h + 1) * 128)
                                mm(psN[:, hs], ones[:, :], dgt[:, h, :], True, False, ["ones", kq["dg"]], [kN])
                                mm(psN[:, hs], ident[:, :], masks[:, 4 + d, :], False, True, ["ident", "masks"], [kN])
                                mm(psP[:, hs], ones[:, :], dgt[:, h, :], True, False, ["ones", kq["dg"]], [kP])
                                mm(psP[:, hs], ident[:, :], masks[:, 2 + d, :], False, True, ["ident", "masks"], [kP])
                            yield
                            for h in range(4):
                                hs = slice(h * 128, (h + 1) * 128)
                                act(Dmt[:, h, :], psP[:, hs], AF.Exp, [kP, "g_gc"], [kq["Dm"]], bias=gc[:, tt, u0 + h:u0 + h + 1], scale=-1.0)
                                act(DTt[:, h, :], psN[:, hs], AF.Exp, [kN, "g_ngc"], [kq["DT"]], bias=ngc[:, tt, u0 + h:u0 + h + 1], scale=1.0)
                                act(kdst[:, h:h + 1], psN[:, h * 128 + last:h * 128 + last + 1], AF.Exp, [kN, "g_ngc"], [kq["kds"]],
                                    bias=ngc[:, tt, u0 + h:u0 + h + 1], scale=1.0)
                            for pr in range(2):
                                for hf in range(2):
                                    h = 2 * pr + hf
                                    A(lambda e, pr=pr, hf=hf, h=h: e.activation(
                                        out=eg2t[hf * 64:(hf + 1) * 64, pr:pr + 1],
                                        in_=psN[hf * 64:(hf + 1) * 64, h * 128 + last:h * 128 + last + 1], func=AF.Exp), [kN], [kq["eg2"]])
                            for c4 in range(4):
                                for hf in range(2):
                                    rows = slice(hf * 64, (hf + 1) * 64)
                                    G(lambda e, c4=c4, hf=hf, rows=rows: e.tensor_copy(out=qkzt[rows, c4, hf, :], in_=qkT[rows, c4, ts_]),
                                      ["g_qkT"], [kq["qkz"]])
                            psK, kK = qbank()
                            psQ, kQ = qbank()
                            for h in range(4):
                                hs = slice(h * 128, (h + 1) * 128)
                                kfull = qkT[:, 2 + h // 2, ts_]
                                mm(psK[:, hs], kfull, qkzt[:, 2 + h // 2, h % 2, :], True, True, ["g_qkT", kq["qkz"]], [kK])
                                mm(psQ[:, hs], kfull, qkzt[:, h // 2, h % 2, :], True, True, ["g_qkT", kq["qkz"]], [kQ])
                            yield
                            for h in range(4):
                                hs = slice(h * 128, (h + 1) * 128)
                                V(lambda e, h=h, hs=hs: e.scalar_tensor_tensor(
                                    out=Bt[0][:, h, :], in0=psK[:, hs], scalar=nbeta[:, tt, u0 + h:u0 + h + 1], in1=Dmt[:, h, :],
                                    op0=ALU.mult, op1=ALU.mult), [kK, "g_nbeta", kq["Dm"]], [kq["B0"]])
                            V(lambda e: e.tensor_tensor(out=qkdt.rearrange("p a b -> p (a b)"), in0=psQ[:, :],
                                                        in1=DTt.rearrange("p a b -> p (a b)"), op=ALU.mult), [kQ, kq["DT"]], [kq["qkd"]])
                            yield
                            pb, pk = qbank()
                            for h in range(4):
                                tr(pb[:, h * 128:(h + 1) * 128], Bt[0][:, h, :], ident[:], [kq["B0"], "ident"], [pk])
                            yield
                            V(lambda e, pb=pb: e.tensor_copy(out=BTt[0].rearrange("p a b -> p (a b)"), in_=pb[:, :]), [pk], [kq["BT0"]])
                            for h in range(4):
                                V(lambda e, pb=pb, h=h: e.tensor_tensor(out=Yt[0][:, h, :], in0=pb[:, h * 128:(h + 1) * 128], in1=ident[:],
                                                                        op=ALU.add), [pk, "ident"], [kq["Y0"]])
                            yield
                            cur = 0
                            for lev in range(1, 7):
                                nxt = 1 - cur
                                pb, pk = qbank()
                                for h in range(4):
                                    mm(pb[:, h * 128:(h + 1) * 128], BTt[cur][:, h, :], Bt[cur][:, h, :], True, True,
                                       [kq["BT%d" % cur], kq["B%d" % cur]], [pk])
                                if lev < 6:
                                    pb2, pk2 = qbank()
                                    for h in range(4):
                                        mm(pb2[:, h * 128:(h + 1) * 128], Bt[cur][:, h, :], BTt[cur][:, h, :], True, True,
                                           [kq["BT%d" % cur], kq["B%d" % cur]], [pk2])
                                yield
                                A(lambda e, pb=pb, nxt=nxt: e.copy(out=Bt[nxt].rearrange("p a b -> p (a b)"), in_=pb[:, :]),
                                  [pk], [kq["B%d" % nxt]])
                                if lev < 6:
                                    V(lambda e, pb2=pb2, nxt=nxt: e.tensor_copy(out=BTt[nxt].rearrange("p a b -> p (a b)"), in_=pb2[:, :]),
                                      [pk2], [kq["BT%d" % nxt]])
                                pb3, pk3 = qbank()
                                for h in range(4):
                                    mm(pb3[:, h * 128:(h + 1) * 128], Bt[nxt][:, h, :], Yt[cur][:, h, :], True, True,
                                       [kq["B%d" % nxt], kq["Y%d" % cur]], [pk3])
                                yield
                                V(lambda e, pb3=pb3, nxt=nxt, cur=cur: e.tensor_tensor(
                                    out=Yt[nxt].rearrange("p a b -> p (a b)"), in0=pb3[:, :],
                                    in1=Yt[cur].rearrange("p a b -> p (a b)"), op=ALU.add), [pk3, kq["Y%d" % cur]], [kq["Y%d" % nxt]])
                                cur = nxt
                                yield
                            Yf, Yk = Yt[cur], kq["Y%d" % cur]
                            for h in range(4):
                                hc = slice(h * 64, (h + 1) * 64)
                                pc_ = slice((h % 2) * 64, (h % 2) * 64 + 64)
                                G(lambda e, h=h, hc=hc: e.tensor_scalar(out=vbt[:, h, :], in0=vtok[:, tt, hc],
                                                                        scalar1=beta[:, tt, u0 + h:u0 + h + 1], scalar2=None, op0=ALU.mult),
                                  ["g_vtok", "g_beta"], [kq["vb"]])
                                G(lambda e, h=h, hc=hc, pc_=pc_: e.tensor_scalar(out=kbgpt[:, h, pc_], in0=ktok[:, tt, hc],
                                                                                 scalar1=bege[:, tt, u0 + h:u0 + h + 1], scalar2=None,
                                                                                 op0=ALU.mult), ["g_ktok", "g_bege"], [kq["kbgp"]])
                                G(lambda e, h=h, hc=hc, pc_=pc_: e.tensor_scalar(out=kdcpt[:, h, pc_], in0=ktok[:, tt, hc],
                                                                                 scalar1=kdst[:, h:h + 1], scalar2=None, op0=ALU.mult),
                                  ["g_ktok", kq["kds"]], [kq["kdcp"]])
                            pbu, pku = qbank()
                            for h in range(4):
                                mm(pbu[:, h * 64:(h + 1) * 64], Yf[:, h, :], vbt[:, h, :], True, True, [Yk, kq["vb"]], [pku])
                            pbw, pkw = qbank()
                            for pr in range(2):
                                for hf in range(2):
                                    h = 2 * pr + hf
                                    mm(pbw[:, pr * 128:(pr + 1) * 128], kbgpt[:, h, :], Yf[:, h, :], hf == 0, hf == 1, [kq["kbgp"], Yk], [pkw])
                            yield
                            A(lambda e: e.copy(out=usbt, in_=pbu[:, 0:256]), [pku], [kq["usb"]])
                            for pr in range(2):
                                for hf in range(2):
                                    rows = slice(hf * 64, (hf + 1) * 64)
                                    V(lambda e, pr=pr, hf=hf, rows=rows: e.tensor_copy(out=wTmt[rows, 2 * pr + hf, :],
                                                                                       in_=pbw[rows, pr * 128:(pr + 1) * 128]), [pkw], [kq["wTm"]])
                            yield
                            pbv, pkv = qbank()
                            pbo, pko = qbank()
                            for h in range(4):
                                hc = slice(h * 64, (h + 1) * 64)
                                mm(pbv[:, hc], wTmt[:, h, :], Sst[:, d, h // 2, :], True, True, [kq["wTm"], ("g_S", d)], [pkv])
                                mm(pbo[:, hc], qkzt[:, h // 2, h % 2, :], Sst[:, d, h // 2, :], True, True, [kq["qkz"], ("g_S", d)], [pko])
                            yield
                            V(lambda e: e.tensor_tensor(out=vnewt, in0=usbt, in1=pbv[:, 0:256], op=ALU.subtract), [kq["usb"], pkv], [kq["vnew"]])
                            pb2, pk2 = qbank()
                            for h in range(4):
                                hc = slice(h * 64, (h + 1) * 64)
                                mm(pb2[:, hc], qkdt[:, h, :], vnewt[:, hc], True, True, [kq["qkd"], kq["vnew"]], [pk2])
                            pbs, pks = qbank()
                            for pr in range(2):
                                for hf in range(2):
                                    h = 2 * pr + hf
                                    mm(pbs[:, pr * 64:(pr + 1) * 64], kdcpt[:, h, :], vnewt[:, h * 64:(h + 1) * 64], hf == 0, hf == 1,
                                       [kq["kdcp"], kq["vnew"]], [pks])
                            yield
                            A(lambda e: e.copy(out=o2st, in_=pb2[:, 0:256]), [pk2], [kq["o2s"]])
                            for h in range(4):
                                hc = slice(h * 64, (h + 1) * 64)
                                V(lambda e, h=h, hc=hc: e.scalar_tensor_tensor(
                                    out=otmpt[:, hc], in0=pbo[:, hc], scalar=egc[:, tt, u0 + h:u0 + h + 1], in1=o2st[:, hc],
                                    op0=ALU.mult, op1=ALU.add), [pko, "g_egc", kq["o2s"]], [kq["otmp"]])
                            G(lambda e: e.tensor_tensor(out=oacc[:, tt, :], in0=oacc[:, tt, :], in1=otmpt, op=ALU.add),
                              [("g_oacc", tt), kq["otmp"]], [("g_oacc", tt)])
                            for pr in range(2):
                                V(lambda e, pr=pr: e.scalar_tensor_tensor(
                                    out=Sst[:, d, pr, :], in0=Sst[:, d, pr, :], scalar=eg2t[:, pr:pr + 1],
                                    in1=pbs[:, pr * 64:(pr + 1) * 64], op0=ALU.mult, op1=ALU.add),
                                  [("g_S", d), kq["eg2"], pks], [("g_S", d)])
                            if fin and grp == 0:
                                DS(lambda e: e.dma_start(out=D["nsg"][sidx, l, d].rearrange("(a b) k v -> (b k) a v", b=2),
                                                         in_=Sst[:, d, :, :]), [("g_S", d)], [])
                            yield

                        sched = [[], []]
                        for d in range(2):
                            for sidx, tiles in enumerate(seqs):
                                order = tiles if d == 0 else tuple(reversed(tiles))
                                for qi, tt in enumerate(order):
                                    sched[d].append((tt, qi == 0, qi == len(order) - 1, sidx))
                        for (f_, b_) in zip(sched[0][:_GBQ], sched[1][:_GBQ]):
                            gens = [quad(0, 0, *f_), quad(1, 1, *b_)]
                            alive = [True, True]
                            while any(alive):
                                for gi in range(2):
                                    if alive[gi]:
                                        try:
                                            next(gens[gi])
                                        except StopIteration:
                                            alive[gi] = False
                      S.mute = False
                      if _GB >= 4:
                        norm_gate_out(ph, "g", oacc, "g_oacc", ggn, "ggn", l, zT, "g_zT", 1)
                S.barrier()
                stage("B")
                if grp == groups[0] and l == layers[0]:
                    tap("brB", brT[:, 2:4, :].rearrange("p a b -> p (a b)"), [("brT", 1)])

                with ExitStack() as ph:
                    mT = sb("m_T", [128, 8, 1024], BF16, ph)
                    wmg = [sb("m_wg%d" % i, [128, 4, 8, 128], BF16, ph) for i in range(2)]
                    wbr = [sb("m_wb%d" % i, [128, 4, 2, 128], BF16, ph) for i in range(2)]
                    gts = [sb("m_gt%d" % i, [128, 512], BF16, ph) for i in range(2)]
                    acc = sb("m_acc", [128, 512], F32, ph)
                    tmpm = sb("m_tmp", [128, 512], F32, ph)
                    gi = 0
                    for dc in range(8):
                        wg_t, wgk = wmg[dc % 2], "m_wg%d" % (dc % 2)
                        wb_t, wbk = wbr[dc % 2], "m_wb%d" % (dc % 2)
                        for n in range(4):
                            c0 = 3856 + n * 1024 + dc * 128
                            DG(lambda e, n=n, c0=c0, wg_t=wg_t: e.dma_start(
                                out=wg_t[:, n, :, :], in_=D["w_in"][l, :, c0:c0 + 128].rearrange("(k p) c -> p k c", p=128)),
                               [], [wgk])
                            DG(lambda e, n=n, dc=dc, wb_t=wb_t: e.dma_start(
                                out=wb_t[:, n, :, :],
                                in_=D["w_branch"][l, n, :, dc * 128:(dc + 1) * 128].rearrange("(k p) c -> p k c", p=128)),
                               [], [wbk])
                        for tb in range(2):
                            blk = slice(tb * 512, (tb + 1) * 512)
                            for n in range(4):
                                pb, pk = bank()
                                for k in range(8):
                                    mm(pb[:, :], wg_t[:, n, k, :], hnT[:, k, blk], k == 0, k == 7, [wgk, "hnT"], [pk])
                                gt, gtk = gts[gi % 2], "m_gt%d" % (gi % 2)
                                gi += 1
                                act(gt[:], pb[:, :], AF.Sigmoid, [pk], [gtk])
                                pb2, pk2 = bank()
                                for kk in range(2):
                                    mm(pb2[:, :], wb_t[:, n, kk, :], brT[:, n * 2 + kk, blk], kk == 0, kk == 1, [wbk, ("brT", n)], [pk2])
                                if n == 0:
                                    V(lambda e, pb2=pb2, gt=gt: e.tensor_tensor(out=acc[:], in0=pb2[:, :], in1=gt[:], op=ALU.mult),
                                      [pk2, gtk], ["m_acc"])
                                else:
                                    V(lambda e, pb2=pb2, gt=gt: e.tensor_tensor(out=tmpm[:], in0=pb2[:, :], in1=gt[:], op=ALU.mult),
                                      [pk2, gtk], ["m_tmp"])
                                    if n < 3:
                                        G(lambda e: e.tensor_tensor(out=acc[:], in0=acc[:], in1=tmpm[:], op=ALU.add),
                                          ["m_acc", "m_tmp"], ["m_acc"])
                                    else:
                                        G(lambda e, dc=dc, blk=blk: e.tensor_tensor(out=mT[:, dc, blk], in0=acc[:], in1=tmpm[:], op=ALU.add),
                                          ["m_acc", "m_tmp"], [("m_T", dc)])
                    for half in range(2):
                        i = wcnt[0] % 3
                        wcnt[0] += 1
                        wo, wok = wbs[i], "wb%d" % i
                        DG(lambda e, half=half, wo=wo: e.dma_start(
                            out=wo[:], in_=D["w_out"][l, :, half * 512:(half + 1) * 512].rearrange("(k p) n -> p k n", p=128)),
                           [], [wok])
                        for q in range(4):
                            oc = half * 4 + q
                            for tb in range(2):
                                blk = slice(tb * 512, (tb + 1) * 512)
                                pb, pk = bank()
                                for k in range(8):
                                    mm(pb[:, :], wo[:, k, q * 128:(q + 1) * 128], mT[:, k, blk], k == 0, k == 7, [wok, ("m_T", k)], [pk])
                                V(lambda e, oc=oc, blk=blk, pb=pb: e.scalar_tensor_tensor(
                                    out=hT[:, oc, blk], in0=pb[:, :], scalar=modT[:, l, 16 + oc, j:j + 1], in1=hT[:, oc, blk],
                                    op0=ALU.mult, op1=ALU.add), [pk, "modT", "hT"], ["hT"])
                S.barrier()
                stage("merge")
                if grp == groups[0] and l == layers[0]:
                    tap("hT1", hT[:].rearrange("p a b -> p (a b)"), ["hT"])

            with ExitStack() as ph:
                ynT = sb("f_yn", [128, 8, 1024], F32, ph)
                rmsnorm_T(ph, lambda k: fnT[:, k:k + 1], lambda k: None,
                          lambda k, tb: ynT[:, k, tb * 512:(tb + 1) * 512], "f_yn")
                for tt in range(8):
                    s_t, sk = stg[scnt[0] % 2], "stg%d" % (scnt[0] % 2)
                    scnt[0] += 1
                    for half in range(2):
                        pb, pk = bank()
                        for q in range(4):
                            k = half * 4 + q
                            tr(pb[:, q * 128:(q + 1) * 128], ynT[:, k, tt * 128:(tt + 1) * 128], ident[:], ["f_yn", "ident"], [pk])
                        if half == 0:
                            A(lambda e, pb=pb, s_t=s_t: e.copy(out=s_t[:, 0:512], in_=pb[:, :]), [pk], [sk])
                        else:
                            V(lambda e, pb=pb, s_t=s_t: e.tensor_copy(out=s_t[:, 512:1024], in_=pb[:, :]), [pk], [sk])
                    DS(lambda e, tt=tt, s_t=s_t: e.dma_start(out=yout[tt * 128:(tt + 1) * 128, :], in_=s_t[:]), [sk], [])
            S.barrier()
        try:
            run_groups()
        except _Stop:
            S.barrier()
        with nc.allow_non_contiguous_dma(reason="small transposed parameter loads"):
            stats = S.emit()
    return nc, stats


_CACHE = {}


def _in_maps(inp):
    f = lambda a: np.ascontiguousarray(np.asarray(a, dtype=np.float32))
    cst = _consts(f(inp["na_bias"]))
    shared = dict(
        w_ada=f(inp["w_ada"]), b_ada=f(inp["b_ada"]), norm_g=f(inp["norm_g"]), w_in=f(inp["w_in"]), conv_w=f(inp["conv_w"]),
        a_log=f(inp["gdn_a_log"]).reshape(2, 8), dt_bias=f(inp["gdn_dt_bias"]).reshape(2, 8), gdn_norm=f(inp["gdn_norm"]),
        q_norm=f(inp["attn_q_norm"]), k_norm=f(inp["attn_k_norm"]), ret_norm=f(inp["ret_norm"]),
        w_branch=f(inp["w_branch"]), w_out=f(inp["w_out"]), final_norm=f(inp["final_norm"]).reshape(1, 1024), **cst)
    xp, xs = f(inp["x_prompt"]), f(inp["x_sample"])
    maps = []
    for c in range(8):
        m = dict(shared)
        m["xp"] = xp[4 * c:4 * c + 4].reshape(1024, 1024)
        m["xs"] = xs[c]
        m["cak"] = f(inp["cache_attn_k"][c]).reshape(2, 512, 128)
        m["cav"] = f(inp["cache_attn_v"][c]).reshape(2, 512, 128)
        m["cnk"] = f(inp["cache_na_k"][c]).reshape(2, 512, 256)
        m["cnv"] = f(inp["cache_na_v"][c]).reshape(2, 512, 256)
        m["sg"] = f(inp["state_gdn"][c])
        m["sr"] = f(inp["state_ret"][c])
        m["cond"] = np.stack([f(inp["c_ctx"]), f(inp["c"][c])])
        maps.append(m)
    return maps


def kernel(**inputs):
    if "nc" not in _CACHE:
        _CACHE["nc"] = build()[0]
    nc = _CACHE["nc"]
    maps = _in_maps(inputs)
    res = run_bass_kernel_spmd(nc, maps, core_ids=list(range(8)))
    R = res.results
    cat = lambda n: np.concatenate([np.asarray(r[n]) for r in R], axis=0)
    y_prompt = cat("yp").reshape(32, 256, 1024)
    y_sample = np.stack([np.asarray(r["ys"]) for r in R])
    nak = cat("nak").reshape(32, 2, 256, 2, 64)
    nav = cat("nav").reshape(32, 2, 256, 2, 64)
    nnk = cat("nnk").reshape(32, 2, 256, 4, 64)
    nnv = cat("nnv").reshape(32, 2, 256, 4, 64)
    nsg = cat("nsg")
    nsr = cat("nsr")
    return tuple(np.ascontiguousarray(a, dtype=np.float32) for a in (y_prompt, y_sample, nak, nav, nnk, nnv, nsg, nsr))
```

```python
import numpy as np
import concourse.bass as bass
import concourse.mybir as mybir
from concourse.bass_utils import run_bass_kernel_spmd
from contextlib import ExitStack

F32 = mybir.dt.float32
BF16 = mybir.dt.bfloat16
ALU = mybir.AluOpType
AF = mybir.ActivationFunctionType
AX = mybir.AxisListType
EPS = 1e-6
NEG = -30000.0
BIG = 1.0e5


import types
import os
_CL = int(os.environ.get('CLEVEL', '9'))
_BL = int(os.environ.get('BLEVEL', '9'))
_GB = int(os.environ.get('GB', '9'))
_GBQ = int(os.environ.get('GBQ', '99'))
_GBS = int(os.environ.get('GBS', '9'))
_STRICT = bool(int(os.environ.get('STRICT', '0')))


def _freeze(fn, _depth=0):
    if fn is None or fn.__closure__ is None:
        return fn
    cells = []
    for c in fn.__closure__:
        try:
            v = c.cell_contents
        except ValueError:
            cells.append(c)
            continue
        if isinstance(v, types.FunctionType) and v.__closure__ is not None and _depth < 3:
            v = _freeze(v, _depth + 1)
        cells.append(types.CellType(v))
    return types.FunctionType(fn.__code__, fn.__globals__, fn.__name__, fn.__defaults__, tuple(cells))


class _Op:
    __slots__ = ("fn", "waits", "dma", "dma_n")

    def __init__(self, fn, waits, dma, dma_n):
        self.fn, self.waits, self.dma, self.dma_n = fn, waits, dma, dma_n


class Sched:
    KD = 8
    BLK = {"pe": "tensor", "act": "scalar", "dve": "vector", "pool": "gpsimd", "sp": "sync"}

    def __init__(self, nc, es):
        self.nc = nc
        self.ops = {e: [] for e in self.BLK}
        self.state = {}
        self.ndma = {e: 0 for e in self.BLK}
        self.seen_c = {e: {} for e in self.BLK}
        self.seen_d = {e: set() for e in self.BLK}
        self.last = {}
        self.csem = {e: es.enter_context(nc.semaphore("c_" + e)) for e in ("pe", "act", "dve", "pool")}
        self.dsem = {e: [es.enter_context(nc.semaphore("d_%s%d" % (e, i))) for i in range(self.KD)]
                     for e in ("sp", "pool")}

    def _split(self, key):
        if isinstance(key, tuple):
            return key[0], key[1:]
        return key, None

    def _recs(self, key):
        name, sub = self._split(key)
        d = self.state.get(name)
        if not d:
            return []
        if sub is None:
            return list(d.values())
        out = []
        if sub in d:
            out.append(d[sub])
        if None in d:
            out.append(d[None])
        return out

    def _filter(self, eng, raw, other, dma):
        waits = []
        for d in sorted(raw | other):
            if d[0] == "c":
                if d[1] == eng and not dma:
                    if eng == "pe" or (d not in raw and not _STRICT):
                        continue
                if self.seen_c[eng].get(d[1], -1) >= d[2]:
                    continue
                self.seen_c[eng][d[1]] = d[2]
                waits.append(d)
            else:
                if d in self.seen_d[eng]:
                    continue
                self.seen_d[eng].add(d)
                waits.append(d)
        best, fin = {}, []
        for w in waits:
            if w[0] == "c":
                if w[1] not in best or best[w[1]][2] < w[2]:
                    best[w[1]] = w
            else:
                fin.append(w)
        fin.extend(best.values())
        return fin

    mute = False

    def op(self, eng, fn, reads=(), writes=(), dma=False):
        if self.mute:
            return None
        fn = _freeze(fn)
        idx = len(self.ops[eng])
        raw, other = set(), set()
        for key in reads:
            for rec in self._recs(key):
                if rec[0] is not None:
                    raw.add(rec[0])
        for key in writes:
            for rec in self._recs(key):
                if rec[0] is not None:
                    other.add(rec[0])
                other.update(rec[1])
        if dma:
            n = self.ndma[eng]
            self.ndma[eng] += 1
            ev = ("d", eng, n)
            self.last[("d", eng, n % self.KD)] = ev
        else:
            n = None
            ev = ("c", eng, idx)
            self.last[("c", eng)] = ev
        fin = self._filter(eng, raw, other, dma)
        self.ops[eng].append(_Op(fn, fin, dma, n))
        for key in reads:
            name, sub = self._split(key)
            self.state.setdefault(name, {}).setdefault(sub, [None, []])[1].append(ev)
        for key in writes:
            name, sub = self._split(key)
            d = self.state.setdefault(name, {})
            if sub is None:
                d.clear()
            d[sub] = [ev, []]
        return ev

    def barrier(self):
        evs = set(self.last.values())
        for e in self.BLK:
            fin = self._filter(e, set(evs), set(), True)
            if fin:
                self.ops[e].append(_Op(None, fin, False, None))
        self.state = {}

    def emit(self):
        for e in ("sp", "pool"):
            n = self.ndma[e]
            if n:
                self.ops[e].append(_Op(None, [("d", e, i) for i in range(max(0, n - self.KD), n)], False, None))
        waited = {e: set() for e in self.BLK}
        for e, ops in self.ops.items():
            for o in ops:
                for w in o.waits:
                    if w[0] == "c":
                        waited[w[1]].add(w[2])
        val = {}
        for e in self.BLK:
            for rank, idx in enumerate(sorted(waited[e])):
                val[(e, idx)] = rank + 1
        KD = self.KD
        self.maxval = {e: len(waited[e]) for e in self.BLK}
        self.maxdma = {e: 16 * ((self.ndma[e] - 1) // KD + 1) for e in ("sp", "pool")}
        if os.environ.get("SEMDBG"):
            print("SEM max values", self.maxval, self.maxdma, flush=True)
        with self.nc.Block() as block:
            for e in self.BLK:
                def body(engine, e=e):
                    for idx, o in enumerate(self.ops[e]):
                        for w in o.waits:
                            if w[0] == "c":
                                engine.wait_ge(self.csem[w[1]], val[(w[1], w[2])])
                            else:
                                engine.wait_ge(self.dsem[w[1]][w[2] % KD], 16 * (w[2] // KD + 1))
                        if o.fn is None:
                            continue
                        if o.dma:
                            n = o.dma_n
                            if n >= KD:
                                engine.wait_ge(self.dsem[e][n % KD], 16 * (n // KD))
                            o.fn(engine).then_inc(self.dsem[e][n % KD], 16)
                        else:
                            ins = o.fn(engine)
                            if idx in waited[e]:
                                ins.then_inc(self.csem[e], 1)
                getattr(block, self.BLK[e])(body)
        return {e: len(self.ops[e]) for e in self.BLK}


def _na_pairs():
    t = np.arange(1024)
    r, c = t // 64, t % 64
    rs = np.clip(r - 4, 0, 8)
    cs = np.clip(c - 8, 0, 48)
    valid = ((r[None, :] >= rs[:, None]) & (r[None, :] < rs[:, None] + 8) &
             (c[None, :] >= cs[:, None]) & (c[None, :] < cs[:, None] + 16))
    dr = r[None, :] - r[:, None] + 7
    dc = np.clip(c[None, :] - c[:, None] + 15, 0, 30)
    pairs = []
    for qt in range(8):
        for kt in range(8):
            if valid[qt * 128:(qt + 1) * 128, kt * 128:(kt + 1) * 128].any():
                pairs.append((qt, kt))
    return valid, dr, dc, pairs


_NA = _na_pairs()
NPAIR = len(_NA[3])


def _consts(na_bias):
    c = {}
    c["c_ident"] = np.eye(128, dtype=np.float32)
    p = np.arange(128)[:, None]
    f = np.arange(128)[None, :]
    m = np.zeros((6, 128, 128), np.float32)
    m[0] = (p <= f)
    m[1] = (p >= f)
    m[2] = np.where(f < p, 0.0, BIG)
    m[3] = np.where(f > p, 0.0, BIG)
    m[4] = np.where(f >= p, 0.0, -BIG)
    m[5] = np.where(f <= p, 0.0, -BIG)
    c["c_masks"] = m
    h = np.arange(4, dtype=np.float64)
    lgf = np.log1p(-np.exp2(-(5.0 + h)))
    lgb = np.log1p(-np.exp2(-(5.5 + h)))
    j = np.arange(128, dtype=np.float64)
    dct = np.zeros((4, 128, 128), np.float64)
    for hh in range(4):
        d = j[None, :] - j[:, None]
        dct[hh] = np.where(d > 0, np.exp(np.maximum(d, 0) * lgf[hh]), 0.0) + \
            np.where(d < 0, np.exp(np.maximum(-d, 0) * lgb[hh]), 0.0) + np.where(d == 0, 2.0, 0.0)
    c["c_dct"] = (dct * 0.125).astype(np.float32)
    rqt = np.zeros((2, 4, 128), np.float64)
    kdt = np.zeros((128, 4, 2, 64), np.float64)
    cdt = np.zeros((128, 4, 64), np.float64)
    for hh in range(4):
        rqt[0, hh] = np.exp((j + 1.0) * lgf[hh]) * 0.125
        rqt[1, hh] = np.exp((128.0 - j) * lgb[hh]) * 0.125
        kdt[:, hh, 0, :] = np.exp((127.0 - j) * lgf[hh])[:, None]
        kdt[:, hh, 1, :] = np.exp(j * lgb[hh])[:, None]
        hf_, pr_ = hh % 2, hh // 2
        cdt[hf_ * 64:(hf_ + 1) * 64, 0 * 2 + pr_, :] = np.exp(128.0 * lgf[hh])
        cdt[hf_ * 64:(hf_ + 1) * 64, 1 * 2 + pr_, :] = np.exp(128.0 * lgb[hh])
    c["c_rqt"] = rqt.astype(np.float32)
    c["c_kdt"] = kdt.astype(np.float32)
    c["c_cdt"] = cdt.astype(np.float32)
    t = np.arange(1024)
    row = (t // 64).astype(np.float32)
    col = (t % 64).astype(np.float32)
    inv = (10000.0 ** (-np.arange(16, dtype=np.float32) / 16)).astype(np.float32)
    ar = row[:, None] * inv[None, :]
    ac = col[:, None] * inv[None, :]
    cc = np.concatenate([np.cos(ar), np.cos(ar), np.cos(ac), np.cos(ac)], axis=1)
    ss = np.concatenate([-np.sin(ar), np.sin(ar), -np.sin(ac), np.sin(ac)], axis=1)
    c["c_rope"] = np.stack([np.tile(cc, (1, 6)), np.tile(ss, (1, 6))]).astype(np.float32)
    valid, dr, dc, pairs = _NA
    nab = np.empty((2, NPAIR, 4, 128, 128), np.float32)
    for pi, (qt, kt) in enumerate(pairs):
        qs = slice(qt * 128, (qt + 1) * 128)
        ks = slice(kt * 128, (kt + 1) * 128)
        v = valid[qs, ks].T
        g = na_bias[:, :, dr[qs, ks].T, dc[qs, ks].T]
        nab[:, pi] = np.where(v[None, None], g, np.float32(NEG))
    c["c_nab"] = nab
    return c


IN_SHAPES = dict(
    xp=[1024, 1024], xs=[1024, 1024], cak=[2, 512, 128], cav=[2, 512, 128], cnk=[2, 512, 256], cnv=[2, 512, 256],
    sg=[2, 2, 4, 64, 64], sr=[2, 2, 4, 64, 64], cond=[2, 1024],
    w_ada=[2, 1024, 3072], b_ada=[2, 3072], norm_g=[2, 1024], w_in=[2, 1024, 7952], conv_w=[2, 5, 768],
    a_log=[2, 8], dt_bias=[2, 8], gdn_norm=[2, 64], q_norm=[2, 64], k_norm=[2, 64], ret_norm=[2, 64],
    w_branch=[2, 4, 256, 1024], w_out=[2, 1024, 1024], final_norm=[1, 1024],
    c_ident=[128, 128], c_masks=[6, 128, 128], c_dct=[4, 128, 128], c_rqt=[2, 4, 128], c_kdt=[128, 4, 2, 64],
    c_cdt=[128, 4, 64], c_rope=[2, 1024, 384], c_nab=[2, NPAIR, 4, 128, 128])
OUT_SHAPES = dict(yp=[1024, 1024], ys=[1024, 1024], nak=[4, 2, 256, 128], nav=[4, 2, 256, 128],
                  nnk=[4, 2, 256, 256], nnv=[4, 2, 256, 256], nsg=[4, 2, 2, 4, 64, 64], nsr=[4, 2, 2, 4, 64, 64])


class _Stop(Exception):
    pass


def build(taps=None, groups=(0, 1), layers=(0, 1), stop_after=None):
    taps = taps or {}
    nc = bass.Bass("TRN2", target_bir_lowering=False)
    D = {}
    for n, s in IN_SHAPES.items():
        D[n] = nc.dram_tensor(n, list(s), F32, kind="ExternalInput").ap()
    for n, s in OUT_SHAPES.items():
        D[n] = nc.dram_tensor(n, list(s), F32, kind="ExternalOutput").ap()
    for n, s in taps.items():
        D["tap_" + n] = nc.dram_tensor("tap_" + n, list(s), F32, kind="ExternalOutput").ap()

    with ExitStack() as es:
        S = Sched(nc, es)

        uid = [0]

        def sb(name, shape, dt=F32, st=es):
            uid[0] += 1
            return st.enter_context(nc.sbuf_tensor("%s_%d" % (name, uid[0]), list(shape), dt))

        PS = [es.enter_context(nc.psum_tensor("ps%d" % i, [128, 512], F32)) for i in range(8)]
        rr = [0]

        def bank():
            i = 4 + rr[0] % 4
            rr[0] += 1
            return PS[i], ("ps", i)

        def V(fn, r, w): return S.op("dve", fn, r, w)
        def A(fn, r, w): return S.op("act", fn, r, w)
        def G(fn, r, w): return S.op("pool", fn, r, w)
        def P(fn, r, w): return S.op("pe", fn, r, w)
        def DS(fn, r, w): return S.op("sp", fn, r, w, dma=True)
        def DG(fn, r, w): return S.op("pool", fn, r, w, dma=True)

        def mm(out, lhsT, rhs, start, stop, r, w):
            P(lambda e: e.matmul(out, lhsT=lhsT, rhs=rhs, start=start, stop=stop), r, w)

        def tr(out, in_, idt, r, w):
            P(lambda e: e.transpose(out, in_, idt), r, w)

        def act(out, in_, func, r, w, bias=0.0, scale=1.0):
            A(lambda e: e.activation(out=out, in_=in_, func=func, bias=bias, scale=scale), r, w)

        def tap(name, ap, reads):
            if name in taps and name not in os.environ.get("NOTAP", "").split(","):
                DG(lambda e: e.dma_start(out=D["tap_" + name], in_=ap), reads, [])

        ident = sb("ident", [128, 128])
        identb = sb("identb", [128, 128], BF16)
        ones = sb("ones", [128, 128])
        onesblk = sb("onesblk", [128, 128])
        onespad = sb("onespad", [128, 2, 128], BF16)
        masks = sb("masks", [128, 6, 128])
        dct = sb("dct", [128, 4, 128])
        rqt = sb("rqt", [128, 2, 4, 128])
        kdt = sb("kdt", [128, 4, 2, 64])
        cdt = sb("cdt", [128, 4, 64])
        ngT = sb("ngT", [128, 2, 8])
        fnT = sb("fnT", [128, 8])
        baT = sb("baT", [128, 2, 24])
        cwT = sb("cwT", [128, 2, 6, 5])
        gqk = sb("gqk", [128, 2, 6, 64])
        gkn = sb("gkn", [128, 2, 2, 64])
        ggn = sb("ggn", [128, 2, 4, 64])
        grn = sb("grn", [128, 2, 4, 64])
        alb = sb("alb", [128, 2, 8])
        dtb = sb("dtb", [128, 2, 8])
        nega = sb("nega", [128, 2, 8])
        condT = sb("condT", [128, 8, 2])
        scond = sb("scond", [128, 8, 2])
        modT = sb("modT", [128, 2, 24, 2])
        gmul = sb("gmul", [128, 2, 8, 2])

        DS(lambda e: e.dma_start(out=ident[:], in_=D["c_ident"]), [], ["ident"])
        DG(lambda e: e.dma_start(out=identb[:], in_=D["c_ident"]), [], ["identb"])
        V(lambda e: e.memset(ones[:], 1.0), [], ["ones"])
        V(lambda e: e.memset(onesblk[:], 0.0), [], ["onesblk"])
        V(lambda e: e.memset(onesblk[0:64, 0:64], 1.0), [], ["onesblk"])
        V(lambda e: e.memset(onesblk[64:128, 64:128], 1.0), [], ["onesblk"])
        V(lambda e: e.memset(onespad[:], 0.0), [], ["onespad"])
        V(lambda e: e.memset(onespad[:, 0, 0:64], 1.0), [], ["onespad"])
        V(lambda e: e.memset(onespad[:, 1, 64:128], 1.0), [], ["onespad"])
        DS(lambda e: e.dma_start(out=masks[:], in_=D["c_masks"].rearrange("m p f -> p m f")), [], ["masks"])
        DS(lambda e: e.dma_start(out=dct[:], in_=D["c_dct"].rearrange("m p f -> p m f")), [], ["dct"])
        DS(lambda e: e.dma_start(out=rqt[:].rearrange("p a b c -> p (a b c)"),
                                 in_=D["c_rqt"].rearrange("a b c -> (a b c)").partition_broadcast(128)), [], ["rqt"])
        DS(lambda e: e.dma_start(out=kdt[:], in_=D["c_kdt"]), [], ["kdt"])
        DS(lambda e: e.dma_start(out=cdt[:], in_=D["c_cdt"]), [], ["cdt"])
        DS(lambda e: e.dma_start(out=ngT[:], in_=D["norm_g"].rearrange("l (c p) -> p l c", p=128)), [], ["ngT"])
        DS(lambda e: e.dma_start(out=fnT[:], in_=D["final_norm"].rearrange("o (c p) -> p (o c)", p=128)), [], ["fnT"])
        DS(lambda e: e.dma_start(out=baT[:], in_=D["b_ada"].rearrange("l (c p) -> p l c", p=128)), [], ["baT"])
        for l in range(2):
            for c6 in range(6):
                DS(lambda e, l=l, c6=c6: e.dma_start(
                    out=cwT[:, l, c6, :], in_=D["conv_w"][l, :, c6 * 128:(c6 + 1) * 128].rearrange("j p -> p j")),
                   [], ["cwT"])
        for jj in range(2):
            DS(lambda e, jj=jj: e.dma_start(out=condT[:, :, jj], in_=D["cond"][jj].rearrange("(c p) -> p c", p=128)), [], ["condT"])
        for l in range(2):
            for hh in range(6):
                src = "q_norm" if hh < 4 else "k_norm"
                DS(lambda e, l=l, hh=hh, src=src: e.dma_start(out=gqk[:, l, hh, :], in_=D[src][l].partition_broadcast(128)),
                   [], ["gqk"])
            for hh in range(2):
                DS(lambda e, l=l, hh=hh: e.dma_start(out=gkn[:, l, hh, :], in_=D["k_norm"][l].partition_broadcast(128)),
                   [], ["gkn"])
            for hh in range(4):
                DS(lambda e, l=l, hh=hh: e.dma_start(out=ggn[:, l, hh, :], in_=D["gdn_norm"][l].partition_broadcast(128)),
                   [], ["ggn"])
                DS(lambda e, l=l, hh=hh: e.dma_start(out=grn[:, l, hh, :], in_=D["ret_norm"][l].partition_broadcast(128)),
                   [], ["grn"])
            DS(lambda e, l=l: e.dma_start(out=alb[:, l, :], in_=D["a_log"][l].partition_broadcast(128)), [], ["alb"])
            DS(lambda e, l=l: e.dma_start(out=dtb[:, l, :], in_=D["dt_bias"][l].partition_broadcast(128)), [], ["dtb"])
        for l in range(2):
            V(lambda e, l=l: e.tensor_scalar(out=gqk[:, l, 0:4, :], in0=gqk[:, l, 0:4, :], scalar1=0.125, scalar2=None,
                                             op0=ALU.mult), ["gqk"], ["gqk"])
        act(nega[:], alb[:], AF.Exp, ["alb"], ["nega"])
        V(lambda e: e.tensor_scalar(out=nega[:], in0=nega[:], scalar1=-1.0, scalar2=None, op0=ALU.mult), ["nega"], ["nega"])
        act(scond[:], condT[:], AF.Silu, ["condT"], ["scond"])

        with ExitStack() as ph:
            wa = [sb("wa%d" % i, [128, 8, 512], F32, ph) for i in range(2)]
            cnt = 0
            for l in range(2):
                pb, pk = PS[0], ("ps", 0)
                for ch in range(6):
                    w_t, wk = wa[cnt % 2], "wa%d" % (cnt % 2)
                    cnt += 1
                    DS(lambda e, l=l, ch=ch, w_t=w_t: e.dma_start(
                        out=w_t[:], in_=D["w_ada"][l, :, ch * 512:(ch + 1) * 512].rearrange("(k p) n -> p k n", p=128)),
                       [], [wk])
                    for oc in range(4):
                        col = (ch * 4 + oc) * 2
                        for k in range(8):
                            mm(pb[:, col:col + 2], w_t[:, k, oc * 128:(oc + 1) * 128], scond[:, k, :], k == 0, k == 7,
                               [wk, "scond"], [pk])
                for j in range(2):
                    V(lambda e, l=l, j=j, pb=pb: e.tensor_tensor(
                        out=modT[:, l, :, j], in0=pb[:, 0:48].rearrange("p (a b) -> p a b", b=2)[:, :, j],
                        in1=baT[:, l, :], op=ALU.add), [pk, "baT"], ["modT"])
                    V(lambda e, l=l, j=j: e.scalar_tensor_tensor(
                        out=gmul[:, l, :, j], in0=modT[:, l, 8:16, j], scalar=1.0, in1=ngT[:, l, :],
                        op0=ALU.add, op1=ALU.mult), ["modT", "ngT"], ["gmul"])
            tap("modT", modT[:].rearrange("p a b c -> p (a b c)"), ["modT"])
        S.barrier()

        hT = sb("hT", [128, 8, 1024])
        hnT = sb("hnT", [128, 8, 1024], BF16)
        brT = sb("brT", [128, 8, 1024], BF16)
        wbs = [sb("wb%d" % i, [128, 8, 512], BF16) for i in range(3)]
        stg = [sb("stg%d" % i, [128, 1024]) for i in range(2)]
        wcnt = [0]
        scnt = [0]

        def load_w(l, c0, c1):
            i = wcnt[0] % 3
            wcnt[0] += 1
            t, k = wbs[i], "wb%d" % i
            DG(lambda e: e.dma_start(out=t[:, :, 0:c1 - c0],
                                     in_=D["w_in"][l, :, c0:c1].rearrange("(k p) n -> p k n", p=128)), [], [k])
            return t, k

        def proj_T(wt, wk, a, b, tt, pb, pk):
            for k in range(8):
                mm(pb[:, 0:b - a], hnT[:, k, tt * 128:(tt + 1) * 128], wt[:, k, a:b], k == 0, k == 7, [wk, "hnT"], [pk])

        def proj_F(wt, wk, a, m, tb, pb, pk):
            for k in range(8):
                mm(pb[0:m, :], wt[:, k, a:a + m], hnT[:, k, tb * 512:(tb + 1) * 512], k == 0, k == 7, [wk, "hnT"], [pk])

        def rstd_from(ss_ap, out_ap, n, r, w):
            act(out_ap, ss_ap, AF.Sqrt, r, w, bias=EPS, scale=1.0 / n)
            V(lambda e: e.reciprocal(out=out_ap, in_=out_ap), w, w)

        def rmsnorm_T(st, gcol_fn, scol_fn, out_fn, outkey):
            sq = sb("rn_sq", [128, 8, 512], F32, st)
            rs = sb("rn_rs", [128, 512], F32, st)
            tmp = sb("rn_tmp", [128, 512], F32, st)
            for tb in range(2):
                blk = slice(tb * 512, (tb + 1) * 512)
                act(sq[:], hT[:, :, blk], AF.Square, ["hT"], ["rn_sq"])
                pb, pk = bank()
                for k in range(8):
                    mm(pb[:, :], ones[:, :], sq[:, k, :], k == 0, k == 7, ["ones", "rn_sq"], [pk])
                rstd_from(pb[:, :], rs[:], 1024.0, [pk], ["rn_rs"])
                for k in range(8):
                    V(lambda e, k=k, blk=blk: e.tensor_tensor(out=tmp[:], in0=hT[:, k, blk], in1=rs[:], op=ALU.mult),
                      ["hT", "rn_rs"], ["rn_tmp"])
                    sc = scol_fn(k)
                    if sc is None:
                        V(lambda e, k=k, tb=tb: e.tensor_scalar(out=out_fn(k, tb), in0=tmp[:], scalar1=gcol_fn(k),
                                                                scalar2=None, op0=ALU.mult), ["rn_tmp", "gmul", "fnT"], [outkey])
                    else:
                        V(lambda e, k=k, tb=tb, sc=sc: e.tensor_scalar(out=out_fn(k, tb), in0=tmp[:], scalar1=gcol_fn(k),
                                                                       scalar2=sc, op0=ALU.mult, op1=ALU.add),
                          ["rn_tmp", "gmul", "modT"], [outkey])

        def norm_gate_out(st, tag, o_acc, okey, gtile, gkey, l, zT, zkey, br):
            sq = sb(tag + "_sq", [128, 256], F32, st)
            ss = sb(tag + "_ss", [128, 4], F32, st)
            on = sb(tag + "_on", [128, 256], F32, st)
            for tt in range(8):
                act(sq[:], o_acc[:, tt, :], AF.Square, [(okey, tt)], [tag + "_sq"])
                V(lambda e: e.tensor_reduce(out=ss[:], in_=sq[:].rearrange("p (h d) -> p h d", h=4), axis=AX.X, op=ALU.add),
                  [tag + "_sq"], [tag + "_ss"])
                rstd_from(ss[:], ss[:], 64.0, [tag + "_ss"], [tag + "_ss"])
                V(lambda e, tt=tt: e.tensor_tensor(out=on[:].rearrange("p (h d) -> p h d", h=4),
                                                   in0=o_acc[:, tt, :].rearrange("p (h d) -> p h d", h=4),
                                                   in1=ss[:].unsqueeze(2).to_broadcast([128, 4, 64]), op=ALU.mult),
                  [(okey, tt), tag + "_ss"], [tag + "_on"])
                G(lambda e: e.tensor_tensor(out=on[:].rearrange("p (h d) -> p h d", h=4),
                                            in0=on[:].rearrange("p (h d) -> p h d", h=4), in1=gtile[:, l, :, :], op=ALU.mult),
                  [tag + "_on", gkey], [tag + "_on"])
                pb, pk = bank()
                for c in range(2):
                    tr(pb[:, c * 128:(c + 1) * 128], on[:, c * 128:(c + 1) * 128], ident[:], [tag + "_on", "ident"], [pk])
                V(lambda e, tt=tt, pb=pb: e.tensor_tensor(out=brT[:, br * 2:br * 2 + 2, tt * 128:(tt + 1) * 128],
                                                          in0=pb[:, 0:256].rearrange("p (c t) -> p c t", c=2),
                                                          in1=zT[:, :, tt * 128:(tt + 1) * 128], op=ALU.mult),
                  [pk, zkey], [("brT", br)])

        def attention(st, tag, qT, kT, vpad, kvmap, vslot, qblocks, keys_fn, zT, zkey, br):
            pts = [sb(tag + "_p%d" % i, [128, 512], BF16, st) for i in range(3)]
            rden = sb(tag + "_rd", [128, 512], F32, st)
            osb = sb(tag + "_o", [128, 512], F32, st)
            pc = 0
            it = 0
            for (q0, qn) in qblocks:
                keys = keys_fn(q0)
                for pr in range(2):
                    bo, bd = (0, 1) if it % 2 == 0 else (2, 3)
                    it += 1
                    psO, psD = PS[bo], PS[bd]
                    ko, kd = ("ps", bo), ("ps", bd)
                    nk = len(keys) * 2
                    ci = 0
                    for (kidx, bias_fn) in keys:
                        for hh in range(2):
                            h = 2 * pr + hh
                            pb, pk = bank()
                            mm(pb[:, 0:qn], kT[:, kvmap(h), kidx * 128:(kidx + 1) * 128], qT[:, h, q0:q0 + qn],
                               True, bias_fn is None, [tag + "_kT", tag + "_qT"], [pk])
                            if bias_fn is not None:
                                bap, bkey = bias_fn(h)
                                mm(pb[:, 0:qn], identb[:, :], bap, False, True, ["identb", bkey], [pk])
                            pt, ptk = pts[pc % 3], tag + "_p%d" % (pc % 3)
                            pc += 1
                            act(pt[:, 0:qn], pb[:, 0:qn], AF.Exp, [pk], [ptk])
                            mm(psO[:, 0:qn], vpad[:, kidx, vslot(h), :], pt[:, 0:qn], ci == 0, ci == nk - 1,
                               [tag + "_vp", ptk], [ko])
                            mm(psD[:, 0:qn], onespad[:, hh, :], pt[:, 0:qn], ci == 0, ci == nk - 1, ["onespad", ptk], [kd])
                            ci += 1
                    V(lambda e, psD=psD, qn=qn: e.reciprocal(out=rden[:, 0:qn], in_=psD[:, 0:qn]), [kd], [tag + "_rd"])
                    V(lambda e, psO=psO, qn=qn: e.tensor_tensor(out=osb[:, 0:qn], in0=psO[:, 0:qn], in1=rden[:, 0:qn],
                                                                op=ALU.mult), [ko, tag + "_rd"], [tag + "_o"])
                    G(lambda e, pr=pr, q0=q0, qn=qn: e.tensor_tensor(out=brT[:, br * 2 + pr, q0:q0 + qn], in0=osb[:, 0:qn],
                                                                     in1=zT[:, pr, q0:q0 + qn], op=ALU.mult),
                      [tag + "_o", zkey], [("brT", br)])

        def zproj(wt, wk, a, zT, zkey):
            for c in range(2):
                for tb in range(2):
                    pb, pk = bank()
                    proj_F(wt, wk, a + c * 128, 128, tb, pb, pk)
                    act(zT[:, c, tb * 512:(tb + 1) * 512], pb[:, :], AF.Silu, [pk], [zkey])

        def stage(name):
            if stop_after == name or (stop_after == "Dproj" and name == "D"):
                raise _Stop()

        def run_groups():
          for grp in (groups if stop_after != "p0" else ()):
            G(lambda e: e.memset(brT[:], 0.0), [], ["brT"])
            xin = D["xp"] if grp == 0 else D["xs"]
            yout = D["yp"] if grp == 0 else D["ys"]
            for tt in range(8):
                s_t, sk = stg[scnt[0] % 2], "stg%d" % (scnt[0] % 2)
                scnt[0] += 1
                DS(lambda e, tt=tt, s_t=s_t: e.dma_start(out=s_t[:], in_=xin[tt * 128:(tt + 1) * 128, :]), [], [sk])
                for half in range(2):
                    pb, pk = bank()
                    for q in range(4):
                        k = half * 4 + q
                        tr(pb[:, q * 128:(q + 1) * 128], s_t[:, k * 128:(k + 1) * 128], ident[:], [sk, "ident"], [pk])
                    eng = A if half == 0 else V
                    if half == 0:
                        A(lambda e, tt=tt, pb=pb: e.copy(out=hT[:, 0:4, tt * 128:(tt + 1) * 128],
                                                         in_=pb[:, :].rearrange("p (a b) -> p a b", a=4)), [pk], ["hT"])
                    else:
                        V(lambda e, tt=tt, pb=pb: e.tensor_copy(out=hT[:, 4:8, tt * 128:(tt + 1) * 128],
                                                                in_=pb[:, :].rearrange("p (a b) -> p a b", a=4)), [pk], ["hT"])
            for l in layers:
                j = grp
                with ExitStack() as ph:
                    rmsnorm_T(ph, lambda k: gmul[:, l, k, j:j + 1], lambda k: modT[:, l, k, j:j + 1],
                              lambda k, tb: hnT[:, k, tb * 512:(tb + 1) * 512], "hnT")
                S.barrier()
                stage("norm")
                if grp == groups[0] and l == layers[0]:
                    tap("hnT", hnT[:].rearrange("p a b -> p (a b)"), ["hnT"])

                with ExitStack() as ph:
                    nkt = 8 if grp == 0 else 12
                    qT = sb("a_qT", [64, 4, 1024], BF16, ph)
                    kT = sb("a_kT", [64, 2, 128 * nkt], BF16, ph)
                    vpad = sb("a_vp", [128, nkt, 4, 128], BF16, ph)
                    zT = sb("a_zT", [128, 2, 1024], BF16, ph)
                    sq = sb("a_sq", [128, 384], F32, ph)
                    ss = sb("a_ss", [128, 6], F32, ph)
                    qk = sb("a_qk", [128, 384], F32, ph)
                    qk2 = sb("a_qk2", [128, 384], F32, ph)
                    kout = sb("a_ko", [128, 128], F32, ph)
                    vout = sb("a_vo", [128, 128], F32, ph)
                    rp = sb("a_rp", [128, 2, 384], F32, ph)
                    G(lambda e: e.memset(vpad[:], 0.0), [], ["a_vp"])
                    w0, w0k = load_w(l, 0, 512)
                    w1, w1k = load_w(l, 512, 768)
                    for tt in range(8):
                        pb, pk = bank()
                        proj_T(w0, w0k, 0, 512, tt, pb, pk)
                        act(sq[:], pb[:, 0:384], AF.Square, [pk], ["a_sq"])
                        V(lambda e: e.tensor_reduce(out=ss[:], in_=sq[:].rearrange("p (h d) -> p h d", h=6), axis=AX.X,
                                                    op=ALU.add), ["a_sq"], ["a_ss"])
                        rstd_from(ss[:], ss[:], 64.0, ["a_ss"], ["a_ss"])
                        V(lambda e, pb=pb: e.tensor_tensor(out=qk[:].rearrange("p (h d) -> p h d", h=6),
                                                           in0=pb[:, 0:384].rearrange("p (h d) -> p h d", h=6),
                                                           in1=ss[:].unsqueeze(2).to_broadcast([128, 6, 64]), op=ALU.mult),
                          [pk, "a_ss"], ["a_qk"])
                        if grp == 0:
                            b_, s0 = tt // 2, (tt % 2) * 128
                            G(lambda e: e.tensor_tensor(out=kout[:].rearrange("p (h d) -> p h d", h=2),
                                                        in0=qk[:, 256:384].rearrange("p (h d) -> p h d", h=2),
                                                        in1=gkn[:, l, :, :], op=ALU.mult), ["a_qk", "gkn"], ["a_ko"])
                            DS(lambda e, b_=b_, s0=s0: e.dma_start(out=D["nak"][b_, l, s0:s0 + 128, :], in_=kout[:]),
                               ["a_ko"], [])
                            A(lambda e, pb=pb: e.copy(out=vout[:], in_=pb[:, 384:512]), [pk], ["a_vo"])
                            DS(lambda e, b_=b_, s0=s0: e.dma_start(out=D["nav"][b_, l, s0:s0 + 128, :], in_=vout[:]),
                               ["a_vo"], [])
                        G(lambda e: e.tensor_tensor(out=qk[:].rearrange("p (h d) -> p h d", h=6),
                                                    in0=qk[:].rearrange("p (h d) -> p h d", h=6), in1=gqk[:, l, :, :],
                                                    op=ALU.mult), ["a_qk", "gqk"], ["a_qk"])
                        src, srck = qk, "a_qk"
                        if grp == 1:
                            DS(lambda e, tt=tt: e.dma_start(out=rp[:], in_=D["c_rope"][:, tt * 128:(tt + 1) * 128, :]
                                                            .rearrange("a p f -> p a f")), [], ["a_rp"])
                            V(lambda e: e.tensor_tensor(out=qk2[:], in0=qk[:], in1=rp[:, 0, :], op=ALU.mult),
                              ["a_qk", "a_rp"], ["a_qk2"])
                            qv = qk[:].rearrange("p (g s d) -> p g s d", s=2, d=16)
                            sv = rp[:, 1, :].rearrange("p (g s d) -> p g s d", s=2, d=16)
                            G(lambda e, qv=qv, sv=sv: e.tensor_tensor(
                                out=sq[:].rearrange("p (g s d) -> p g s d", s=2, d=16)[:, :, 0, :], in0=qv[:, :, 1, :],
                                in1=sv[:, :, 0, :], op=ALU.mult), ["a_qk", "a_rp"], ["a_sq"])
                            G(lambda e, qv=qv, sv=sv: e.tensor_tensor(
                                out=sq[:].rearrange("p (g s d) -> p g s d", s=2, d=16)[:, :, 1, :], in0=qv[:, :, 0, :],
                                in1=sv[:, :, 1, :], op=ALU.mult), ["a_qk", "a_rp"], ["a_sq"])
                            V(lambda e: e.tensor_tensor(out=qk2[:], in0=qk2[:], in1=sq[:], op=ALU.add),
                              ["a_qk2", "a_sq"], ["a_qk2"])
                            src, srck = qk2, "a_qk2"
                        pq, pqk = bank()
                        for h in range(4):
                            tr(pq[0:64, h * 128:(h + 1) * 128], src[:, h * 64:(h + 1) * 64], ident[:], [srck, "ident"], [pqk])
                        A(lambda e, tt=tt, pq=pq: e.copy(out=qT[:, :, tt * 128:(tt + 1) * 128],
                                                         in_=pq[0:64, :].rearrange("p (a b) -> p a b", a=4)), [pqk], ["a_qT"])
                        pk2, pk2k = bank()
                        for h in range(2):
                            tr(pk2[0:64, h * 128:(h + 1) * 128], src[:, 256 + h * 64:256 + (h + 1) * 64], ident[:],
                               [srck, "ident"], [pk2k])
                        V(lambda e, tt=tt, pk2=pk2: e.tensor_copy(out=kT[:, :, tt * 128:(tt + 1) * 128],
                                                                  in_=pk2[0:64, 0:256].rearrange("p (a b) -> p a b", a=2)),
                          [pk2k], ["a_kT"])
                        for kv in range(2):
                            for pos in range(2):
                                A(lambda e, tt=tt, kv=kv, pos=pos, pb=pb: e.copy(
                                    out=vpad[:, tt, kv * 2 + pos, pos * 64:(pos + 1) * 64],
                                    in_=pb[:, 384 + kv * 64:384 + (kv + 1) * 64]), [pk], ["a_vp"])
                    if grp == 1:
                        for ct in range(4):
                            s_t, sk = stg[scnt[0] % 2], "stg%d" % (scnt[0] % 2)
                            scnt[0] += 1
                            DS(lambda e, ct=ct, s_t=s_t: e.dma_start(out=s_t[:, 0:128], in_=D["cak"][l, ct * 128:(ct + 1) * 128, :]),
                               [], [sk])
                            DS(lambda e, ct=ct, s_t=s_t: e.dma_start(out=s_t[:, 128:256], in_=D["cav"][l, ct * 128:(ct + 1) * 128, :]),
                               [], [sk])
                            pk2, pk2k = bank()
                            for h in range(2):
                                tr(pk2[0:64, h * 128:(h + 1) * 128], s_t[:, h * 64:(h + 1) * 64], ident[:], [sk, "ident"], [pk2k])
                            V(lambda e, ct=ct, pk2=pk2: e.tensor_copy(
                                out=kT[:, :, (8 + ct) * 128:(9 + ct) * 128],
                                in_=pk2[0:64, 0:256].rearrange("p (a b) -> p a b", a=2)), [pk2k], ["a_kT"])
                            for kv in range(2):
                                for pos in range(2):
                                    A(lambda e, ct=ct, kv=kv, pos=pos, s_t=s_t: e.copy(
                                        out=vpad[:, 8 + ct, kv * 2 + pos, pos * 64:(pos + 1) * 64],
                                        in_=s_t[:, 128 + kv * 64:128 + (kv + 1) * 64]), [sk], ["a_vp"])
                    zproj(w1, w1k, 0, zT, "a_zT")
                    if grp == 0:
                        qblocks = [(b_ * 256, 256) for b_ in range(4)]
                        keys_fn = lambda q0: [((q0 // 128) + i, None) for i in range(2)]
                    else:
                        qblocks = [(0, 512), (512, 512)]
                        keys_fn = lambda q0: [(i, None) for i in range(12)]
                    attention(ph, "a", qT, kT, vpad, lambda h: h // 2, lambda h: (h // 2) * 2 + (h % 2), qblocks, keys_fn,
                              zT, "a_zT", 0)
                S.barrier()
                stage("A")
                if grp == groups[0] and l == layers[0]:
                    tap("brA", brT[:, 0:2, :].rearrange("p a b -> p (a b)"), [("brT", 0)])

                with ExitStack() as ph:
                    nkt = 8 if grp == 0 else 12
                    qT = sb("d_qT", [64, 4, 1024], BF16, ph)
                    kT = sb("d_kT", [64, 4, 128 * nkt], BF16, ph)
                    vpad = sb("d_vp", [128, nkt, 4, 128], BF16, ph)
                    zT = sb("d_zT", [128, 2, 1024], BF16, ph)
                    kout = sb("d_ko", [128, 256], F32, ph)
                    vout = sb("d_vo", [128, 256], F32, ph)
                    G(lambda e: e.memset(vpad[:], 0.0), [], ["d_vp"])
                    w0, w0k = load_w(l, 2832, 3344)
                    w1, w1k = load_w(l, 3344, 3856)
                    for c in range(2):
                        for tb in range(2):
                            blk = slice(tb * 512, (tb + 1) * 512)
                            pb, pk = bank()
                            proj_F(w0, w0k, c * 128, 128, tb, pb, pk)
                            for hf in range(2):
                                V(lambda e, c=c, hf=hf, blk=blk, pb=pb: e.tensor_scalar(
                                    out=qT[:, 2 * c + hf, blk], in0=pb[hf * 64:(hf + 1) * 64, :], scalar1=0.125, scalar2=None,
                                    op0=ALU.mult), [pk], ["d_qT"])
                            pb, pk = bank()
                            proj_F(w0, w0k, 256 + c * 128, 128, tb, pb, pk)
                            for hf in range(2):
                                A(lambda e, c=c, hf=hf, blk=blk, pb=pb: e.copy(out=kT[:, 2 * c + hf, blk],
                                                                               in_=pb[hf * 64:(hf + 1) * 64, :]), [pk], ["d_kT"])
                    for tt in range(8):
                        pb, pk = bank()
                        proj_T(w1, w1k, 0, 256, tt, pb, pk)
                        for h in range(4):
                            A(lambda e, tt=tt, h=h, pb=pb: e.copy(out=vpad[:, tt, h, (h % 2) * 64:(h % 2) * 64 + 64],
                                                                  in_=pb[:, h * 64:(h + 1) * 64]), [pk], ["d_vp"])
                        if grp == 0:
                            b_, s0 = tt // 2, (tt % 2) * 128
                            A(lambda e, pb=pb: e.copy(out=vout[:], in_=pb[:, 0:256]), [pk], ["d_vo"])
                            DS(lambda e, b_=b_, s0=s0: e.dma_start(out=D["nnv"][b_, l, s0:s0 + 128, :], in_=vout[:]),
                               ["d_vo"], [])
                            pb2, pk2k = bank()
                            proj_T(w0, w0k, 256, 512, tt, pb2, pk2k)
                            V(lambda e, pb2=pb2: e.tensor_copy(out=kout[:], in_=pb2[:, 0:256]), [pk2k], ["d_ko"])
                            DS(lambda e, b_=b_, s0=s0: e.dma_start(out=D["nnk"][b_, l, s0:s0 + 128, :], in_=kout[:]),
                               ["d_ko"], [])
                    if grp == 1:
                        for ct in range(4):
                            s_t, sk = stg[scnt[0] % 2], "stg%d" % (scnt[0] % 2)
                            scnt[0] += 1
                            DS(lambda e, ct=ct, s_t=s_t: e.dma_start(out=s_t[:, 0:256], in_=D["cnk"][l, ct * 128:(ct + 1) * 128, :]),
                               [], [sk])
                            DS(lambda e, ct=ct, s_t=s_t: e.dma_start(out=s_t[:, 256:512], in_=D["cnv"][l, ct * 128:(ct + 1) * 128, :]),
                               [], [sk])
                            pk2, pk2k = bank()
                            for h in range(4):
                                tr(pk2[0:64, h * 128:(h + 1) * 128], s_t[:, h * 64:(h + 1) * 64], ident[:], [sk, "ident"], [pk2k])
                            V(lambda e, ct=ct, pk2=pk2: e.tensor_copy(
                                out=kT[:, :, (8 + ct) * 128:(9 + ct) * 128],
                                in_=pk2[0:64, :].rearrange("p (a b) -> p a b", a=4)), [pk2k], ["d_kT"])
                            for h in range(4):
                                A(lambda e, ct=ct, h=h, s_t=s_t: e.copy(
                                    out=vpad[:, 8 + ct, h, (h % 2) * 64:(h % 2) * 64 + 64],
                                    in_=s_t[:, 256 + h * 64:256 + (h + 1) * 64]), [sk], ["d_vp"])
                    zproj(w1, w1k, 256, zT, "d_zT")
                    if stop_after == "Dproj":
                        pass
                    elif grp == 0:
                        qblocks = [(b_ * 256, 256) for b_ in range(4)]
                        keys_fn = lambda q0: [((q0 // 128) + i, None) for i in range(2)]
                        attention(ph, "d", qT, kT, vpad, lambda h: h, lambda h: h, qblocks, keys_fn, zT, "d_zT", 3)
                    else:
                        nbt = [sb("d_nb%d" % i, [128, 6, 4, 128], BF16, ph) for i in range(2)]
                        pairs = _NA[3]
                        qblocks = [(qt * 128, 128) for qt in range(8)]
                        cache = {}

                        def keys_fn(q0):
                            qt = q0 // 128
                            pis = [pi for pi, (a, b) in enumerate(pairs) if a == qt]
                            nb, nbk = nbt[qt % 2], "d_nb%d" % (qt % 2)
                            DG(lambda e: e.dma_start(out=nb[:, 0:len(pis), :, :],
                                                     in_=D["c_nab"][l, pis[0]:pis[0] + len(pis)].rearrange("a h k q -> k a h q")),
                               [], [nbk])
                            out = []
                            for ii, pi in enumerate(pis):
                                out.append((pairs[pi][1], (lambda h, ii=ii: (nb[:, ii, h, :], nbk))))
                            out += [(8 + i, None) for i in range(4)]
                            return out
                        attention(ph, "d", qT, kT, vpad, lambda h: h, lambda h: h, qblocks, keys_fn, zT, "d_zT", 3)
                S.barrier()
                stage("D")
                if grp == groups[0] and l == layers[0]:
                    tap("brD", brT[:, 6:8, :].rearrange("p a b -> p (a b)"), [("brT", 3)])

                with ExitStack() as ph:
                    qTm = sb("r_qTm", [128, 2, 2, 1024], BF16, ph)
                    kTr = sb("r_kT", [128, 2, 1024], BF16, ph)
                    kdp = sb("r_kdp", [128, 4, 2, 128], BF16, ph)
                    vtk = sb("r_v", [128, 8, 256], BF16, ph)
                    zT = sb("r_zT", [128, 2, 1024], BF16, ph)
                    U = sb("r_U", [128, 8, 256], F32, ph)
                    Sin = sb("r_Sin", [128, 8, 256], F32, ph)
                    Sfin = sb("r_Sfin", [128, 256], F32, ph)
                    Sinb = sb("r_Sinb", [128, 256], BF16, ph)
                    qkm = sb("r_qkm", [128, 4, 128], BF16, ph)
                    qdm = sb("r_qdm", [128, 4, 2, 128], BF16, ph)
                    oacc = sb("r_oacc", [128, 8, 256], F32, ph)
                    tmpS = sb("r_tmpS", [128, 256], F32, ph)
                    R2 = int(os.environ.get("R2", "511"))
                    if R2 & 1:
                        G(lambda e: e.memset(qTm[:], 0.0), [], ["r_qTm"])
                        G(lambda e: e.memset(kdp[:], 0.0), [], ["r_kdp"])
                    w0, w0k = load_w(l, 1808, 2320)
                    w1, w1k = load_w(l, 2320, 2832)
                    for c in range(2):
                        for tb in range(2):
                            blk = slice(tb * 512, (tb + 1) * 512)
                            pb, pk = bank()
                            if R2 & 2:
                                proj_F(w0, w0k, c * 128, 128, tb, pb, pk)
                            for hf in (range(2) if R2 & 2 else []):
                                rows = slice(hf * 64, (hf + 1) * 64)
                                if hf == 0:
                                    A(lambda e, c=c, hf=hf, blk=blk, rows=rows, pb=pb: e.copy(out=qTm[rows, c, hf, blk], in_=pb[rows, :]),
                                      [pk], ["r_qTm"])
                                else:
                                    V(lambda e, c=c, hf=hf, blk=blk, rows=rows, pb=pb: e.tensor_copy(out=qTm[rows, c, hf, blk], in_=pb[rows, :]),
                                      [pk], ["r_qTm"])
                            pb, pk = bank()
                            if R2 & 4:
                                proj_F(w0, w0k, 256 + c * 128, 128, tb, pb, pk)
                                V(lambda e, c=c, blk=blk, pb=pb: e.tensor_copy(out=kTr[:, c, blk], in_=pb[:, :]), [pk], ["r_kT"])
                    for tt in range(8):
                        pb, pk = bank()
                        if R2 & 8:
                            proj_T(w0, w0k, 256, 512, tt, pb, pk)
                        for d in (range(2) if R2 & 8 else []):
                            for h in range(4):
                                hf = h % 2
                                V(lambda e, d=d, h=h, hf=hf, pb=pb: e.tensor_tensor(
                                    out=kdp[:, h, d, hf * 64:(hf + 1) * 64], in0=pb[:, h * 64:(h + 1) * 64],
                                    in1=kdt[:, h, d, :], op=ALU.mult), [pk, "kdt"], ["r_kdp"])
                        pb, pk = bank()
                        if R2 & 16:
                            proj_T(w1, w1k, 0, 256, tt, pb, pk)
                            A(lambda e, tt=tt, pb=pb: e.copy(out=vtk[:, tt, :], in_=pb[:, 0:256]), [pk], [("r_v", tt)])
                        pb, pk = bank()
                        for d in (range(2) if R2 & 32 else []):
                            for pr in range(2):
                                cs_ = slice((d * 2 + pr) * 64, (d * 2 + pr + 1) * 64)
                                for hf in range(2):
                                    h = 2 * pr + hf
                                    mm(pb[:, cs_], kdp[:, h, d, :], vtk[:, tt, h * 64:(h + 1) * 64], hf == 0, hf == 1,
                                       ["r_kdp", ("r_v", tt)], [pk])
                        if R2 & 32:
                            V(lambda e, tt=tt, pb=pb: e.tensor_copy(out=U[:, tt, :], in_=pb[:, 0:256]), [pk], [("r_U", tt)])
                    if R2 & 64:
                        zproj(w1, w1k, 256, zT, "r_zT")
                    seqs = [(2 * b_, 2 * b_ + 1) for b_ in range(4)] if grp == 0 else [tuple(range(8))]
                    cdv = cdt[:].rearrange("p a d -> p (a d)")
                    FW, BW = slice(0, 128), slice(128, 256)
                    for si, tiles in enumerate(seqs if R2 & 128 else []):
                        first, lastt = tiles[0], tiles[-1]
                        if grp == 0:
                            G(lambda e, first=first: e.memset(Sin[:, first, FW], 0.0), [], [("r_Sin", "f", first)])
                            G(lambda e, lastt=lastt: e.memset(Sin[:, lastt, BW], 0.0), [], [("r_Sin", "b", lastt)])
                        else:
                            DS(lambda e: e.dma_start(out=Sin[:, 0, FW].rearrange("p (a v) -> p a v", a=2),
                                                     in_=D["sr"][l, 0].rearrange("(a b) k v -> (b k) a v", b=2)), [], [("r_Sin", "f", 0)])
                            DS(lambda e: e.dma_start(out=Sin[:, 7, BW].rearrange("p (a v) -> p a v", a=2),
                                                     in_=D["sr"][l, 1].rearrange("(a b) k v -> (b k) a v", b=2)), [], [("r_Sin", "b", 7)])
                        for tt in tiles:
                            dst = Sin[:, tt + 1, FW] if tt != lastt else Sfin[:, FW]
                            dk = ("r_Sin", "f", tt + 1) if tt != lastt else ("r_Sfin", "f")
                            V(lambda e, tt=tt: e.tensor_tensor(out=tmpS[:, FW], in0=Sin[:, tt, FW], in1=cdv[:, FW], op=ALU.mult),
                              [("r_Sin", "f", tt), "cdt"], [("r_tmpS", "f")])
                            V(lambda e, tt=tt, dst=dst: e.tensor_tensor(out=dst, in0=tmpS[:, FW], in1=U[:, tt, FW], op=ALU.add),
                              [("r_tmpS", "f"), ("r_U", tt)], [dk])
                        for tt in reversed(tiles):
                            dst = Sin[:, tt - 1, BW] if tt != first else Sfin[:, BW]
                            dk = ("r_Sin", "b", tt - 1) if tt != first else ("r_Sfin", "b")
                            G(lambda e, tt=tt: e.tensor_tensor(out=tmpS[:, BW], in0=Sin[:, tt, BW], in1=cdv[:, BW], op=ALU.mult),
                              [("r_Sin", "b", tt), "cdt"], [("r_tmpS", "b")])
                            G(lambda e, tt=tt, dst=dst: e.tensor_tensor(out=dst, in0=tmpS[:, BW], in1=U[:, tt, BW], op=ALU.add),
                              [("r_tmpS", "b"), ("r_U", tt)], [dk])
                        if grp == 0 and R2 & 256:
                            b_ = si
                            DS(lambda e, b_=b_: e.dma_start(out=D["nsr"][b_, l, 0].rearrange("(a b) k v -> (b k) a v", b=2),
                                                            in_=Sfin[:, FW].rearrange("p (a v) -> p a v", a=2)),
                               [("r_Sfin", "f")], [])
                            DS(lambda e, b_=b_: e.dma_start(out=D["nsr"][b_, l, 1].rearrange("(a b) k v -> (b k) a v", b=2),
                                                            in_=Sfin[:, BW].rearrange("p (a v) -> p a v", a=2)),
                               [("r_Sfin", "b")], [])
                    _m = int(os.environ.get("L3", "31"))
                    for tt in range(int(os.environ.get("L3N", "8")) if _CL >= 3 else 0):
                        ts_ = slice(tt * 128, (tt + 1) * 128)
                        pb, pk = bank()
                        for h in (range(4) if _m & 1 else []):
                            mm(pb[:, h * 128:(h + 1) * 128], kTr[:, h // 2, ts_], qTm[:, h // 2, h % 2, ts_], True, True,
                               ["r_kT", "r_qTm"], [pk])
                        if _m & 2:
                            V(lambda e, pb=pb: e.tensor_tensor(out=qkm[:].rearrange("p a b -> p (a b)"), in0=pb[:, :],
                                                               in1=dct[:].rearrange("p a b -> p (a b)"), op=ALU.mult),
                              [pk, "dct"], ["r_qkm"])
                        for d in (range(2) if _m & 4 else []):
                            G(lambda e, d=d, ts_=ts_: e.tensor_tensor(
                                out=qdm[:, :, d, :], in0=qTm[:, :, :, ts_].rearrange("p c f t -> p (c f) t"),
                                in1=rqt[:, d, :, :], op=ALU.mult), ["r_qTm", "rqt"], ["r_qdm"])
                        if _m & 8:
                            A(lambda e, tt=tt: e.copy(out=Sinb[:], in_=Sin[:, tt, :]), [("r_Sin", "f", tt), ("r_Sin", "b", tt)], ["r_Sinb"])
                        pb, pk = bank()
                        for h in (range(4) if _m & 16 else []):
                            pr = h // 2
                            hc = slice(h * 64, (h + 1) * 64)
                            mm(pb[:, hc], qkm[:, h, :], vtk[:, tt, hc], True, False, ["r_qkm", ("r_v", tt)], [pk])
                            mm(pb[:, hc], qdm[:, h, 0, :], Sinb[:, pr * 64:(pr + 1) * 64], False, False, ["r_qdm", "r_Sinb"], [pk])
                            mm(pb[:, hc], qdm[:, h, 1, :], Sinb[:, (2 + pr) * 64:(3 + pr) * 64], False, True, ["r_qdm", "r_Sinb"], [pk])
                        if _m & 16:
                            A(lambda e, tt=tt, pb=pb: e.copy(out=oacc[:, tt, :], in_=pb[:, 0:256]), [pk], [("r_oacc", tt)])
                    if _CL >= 4:
                        norm_gate_out(ph, "r", oacc, "r_oacc", grn, "grn", l, zT, "r_zT", 2)
                S.barrier()
                stage("C")
                if grp == groups[0] and l == layers[0]:
                    tap("brC", brT[:, 4:6, :].rearrange("p a b -> p (a b)"), [("brT", 2)])

                with ExitStack() as ph:
                  if _BL >= 1:
                      gx = [sb("g_x%d" % i, [128, 1024], F32, ph) for i in range(1)]
                      gy = [sb("g_y%d" % i, [128, 1024], F32, ph) for i in range(1)]
                      qkT = sb("g_qkT", [128, 4, 1024], F32, ph)
                      ktok = sb("g_ktok", [128, 8, 256], F32, ph)
                      vtok = sb("g_vtok", [128, 8, 256], F32, ph)
                      zT = sb("g_zT", [128, 2, 1024], BF16, ph)
                      ab = sb("g_ab", [128, 8, 16], F32, ph)
                      beta = sb("g_beta", [128, 8, 8], F32, ph)
                      nbeta = sb("g_nbeta", [128, 8, 8], F32, ph)
                      la = sb("g_la", [128, 8, 8], F32, ph)
                      gc = sb("g_gc", [128, 8, 8], F32, ph)
                      ngc = sb("g_ngc", [128, 8, 8], F32, ph)
                      egc = sb("g_egc", [128, 8, 8], F32, ph)
                      bege = sb("g_bege", [128, 8, 8], F32, ph)
                      oacc = sb("g_oacc", [128, 8, 256], F32, ph)
                      dg = sb("g_dg", [128, 4, 128], F32, ph)
                      Dm = sb("g_Dm", [128, 4, 128], F32, ph)
                      DT = sb("g_DT", [128, 4, 128], F32, ph)
                      Bm = [sb("g_B%d" % i, [128, 4, 128], F32, ph) for i in range(2)]
                      BTm = [sb("g_BT%d" % i, [128, 4, 128], F32, ph) for i in range(2)]
                      Ym = [sb("g_Y%d" % i, [128, 4, 128], F32, ph) for i in range(2)]
                      qkd = sb("g_qkd", [128, 4, 128], F32, ph)
                      vb = sb("g_vb", [128, 4, 64], F32, ph)
                      kbgp = sb("g_kbgp", [128, 4, 128], F32, ph)
                      kdcp = sb("g_kdcp", [128, 4, 128], F32, ph)
                      kds = sb("g_kds", [128, 4], F32, ph)
                      eg2 = sb("g_eg2", [128, 2], F32, ph)
                      wTm = sb("g_wTm", [128, 4, 128], F32, ph)

                      Sst = sb("g_S", [128, 2, 2, 64], F32, ph)
                      G(lambda e: e.memset(kbgp[:], 0.0), [], ["g_kbgp"])
                      G(lambda e: e.memset(kdcp[:], 0.0), [], ["g_kdcp"])
                      G(lambda e: e.memset(wTm[:], 0.0), [], ["g_wTm"])
                      wA, wAk = load_w(l, 768, 1280)
                      wB, wBk = load_w(l, 1280, 1552)
                      wC, wCk = load_w(l, 1552, 1808)
                      nseq = 4 if grp == 0 else 1
                      L = 1024 // nseq
                      for c6 in range(6):
                          xt, xk = gx[0], "g_x0"
                          yt, yk = gy[0], "g_y0"
                          wt, wk, a = (wA, wAk, c6 * 128) if c6 < 4 else (wB, wBk, (c6 - 4) * 128)
                          for tb in range(2):
                              pb, pk = bank()
                              proj_F(wt, wk, a, 128, tb, pb, pk)
                              A(lambda e, tb=tb, pb=pb, xt=xt: e.copy(out=xt[:, tb * 512:(tb + 1) * 512], in_=pb[:, :]), [pk], [xk])
                          x3 = xt[:].rearrange("p (s t) -> p s t", s=nseq)
                          y3 = yt[:].rearrange("p (s t) -> p s t", s=nseq)
                          V(lambda e, c6=c6, xt=xt, yt=yt: e.tensor_scalar(out=yt[:], in0=xt[:], scalar1=cwT[:, l, c6, 2:3],
                                                                           scalar2=None, op0=ALU.mult), [xk, "cwT"], [yk])
                          for jj in (0, 1, 3, 4):
                              dsh = jj - 2
                              lo, hi = max(0, -dsh), L - max(0, dsh)
                              V(lambda e, c6=c6, jj=jj, x3=x3, y3=y3, lo=lo, hi=hi, dsh=dsh: e.scalar_tensor_tensor(
                                  out=y3[:, :, lo:hi], in0=x3[:, :, lo + dsh:hi + dsh], scalar=cwT[:, l, c6, jj:jj + 1],
                                  in1=y3[:, :, lo:hi], op0=ALU.mult, op1=ALU.add), [xk, yk, "cwT"], [yk])
                          act(yt[:], yt[:], AF.Silu, [yk], [yk])
                          if c6 < 4:
                              for tb in range(2):
                                  blk = slice(tb * 512, (tb + 1) * 512)
                                  sqv = xt[:, 0:512]
                                  rsv = xt[:, 512:1024]
                                  act(sqv, yt[:, blk], AF.Square, [yk], [xk])
                                  pb, pk = bank()
                                  mm(pb[:, :], onesblk[:, :], sqv, True, True, ["onesblk", xk], [pk])
                                  act(rsv, pb[:, :], AF.Sqrt, [pk], [xk], bias=EPS, scale=1.0)
                                  V(lambda e, rsv=rsv: e.reciprocal(out=rsv, in_=rsv), [xk], [xk])
                                  if c6 < 2:
                                      V(lambda e, c6=c6, blk=blk, yt=yt, rsv=rsv: e.scalar_tensor_tensor(
                                          out=qkT[:, c6, blk], in0=yt[:, blk], scalar=0.125, in1=rsv, op0=ALU.mult,
                                          op1=ALU.mult), [yk, xk], [("g_qkT", c6)])
                                  else:
                                      V(lambda e, c6=c6, blk=blk, yt=yt, rsv=rsv: e.tensor_tensor(out=qkT[:, c6, blk], in0=yt[:, blk], in1=rsv,
                                                                                        op=ALU.mult), [yk, xk], [("g_qkT", c6)])
                          if c6 >= 2:
                              srcT = qkT[:, c6, :] if c6 < 4 else yt[:]
                              srck = ("g_qkT", c6) if c6 < 4 else yk
                              dst = ktok if c6 < 4 else vtok
                              dstk = "g_ktok" if c6 < 4 else "g_vtok"
                              cc = c6 % 2
                              for half in range(2):
                                  pb, pk = bank()
                                  for q in range(4):
                                      tt = half * 4 + q
                                      tr(pb[:, q * 128:(q + 1) * 128], srcT[:, tt * 128:(tt + 1) * 128], ident[:], [srck, "ident"], [pk])
                                  A(lambda e, half=half, pb=pb, dst=dst, cc=cc: e.copy(
                                      out=dst[:, half * 4:half * 4 + 4, cc * 128:(cc + 1) * 128],
                                      in_=pb[:, :].rearrange("p (a b) -> p a b", a=4)), [pk], [dstk])
                      zproj(wC, wCk, 0, zT, "g_zT")
                      if _GB >= 2:
                        pb, pk = bank()
                        for tt in range(8):
                            for k in range(8):
                                mm(pb[:, tt * 16:(tt + 1) * 16], hnT[:, k, tt * 128:(tt + 1) * 128], wB[:, k, 256:272], k == 0, k == 7,
                                   [wBk, "hnT"], [pk])
                        V(lambda e, pb=pb: e.tensor_copy(out=ab[:].rearrange("p a b -> p (a b)"), in_=pb[:, 0:128]), [pk], ["g_ab"])
                        act(beta[:], ab[:, :, 0:8], AF.Sigmoid, ["g_ab"], ["g_beta"])
                        V(lambda e: e.tensor_scalar(out=nbeta[:], in0=beta[:], scalar1=-1.0, scalar2=None, op0=ALU.mult),
                          ["g_beta"], ["g_nbeta"])
                        V(lambda e: e.tensor_tensor(out=la[:], in0=ab[:, :, 8:16], in1=dtb[:, l, :].unsqueeze(1).to_broadcast([128, 8, 8]),
                                                    op=ALU.add), ["g_ab", "dtb"], ["g_la"])
                        V(lambda e: e.tensor_scalar(out=la[:], in0=la[:], scalar1=30.0, scalar2=None, op0=ALU.min), ["g_la"], ["g_la"])
                        act(la[:], la[:], AF.Exp, ["g_la"], ["g_la"])
                        act(la[:], la[:], AF.Ln, ["g_la"], ["g_la"], bias=1.0, scale=1.0)
                        V(lambda e: e.tensor_tensor(out=la[:], in0=la[:], in1=nega[:, l, :].unsqueeze(1).to_broadcast([128, 8, 8]),
                                                    op=ALU.mult), ["g_la", "nega"], ["g_la"])
                        pb, pk = bank()
                        for tt in range(8):
                            for d in range(2):
                                mm(pb[:, tt * 8 + d * 4:tt * 8 + d * 4 + 4], masks[:, d, :], la[:, tt, d * 4:(d + 1) * 4], True, True,
                                   ["masks", "g_la"], [pk])
                        V(lambda e, pb=pb: e.tensor_copy(out=gc[:].rearrange("p a b -> p (a b)"), in_=pb[:, 0:64]), [pk], ["g_gc"])
                        V(lambda e: e.tensor_scalar(out=ngc[:], in0=gc[:], scalar1=-1.0, scalar2=None, op0=ALU.mult), ["g_gc"], ["g_ngc"])
                        act(egc[:], gc[:], AF.Exp, ["g_gc"], ["g_egc"])
                        V(lambda e: e.tensor_tensor(out=bege[:], in0=beta[:], in1=egc[:], op=ALU.mult), ["g_beta", "g_egc"], ["g_bege"])
                        tap("g_gc", gc[:].rearrange("p a b -> p (a b)"), ["g_gc"])
                        tap("g_qkT", qkT[:].rearrange("p a b -> p (a b)"), ["g_qkT"])
                      if _GB >= 3:
                        S.barrier()
                        wv = [w_[:].rearrange("p a b -> p (a b)").bitcast(F32) for w_ in wbs]

                        def v4(ap):
                            return ap.rearrange("p (h f) -> p h f", h=4)
                        sets = []
                        sets.append(dict(
                            dg=dg[:], Dm=Dm[:], DT=DT[:], B=[Bm[0][:], Bm[1][:]], BT=[BTm[0][:], BTm[1][:]], Y=[Ym[0][:], Ym[1][:]],
                            qkd=qkd[:], vb=vb[:], kbgp=kbgp[:], kdcp=kdcp[:], kds=kds[:], eg2=eg2[:], wTm=wTm[:],
                            qkz=gx[0][:].rearrange("p (c f t) -> p c f t", c=4, f=2),
                            usb=gy[0][:, 0:256], vnew=gy[0][:, 256:512], o2s=gy[0][:, 512:768], otmp=gy[0][:, 768:1024]))
                        kds1 = sb("g_kds1", [128, 4], F32, ph)
                        eg21 = sb("g_eg21", [128, 2], F32, ph)
                        vb1 = sb("g_vb1", [128, 4, 64], F32, ph)
                        DT1 = sb("g_DT1", [128, 4, 128], F32, ph)
                        sets.append(dict(
                            dg=v4(stg[1][:, 0:512]), Dm=v4(stg[1][:, 512:1024]), DT=DT1[:],
                            B=[v4(wv[0][:, 0:512]), v4(wv[0][:, 512:1024])], BT=[v4(wv[0][:, 1024:1536]), v4(wv[0][:, 1536:2048])],
                            Y=[v4(wv[1][:, 0:512]), v4(wv[1][:, 512:1024])],
                            qkz=wv[1][:, 1024:2048].rearrange("p (c f t) -> p c f t", c=4, f=2),
                            qkd=v4(wv[2][:, 0:512]), kbgp=v4(wv[2][:, 512:1024]), kdcp=v4(wv[2][:, 1024:1536]), wTm=v4(wv[2][:, 1536:2048]),
                            vb=vb1[:], kds=kds1[:], eg2=eg21[:],
                            usb=stg[0][:, 0:256], vnew=stg[0][:, 256:512], o2s=stg[0][:, 512:768], otmp=stg[0][:, 768:1024]))
                        G(lambda e: e.memset(gx[0][:], 0.0), [], ["q0_qkz"])
                        G(lambda e: e.memset(wv[1][:, 1024:2048], 0.0), [], ["q1_qkz"])
                        G(lambda e: e.memset(wv[2][:, 512:2048], 0.0), [], ["q1_kbgp", "q1_kdcp", "q1_wTm"])
                        G(lambda e: e.memset(oacc[:], 0.0), [], ["g_oacc"])
                        seqs = [(2 * b_, 2 * b_ + 1) for b_ in range(4)] if grp == 0 else [tuple(range(8))]

                        def quad(si_, d, tt, init, fin, sidx):
                            T_ = sets[si_]
                            kp = "q%d_" % si_
                            if si_ == 0:
                                kq = dict(kbgp="g_kbgp", kdcp="g_kdcp", wTm="g_wTm")
                            else:
                                kq = dict(kbgp="q1_kbgp", kdcp="q1_kdcp", wTm="q1_wTm")
                            kq = {**{n: kp + n for n in ("dg", "Dm", "DT", "B0", "B1", "BT0", "BT1", "Y0", "Y1", "qkd", "vb", "kds", "eg2",
                                                         "qkz", "usb", "vnew", "o2s", "otmp")}, **kq}
                            qrr = [0]

                            def qbank():
                                i = 4 * si_ + qrr[0] % 4
                                qrr[0] += 1
                                return PS[i], ("ps", i)
                            last = 127 if d == 0 else 0
                            ts_ = slice(tt * 128, (tt + 1) * 128)
                            u0 = d * 4
                            dgt, Dmt, DTt, Bt, BTt, Yt = T_["dg"], T_["Dm"], T_["DT"], T_["B"], T_["BT"], T_["Y"]
                            qkdt, vbt, kbgpt, kdcpt, kdst, eg2t, wTmt, qkzt = (T_["qkd"], T_["vb"], T_["kbgp"], T_["kdcp"], T_["kds"],
                                                                               T_["eg2"], T_["wTm"], T_["qkz"])
                            usbt, vnewt, o2st, otmpt = T_["usb"], T_["vnew"], T_["o2s"], T_["otmp"]
                            if init:
                                if grp == 0:
                                    V(lambda e: e.memset(Sst[:, d, :, :], 0.0), [], [("g_S", d)])
                                else:
                                    DS(lambda e: e.dma_start(out=Sst[:, d, :, :],
                                                             in_=D["sg"][l, d].rearrange("(a b) k v -> (b k) a v", b=2)), [], [("g_S", d)])
                            for h in range(4):
                                A(lambda e, h=h: e.mul(out=dgt[:, h, :], in_=ident[:], mul=gc[:, tt, u0 + h:u0 + h + 1]),
                                  ["ident", "g_gc"], [kq["dg"]])
                            psN, kN = qbank()
                            psP, kP = qbank()
                            for h in range(4):
                                hs = slice(h * 128, (h + 1) * 128)
                                mm(psN[:, hs], ones[:, :], dgt[:, h, :], True, False, ["ones", kq["dg"]], [kN])
                                mm(psN[:, hs], ident[:, :], masks[:, 4 + d, :], False, True, ["ident", "masks"], [kN])
                                mm(psP[:, hs], ones[:, :], dgt[:, h, :], True, False, ["ones", kq["dg"]], [kP])
                                mm(psP[:, hs], ident[:, :], masks[:, 2 + d, :], False, True, ["ident", "masks"], [kP])
                            yield
                            for h in range(4):
                                hs = slice(h * 128, (h + 1) * 128)
                                act(Dmt[:, h, :], psP[:, hs], AF.Exp, [kP, "g_gc"], [kq["Dm"]], bias=gc[:, tt, u0 + h:u0 + h + 1], scale=-1.0)
                                act(DTt[:, h, :], psN[:, hs], AF.Exp, [kN, "g_ngc"], [kq["DT"]], bias=ngc[:, tt, u0 + h:u0 + h + 1], scale=1.0)
                                act(kdst[:, h:h + 1], psN[:, h * 128 + last:h * 128 + last + 1], AF.Exp, [kN, "g_ngc"], [kq["kds"]],
                                    bias=ngc[:, tt, u0 + h:u0 + h + 1], scale=1.0)
                            for pr in range(2):
                                for hf in range(2):
                                    h = 2 * pr + hf
                                    A(lambda e, pr=pr, hf=hf, h=h: e.activation(
                                        out=eg2t[hf * 64:(hf + 1) * 64, pr:pr + 1],
                                        in_=psN[hf * 64:(hf + 1) * 64, h * 128 + last:h * 128 + last + 1], func=AF.Exp), [kN], [kq["eg2"]])
                            for hf in range(2):
                                rows = slice(hf * 64, (hf + 1) * 64)
                                V(lambda e, hf=hf, rows=rows: e.tensor_copy(out=qkzt[rows, :, hf, :], in_=qkT[rows, :, ts_]),
                                  ["g_qkT"], [kq["qkz"]])
                            psK, kK = qbank()
                            psQ, kQ = qbank()
                            for h in range(4):
                                hs = slice(h * 128, (h + 1) * 128)
                                kfull = qkT[:, 2 + h // 2, ts_]
                                mm(psK[:, hs], kfull, qkzt[:, 2 + h // 2, h % 2, :], True, True, ["g_qkT", kq["qkz"]], [kK])
                                mm(psQ[:, hs], kfull, qkzt[:, h // 2, h % 2, :], True, True, ["g_qkT", kq["qkz"]], [kQ])
                            yield
                            for h in range(4):
                                hs = slice(h * 128, (h + 1) * 128)
                                V(lambda e, h=h, hs=hs: e.scalar_tensor_tensor(
                                    out=Bt[0][:, h, :], in0=psK[:, hs], scalar=nbeta[:, tt, u0 + h:u0 + h + 1], in1=Dmt[:, h, :],
                                    op0=ALU.mult, op1=ALU.mult), [kK, "g_nbeta", kq["Dm"]], [kq["B0"]])
                            V(lambda e: e.tensor_tensor(out=qkdt.rearrange("p a b -> p (a b)"), in0=psQ[:, :],
                                                        in1=DTt.rearrange("p a b -> p (a b)"), op=ALU.mult), [kQ, kq["DT"]], [kq["qkd"]])
                            yield
                            pb, pk = qbank()
                            for h in range(4):
                                tr(pb[:, h * 128:(h + 1) * 128], Bt[0][:, h, :], ident[:], [kq["B0"], "ident"], [pk])
                            yield
                            V(lambda e, pb=pb: e.tensor_copy(out=BTt[0].rearrange("p a b -> p (a b)"), in_=pb[:, :]), [pk], [kq["BT0"]])
                            for h in range(4):
                                V(lambda e, pb=pb, h=h: e.tensor_tensor(out=Yt[0][:, h, :], in0=pb[:, h * 128:(h + 1) * 128], in1=ident[:],
                                                                        op=ALU.add), [pk, "ident"], [kq["Y0"]])
                            yield
                            cur = 0
                            for lev in range(1, 7):
                                nxt = 1 - cur
                                pb, pk = qbank()
                                for h in range(4):
                                    mm(pb[:, h * 128:(h + 1) * 128], BTt[cur][:, h, :], Bt[cur][:, h, :], True, True,
                                       [kq["BT%d" % cur], kq["B%d" % cur]], [pk])
                                if lev < 6:
                                    pb2, pk2 = qbank()
                                    for h in range(4):
                                        mm(pb2[:, h * 128:(h + 1) * 128], Bt[cur][:, h, :], BTt[cur][:, h, :], True, True,
                                           [kq["BT%d" % cur], kq["B%d" % cur]], [pk2])
                                yield
                                A(lambda e, pb=pb, nxt=nxt: e.copy(out=Bt[nxt].rearrange("p a b -> p (a b)"), in_=pb[:, :]),
                                  [pk], [kq["B%d" % nxt]])
                                if lev < 6:
                                    V(lambda e, pb2=pb2, nxt=nxt: e.tensor_copy(out=BTt[nxt].rearrange("p a b -> p (a b)"), in_=pb2[:, :]),
                                      [pk2], [kq["BT%d" % nxt]])
                                pb3, pk3 = qbank()
                                for h in range(4):
                                    mm(pb3[:, h * 128:(h + 1) * 128], Bt[nxt][:, h, :], Yt[cur][:, h, :], True, True,
                                       [kq["B%d" % nxt], kq["Y%d" % cur]], [pk3])
                                yield
                                V(lambda e, pb3=pb3, nxt=nxt, cur=cur: e.tensor_tensor(
                                    out=Yt[nxt].rearrange("p a b -> p (a b)"), in0=pb3[:, :],
                                    in1=Yt[cur].rearrange("p a b -> p (a b)"), op=ALU.add), [pk3, kq["Y%d" % cur]], [kq["Y%d" % nxt]])
                                cur = nxt
                                yield
                            Yf, Yk = Yt[cur], kq["Y%d" % cur]
                            k4 = ktok[:, tt, :].rearrange("p (h d) -> p h d", h=4)
                            G(lambda e: e.tensor_tensor(out=vbt, in0=vtok[:, tt, :].rearrange("p (h d) -> p h d", h=4),
                                                        in1=beta[:, tt, u0:u0 + 4].unsqueeze(2).to_broadcast([128, 4, 64]), op=ALU.mult),
                              ["g_vtok", "g_beta"], [kq["vb"]])
                            for hf in range(2):
                                pc_ = slice(hf * 64, hf * 64 + 64)
                                G(lambda e, hf=hf, pc_=pc_: e.tensor_tensor(
                                    out=kbgpt[:, hf::2, pc_], in0=k4[:, hf::2, :],
                                    in1=bege[:, tt, u0 + hf:u0 + 4:2].unsqueeze(2).to_broadcast([128, 2, 64]), op=ALU.mult),
                                  ["g_ktok", "g_bege"], [kq["kbgp"]])
                                V(lambda e, hf=hf, pc_=pc_: e.tensor_tensor(
                                    out=kdcpt[:, hf::2, pc_], in0=k4[:, hf::2, :],
                                    in1=kdst[:, hf::2].unsqueeze(2).to_broadcast([128, 2, 64]), op=ALU.mult),
                                  ["g_ktok", kq["kds"]], [kq["kdcp"]])
                            pbu, pku = qbank()
                            for h in range(4):
                                mm(pbu[:, h * 64:(h + 1) * 64], Yf[:, h, :], vbt[:, h, :], True, True, [Yk, kq["vb"]], [pku])
                            pbw, pkw = qbank()
                            for pr in range(2):
                                for hf in range(2):
                                    h = 2 * pr + hf
                                    mm(pbw[:, pr * 128:(pr + 1) * 128], kbgpt[:, h, :], Yf[:, h, :], hf == 0, hf == 1, [kq["kbgp"], Yk], [pkw])
                            yield
                            A(lambda e: e.copy(out=usbt, in_=pbu[:, 0:256]), [pku], [kq["usb"]])
                            for pr in range(2):
                                for hf in range(2):
                                    rows = slice(hf * 64, (hf + 1) * 64)
                                    V(lambda e, pr=pr, hf=hf, rows=rows: e.tensor_copy(out=wTmt[rows, 2 * pr + hf, :],
                                                                                       in_=pbw[rows, pr * 128:(pr + 1) * 128]), [pkw], [kq["wTm"]])
                            yield
                            pbv, pkv = qbank()
                            pbo, pko = qbank()
                            for h in range(4):
                                hc = slice(h * 64, (h + 1) * 64)
                                mm(pbv[:, hc], wTmt[:, h, :], Sst[:, d, h // 2, :], True, True, [kq["wTm"], ("g_S", d)], [pkv])
                                mm(pbo[:, hc], qkzt[:, h // 2, h % 2, :], Sst[:, d, h // 2, :], True, True, [kq["qkz"], ("g_S", d)], [pko])
                            yield
                            V(lambda e: e.tensor_tensor(out=vnewt, in0=usbt, in1=pbv[:, 0:256], op=ALU.subtract), [kq["usb"], pkv], [kq["vnew"]])
                            pb2, pk2 = qbank()
                            for h in range(4):
                                hc = slice(h * 64, (h + 1) * 64)
                                mm(pb2[:, hc], qkdt[:, h, :], vnewt[:, hc], True, True, [kq["qkd"], kq["vnew"]], [pk2])
                            pbs, pks = qbank()
                            for pr in range(2):
                                for hf in range(2):
                                    h = 2 * pr + hf
                                    mm(pbs[:, pr * 64:(pr + 1) * 64], kdcpt[:, h, :], vnewt[:, h * 64:(h + 1) * 64], hf == 0, hf == 1,
                                       [kq["kdcp"], kq["vnew"]], [pks])
                            yield
                            A(lambda e: e.copy(out=o2st, in_=pb2[:, 0:256]), [pk2], [kq["o2s"]])
                            for h in range(4):
                                hc = slice(h * 64, (h + 1) * 64)
                                V(lambda e, h=h, hc=hc: e.scalar_tensor_tensor(
                                    out=otmpt[:, hc], in0=pbo[:, hc], scalar=egc[:, tt, u0 + h:u0 + h + 1], in1=o2st[:, hc],
                                    op0=ALU.mult, op1=ALU.add), [pko, "g_egc", kq["o2s"]], [kq["otmp"]])
                            G(lambda e: e.tensor_tensor(out=oacc[:, tt, :], in0=oacc[:, tt, :], in1=otmpt, op=ALU.add),
                              [("g_oacc", tt), kq["otmp"]], [("g_oacc", tt)])
                            for pr in range(2):
                                V(lambda e, pr=pr: e.scalar_tensor_tensor(
                                    out=Sst[:, d, pr, :], in0=Sst[:, d, pr, :], scalar=eg2t[:, pr:pr + 1],
                                    in1=pbs[:, pr * 64:(pr + 1) * 64], op0=ALU.mult, op1=ALU.add),
                                  [("g_S", d), kq["eg2"], pks], [("g_S", d)])
                            if fin and grp == 0:
                                DS(lambda e: e.dma_start(out=D["nsg"][sidx, l, d].rearrange("(a b) k v -> (b k) a v", b=2),
                                                         in_=Sst[:, d, :, :]), [("g_S", d)], [])
                            yield

                        sched = [[], []]
                        for d in range(2):
                            for sidx, tiles in enumerate(seqs):
                                order = tiles if d == 0 else tuple(reversed(tiles))
                                for qi, tt in enumerate(order):
                                    sched[d].append((tt, qi == 0, qi == len(order) - 1, sidx))
                        for (f_, b_) in zip(sched[0][:_GBQ], sched[1][:_GBQ]):
                            gens = [quad(0, 0, *f_), quad(1, 1, *b_)]
                            alive = [True, True]
                            while any(alive):
                                for gi in range(2):
                                    if alive[gi]:
                                        try:
                                            next(gens[gi])
                                        except StopIteration:
                                            alive[gi] = False
                      S.mute = False
                      if _GB >= 4:
                        norm_gate_out(ph, "g", oacc, "g_oacc", ggn, "ggn", l, zT, "g_zT", 1)
                S.barrier()
                stage("B")
                if grp == groups[0] and l == layers[0]:
                    tap("brB", brT[:, 2:4, :].rearrange("p a b -> p (a b)"), [("brT", 1)])

                with ExitStack() as ph:
                    mT = sb("m_T", [128, 8, 1024], BF16, ph)
                    wmg = [sb("m_wg%d" % i, [128, 4, 8, 128], BF16, ph) for i in range(2)]
                    wbr = [sb("m_wb%d" % i, [128, 4, 2, 128], BF16, ph) for i in range(2)]
                    gts = [sb("m_gt%d" % i, [128, 512], BF16, ph) for i in range(2)]
                    acc = sb("m_acc", [128, 512], F32, ph)
                    tmpm = sb("m_tmp", [128, 512], F32, ph)
                    gi = 0
                    for dc in range(8):
                        wg_t, wgk = wmg[dc % 2], "m_wg%d" % (dc % 2)
                        wb_t, wbk = wbr[dc % 2], "m_wb%d" % (dc % 2)
                        for n in range(4):
                            c0 = 3856 + n * 1024 + dc * 128
                            DG(lambda e, n=n, c0=c0, wg_t=wg_t: e.dma_start(
                                out=wg_t[:, n, :, :], in_=D["w_in"][l, :, c0:c0 + 128].rearrange("(k p) c -> p k c", p=128)),
                               [], [wgk])
                            DG(lambda e, n=n, dc=dc, wb_t=wb_t: e.dma_start(
                                out=wb_t[:, n, :, :],
                                in_=D["w_branch"][l, n, :, dc * 128:(dc + 1) * 128].rearrange("(k p) c -> p k c", p=128)),
                               [], [wbk])
                        for tb in range(2):
                            blk = slice(tb * 512, (tb + 1) * 512)
                            for n in range(4):
                                pb, pk = bank()
                                for k in range(8):
                                    mm(pb[:, :], wg_t[:, n, k, :], hnT[:, k, blk], k == 0, k == 7, [wgk, "hnT"], [pk])
                                gt, gtk = gts[gi % 2], "m_gt%d" % (gi % 2)
                                gi += 1
                                act(gt[:], pb[:, :], AF.Sigmoid, [pk], [gtk])
                                pb2, pk2 = bank()
                                for kk in range(2):
                                    mm(pb2[:, :], wb_t[:, n, kk, :], brT[:, n * 2 + kk, blk], kk == 0, kk == 1, [wbk, ("brT", n)], [pk2])
                                if n == 0:
                                    V(lambda e, pb2=pb2, gt=gt: e.tensor_tensor(out=acc[:], in0=pb2[:, :], in1=gt[:], op=ALU.mult),
                                      [pk2, gtk], ["m_acc"])
                                else:
                                    V(lambda e, pb2=pb2, gt=gt: e.tensor_tensor(out=tmpm[:], in0=pb2[:, :], in1=gt[:], op=ALU.mult),
                                      [pk2, gtk], ["m_tmp"])
                                    if n < 3:
                                        G(lambda e: e.tensor_tensor(out=acc[:], in0=acc[:], in1=tmpm[:], op=ALU.add),
                                          ["m_acc", "m_tmp"], ["m_acc"])
                                    else:
                                        G(lambda e, dc=dc, blk=blk: e.tensor_tensor(out=mT[:, dc, blk], in0=acc[:], in1=tmpm[:], op=ALU.add),
                                          ["m_acc", "m_tmp"], [("m_T", dc)])
                    for half in range(2):
                        i = wcnt[0] % 3
                        wcnt[0] += 1
                        wo, wok = wbs[i], "wb%d" % i
                        DG(lambda e, half=half, wo=wo: e.dma_start(
                            out=wo[:], in_=D["w_out"][l, :, half * 512:(half + 1) * 512].rearrange("(k p) n -> p k n", p=128)),
                           [], [wok])
                        for q in range(4):
                            oc = half * 4 + q
                            for tb in range(2):
                                blk = slice(tb * 512, (tb + 1) * 512)
                                pb, pk = bank()
                                for k in range(8):
                                    mm(pb[:, :], wo[:, k, q * 128:(q + 1) * 128], mT[:, k, blk], k == 0, k == 7, [wok, ("m_T", k)], [pk])
                                V(lambda e, oc=oc, blk=blk, pb=pb: e.scalar_tensor_tensor(
                                    out=hT[:, oc, blk], in0=pb[:, :], scalar=modT[:, l, 16 + oc, j:j + 1], in1=hT[:, oc, blk],
                                    op0=ALU.mult, op1=ALU.add), [pk, "modT", "hT"], ["hT"])
                S.barrier()
                stage("merge")
                if grp == groups[0] and l == layers[0]:
                    tap("hT1", hT[:].rearrange("p a b -> p (a b)"), ["hT"])

            with ExitStack() as ph:
                ynT = sb("f_yn", [128, 8, 1024], F32, ph)
                rmsnorm_T(ph, lambda k: fnT[:, k:k + 1], lambda k: None,
                          lambda k, tb: ynT[:, k, tb * 512:(tb + 1) * 512], "f_yn")
                for tt in range(8):
                    s_t, sk = stg[scnt[0] % 2], "stg%d" % (scnt[0] % 2)
                    scnt[0] += 1
                    for half in range(2):
                        pb, pk = bank()
                        for q in range(4):
                            k = half * 4 + q
                            tr(pb[:, q * 128:(q + 1) * 128], ynT[:, k, tt * 128:(tt + 1) * 128], ident[:], ["f_yn", "ident"], [pk])
                        if half == 0:
                            A(lambda e, pb=pb, s_t=s_t: e.copy(out=s_t[:, 0:512], in_=pb[:, :]), [pk], [sk])
                        else:
                            V(lambda e, pb=pb, s_t=s_t: e.tensor_copy(out=s_t[:, 512:1024], in_=pb[:, :]), [pk], [sk])
                    DS(lambda e, tt=tt, s_t=s_t: e.dma_start(out=yout[tt * 128:(tt + 1) * 128, :], in_=s_t[:]), [sk], [])
            S.barrier()
        try:
            run_groups()
        except _Stop:
            S.barrier()
        with nc.allow_non_contiguous_dma(reason="small transposed parameter loads"):
            stats = S.emit()
    return nc, stats


_CACHE = {}


def _in_maps(inp):
    f = lambda a: np.ascontiguousarray(np.asarray(a, dtype=np.float32))
    cst = _consts(f(inp["na_bias"]))
    shared = dict(
        w_ada=f(inp["w_ada"]), b_ada=f(inp["b_ada"]), norm_g=f(inp["norm_g"]), w_in=f(inp["w_in"]), conv_w=f(inp["conv_w"]),
        a_log=f(inp["gdn_a_log"]).reshape(2, 8), dt_bias=f(inp["gdn_dt_bias"]).reshape(2, 8), gdn_norm=f(inp["gdn_norm"]),
        q_norm=f(inp["attn_q_norm"]), k_norm=f(inp["attn_k_norm"]), ret_norm=f(inp["ret_norm"]),
        w_branch=f(inp["w_branch"]), w_out=f(inp["w_out"]), final_norm=f(inp["final_norm"]).reshape(1, 1024), **cst)
    xp, xs = f(inp["x_prompt"]), f(inp["x_sample"])
    maps = []
    for c in range(8):
        m = dict(shared)
        m["xp"] = xp[4 * c:4 * c + 4].reshape(1024, 1024)
        m["xs"] = xs[c]
        m["cak"] = f(inp["cache_attn_k"][c]).reshape(2, 512, 128)
        m["cav"] = f(inp["cache_attn_v"][c]).reshape(2, 512, 128)
        m["cnk"] = f(inp["cache_na_k"][c]).reshape(2, 512, 256)
        m["cnv"] = f(inp["cache_na_v"][c]).reshape(2, 512, 256)
        m["sg"] = f(inp["state_gdn"][c])
        m["sr"] = f(inp["state_ret"][c])
        m["cond"] = np.stack([f(inp["c_ctx"]), f(inp["c"][c])])
        maps.append(m)
    return maps


def kernel(**inputs):
    if "nc" not in _CACHE:
        _CACHE["nc"] = build()[0]
    nc = _CACHE["nc"]
    maps = _in_maps(inputs)
    res = run_bass_kernel_spmd(nc, maps, core_ids=list(range(8)))
    R = res.results
    cat = lambda n: np.concatenate([np.asarray(r[n]) for r in R], axis=0)
    y_prompt = cat("yp").reshape(32, 256, 1024)
    y_sample = np.stack([np.asarray(r["ys"]) for r in R])
    nak = cat("nak").reshape(32, 2, 256, 2, 64)
    nav = cat("nav").reshape(32, 2, 256, 2, 64)
    nnk = cat("nnk").reshape(32, 2, 256, 4, 64)
    nnv = cat("nnv").reshape(32, 2, 256, 4, 64)
    nsg = cat("nsg")
    nsr = cat("nsr")
    return tuple(np.ascontiguousarray(a, dtype=np.float32) for a in (y_prompt, y_sample, nak, nav, nnk, nnv, nsg, nsr))
```

```python
import numpy as np
import concourse.bass as bass
import concourse.mybir as mybir
from concourse.bass_utils import run_bass_kernel_spmd
from contextlib import ExitStack

F32 = mybir.dt.float32
BF16 = mybir.dt.bfloat16
ALU = mybir.AluOpType
AF = mybir.ActivationFunctionType
AX = mybir.AxisListType
EPS = 1e-6
NEG = -30000.0
BIG = 1.0e5


import types
import os
_CL = int(os.environ.get('CLEVEL', '9'))
_BL = int(os.environ.get('BLEVEL', '9'))
_GB = int(os.environ.get('GB', '9'))
_GBQ = int(os.environ.get('GBQ', '99'))
_GBS = int(os.environ.get('GBS', '9'))
_SKIP = os.environ.get('SKIP', '')
_STRICT = bool(int(os.environ.get('STRICT', '0')))


def _freeze(fn, _depth=0):
    if fn is None or fn.__closure__ is None:
        return fn
    cells = []
    for c in fn.__closure__:
        try:
            v = c.cell_contents
        except ValueError:
            cells.append(c)
            continue
        if isinstance(v, types.FunctionType) and v.__closure__ is not None and _depth < 3:
            v = _freeze(v, _depth + 1)
        cells.append(types.CellType(v))
    return types.FunctionType(fn.__code__, fn.__globals__, fn.__name__, fn.__defaults__, tuple(cells))


class _Op:
    __slots__ = ("fn", "waits", "dma", "dma_n")

    def __init__(self, fn, waits, dma, dma_n):
        self.fn, self.waits, self.dma, self.dma_n = fn, waits, dma, dma_n


class Sched:
    KD = 8
    BLK = {"pe": "tensor", "act": "scalar", "dve": "vector", "pool": "gpsimd", "sp": "sync"}

    def __init__(self, nc, es):
        self.nc = nc
        self.ops = {e: [] for e in self.BLK}
        self.state = {}
        self.ndma = {e: 0 for e in self.BLK}
        self.seen_c = {e: {} for e in self.BLK}
        self.seen_d = {e: set() for e in self.BLK}
        self.last = {}
        self.csem = {e: es.enter_context(nc.semaphore("c_" + e)) for e in ("pe", "act", "dve", "pool")}
        self.dsem = {e: [es.enter_context(nc.semaphore("d_%s%d" % (e, i))) for i in range(self.KD)]
                     for e in ("sp", "pool")}

    def _split(self, key):
        if isinstance(key, tuple):
            return key[0], key[1:]
        return key, None

    def _recs(self, key):
        name, sub = self._split(key)
        d = self.state.get(name)
        if not d:
            return []
        if sub is None:
            return list(d.values())
        out = []
        if sub in d:
            out.append(d[sub])
        if None in d:
            out.append(d[None])
        return out

    def _filter(self, eng, raw, other, dma):
        waits = []
        for d in sorted(raw | other):
            if d[0] == "c":
                if d[1] == eng and not dma:
                    if eng == "pe" or (d not in raw and not _STRICT):
                        continue
                if self.seen_c[eng].get(d[1], -1) >= d[2]:
                    continue
                self.seen_c[eng][d[1]] = d[2]
                waits.append(d)
            else:
                if d in self.seen_d[eng]:
                    continue
                self.seen_d[eng].add(d)
                waits.append(d)
        best, fin = {}, []
        for w in waits:
            if w[0] == "c":
                if w[1] not in best or best[w[1]][2] < w[2]:
                    best[w[1]] = w
            else:
                fin.append(w)
        fin.extend(best.values())
        return fin

    mute = False

    def op(self, eng, fn, reads=(), writes=(), dma=False):
        if self.mute:
            return None
        fn = _freeze(fn)
        idx = len(self.ops[eng])
        raw, other = set(), set()
        for key in reads:
            for rec in self._recs(key):
                if rec[0] is not None:
                    raw.add(rec[0])
        for key in writes:
            for rec in self._recs(key):
                if rec[0] is not None:
                    other.add(rec[0])
                other.update(rec[1])
        if dma:
            n = self.ndma[eng]
            self.ndma[eng] += 1
            ev = ("d", eng, n)
            self.last[("d", eng, n % self.KD)] = ev
        else:
            n = None
            ev = ("c", eng, idx)
            self.last[("c", eng)] = ev
        fin = self._filter(eng, raw, other, dma)
        self.ops[eng].append(_Op(fn, fin, dma, n))
        for key in reads:
            name, sub = self._split(key)
            self.state.setdefault(name, {}).setdefault(sub, [None, []])[1].append(ev)
        for key in writes:
            name, sub = self._split(key)
            d = self.state.setdefault(name, {})
            if sub is None:
                d.clear()
            d[sub] = [ev, []]
        return ev

    def barrier(self):
        evs = set(self.last.values())
        for e in self.BLK:
            fin = self._filter(e, set(evs), set(), True)
            if fin:
                self.ops[e].append(_Op(None, fin, False, None))
        self.state = {}

    def emit(self):
        for e in ("sp", "pool"):
            n = self.ndma[e]
            if n:
                self.ops[e].append(_Op(None, [("d", e, i) for i in range(max(0, n - self.KD), n)], False, None))
        waited = {e: set() for e in self.BLK}
        for e, ops in self.ops.items():
            for o in ops:
                for w in o.waits:
                    if w[0] == "c":
                        waited[w[1]].add(w[2])
        val = {}
        for e in self.BLK:
            for rank, idx in enumerate(sorted(waited[e])):
                val[(e, idx)] = rank + 1
        KD = self.KD
        self.maxval = {e: len(waited[e]) for e in self.BLK}
        self.maxdma = {e: 16 * ((self.ndma[e] - 1) // KD + 1) for e in ("sp", "pool")}
        if os.environ.get("SEMDBG"):
            print("SEM max values", self.maxval, self.maxdma, flush=True)
        with self.nc.Block() as block:
            for e in self.BLK:
                def body(engine, e=e):
                    for idx, o in enumerate(self.ops[e]):
                        for w in o.waits:
                            if w[0] == "c":
                                engine.wait_ge(self.csem[w[1]], val[(w[1], w[2])])
                            else:
                                engine.wait_ge(self.dsem[w[1]][w[2] % KD], 16 * (w[2] // KD + 1))
                        if o.fn is None:
                            continue
                        if o.dma:
                            n = o.dma_n
                            if n >= KD:
                                engine.wait_ge(self.dsem[e][n % KD], 16 * (n // KD))
                            o.fn(engine).then_inc(self.dsem[e][n % KD], 16)
                        else:
                            ins = o.fn(engine)
                            if idx in waited[e]:
                                ins.then_inc(self.csem[e], 1)
                getattr(block, self.BLK[e])(body)
        return {e: len(self.ops[e]) for e in self.BLK}


def _na_pairs():
    t = np.arange(1024)
    r, c = t // 64, t % 64
    rs = np.clip(r - 4, 0, 8)
    cs = np.clip(c - 8, 0, 48)
    valid = ((r[None, :] >= rs[:, None]) & (r[None, :] < rs[:, None] + 8) &
             (c[None, :] >= cs[:, None]) & (c[None, :] < cs[:, None] + 16))
    dr = r[None, :] - r[:, None] + 7
    dc = np.clip(c[None, :] - c[:, None] + 15, 0, 30)
    pairs = []
    for qt in range(8):
        for kt in range(8):
            if valid[qt * 128:(qt + 1) * 128, kt * 128:(kt + 1) * 128].any():
                pairs.append((qt, kt))
    return valid, dr, dc, pairs


_NA = _na_pairs()
NPAIR = len(_NA[3])


def _consts(na_bias):
    c = {}
    c["c_ident"] = np.eye(128, dtype=np.float32)
    p = np.arange(128)[:, None]
    f = np.arange(128)[None, :]
    m = np.zeros((6, 128, 128), np.float32)
    m[0] = (p <= f)
    m[1] = (p >= f)
    m[2] = np.where(f < p, 0.0, BIG)
    m[3] = np.where(f > p, 0.0, BIG)
    m[4] = np.where(f >= p, 0.0, -BIG)
    m[5] = np.where(f <= p, 0.0, -BIG)
    c["c_masks"] = m
    h = np.arange(4, dtype=np.float64)
    lgf = np.log1p(-np.exp2(-(5.0 + h)))
    lgb = np.log1p(-np.exp2(-(5.5 + h)))
    j = np.arange(128, dtype=np.float64)
    dct = np.zeros((4, 128, 128), np.float64)
    for hh in range(4):
        d = j[None, :] - j[:, None]
        dct[hh] = np.where(d > 0, np.exp(np.maximum(d, 0) * lgf[hh]), 0.0) + \
            np.where(d < 0, np.exp(np.maximum(-d, 0) * lgb[hh]), 0.0) + np.where(d == 0, 2.0, 0.0)
    c["c_dct"] = (dct * 0.125).astype(np.float32)
    rqt = np.zeros((2, 4, 128), np.float64)
    kdt = np.zeros((128, 4, 2, 64), np.float64)
    cdt = np.zeros((128, 4, 64), np.float64)
    for hh in range(4):
        rqt[0, hh] = np.exp((j + 1.0) * lgf[hh]) * 0.125
        rqt[1, hh] = np.exp((128.0 - j) * lgb[hh]) * 0.125
        kdt[:, hh, 0, :] = np.exp((127.0 - j) * lgf[hh])[:, None]
        kdt[:, hh, 1, :] = np.exp(j * lgb[hh])[:, None]
        hf_, pr_ = hh % 2, hh // 2
        cdt[hf_ * 64:(hf_ + 1) * 64, 0 * 2 + pr_, :] = np.exp(128.0 * lgf[hh])
        cdt[hf_ * 64:(hf_ + 1) * 64, 1 * 2 + pr_, :] = np.exp(128.0 * lgb[hh])
    c["c_rqt"] = rqt.astype(np.float32)
    c["c_kdt"] = kdt.astype(np.float32)
    c["c_cdt"] = cdt.astype(np.float32)
    t = np.arange(1024)
    row = (t // 64).astype(np.float32)
    col = (t % 64).astype(np.float32)
    inv = (10000.0 ** (-np.arange(16, dtype=np.float32) / 16)).astype(np.float32)
    ar = row[:, None] * inv[None, :]
    ac = col[:, None] * inv[None, :]
    cc = np.concatenate([np.cos(ar), np.cos(ar), np.cos(ac), np.cos(ac)], axis=1)
    ss = np.concatenate([-np.sin(ar), np.sin(ar), -np.sin(ac), np.sin(ac)], axis=1)
    c["c_rope"] = np.stack([np.tile(cc, (1, 6)), np.tile(ss, (1, 6))]).astype(np.float32)
    valid, dr, dc, pairs = _NA
    nab = np.empty((2, NPAIR, 4, 128, 128), np.float32)
    for pi, (qt, kt) in enumerate(pairs):
        qs = slice(qt * 128, (qt + 1) * 128)
        ks = slice(kt * 128, (kt + 1) * 128)
        v = valid[qs, ks].T
        g = na_bias[:, :, dr[qs, ks].T, dc[qs, ks].T]
        nab[:, pi] = np.where(v[None, None], g, np.float32(NEG))
    c["c_nab"] = nab
    return c


IN_SHAPES = dict(
    xp=[1024, 1024], xs=[1024, 1024], cak=[2, 512, 128], cav=[2, 512, 128], cnk=[2, 512, 256], cnv=[2, 512, 256],
    sg=[2, 2, 4, 64, 64], sr=[2, 2, 4, 64, 64], cond=[2, 1024],
    w_ada=[2, 1024, 3072], b_ada=[2, 3072], norm_g=[2, 1024], w_in=[2, 1024, 7952], conv_w=[2, 5, 768],
    a_log=[2, 8], dt_bias=[2, 8], gdn_norm=[2, 64], q_norm=[2, 64], k_norm=[2, 64], ret_norm=[2, 64],
    w_branch=[2, 4, 256, 1024], w_out=[2, 1024, 1024], final_norm=[1, 1024],
    c_ident=[128, 128], c_masks=[6, 128, 128], c_dct=[4, 128, 128], c_rqt=[2, 4, 128], c_kdt=[128, 4, 2, 64],
    c_cdt=[128, 4, 64], c_rope=[2, 1024, 384], c_nab=[2, NPAIR, 4, 128, 128])
OUT_SHAPES = dict(yp=[1024, 1024], ys=[1024, 1024], nak=[4, 2, 256, 128], nav=[4, 2, 256, 128],
                  nnk=[4, 2, 256, 256], nnv=[4, 2, 256, 256], nsg=[4, 2, 2, 4, 64, 64], nsr=[4, 2, 2, 4, 64, 64])


class _Stop(Exception):
    pass


def build(taps=None, groups=(0, 1), layers=(0, 1), stop_after=None):
    taps = taps or {}
    nc = bass.Bass("TRN2", target_bir_lowering=False)
    D = {}
    for n, s in IN_SHAPES.items():
        D[n] = nc.dram_tensor(n, list(s), F32, kind="ExternalInput").ap()
    for n, s in OUT_SHAPES.items():
        D[n] = nc.dram_tensor(n, list(s), F32, kind="ExternalOutput").ap()
    for n, s in taps.items():
        D["tap_" + n] = nc.dram_tensor("tap_" + n, list(s), F32, kind="ExternalOutput").ap()

    with ExitStack() as es:
        S = Sched(nc, es)

        uid = [0]

        def sb(name, shape, dt=F32, st=es):
            uid[0] += 1
            return st.enter_context(nc.sbuf_tensor("%s_%d" % (name, uid[0]), list(shape), dt))

        PS = [es.enter_context(nc.psum_tensor("ps%d" % i, [128, 512], F32)) for i in range(8)]
        rr = [0]

        def bank():
            i = 4 + rr[0] % 4
            rr[0] += 1
            return PS[i], ("ps", i)

        def V(fn, r, w): return S.op("dve", fn, r, w)
        def A(fn, r, w): return S.op("act", fn, r, w)
        def G(fn, r, w): return S.op("pool", fn, r, w)
        def P(fn, r, w): return S.op("pe", fn, r, w)
        def DS(fn, r, w): return S.op("sp", fn, r, w, dma=True)
        def DG(fn, r, w): return S.op("pool", fn, r, w, dma=True)

        def mm(out, lhsT, rhs, start, stop, r, w):
            P(lambda e: e.matmul(out, lhsT=lhsT, rhs=rhs, start=start, stop=stop), r, w)

        def tr(out, in_, idt, r, w):
            P(lambda e: e.transpose(out, in_, idt), r, w)

        def act(out, in_, func, r, w, bias=0.0, scale=1.0):
            A(lambda e: e.activation(out=out, in_=in_, func=func, bias=bias, scale=scale), r, w)

        def tap(name, ap, reads):
            if name in taps and name not in os.environ.get("NOTAP", "").split(","):
                DG(lambda e: e.dma_start(out=D["tap_" + name], in_=ap), reads, [])

        ident = sb("ident", [128, 128])
        identb = sb("identb", [128, 128], BF16)
        ones = sb("ones", [128, 128])
        onesblk = sb("onesblk", [128, 128])
        onespad = sb("onespad", [128, 2, 128], BF16)
        masks = sb("masks", [128, 6, 128])
        dct = sb("dct", [128, 4, 128])
        rqt = sb("rqt", [128, 2, 4, 128])
        kdt = sb("kdt", [128, 4, 2, 64])
        cdt = sb("cdt", [128, 4, 64])
        ngT = sb("ngT", [128, 2, 8])
        fnT = sb("fnT", [128, 8])
        baT = sb("baT", [128, 2, 24])
        cwT = sb("cwT", [128, 2, 6, 5])
        gqk = sb("gqk", [128, 2, 6, 64])
        gkn = sb("gkn", [128, 2, 2, 64])
        ggn = sb("ggn", [128, 2, 4, 64])
        grn = sb("grn", [128, 2, 4, 64])
        alb = sb("alb", [128, 2, 8])
        dtb = sb("dtb", [128, 2, 8])
        nega = sb("nega", [128, 2, 8])
        condT = sb("condT", [128, 8, 2])
        scond = sb("scond", [128, 8, 2])
        modT = sb("modT", [128, 2, 24, 2])
        gmul = sb("gmul", [128, 2, 8, 2])

        DS(lambda e: e.dma_start(out=ident[:], in_=D["c_ident"]), [], ["ident"])
        DG(lambda e: e.dma_start(out=identb[:], in_=D["c_ident"]), [], ["identb"])
        V(lambda e: e.memset(ones[:], 1.0), [], ["ones"])
        V(lambda e: e.memset(onesblk[:], 0.0), [], ["onesblk"])
        V(lambda e: e.memset(onesblk[0:64, 0:64], 1.0), [], ["onesblk"])
        V(lambda e: e.memset(onesblk[64:128, 64:128], 1.0), [], ["onesblk"])
        V(lambda e: e.memset(onespad[:], 0.0), [], ["onespad"])
        V(lambda e: e.memset(onespad[:, 0, 0:64], 1.0), [], ["onespad"])
        V(lambda e: e.memset(onespad[:, 1, 64:128], 1.0), [], ["onespad"])
        DS(lambda e: e.dma_start(out=masks[:], in_=D["c_masks"].rearrange("m p f -> p m f")), [], ["masks"])
        DS(lambda e: e.dma_start(out=dct[:], in_=D["c_dct"].rearrange("m p f -> p m f")), [], ["dct"])
        DS(lambda e: e.dma_start(out=rqt[:].rearrange("p a b c -> p (a b c)"),
                                 in_=D["c_rqt"].rearrange("a b c -> (a b c)").partition_broadcast(128)), [], ["rqt"])
        DS(lambda e: e.dma_start(out=kdt[:], in_=D["c_kdt"]), [], ["kdt"])
        DS(lambda e: e.dma_start(out=cdt[:], in_=D["c_cdt"]), [], ["cdt"])
        DS(lambda e: e.dma_start(out=ngT[:], in_=D["norm_g"].rearrange("l (c p) -> p l c", p=128)), [], ["ngT"])
        DS(lambda e: e.dma_start(out=fnT[:], in_=D["final_norm"].rearrange("o (c p) -> p (o c)", p=128)), [], ["fnT"])
        DS(lambda e: e.dma_start(out=baT[:], in_=D["b_ada"].rearrange("l (c p) -> p l c", p=128)), [], ["baT"])
        for l in range(2):
            for c6 in range(6):
                DS(lambda e, l=l, c6=c6: e.dma_start(
                    out=cwT[:, l, c6, :], in_=D["conv_w"][l, :, c6 * 128:(c6 + 1) * 128].rearrange("j p -> p j")),
                   [], ["cwT"])
        for jj in range(2):
            DS(lambda e, jj=jj: e.dma_start(out=condT[:, :, jj], in_=D["cond"][jj].rearrange("(c p) -> p c", p=128)), [], ["condT"])
        for l in range(2):
            for hh in range(6):
                src = "q_norm" if hh < 4 else "k_norm"
                DS(lambda e, l=l, hh=hh, src=src: e.dma_start(out=gqk[:, l, hh, :], in_=D[src][l].partition_broadcast(128)),
                   [], ["gqk"])
            for hh in range(2):
                DS(lambda e, l=l, hh=hh: e.dma_start(out=gkn[:, l, hh, :], in_=D["k_norm"][l].partition_broadcast(128)),
                   [], ["gkn"])
            for hh in range(4):
                DS(lambda e, l=l, hh=hh: e.dma_start(out=ggn[:, l, hh, :], in_=D["gdn_norm"][l].partition_broadcast(128)),
                   [], ["ggn"])
                DS(lambda e, l=l, hh=hh: e.dma_start(out=grn[:, l, hh, :], in_=D["ret_norm"][l].partition_broadcast(128)),
                   [], ["grn"])
            DS(lambda e, l=l: e.dma_start(out=alb[:, l, :], in_=D["a_log"][l].partition_broadcast(128)), [], ["alb"])
            DS(lambda e, l=l: e.dma_start(out=dtb[:, l, :], in_=D["dt_bias"][l].partition_broadcast(128)), [], ["dtb"])
        for l in range(2):
            V(lambda e, l=l: e.tensor_scalar(out=gqk[:, l, 0:4, :], in0=gqk[:, l, 0:4, :], scalar1=0.125, scalar2=None,
                                             op0=ALU.mult), ["gqk"], ["gqk"])
        act(nega[:], alb[:], AF.Exp, ["alb"], ["nega"])
        V(lambda e: e.tensor_scalar(out=nega[:], in0=nega[:], scalar1=-1.0, scalar2=None, op0=ALU.mult), ["nega"], ["nega"])
        act(scond[:], condT[:], AF.Silu, ["condT"], ["scond"])

        with ExitStack() as ph:
            wa = [sb("wa%d" % i, [128, 8, 512], F32, ph) for i in range(2)]
            cnt = 0
            for l in range(2):
                pb, pk = PS[0], ("ps", 0)
                for ch in range(6):
                    w_t, wk = wa[cnt % 2], "wa%d" % (cnt % 2)
                    cnt += 1
                    DS(lambda e, l=l, ch=ch, w_t=w_t: e.dma_start(
                        out=w_t[:], in_=D["w_ada"][l, :, ch * 512:(ch + 1) * 512].rearrange("(k p) n -> p k n", p=128)),
                       [], [wk])
                    for oc in range(4):
                        col = (ch * 4 + oc) * 2
                        for k in range(8):
                            mm(pb[:, col:col + 2], w_t[:, k, oc * 128:(oc + 1) * 128], scond[:, k, :], k == 0, k == 7,
                               [wk, "scond"], [pk])
                for j in range(2):
                    V(lambda e, l=l, j=j, pb=pb: e.tensor_tensor(
                        out=modT[:, l, :, j], in0=pb[:, 0:48].rearrange("p (a b) -> p a b", b=2)[:, :, j],
                        in1=baT[:, l, :], op=ALU.add), [pk, "baT"], ["modT"])
                    V(lambda e, l=l, j=j: e.scalar_tensor_tensor(
                        out=gmul[:, l, :, j], in0=modT[:, l, 8:16, j], scalar=1.0, in1=ngT[:, l, :],
                        op0=ALU.add, op1=ALU.mult), ["modT", "ngT"], ["gmul"])
            tap("modT", modT[:].rearrange("p a b c -> p (a b c)"), ["modT"])
        S.barrier()

        hT = sb("hT", [128, 8, 1024])
        hnT = sb("hnT", [128, 8, 1024], BF16)
        brT = sb("brT", [128, 8, 1024], BF16)
        wbs = [sb("wb%d" % i, [128, 8, 512], BF16) for i in range(3)]
        stg = [sb("stg%d" % i, [128, 1024]) for i in range(2)]
        wcnt = [0]
        scnt = [0]

        pend_w = {}

        def load_w(l, c0, c1):
            if (l, c0, c1) in pend_w:
                return pend_w.pop((l, c0, c1))
            i = wcnt[0] % 3
            wcnt[0] += 1
            t, k = wbs[i], "wb%d" % i
            DG(lambda e: e.dma_start(out=t[:, :, 0:c1 - c0],
                                     in_=D["w_in"][l, :, c0:c1].rearrange("(k p) n -> p k n", p=128)), [], [k])
            return t, k

        def prefetch_w(l, c0, c1):
            if not S.mute:
                pend_w[(l, c0, c1)] = load_w(l, c0, c1)

        def proj_T(wt, wk, a, b, tt, pb, pk):
            for k in range(8):
                mm(pb[:, 0:b - a], hnT[:, k, tt * 128:(tt + 1) * 128], wt[:, k, a:b], k == 0, k == 7, [wk, "hnT"], [pk])

        def proj_F(wt, wk, a, m, tb, pb, pk):
            for k in range(8):
                mm(pb[0:m, :], wt[:, k, a:a + m], hnT[:, k, tb * 512:(tb + 1) * 512], k == 0, k == 7, [wk, "hnT"], [pk])

        def rstd_from(ss_ap, out_ap, n, r, w):
            act(out_ap, ss_ap, AF.Sqrt, r, w, bias=EPS, scale=1.0 / n)
            V(lambda e: e.reciprocal(out=out_ap, in_=out_ap), w, w)

        def rmsnorm_T(st, gcol_fn, scol_fn, out_fn, outkey):
            sq = sb("rn_sq", [128, 8, 512], F32, st)
            rs = sb("rn_rs", [128, 512], F32, st)
            tmp = sb("rn_tmp", [128, 512], F32, st)
            for tb in range(2):
                blk = slice(tb * 512, (tb + 1) * 512)
                act(sq[:], hT[:, :, blk], AF.Square, ["hT"], ["rn_sq"])
                pb, pk = bank()
                for k in range(8):
                    mm(pb[:, :], ones[:, :], sq[:, k, :], k == 0, k == 7, ["ones", "rn_sq"], [pk])
                rstd_from(pb[:, :], rs[:], 1024.0, [pk], ["rn_rs"])
                for k in range(8):
                    V(lambda e, k=k, blk=blk: e.tensor_tensor(out=tmp[:], in0=hT[:, k, blk], in1=rs[:], op=ALU.mult),
                      ["hT", "rn_rs"], ["rn_tmp"])
                    sc = scol_fn(k)
                    if sc is None:
                        V(lambda e, k=k, tb=tb: e.tensor_scalar(out=out_fn(k, tb), in0=tmp[:], scalar1=gcol_fn(k),
                                                                scalar2=None, op0=ALU.mult), ["rn_tmp", "gmul", "fnT"], [outkey])
                    else:
                        V(lambda e, k=k, tb=tb, sc=sc: e.tensor_scalar(out=out_fn(k, tb), in0=tmp[:], scalar1=gcol_fn(k),
                                                                       scalar2=sc, op0=ALU.mult, op1=ALU.add),
                          ["rn_tmp", "gmul", "modT"], [outkey])

        def norm_gate_out(st, tag, o_acc, okey, gtile, gkey, l, zT, zkey, br):
            sq = sb(tag + "_sq", [128, 256], F32, st)
            ss = sb(tag + "_ss", [128, 4], F32, st)
            on = sb(tag + "_on", [128, 256], F32, st)
            for tt in range(8):
                act(sq[:], o_acc[:, tt, :], AF.Square, [(okey, tt)], [tag + "_sq"])
                V(lambda e: e.tensor_reduce(out=ss[:], in_=sq[:].rearrange("p (h d) -> p h d", h=4), axis=AX.X, op=ALU.add),
                  [tag + "_sq"], [tag + "_ss"])
                rstd_from(ss[:], ss[:], 64.0, [tag + "_ss"], [tag + "_ss"])
                V(lambda e, tt=tt: e.tensor_tensor(out=on[:].rearrange("p (h d) -> p h d", h=4),
                                                   in0=o_acc[:, tt, :].rearrange("p (h d) -> p h d", h=4),
                                                   in1=ss[:].unsqueeze(2).to_broadcast([128, 4, 64]), op=ALU.mult),
                  [(okey, tt), tag + "_ss"], [tag + "_on"])
                G(lambda e: e.tensor_tensor(out=on[:].rearrange("p (h d) -> p h d", h=4),
                                            in0=on[:].rearrange("p (h d) -> p h d", h=4), in1=gtile[:, l, :, :], op=ALU.mult),
                  [tag + "_on", gkey], [tag + "_on"])
                pb, pk = bank()
                for c in range(2):
                    tr(pb[:, c * 128:(c + 1) * 128], on[:, c * 128:(c + 1) * 128], ident[:], [tag + "_on", "ident"], [pk])
                V(lambda e, tt=tt, pb=pb: e.tensor_tensor(out=brT[:, br * 2:br * 2 + 2, tt * 128:(tt + 1) * 128],
                                                          in0=pb[:, 0:256].rearrange("p (c t) -> p c t", c=2),
                                                          in1=zT[:, :, tt * 128:(tt + 1) * 128], op=ALU.mult),
                  [pk, zkey], [("brT", br)])

        def attention(st, tag, qT, kT, vpad, kvmap, vslot, qblocks, keys_fn, zT, zkey, br):
            pts = [sb(tag + "_p%d" % i, [128, 512], BF16, st) for i in range(3)]
            rden = sb(tag + "_rd", [128, 512], F32, st)
            osb = sb(tag + "_o", [128, 512], F32, st)
            pc = 0
            it = 0
            for (q0, qn) in qblocks:
                keys = keys_fn(q0)
                for pr in range(2):
                    bo, bd = (0, 1) if it % 2 == 0 else (2, 3)
                    it += 1
                    psO, psD = PS[bo], PS[bd]
                    ko, kd = ("ps", bo), ("ps", bd)
                    nk = len(keys) * 2
                    ci = 0
                    for (kidx, bias_fn) in keys:
                        for hh in range(2):
                            h = 2 * pr + hh
                            pb, pk = bank()
                            mm(pb[:, 0:qn], kT[:, kvmap(h), kidx * 128:(kidx + 1) * 128], qT[:, h, q0:q0 + qn],
                               True, bias_fn is None, [tag + "_kT", tag + "_qT"], [pk])
                            if bias_fn is not None:
                                bap, bkey = bias_fn(h)
                                mm(pb[:, 0:qn], identb[:, :], bap, False, True, ["identb", bkey], [pk])
                            pt, ptk = pts[pc % 3], tag + "_p%d" % (pc % 3)
                            pc += 1
                            act(pt[:, 0:qn], pb[:, 0:qn], AF.Exp, [pk], [ptk])
                            mm(psO[:, 0:qn], vpad[:, kidx, vslot(h), :], pt[:, 0:qn], ci == 0, ci == nk - 1,
                               [tag + "_vp", ptk], [ko])
                            mm(psD[:, 0:qn], onespad[:, hh, :], pt[:, 0:qn], ci == 0, ci == nk - 1, ["onespad", ptk], [kd])
                            ci += 1
                    V(lambda e, psD=psD, qn=qn: e.reciprocal(out=rden[:, 0:qn], in_=psD[:, 0:qn]), [kd], [tag + "_rd"])
                    V(lambda e, psO=psO, qn=qn: e.tensor_tensor(out=osb[:, 0:qn], in0=psO[:, 0:qn], in1=rden[:, 0:qn],
                                                                op=ALU.mult), [ko, tag + "_rd"], [tag + "_o"])
                    V(lambda e, pr=pr, q0=q0, qn=qn: e.tensor_tensor(out=brT[:, br * 2 + pr, q0:q0 + qn], in0=osb[:, 0:qn],
                                                                     in1=zT[:, pr, q0:q0 + qn], op=ALU.mult),
                      [tag + "_o", zkey], [("brT", br)])

        def zproj(wt, wk, a, zT, zkey):
            for c in range(2):
                for tb in range(2):
                    pb, pk = bank()
                    proj_F(wt, wk, a + c * 128, 128, tb, pb, pk)
                    act(zT[:, c, tb * 512:(tb + 1) * 512], pb[:, :], AF.Silu, [pk], [zkey])

        def stage(name):
            if stop_after == name or (stop_after == "Dproj" and name == "D"):
                raise _Stop()

        def run_groups():
          for grp in (groups if stop_after != "p0" else ()):
            G(lambda e: e.memset(brT[:], 0.0), [], ["brT"])
            xin = D["xp"] if grp == 0 else D["xs"]
            yout = D["yp"] if grp == 0 else D["ys"]
            for tt in range(8):
                s_t, sk = stg[scnt[0] % 2], "stg%d" % (scnt[0] % 2)
                scnt[0] += 1
                DS(lambda e, tt=tt, s_t=s_t: e.dma_start(out=s_t[:], in_=xin[tt * 128:(tt + 1) * 128, :]), [], [sk])
                for half in range(2):
                    pb, pk = bank()
                    for q in range(4):
                        k = half * 4 + q
                        tr(pb[:, q * 128:(q + 1) * 128], s_t[:, k * 128:(k + 1) * 128], ident[:], [sk, "ident"], [pk])
                    eng = A if half == 0 else V
                    if half == 0:
                        A(lambda e, tt=tt, pb=pb: e.copy(out=hT[:, 0:4, tt * 128:(tt + 1) * 128],
                                                         in_=pb[:, :].rearrange("p (a b) -> p a b", a=4)), [pk], ["hT"])
                    else:
                        V(lambda e, tt=tt, pb=pb: e.tensor_copy(out=hT[:, 4:8, tt * 128:(tt + 1) * 128],
                                                                in_=pb[:, :].rearrange("p (a b) -> p a b", a=4)), [pk], ["hT"])
            for l in layers:
                j = grp
                S.mute = False
                with ExitStack() as ph:
                    rmsnorm_T(ph, lambda k: gmul[:, l, k, j:j + 1], lambda k: modT[:, l, k, j:j + 1],
                              lambda k, tb: hnT[:, k, tb * 512:(tb + 1) * 512], "hnT")
                S.barrier()
                stage("norm")
                if grp == groups[0] and l == layers[0]:
                    tap("hnT", hnT[:].rearrange("p a b -> p (a b)"), ["hnT"])

                S.mute = "A" in _SKIP
                with ExitStack() as ph:
                    nkt = 8 if grp == 0 else 12
                    qT = sb("a_qT", [64, 4, 1024], BF16, ph)
                    kT = sb("a_kT", [64, 2, 128 * nkt], BF16, ph)
                    vpad = sb("a_vp", [128, nkt, 4, 128], BF16, ph)
                    zT = sb("a_zT", [128, 2, 1024], BF16, ph)
                    sq = sb("a_sq", [128, 384], F32, ph)
                    ss = sb("a_ss", [128, 6], F32, ph)
                    qk = sb("a_qk", [128, 384], F32, ph)
                    qk2 = sb("a_qk2", [128, 384], F32, ph)
                    kout = sb("a_ko", [128, 128], F32, ph)
                    vout = sb("a_vo", [128, 128], F32, ph)
                    rp = sb("a_rp", [128, 2, 384], F32, ph)
                    G(lambda e: e.memset(vpad[:], 0.0), [], ["a_vp"])
                    w0, w0k = load_w(l, 0, 512)
                    w1, w1k = load_w(l, 512, 768)
                    for tt in range(8):
                        pb, pk = bank()
                        proj_T(w0, w0k, 0, 512, tt, pb, pk)
                        act(sq[:], pb[:, 0:384], AF.Square, [pk], ["a_sq"])
                        V(lambda e: e.tensor_reduce(out=ss[:], in_=sq[:].rearrange("p (h d) -> p h d", h=6), axis=AX.X,
                                                    op=ALU.add), ["a_sq"], ["a_ss"])
                        rstd_from(ss[:], ss[:], 64.0, ["a_ss"], ["a_ss"])
                        V(lambda e, pb=pb: e.tensor_tensor(out=qk[:].rearrange("p (h d) -> p h d", h=6),
                                                           in0=pb[:, 0:384].rearrange("p (h d) -> p h d", h=6),
                                                           in1=ss[:].unsqueeze(2).to_broadcast([128, 6, 64]), op=ALU.mult),
                          [pk, "a_ss"], ["a_qk"])
                        if grp == 0:
                            b_, s0 = tt // 2, (tt % 2) * 128
                            G(lambda e: e.tensor_tensor(out=kout[:].rearrange("p (h d) -> p h d", h=2),
                                                        in0=qk[:, 256:384].rearrange("p (h d) -> p h d", h=2),
                                                        in1=gkn[:, l, :, :], op=ALU.mult), ["a_qk", "gkn"], ["a_ko"])
                            DS(lambda e, b_=b_, s0=s0: e.dma_start(out=D["nak"][b_, l, s0:s0 + 128, :], in_=kout[:]),
                               ["a_ko"], [])
                            A(lambda e, pb=pb: e.copy(out=vout[:], in_=pb[:, 384:512]), [pk], ["a_vo"])
                            DS(lambda e, b_=b_, s0=s0: e.dma_start(out=D["nav"][b_, l, s0:s0 + 128, :], in_=vout[:]),
                               ["a_vo"], [])
                        G(lambda e: e.tensor_tensor(out=qk[:].rearrange("p (h d) -> p h d", h=6),
                                                    in0=qk[:].rearrange("p (h d) -> p h d", h=6), in1=gqk[:, l, :, :],
                                                    op=ALU.mult), ["a_qk", "gqk"], ["a_qk"])
                        src, srck = qk, "a_qk"
                        if grp == 1:
                            DS(lambda e, tt=tt: e.dma_start(out=rp[:], in_=D["c_rope"][:, tt * 128:(tt + 1) * 128, :]
                                                            .rearrange("a p f -> p a f")), [], ["a_rp"])
                            V(lambda e: e.tensor_tensor(out=qk2[:], in0=qk[:], in1=rp[:, 0, :], op=ALU.mult),
                              ["a_qk", "a_rp"], ["a_qk2"])
                            qv = qk[:].rearrange("p (g s d) -> p g s d", s=2, d=16)
                            sv = rp[:, 1, :].rearrange("p (g s d) -> p g s d", s=2, d=16)
                            G(lambda e, qv=qv, sv=sv: e.tensor_tensor(
                                out=sq[:].rearrange("p (g s d) -> p g s d", s=2, d=16)[:, :, 0, :], in0=qv[:, :, 1, :],
                                in1=sv[:, :, 0, :], op=ALU.mult), ["a_qk", "a_rp"], ["a_sq"])
                            G(lambda e, qv=qv, sv=sv: e.tensor_tensor(
                                out=sq[:].rearrange("p (g s d) -> p g s d", s=2, d=16)[:, :, 1, :], in0=qv[:, :, 0, :],
                                in1=sv[:, :, 1, :], op=ALU.mult), ["a_qk", "a_rp"], ["a_sq"])
                            V(lambda e: e.tensor_tensor(out=qk2[:], in0=qk2[:], in1=sq[:], op=ALU.add),
                              ["a_qk2", "a_sq"], ["a_qk2"])
                            src, srck = qk2, "a_qk2"
                        pq, pqk = bank()
                        for h in range(4):
                            tr(pq[0:64, h * 128:(h + 1) * 128], src[:, h * 64:(h + 1) * 64], ident[:], [srck, "ident"], [pqk])
                        A(lambda e, tt=tt, pq=pq: e.copy(out=qT[:, :, tt * 128:(tt + 1) * 128],
                                                         in_=pq[0:64, :].rearrange("p (a b) -> p a b", a=4)), [pqk], ["a_qT"])
                        pk2, pk2k = bank()
                        for h in range(2):
                            tr(pk2[0:64, h * 128:(h + 1) * 128], src[:, 256 + h * 64:256 + (h + 1) * 64], ident[:],
                               [srck, "ident"], [pk2k])
                        V(lambda e, tt=tt, pk2=pk2: e.tensor_copy(out=kT[:, :, tt * 128:(tt + 1) * 128],
                                                                  in_=pk2[0:64, 0:256].rearrange("p (a b) -> p a b", a=2)),
                          [pk2k], ["a_kT"])
                        for kv in range(2):
                            for pos in range(2):
                                A(lambda e, tt=tt, kv=kv, pos=pos, pb=pb: e.copy(
                                    out=vpad[:, tt, kv * 2 + pos, pos * 64:(pos + 1) * 64],
                                    in_=pb[:, 384 + kv * 64:384 + (kv + 1) * 64]), [pk], ["a_vp"])
                    if grp == 1:
                        for ct in range(4):
                            s_t, sk = stg[scnt[0] % 2], "stg%d" % (scnt[0] % 2)
                            scnt[0] += 1
                            DS(lambda e, ct=ct, s_t=s_t: e.dma_start(out=s_t[:, 0:128], in_=D["cak"][l, ct * 128:(ct + 1) * 128, :]),
                               [], [sk])
                            DS(lambda e, ct=ct, s_t=s_t: e.dma_start(out=s_t[:, 128:256], in_=D["cav"][l, ct * 128:(ct + 1) * 128, :]),
                               [], [sk])
                            pk2, pk2k = bank()
                            for h in range(2):
                                tr(pk2[0:64, h * 128:(h + 1) * 128], s_t[:, h * 64:(h + 1) * 64], ident[:], [sk, "ident"], [pk2k])
                            V(lambda e, ct=ct, pk2=pk2: e.tensor_copy(
                                out=kT[:, :, (8 + ct) * 128:(9 + ct) * 128],
                                in_=pk2[0:64, 0:256].rearrange("p (a b) -> p a b", a=2)), [pk2k], ["a_kT"])
                            for kv in range(2):
                                for pos in range(2):
                                    A(lambda e, ct=ct, kv=kv, pos=pos, s_t=s_t: e.copy(
                                        out=vpad[:, 8 + ct, kv * 2 + pos, pos * 64:(pos + 1) * 64],
                                        in_=s_t[:, 128 + kv * 64:128 + (kv + 1) * 64]), [sk], ["a_vp"])
                    zproj(w1, w1k, 0, zT, "a_zT")
                    prefetch_w(l, 2832, 3344)
                    if grp == 0:
                        qblocks = [(b_ * 256, 256) for b_ in range(4)]
                        keys_fn = lambda q0: [((q0 // 128) + i, None) for i in range(2)]
                    else:
                        qblocks = [(0, 512), (512, 512)]
                        keys_fn = lambda q0: [(i, None) for i in range(12)]
                    attention(ph, "a", qT, kT, vpad, lambda h: h // 2, lambda h: (h // 2) * 2 + (h % 2), qblocks, keys_fn,
                              zT, "a_zT", 0)
                S.barrier()
                stage("A")
                if grp == groups[0] and l == layers[0]:
                    tap("brA", brT[:, 0:2, :].rearrange("p a b -> p (a b)"), [("brT", 0)])

                S.mute = "D" in _SKIP
                with ExitStack() as ph:
                    nkt = 8 if grp == 0 else 12
                    qT = sb("d_qT", [64, 4, 1024], BF16, ph)
                    kT = sb("d_kT", [64, 4, 128 * nkt], BF16, ph)
                    vpad = sb("d_vp", [128, nkt, 4, 128], BF16, ph)
                    zT = sb("d_zT", [128, 2, 1024], BF16, ph)
                    kout = sb("d_ko", [128, 256], F32, ph)
                    vout = sb("d_vo", [128, 256], F32, ph)
                    G(lambda e: e.memset(vpad[:], 0.0), [], ["d_vp"])
                    w0, w0k = load_w(l, 2832, 3344)
                    w1, w1k = load_w(l, 3344, 3856)
                    for c in range(2):
                        for tb in range(2):
                            blk = slice(tb * 512, (tb + 1) * 512)
                            pb, pk = bank()
                            proj_F(w0, w0k, c * 128, 128, tb, pb, pk)
                            for hf in range(2):
                                V(lambda e, c=c, hf=hf, blk=blk, pb=pb: e.tensor_scalar(
                                    out=qT[:, 2 * c + hf, blk], in0=pb[hf * 64:(hf + 1) * 64, :], scalar1=0.125, scalar2=None,
                                    op0=ALU.mult), [pk], ["d_qT"])
                            pb, pk = bank()
                            proj_F(w0, w0k, 256 + c * 128, 128, tb, pb, pk)
                            for hf in range(2):
                                A(lambda e, c=c, hf=hf, blk=blk, pb=pb: e.copy(out=kT[:, 2 * c + hf, blk],
                                                                               in_=pb[hf * 64:(hf + 1) * 64, :]), [pk], ["d_kT"])
                    for tt in range(8):
                        pb, pk = bank()
                        proj_T(w1, w1k, 0, 256, tt, pb, pk)
                        for h in range(4):
                            A(lambda e, tt=tt, h=h, pb=pb: e.copy(out=vpad[:, tt, h, (h % 2) * 64:(h % 2) * 64 + 64],
                                                                  in_=pb[:, h * 64:(h + 1) * 64]), [pk], ["d_vp"])
                        if grp == 0:
                            b_, s0 = tt // 2, (tt % 2) * 128
                            A(lambda e, pb=pb: e.copy(out=vout[:], in_=pb[:, 0:256]), [pk], ["d_vo"])
                            DS(lambda e, b_=b_, s0=s0: e.dma_start(out=D["nnv"][b_, l, s0:s0 + 128, :], in_=vout[:]),
                               ["d_vo"], [])
                            pb2, pk2k = bank()
                            proj_T(w0, w0k, 256, 512, tt, pb2, pk2k)
                            V(lambda e, pb2=pb2: e.tensor_copy(out=kout[:], in_=pb2[:, 0:256]), [pk2k], ["d_ko"])
                            DS(lambda e, b_=b_, s0=s0: e.dma_start(out=D["nnk"][b_, l, s0:s0 + 128, :], in_=kout[:]),
                               ["d_ko"], [])
                    if grp == 1:
                        for ct in range(4):
                            s_t, sk = stg[scnt[0] % 2], "stg%d" % (scnt[0] % 2)
                            scnt[0] += 1
                            DS(lambda e, ct=ct, s_t=s_t: e.dma_start(out=s_t[:, 0:256], in_=D["cnk"][l, ct * 128:(ct + 1) * 128, :]),
                               [], [sk])
                            DS(lambda e, ct=ct, s_t=s_t: e.dma_start(out=s_t[:, 256:512], in_=D["cnv"][l, ct * 128:(ct + 1) * 128, :]),
                               [], [sk])
                            pk2, pk2k = bank()
                            for h in range(4):
                                tr(pk2[0:64, h * 128:(h + 1) * 128], s_t[:, h * 64:(h + 1) * 64], ident[:], [sk, "ident"], [pk2k])
                            V(lambda e, ct=ct, pk2=pk2: e.tensor_copy(
                                out=kT[:, :, (8 + ct) * 128:(9 + ct) * 128],
                                in_=pk2[0:64, :].rearrange("p (a b) -> p a b", a=4)), [pk2k], ["d_kT"])
                            for h in range(4):
                                A(lambda e, ct=ct, h=h, s_t=s_t: e.copy(
                                    out=vpad[:, 8 + ct, h, (h % 2) * 64:(h % 2) * 64 + 64],
                                    in_=s_t[:, 256 + h * 64:256 + (h + 1) * 64]), [sk], ["d_vp"])
                    zproj(w1, w1k, 256, zT, "d_zT")
                    prefetch_w(l, 1808, 2320)
                    if stop_after == "Dproj":
                        pass
                    elif grp == 0:
                        qblocks = [(b_ * 256, 256) for b_ in range(4)]
                        keys_fn = lambda q0: [((q0 // 128) + i, None) for i in range(2)]
                        attention(ph, "d", qT, kT, vpad, lambda h: h, lambda h: h, qblocks, keys_fn, zT, "d_zT", 3)
                    else:
                        nbt = [sb("d_nb%d" % i, [128, 6, 4, 128], BF16, ph) for i in range(3)]
                        pairs = _NA[3]
                        qblocks = [(qt * 128, 128) for qt in range(8)]
                        issued = set()

                        def nb_load(qt):
                            if qt in issued or qt > 7:
                                return
                            issued.add(qt)
                            pis_ = [pi for pi, (a, b) in enumerate(pairs) if a == qt]
                            nb_, nbk_ = nbt[qt % 3], "d_nb%d" % (qt % 3)
                            DG(lambda e: e.dma_start(out=nb_[:, 0:len(pis_), :, :],
                                                     in_=D["c_nab"][l, pis_[0]:pis_[0] + len(pis_)].rearrange("a h k q -> k a h q")),
                               [], [nbk_])

                        def keys_fn(q0):
                            qt = q0 // 128
                            pis = [pi for pi, (a, b) in enumerate(pairs) if a == qt]
                            nb, nbk = nbt[qt % 3], "d_nb%d" % (qt % 3)
                            nb_load(qt)
                            nb_load(qt + 1)
                            nb_load(qt + 2)
                            out = []
                            for ii, pi in enumerate(pis):
                                out.append((pairs[pi][1], (lambda h, ii=ii: (nb[:, ii, h, :], nbk))))
                            out += [(8 + i, None) for i in range(4)]
                            return out
                        attention(ph, "d", qT, kT, vpad, lambda h: h, lambda h: h, qblocks, keys_fn, zT, "d_zT", 3)
                S.barrier()
                stage("D")
                if grp == groups[0] and l == layers[0]:
                    tap("brD", brT[:, 6:8, :].rearrange("p a b -> p (a b)"), [("brT", 3)])

                S.mute = "C" in _SKIP
                with ExitStack() as ph:
                    qTm = sb("r_qTm", [128, 2, 2, 1024], BF16, ph)
                    kTr = sb("r_kT", [128, 2, 1024], BF16, ph)
                    kdp = sb("r_kdp", [128, 4, 2, 128], BF16, ph)
                    vtk = sb("r_v", [128, 8, 256], BF16, ph)
                    zT = sb("r_zT", [128, 2, 1024], BF16, ph)
                    U = sb("r_U", [128, 8, 256], F32, ph)
                    Sin = sb("r_Sin", [128, 8, 256], F32, ph)
                    Sfin = sb("r_Sfin", [128, 256], F32, ph)
                    Sinb = sb("r_Sinb", [128, 256], BF16, ph)
                    qkm = sb("r_qkm", [128, 4, 128], BF16, ph)
                    qdm = sb("r_qdm", [128, 4, 2, 128], BF16, ph)
                    oacc = sb("r_oacc", [128, 8, 256], F32, ph)
                    tmpS = sb("r_tmpS", [128, 256], F32, ph)
                    R2 = int(os.environ.get("R2", "511"))
                    if R2 & 1:
                        G(lambda e: e.memset(qTm[:], 0.0), [], ["r_qTm"])
                        G(lambda e: e.memset(kdp[:], 0.0), [], ["r_kdp"])
                    w0, w0k = load_w(l, 1808, 2320)
                    w1, w1k = load_w(l, 2320, 2832)
                    for c in range(2):
                        for tb in range(2):
                            blk = slice(tb * 512, (tb + 1) * 512)
                            pb, pk = bank()
                            if R2 & 2:
                                proj_F(w0, w0k, c * 128, 128, tb, pb, pk)
                            for hf in (range(2) if R2 & 2 else []):
                                rows = slice(hf * 64, (hf + 1) * 64)
                                if hf == 0:
                                    A(lambda e, c=c, hf=hf, blk=blk, rows=rows, pb=pb: e.copy(out=qTm[rows, c, hf, blk], in_=pb[rows, :]),
                                      [pk], ["r_qTm"])
                                else:
                                    V(lambda e, c=c, hf=hf, blk=blk, rows=rows, pb=pb: e.tensor_copy(out=qTm[rows, c, hf, blk], in_=pb[rows, :]),
                                      [pk], ["r_qTm"])
                            pb, pk = bank()
                            if R2 & 4:
                                proj_F(w0, w0k, 256 + c * 128, 128, tb, pb, pk)
                                V(lambda e, c=c, blk=blk, pb=pb: e.tensor_copy(out=kTr[:, c, blk], in_=pb[:, :]), [pk], ["r_kT"])
                    for tt in range(8):
                        pb, pk = bank()
                        if R2 & 8:
                            proj_T(w0, w0k, 256, 512, tt, pb, pk)
                        for d in (range(2) if R2 & 8 else []):
                            for h in range(4):
                                hf = h % 2
                                V(lambda e, d=d, h=h, hf=hf, pb=pb: e.tensor_tensor(
                                    out=kdp[:, h, d, hf * 64:(hf + 1) * 64], in0=pb[:, h * 64:(h + 1) * 64],
                                    in1=kdt[:, h, d, :], op=ALU.mult), [pk, "kdt"], ["r_kdp"])
                        pb, pk = bank()
                        if R2 & 16:
                            proj_T(w1, w1k, 0, 256, tt, pb, pk)
                            A(lambda e, tt=tt, pb=pb: e.copy(out=vtk[:, tt, :], in_=pb[:, 0:256]), [pk], [("r_v", tt)])
                        pb, pk = bank()
                        for d in (range(2) if R2 & 32 else []):
                            for pr in range(2):
                                cs_ = slice((d * 2 + pr) * 64, (d * 2 + pr + 1) * 64)
                                for hf in range(2):
                                    h = 2 * pr + hf
                                    mm(pb[:, cs_], kdp[:, h, d, :], vtk[:, tt, h * 64:(h + 1) * 64], hf == 0, hf == 1,
                                       ["r_kdp", ("r_v", tt)], [pk])
                        if R2 & 32:
                            V(lambda e, tt=tt, pb=pb: e.tensor_copy(out=U[:, tt, :], in_=pb[:, 0:256]), [pk], [("r_U", tt)])
                    if R2 & 64:
                        zproj(w1, w1k, 256, zT, "r_zT")
                    prefetch_w(l, 768, 1280)
                    seqs = [(2 * b_, 2 * b_ + 1) for b_ in range(4)] if grp == 0 else [tuple(range(8))]
                    cdv = cdt[:].rearrange("p a d -> p (a d)")
                    FW, BW = slice(0, 128), slice(128, 256)
                    for si, tiles in enumerate(seqs if R2 & 128 else []):
                        first, lastt = tiles[0], tiles[-1]
                        if grp == 0:
                            G(lambda e, first=first: e.memset(Sin[:, first, FW], 0.0), [], [("r_Sin", "f", first)])
                            G(lambda e, lastt=lastt: e.memset(Sin[:, lastt, BW], 0.0), [], [("r_Sin", "b", lastt)])
                        else:
                            DS(lambda e: e.dma_start(out=Sin[:, 0, FW].rearrange("p (a v) -> p a v", a=2),
                                                     in_=D["sr"][l, 0].rearrange("(a b) k v -> (b k) a v", b=2)), [], [("r_Sin", "f", 0)])
                            DS(lambda e: e.dma_start(out=Sin[:, 7, BW].rearrange("p (a v) -> p a v", a=2),
                                                     in_=D["sr"][l, 1].rearrange("(a b) k v -> (b k) a v", b=2)), [], [("r_Sin", "b", 7)])
                        for tt in tiles:
                            dst = Sin[:, tt + 1, FW] if tt != lastt else Sfin[:, FW]
                            dk = ("r_Sin", "f", tt + 1) if tt != lastt else ("r_Sfin", "f")
                            V(lambda e, tt=tt: e.tensor_tensor(out=tmpS[:, FW], in0=Sin[:, tt, FW], in1=cdv[:, FW], op=ALU.mult),
                              [("r_Sin", "f", tt), "cdt"], [("r_tmpS", "f")])
                            V(lambda e, tt=tt, dst=dst: e.tensor_tensor(out=dst, in0=tmpS[:, FW], in1=U[:, tt, FW], op=ALU.add),
                              [("r_tmpS", "f"), ("r_U", tt)], [dk])
                        for tt in reversed(tiles):
                            dst = Sin[:, tt - 1, BW] if tt != first else Sfin[:, BW]
                            dk = ("r_Sin", "b", tt - 1) if tt != first else ("r_Sfin", "b")
                            G(lambda e, tt=tt: e.tensor_tensor(out=tmpS[:, BW], in0=Sin[:, tt, BW], in1=cdv[:, BW], op=ALU.mult),
                              [("r_Sin", "b", tt), "cdt"], [("r_tmpS", "b")])
                            G(lambda e, tt=tt, dst=dst: e.tensor_tensor(out=dst, in0=tmpS[:, BW], in1=U[:, tt, BW], op=ALU.add),
                              [("r_tmpS", "b"), ("r_U", tt)], [dk])
                        if grp == 0 and R2 & 256:
                            b_ = si
                            DS(lambda e, b_=b_: e.dma_start(out=D["nsr"][b_, l, 0].rearrange("(a b) k v -> (b k) a v", b=2),
                                                            in_=Sfin[:, FW].rearrange("p (a v) -> p a v", a=2)),
                               [("r_Sfin", "f")], [])
                            DS(lambda e, b_=b_: e.dma_start(out=D["nsr"][b_, l, 1].rearrange("(a b) k v -> (b k) a v", b=2),
                                                            in_=Sfin[:, BW].rearrange("p (a v) -> p a v", a=2)),
                               [("r_Sfin", "b")], [])
                    _m = int(os.environ.get("L3", "31"))
                    for tt in range(int(os.environ.get("L3N", "8")) if _CL >= 3 else 0):
                        ts_ = slice(tt * 128, (tt + 1) * 128)
                        pb, pk = bank()
                        for h in (range(4) if _m & 1 else []):
                            mm(pb[:, h * 128:(h + 1) * 128], kTr[:, h // 2, ts_], qTm[:, h // 2, h % 2, ts_], True, True,
                               ["r_kT", "r_qTm"], [pk])
                        if _m & 2:
                            V(lambda e, pb=pb: e.tensor_tensor(out=qkm[:].rearrange("p a b -> p (a b)"), in0=pb[:, :],
                                                               in1=dct[:].rearrange("p a b -> p (a b)"), op=ALU.mult),
                              [pk, "dct"], ["r_qkm"])
                        for d in (range(2) if _m & 4 else []):
                            G(lambda e, d=d, ts_=ts_: e.tensor_tensor(
                                out=qdm[:, :, d, :], in0=qTm[:, :, :, ts_].rearrange("p c f t -> p (c f) t"),
                                in1=rqt[:, d, :, :], op=ALU.mult), ["r_qTm", "rqt"], ["r_qdm"])
                        if _m & 8:
                            A(lambda e, tt=tt: e.copy(out=Sinb[:], in_=Sin[:, tt, :]), [("r_Sin", "f", tt), ("r_Sin", "b", tt)], ["r_Sinb"])
                        pb, pk = bank()
                        for h in (range(4) if _m & 16 else []):
                            pr = h // 2
                            hc = slice(h * 64, (h + 1) * 64)
                            mm(pb[:, hc], qkm[:, h, :], vtk[:, tt, hc], True, False, ["r_qkm", ("r_v", tt)], [pk])
                            mm(pb[:, hc], qdm[:, h, 0, :], Sinb[:, pr * 64:(pr + 1) * 64], False, False, ["r_qdm", "r_Sinb"], [pk])
                            mm(pb[:, hc], qdm[:, h, 1, :], Sinb[:, (2 + pr) * 64:(3 + pr) * 64], False, True, ["r_qdm", "r_Sinb"], [pk])
                        if _m & 16:
                            A(lambda e, tt=tt, pb=pb: e.copy(out=oacc[:, tt, :], in_=pb[:, 0:256]), [pk], [("r_oacc", tt)])
                    if _CL >= 4:
                        norm_gate_out(ph, "r", oacc, "r_oacc", grn, "grn", l, zT, "r_zT", 2)
                S.barrier()
                stage("C")
                if grp == groups[0] and l == layers[0]:
                    tap("brC", brT[:, 4:6, :].rearrange("p a b -> p (a b)"), [("brT", 2)])

                S.mute = "B" in _SKIP
                with ExitStack() as ph:
                  if _BL >= 1:
                      gx = [sb("g_x%d" % i, [128, 1024], F32, ph) for i in range(1)]
                      gy = [sb("g_y%d" % i, [128, 1024], F32, ph) for i in range(1)]
                      qkT = sb("g_qkT", [128, 4, 1024], F32, ph)
                      ktok = sb("g_ktok", [128, 8, 256], F32, ph)
                      vtok = sb("g_vtok", [128, 8, 256], F32, ph)
                      zT = sb("g_zT", [128, 2, 1024], BF16, ph)
                      ab = sb("g_ab", [128, 8, 16], F32, ph)
                      beta = sb("g_beta", [128, 8, 8], F32, ph)
                      nbeta = sb("g_nbeta", [128, 8, 8], F32, ph)
                      la = sb("g_la", [128, 8, 8], F32, ph)
                      gc = sb("g_gc", [128, 8, 8], F32, ph)
                      ngc = sb("g_ngc", [128, 8, 8], F32, ph)
                      egc = sb("g_egc", [128, 8, 8], F32, ph)
                      bege = sb("g_bege", [128, 8, 8], F32, ph)
                      oacc = sb("g_oacc", [128, 8, 256], F32, ph)
                      dg = sb("g_dg", [128, 4, 128], F32, ph)
                      Dm = sb("g_Dm", [128, 4, 128], F32, ph)
                      DT = sb("g_DT", [128, 4, 128], F32, ph)
                      Bm = [sb("g_B%d" % i, [128, 4, 128], F32, ph) for i in range(2)]
                      BTm = [sb("g_BT%d" % i, [128, 4, 128], F32, ph) for i in range(2)]
                      Ym = [sb("g_Y%d" % i, [128, 4, 128], F32, ph) for i in range(2)]
                      qkd = sb("g_qkd", [128, 4, 128], F32, ph)
                      vb = sb("g_vb", [128, 4, 64], F32, ph)
                      kbgp = sb("g_kbgp", [128, 4, 128], F32, ph)
                      kdcp = sb("g_kdcp", [128, 4, 128], F32, ph)
                      kds = sb("g_kds", [128, 4], F32, ph)
                      eg2 = sb("g_eg2", [128, 2], F32, ph)
                      wTm = sb("g_wTm", [128, 4, 128], F32, ph)

                      Sst = sb("g_S", [128, 2, 2, 64], F32, ph)
                      G(lambda e: e.memset(kbgp[:], 0.0), [], ["g_kbgp"])
                      G(lambda e: e.memset(kdcp[:], 0.0), [], ["g_kdcp"])
                      G(lambda e: e.memset(wTm[:], 0.0), [], ["g_wTm"])
                      wA, wAk = load_w(l, 768, 1280)
                      wB, wBk = load_w(l, 1280, 1552)
                      wC, wCk = load_w(l, 1552, 1808)
                      nseq = 4 if grp == 0 else 1
                      L = 1024 // nseq
                      for c6 in range(6):
                          xt, xk = gx[0], "g_x0"
                          yt, yk = gy[0], "g_y0"
                          wt, wk, a = (wA, wAk, c6 * 128) if c6 < 4 else (wB, wBk, (c6 - 4) * 128)
                          for tb in range(2):
                              pb, pk = bank()
                              proj_F(wt, wk, a, 128, tb, pb, pk)
                              A(lambda e, tb=tb, pb=pb, xt=xt: e.copy(out=xt[:, tb * 512:(tb + 1) * 512], in_=pb[:, :]), [pk], [xk])
                          x3 = xt[:].rearrange("p (s t) -> p s t", s=nseq)
                          y3 = yt[:].rearrange("p (s t) -> p s t", s=nseq)
                          V(lambda e, c6=c6, xt=xt, yt=yt: e.tensor_scalar(out=yt[:], in0=xt[:], scalar1=cwT[:, l, c6, 2:3],
                                                                           scalar2=None, op0=ALU.mult), [xk, "cwT"], [yk])
                          for jj in (0, 1, 3, 4):
                              dsh = jj - 2
                              lo, hi = max(0, -dsh), L - max(0, dsh)
                              V(lambda e, c6=c6, jj=jj, x3=x3, y3=y3, lo=lo, hi=hi, dsh=dsh: e.scalar_tensor_tensor(
                                  out=y3[:, :, lo:hi], in0=x3[:, :, lo + dsh:hi + dsh], scalar=cwT[:, l, c6, jj:jj + 1],
                                  in1=y3[:, :, lo:hi], op0=ALU.mult, op1=ALU.add), [xk, yk, "cwT"], [yk])
                          act(yt[:], yt[:], AF.Silu, [yk], [yk])
                          if c6 < 4:
                              for tb in range(2):
                                  blk = slice(tb * 512, (tb + 1) * 512)
                                  sqv = xt[:, 0:512]
                                  rsv = xt[:, 512:1024]
                                  act(sqv, yt[:, blk], AF.Square, [yk], [xk])
                                  pb, pk = bank()
                                  mm(pb[:, :], onesblk[:, :], sqv, True, True, ["onesblk", xk], [pk])
                                  act(rsv, pb[:, :], AF.Sqrt, [pk], [xk], bias=EPS, scale=1.0)
                                  V(lambda e, rsv=rsv: e.reciprocal(out=rsv, in_=rsv), [xk], [xk])
                                  if c6 < 2:
                                      V(lambda e, c6=c6, blk=blk, yt=yt, rsv=rsv: e.scalar_tensor_tensor(
                                          out=qkT[:, c6, blk], in0=yt[:, blk], scalar=0.125, in1=rsv, op0=ALU.mult,
                                          op1=ALU.mult), [yk, xk], [("g_qkT", c6)])
                                  else:
                                      V(lambda e, c6=c6, blk=blk, yt=yt, rsv=rsv: e.tensor_tensor(out=qkT[:, c6, blk], in0=yt[:, blk], in1=rsv,
                                                                                        op=ALU.mult), [yk, xk], [("g_qkT", c6)])
                          if c6 >= 2:
                              srcT = qkT[:, c6, :] if c6 < 4 else yt[:]
                              srck = ("g_qkT", c6) if c6 < 4 else yk
                              dst = ktok if c6 < 4 else vtok
                              dstk = "g_ktok" if c6 < 4 else "g_vtok"
                              cc = c6 % 2
                              for half in range(2):
                                  pb, pk = bank()
                                  for q in range(4):
                                      tt = half * 4 + q
                                      tr(pb[:, q * 128:(q + 1) * 128], srcT[:, tt * 128:(tt + 1) * 128], ident[:], [srck, "ident"], [pk])
                                  A(lambda e, half=half, pb=pb, dst=dst, cc=cc: e.copy(
                                      out=dst[:, half * 4:half * 4 + 4, cc * 128:(cc + 1) * 128],
                                      in_=pb[:, :].rearrange("p (a b) -> p a b", a=4)), [pk], [dstk])
                      zproj(wC, wCk, 0, zT, "g_zT")
                      if _GB >= 2:
                        pb, pk = bank()
                        for tt in range(8):
                            for k in range(8):
                                mm(pb[:, tt * 16:(tt + 1) * 16], hnT[:, k, tt * 128:(tt + 1) * 128], wB[:, k, 256:272], k == 0, k == 7,
                                   [wBk, "hnT"], [pk])
                        V(lambda e, pb=pb: e.tensor_copy(out=ab[:].rearrange("p a b -> p (a b)"), in_=pb[:, 0:128]), [pk], ["g_ab"])
                        act(beta[:], ab[:, :, 0:8], AF.Sigmoid, ["g_ab"], ["g_beta"])
                        V(lambda e: e.tensor_scalar(out=nbeta[:], in0=beta[:], scalar1=-1.0, scalar2=None, op0=ALU.mult),
                          ["g_beta"], ["g_nbeta"])
                        V(lambda e: e.tensor_tensor(out=la[:], in0=ab[:, :, 8:16], in1=dtb[:, l, :].unsqueeze(1).to_broadcast([128, 8, 8]),
                                                    op=ALU.add), ["g_ab", "dtb"], ["g_la"])
                        V(lambda e: e.tensor_scalar(out=la[:], in0=la[:], scalar1=30.0, scalar2=None, op0=ALU.min), ["g_la"], ["g_la"])
                        act(la[:], la[:], AF.Exp, ["g_la"], ["g_la"])
                        act(la[:], la[:], AF.Ln, ["g_la"], ["g_la"], bias=1.0, scale=1.0)
                        V(lambda e: e.tensor_tensor(out=la[:], in0=la[:], in1=nega[:, l, :].unsqueeze(1).to_broadcast([128, 8, 8]),
                                                    op=ALU.mult), ["g_la", "nega"], ["g_la"])
                        pb, pk = bank()
                        for tt in range(8):
                            for d in range(2):
                                mm(pb[:, tt * 8 + d * 4:tt * 8 + d * 4 + 4], masks[:, d, :], la[:, tt, d * 4:(d + 1) * 4], True, True,
                                   ["masks", "g_la"], [pk])
                        V(lambda e, pb=pb: e.tensor_copy(out=gc[:].rearrange("p a b -> p (a b)"), in_=pb[:, 0:64]), [pk], ["g_gc"])
                        V(lambda e: e.tensor_scalar(out=ngc[:], in0=gc[:], scalar1=-1.0, scalar2=None, op0=ALU.mult), ["g_gc"], ["g_ngc"])
                        act(egc[:], gc[:], AF.Exp, ["g_gc"], ["g_egc"])
                        V(lambda e: e.tensor_tensor(out=bege[:], in0=beta[:], in1=egc[:], op=ALU.mult), ["g_beta", "g_egc"], ["g_bege"])
                        tap("g_gc", gc[:].rearrange("p a b -> p (a b)"), ["g_gc"])
                        tap("g_qkT", qkT[:].rearrange("p a b -> p (a b)"), ["g_qkT"])
                      if _GB >= 3:
                        S.barrier()
                        wv = [w_[:].rearrange("p a b -> p (a b)").bitcast(F32) for w_ in wbs]

                        def v4(ap):
                            return ap.rearrange("p (h f) -> p h f", h=4)
                        sets = []
                        sets.append(dict(
                            dg=dg[:], Dm=Dm[:], DT=DT[:], B=[Bm[0][:], Bm[1][:]], BT=[BTm[0][:], BTm[1][:]], Y=[Ym[0][:], Ym[1][:]],
                            qkd=qkd[:], vb=vb[:], kbgp=kbgp[:], kdcp=kdcp[:], kds=kds[:], eg2=eg2[:], wTm=wTm[:],
                            qkz=gx[0][:].rearrange("p (c f t) -> p c f t", c=4, f=2),
                            usb=gy[0][:, 0:256], vnew=gy[0][:, 256:512], o2s=gy[0][:, 512:768], otmp=gy[0][:, 768:1024]))
                        kds1 = sb("g_kds1", [128, 4], F32, ph)
                        eg21 = sb("g_eg21", [128, 2], F32, ph)
                        vb1 = sb("g_vb1", [128, 4, 64], F32, ph)
                        DT1 = sb("g_DT1", [128, 4, 128], F32, ph)
                        sets.append(dict(
                            dg=v4(stg[1][:, 0:512]), Dm=v4(stg[1][:, 512:1024]), DT=DT1[:],
                            B=[v4(wv[0][:, 0:512]), v4(wv[0][:, 512:1024])], BT=[v4(wv[0][:, 1024:1536]), v4(wv[0][:, 1536:2048])],
                            Y=[v4(wv[1][:, 0:512]), v4(wv[1][:, 512:1024])],
                            qkz=wv[1][:, 1024:2048].rearrange("p (c f t) -> p c f t", c=4, f=2),
                            qkd=v4(wv[2][:, 0:512]), kbgp=v4(wv[2][:, 512:1024]), kdcp=v4(wv[2][:, 1024:1536]), wTm=v4(wv[2][:, 1536:2048]),
                            vb=vb1[:], kds=kds1[:], eg2=eg21[:],
                            usb=stg[0][:, 0:256], vnew=stg[0][:, 256:512], o2s=stg[0][:, 512:768], otmp=stg[0][:, 768:1024]))
                        G(lambda e: e.memset(gx[0][:], 0.0), [], ["q0_qkz"])
                        G(lambda e: e.memset(wv[1][:, 1024:2048], 0.0), [], ["q1_qkz"])
                        G(lambda e: e.memset(wv[2][:, 512:2048], 0.0), [], ["q1_kbgp", "q1_kdcp", "q1_wTm"])
                        G(lambda e: e.memset(oacc[:], 0.0), [], ["g_oacc"])
                        seqs = [(2 * b_, 2 * b_ + 1) for b_ in range(4)] if grp == 0 else [tuple(range(8))]

                        def quad(si_, d, tt, init, fin, sidx):
                            T_ = sets[si_]
                            kp = "q%d_" % si_
                            if si_ == 0:
                                kq = dict(kbgp="g_kbgp", kdcp="g_kdcp", wTm="g_wTm")
                            else:
                                kq = dict(kbgp="q1_kbgp", kdcp="q1_kdcp", wTm="q1_wTm")
                            kq = {**{n: kp + n for n in ("dg", "Dm", "DT", "B0", "B1", "BT0", "BT1", "Y0", "Y1", "qkd", "vb", "kds", "eg2",
                                                         "qkz", "usb", "vnew", "o2s", "otmp")}, **kq}
                            qrr = [0]

                            def qbank():
                                i = 4 * si_ + qrr[0] % 4
                                qrr[0] += 1
                                return PS[i], ("ps", i)
                            last = 127 if d == 0 else 0
                            ts_ = slice(tt * 128, (tt + 1) * 128)
                            u0 = d * 4
                            dgt, Dmt, DTt, Bt, BTt, Yt = T_["dg"], T_["Dm"], T_["DT"], T_["B"], T_["BT"], T_["Y"]
                            qkdt, vbt, kbgpt, kdcpt, kdst, eg2t, wTmt, qkzt = (T_["qkd"], T_["vb"], T_["kbgp"], T_["kdcp"], T_["kds"],
                                                                               T_["eg2"], T_["wTm"], T_["qkz"])
                            usbt, vnewt, o2st, otmpt = T_["usb"], T_["vnew"], T_["o2s"], T_["otmp"]
                            if init:
                                if grp == 0:
                                    V(lambda e: e.memset(Sst[:, d, :, :], 0.0), [], [("g_S", d)])
                                else:
                                    DS(lambda e: e.dma_start(out=Sst[:, d, :, :],
                                                             in_=D["sg"][l, d].rearrange("(a b) k v -> (b k) a v", b=2)), [], [("g_S", d)])
                            for h in range(4):
                                A(lambda e, h=h: e.mul(out=dgt[:, h, :], in_=ident[:], mul=gc[:, tt, u0 + h:u0 + h + 1]),
                                  ["ident", "g_gc"], [kq["dg"]])
                            psN, kN = qbank()
                            psP, kP = qbank()
                            for h in range(4):
                                hs = slice(h * 128, (h + 1) * 128)
                                mm(psN[:, hs], ones[:, :], dgt[:, h, :], True, False, ["ones", kq["dg"]], [kN])
                                mm(psN[:, hs], ident[:, :], masks[:, 4 + d, :], False, True, ["ident", "masks"], [kN])
                                mm(psP[:, hs], ones[:, :], dgt[:, h, :], True, False, ["ones", kq["dg"]], [kP])
                                mm(psP[:, hs], ident[:, :], masks[:, 2 + d, :], False, True, ["ident", "masks"], [kP])
                            yield
                            for h in range(4):
                                hs = slice(h * 128, (h + 1) * 128)
                                act(Dmt[:, h, :], psP[:, hs], AF.Exp, [kP, "g_gc"], [kq["Dm"]], bias=gc[:, tt, u0 + h:u0 + h + 1], scale=-1.0)
                                act(DTt[:, h, :], psN[:, hs], AF.Exp, [kN, "g_ngc"], [kq["DT"]], bias=ngc[:, tt, u0 + h:u0 + h + 1], scale=1.0)
                                act(kdst[:, h:h + 1], psN[:, h * 128 + last:h * 128 + last + 1], AF.Exp, [kN, "g_ngc"], [kq["kds"]],
                                    bias=ngc[:, tt, u0 + h:u0 + h + 1], scale=1.0)
                            for pr in range(2):
                                for hf in range(2):
                                    h = 2 * pr + hf
                                    A(lambda e, pr=pr, hf=hf, h=h: e.activation(
                                        out=eg2t[hf * 64:(hf + 1) * 64, pr:pr + 1],
                                        in_=psN[hf * 64:(hf + 1) * 64, h * 128 + last:h * 128 + last + 1], func=AF.Exp), [kN], [kq["eg2"]])
                            for hf in range(2):
                                rows = slice(hf * 64, (hf + 1) * 64)
                                V(lambda e, hf=hf, rows=rows: e.tensor_copy(out=qkzt[rows, :, hf, :], in_=qkT[rows, :, ts_]),
                                  ["g_qkT"], [kq["qkz"]])
                            psK, kK = qbank()
                            psQ, kQ = qbank()
                            for h in range(4):
                                hs = slice(h * 128, (h + 1) * 128)
                                kfull = qkT[:, 2 + h // 2, ts_]
                                mm(psK[:, hs], kfull, qkzt[:, 2 + h // 2, h % 2, :], True, True, ["g_qkT", kq["qkz"]], [kK])
                                mm(psQ[:, hs], kfull, qkzt[:, h // 2, h % 2, :], True, True, ["g_qkT", kq["qkz"]], [kQ])
                            yield
                            for h in range(4):
                                hs = slice(h * 128, (h + 1) * 128)
                                V(lambda e, h=h, hs=hs: e.scalar_tensor_tensor(
                                    out=Bt[0][:, h, :], in0=psK[:, hs], scalar=nbeta[:, tt, u0 + h:u0 + h + 1], in1=Dmt[:, h, :],
                                    op0=ALU.mult, op1=ALU.mult), [kK, "g_nbeta", kq["Dm"]], [kq["B0"]])
                            V(lambda e: e.tensor_tensor(out=qkdt.rearrange("p a b -> p (a b)"), in0=psQ[:, :],
                                                        in1=DTt.rearrange("p a b -> p (a b)"), op=ALU.mult), [kQ, kq["DT"]], [kq["qkd"]])
                            yield
                            pb, pk = qbank()
                            for h in range(4):
                                tr(pb[:, h * 128:(h + 1) * 128], Bt[0][:, h, :], ident[:], [kq["B0"], "ident"], [pk])
                            yield
                            V(lambda e, pb=pb: e.tensor_copy(out=BTt[0].rearrange("p a b -> p (a b)"), in_=pb[:, :]), [pk], [kq["BT0"]])
                            for h in range(4):
                                V(lambda e, pb=pb, h=h: e.tensor_tensor(out=Yt[0][:, h, :], in0=pb[:, h * 128:(h + 1) * 128], in1=ident[:],
                                                                        op=ALU.add), [pk, "ident"], [kq["Y0"]])
                            yield
                            cur = 0
                            for lev in range(1, 7):
                                nxt = 1 - cur
                                pb, pk = qbank()
                                for h in range(4):
                                    mm(pb[:, h * 128:(h + 1) * 128], BTt[cur][:, h, :], Bt[cur][:, h, :], True, True,
                                       [kq["BT%d" % cur], kq["B%d" % cur]], [pk])
                                if lev < 6:
                                    pb2, pk2 = qbank()
                                    for h in range(4):
                                        mm(pb2[:, h * 128:(h + 1) * 128], Bt[cur][:, h, :], BTt[cur][:, h, :], True, True,
                                           [kq["BT%d" % cur], kq["B%d" % cur]], [pk2])
                                yield
                                A(lambda e, pb=pb, nxt=nxt: e.copy(out=Bt[nxt].rearrange("p a b -> p (a b)"), in_=pb[:, :]),
                                  [pk], [kq["B%d" % nxt]])
                                if lev < 6:
                                    V(lambda e, pb2=pb2, nxt=nxt: e.tensor_copy(out=BTt[nxt].rearrange("p a b -> p (a b)"), in_=pb2[:, :]),
                                      [pk2], [kq["BT%d" % nxt]])
                                pb3, pk3 = qbank()
                                for h in range(4):
                                    mm(pb3[:, h * 128:(h + 1) * 128], Bt[nxt][:, h, :], Yt[cur][:, h, :], True, True,
                                       [kq["B%d" % nxt], kq["Y%d" % cur]], [pk3])
                                yield
                                V(lambda e, pb3=pb3, nxt=nxt, cur=cur: e.tensor_tensor(
                                    out=Yt[nxt].rearrange("p a b -> p (a b)"), in0=pb3[:, :],
                                    in1=Yt[cur].rearrange("p a b -> p (a b)"), op=ALU.add), [pk3, kq["Y%d" % cur]], [kq["Y%d" % nxt]])
                                cur = nxt
                                yield
                            Yf, Yk = Yt[cur], kq["Y%d" % cur]
                            k4 = ktok[:, tt, :].rearrange("p (h d) -> p h d", h=4)
                            G(lambda e: e.tensor_tensor(out=vbt, in0=vtok[:, tt, :].rearrange("p (h d) -> p h d", h=4),
                                                        in1=beta[:, tt, u0:u0 + 4].unsqueeze(2).to_broadcast([128, 4, 64]), op=ALU.mult),
                              ["g_vtok", "g_beta"], [kq["vb"]])
                            for hf in range(2):
                                pc_ = slice(hf * 64, hf * 64 + 64)
                                G(lambda e, hf=hf, pc_=pc_: e.tensor_tensor(
                                    out=kbgpt[:, hf::2, pc_], in0=k4[:, hf::2, :],
                                    in1=bege[:, tt, u0 + hf:u0 + 4:2].unsqueeze(2).to_broadcast([128, 2, 64]), op=ALU.mult),
                                  ["g_ktok", "g_bege"], [kq["kbgp"]])
                                V(lambda e, hf=hf, pc_=pc_: e.tensor_tensor(
                                    out=kdcpt[:, hf::2, pc_], in0=k4[:, hf::2, :],
                                    in1=kdst[:, hf::2].unsqueeze(2).to_broadcast([128, 2, 64]), op=ALU.mult),
                                  ["g_ktok", kq["kds"]], [kq["kdcp"]])
                            pbu, pku = qbank()
                            for h in range(4):
                                mm(pbu[:, h * 64:(h + 1) * 64], Yf[:, h, :], vbt[:, h, :], True, True, [Yk, kq["vb"]], [pku])
                            pbw, pkw = qbank()
                            for pr in range(2):
                                for hf in range(2):
                                    h = 2 * pr + hf
                                    mm(pbw[:, pr * 128:(pr + 1) * 128], kbgpt[:, h, :], Yf[:, h, :], hf == 0, hf == 1, [kq["kbgp"], Yk], [pkw])
                            yield
                            A(lambda e: e.copy(out=usbt, in_=pbu[:, 0:256]), [pku], [kq["usb"]])
                            for pr in range(2):
                                for hf in range(2):
                                    rows = slice(hf * 64, (hf + 1) * 64)
                                    V(lambda e, pr=pr, hf=hf, rows=rows: e.tensor_copy(out=wTmt[rows, 2 * pr + hf, :],
                                                                                       in_=pbw[rows, pr * 128:(pr + 1) * 128]), [pkw], [kq["wTm"]])
                            yield
                            pbv, pkv = qbank()
                            pbo, pko = qbank()
                            for h in range(4):
                                hc = slice(h * 64, (h + 1) * 64)
                                mm(pbv[:, hc], wTmt[:, h, :], Sst[:, d, h // 2, :], True, True, [kq["wTm"], ("g_S", d)], [pkv])
                                mm(pbo[:, hc], qkzt[:, h // 2, h % 2, :], Sst[:, d, h // 2, :], True, True, [kq["qkz"], ("g_S", d)], [pko])
                            yield
                            V(lambda e: e.tensor_tensor(out=vnewt, in0=usbt, in1=pbv[:, 0:256], op=ALU.subtract), [kq["usb"], pkv], [kq["vnew"]])
                            pb2, pk2 = qbank()
                            for h in range(4):
                                hc = slice(h * 64, (h + 1) * 64)
                                mm(pb2[:, hc], qkdt[:, h, :], vnewt[:, hc], True, True, [kq["qkd"], kq["vnew"]], [pk2])
                            pbs, pks = qbank()
                            for pr in range(2):
                                for hf in range(2):
                                    h = 2 * pr + hf
                                    mm(pbs[:, pr * 64:(pr + 1) * 64], kdcpt[:, h, :], vnewt[:, h * 64:(h + 1) * 64], hf == 0, hf == 1,
                                       [kq["kdcp"], kq["vnew"]], [pks])
                            yield
                            A(lambda e: e.copy(out=o2st, in_=pb2[:, 0:256]), [pk2], [kq["o2s"]])
                            for h in range(4):
                                hc = slice(h * 64, (h + 1) * 64)
                                V(lambda e, h=h, hc=hc: e.scalar_tensor_tensor(
                                    out=otmpt[:, hc], in0=pbo[:, hc], scalar=egc[:, tt, u0 + h:u0 + h + 1], in1=o2st[:, hc],
                                    op0=ALU.mult, op1=ALU.add), [pko, "g_egc", kq["o2s"]], [kq["otmp"]])
                            G(lambda e: e.tensor_tensor(out=oacc[:, tt, :], in0=oacc[:, tt, :], in1=otmpt, op=ALU.add),
                              [("g_oacc", tt), kq["otmp"]], [("g_oacc", tt)])
                            for pr in range(2):
                                V(lambda e, pr=pr: e.scalar_tensor_tensor(
                                    out=Sst[:, d, pr, :], in0=Sst[:, d, pr, :], scalar=eg2t[:, pr:pr + 1],
                                    in1=pbs[:, pr * 64:(pr + 1) * 64], op0=ALU.mult, op1=ALU.add),
                                  [("g_S", d), kq["eg2"], pks], [("g_S", d)])
                            if fin and grp == 0:
                                DS(lambda e: e.dma_start(out=D["nsg"][sidx, l, d].rearrange("(a b) k v -> (b k) a v", b=2),
                                                         in_=Sst[:, d, :, :]), [("g_S", d)], [])
                            yield

                        sched = [[], []]
                        for d in range(2):
                            for sidx, tiles in enumerate(seqs):
                                order = tiles if d == 0 else tuple(reversed(tiles))
                                for qi, tt in enumerate(order):
                                    sched[d].append((tt, qi == 0, qi == len(order) - 1, sidx))
                        for (f_, b_) in zip(sched[0][:_GBQ], sched[1][:_GBQ]):
                            gens = [quad(0, 0, *f_), quad(1, 1, *b_)]
                            alive = [True, True]
                            while any(alive):
                                for gi in range(2):
                                    if alive[gi]:
                                        try:
                                            next(gens[gi])
                                        except StopIteration:
                                            alive[gi] = False
                      S.mute = False
                      if _GB >= 4:
                        norm_gate_out(ph, "g", oacc, "g_oacc", ggn, "ggn", l, zT, "g_zT", 1)
                S.barrier()
                stage("B")
                if grp == groups[0] and l == layers[0]:
                    tap("brB", brT[:, 2:4, :].rearrange("p a b -> p (a b)"), [("brT", 1)])

                S.mute = "M" in _SKIP
                with ExitStack() as ph:
                    mT = sb("m_T", [128, 8, 1024], BF16, ph)
                    wmg = [sb("m_wg%d" % i, [128, 4, 8, 128], BF16, ph) for i in range(2)]
                    wbr = [sb("m_wb%d" % i, [128, 4, 2, 128], BF16, ph) for i in range(2)]
                    gts = [sb("m_gt%d" % i, [128, 512], BF16, ph) for i in range(2)]
                    acc = sb("m_acc", [128, 512], F32, ph)
                    tmpm = sb("m_tmp", [128, 512], F32, ph)
                    gi = 0
                    for dc in range(8):
                        wg_t, wgk = wmg[dc % 2], "m_wg%d" % (dc % 2)
                        wb_t, wbk = wbr[dc % 2], "m_wb%d" % (dc % 2)
                        for n in range(4):
                            c0 = 3856 + n * 1024 + dc * 128
                            DG(lambda e, n=n, c0=c0, wg_t=wg_t: e.dma_start(
                                out=wg_t[:, n, :, :], in_=D["w_in"][l, :, c0:c0 + 128].rearrange("(k p) c -> p k c", p=128)),
                               [], [wgk])
                            DG(lambda e, n=n, dc=dc, wb_t=wb_t: e.dma_start(
                                out=wb_t[:, n, :, :],
                                in_=D["w_branch"][l, n, :, dc * 128:(dc + 1) * 128].rearrange("(k p) c -> p k c", p=128)),
                               [], [wbk])
                        for tb in range(2):
                            blk = slice(tb * 512, (tb + 1) * 512)
                            for n in range(4):
                                pb, pk = bank()
                                for k in range(8):
                                    mm(pb[:, :], wg_t[:, n, k, :], hnT[:, k, blk], k == 0, k == 7, [wgk, "hnT"], [pk])
                                gt, gtk = gts[gi % 2], "m_gt%d" % (gi % 2)
                                gi += 1
                                act(gt[:], pb[:, :], AF.Sigmoid, [pk], [gtk])
                                pb2, pk2 = bank()
                                for kk in range(2):
                                    mm(pb2[:, :], wb_t[:, n, kk, :], brT[:, n * 2 + kk, blk], kk == 0, kk == 1, [wbk, ("brT", n)], [pk2])
                                if n == 0:
                                    V(lambda e, pb2=pb2, gt=gt: e.tensor_tensor(out=acc[:], in0=pb2[:, :], in1=gt[:], op=ALU.mult),
                                      [pk2, gtk], ["m_acc"])
                                else:
                                    V(lambda e, pb2=pb2, gt=gt: e.tensor_tensor(out=tmpm[:], in0=pb2[:, :], in1=gt[:], op=ALU.mult),
                                      [pk2, gtk], ["m_tmp"])
                                    if n < 3:
                                        V(lambda e: e.tensor_tensor(out=acc[:], in0=acc[:], in1=tmpm[:], op=ALU.add),
                                          ["m_acc", "m_tmp"], ["m_acc"])
                                    else:
                                        V(lambda e, dc=dc, blk=blk: e.tensor_tensor(out=mT[:, dc, blk], in0=acc[:], in1=tmpm[:], op=ALU.add),
                                          ["m_acc", "m_tmp"], [("m_T", dc)])
                    for half in range(2):
                        i = wcnt[0] % 3
                        wcnt[0] += 1
                        wo, wok = wbs[i], "wb%d" % i
                        DG(lambda e, half=half, wo=wo: e.dma_start(
                            out=wo[:], in_=D["w_out"][l, :, half * 512:(half + 1) * 512].rearrange("(k p) n -> p k n", p=128)),
                           [], [wok])
                        for q in range(4):
                            oc = half * 4 + q
                            for tb in range(2):
                                blk = slice(tb * 512, (tb + 1) * 512)
                                pb, pk = bank()
                                for k in range(8):
                                    mm(pb[:, :], wo[:, k, q * 128:(q + 1) * 128], mT[:, k, blk], k == 0, k == 7, [wok, ("m_T", k)], [pk])
                                V(lambda e, oc=oc, blk=blk, pb=pb: e.scalar_tensor_tensor(
                                    out=hT[:, oc, blk], in0=pb[:, :], scalar=modT[:, l, 16 + oc, j:j + 1], in1=hT[:, oc, blk],
                                    op0=ALU.mult, op1=ALU.add), [pk, "modT", "hT"], ["hT"])
                S.barrier()
                stage("merge")
                if grp == groups[0] and l == layers[0]:
                    tap("hT1", hT[:].rearrange("p a b -> p (a b)"), ["hT"])

            S.mute = False
            with ExitStack() as ph:
                ynT = sb("f_yn", [128, 8, 1024], F32, ph)
                rmsnorm_T(ph, lambda k: fnT[:, k:k + 1], lambda k: None,
                          lambda k, tb: ynT[:, k, tb * 512:(tb + 1) * 512], "f_yn")
                for tt in range(8):
                    s_t, sk = stg[scnt[0] % 2], "stg%d" % (scnt[0] % 2)
                    scnt[0] += 1
                    for half in range(2):
                        pb, pk = bank()
                        for q in range(4):
                            k = half * 4 + q
                            tr(pb[:, q * 128:(q + 1) * 128], ynT[:, k, tt * 128:(tt + 1) * 128], ident[:], ["f_yn", "ident"], [pk])
                        if half == 0:
                            A(lambda e, pb=pb, s_t=s_t: e.copy(out=s_t[:, 0:512], in_=pb[:, :]), [pk], [sk])
                        else:
                            V(lambda e, pb=pb, s_t=s_t: e.tensor_copy(out=s_t[:, 512:1024], in_=pb[:, :]), [pk], [sk])
                    DS(lambda e, tt=tt, s_t=s_t: e.dma_start(out=yout[tt * 128:(tt + 1) * 128, :], in_=s_t[:]), [sk], [])
            S.barrier()
        try:
            run_groups()
        except _Stop:
            S.barrier()
        with nc.allow_non_contiguous_dma(reason="small transposed parameter loads"):
            stats = S.emit()
    return nc, stats


_CACHE = {}


def _in_maps(inp):
    f = lambda a: np.ascontiguousarray(np.asarray(a, dtype=np.float32))
    cst = _consts(f(inp["na_bias"]))
    shared = dict(
        w_ada=f(inp["w_ada"]), b_ada=f(inp["b_ada"]), norm_g=f(inp["norm_g"]), w_in=f(inp["w_in"]), conv_w=f(inp["conv_w"]),
        a_log=f(inp["gdn_a_log"]).reshape(2, 8), dt_bias=f(inp["gdn_dt_bias"]).reshape(2, 8), gdn_norm=f(inp["gdn_norm"]),
        q_norm=f(inp["attn_q_norm"]), k_norm=f(inp["attn_k_norm"]), ret_norm=f(inp["ret_norm"]),
        w_branch=f(inp["w_branch"]), w_out=f(inp["w_out"]), final_norm=f(inp["final_norm"]).reshape(1, 1024), **cst)
    xp, xs = f(inp["x_prompt"]), f(inp["x_sample"])
    maps = []
    for c in range(8):
        m = dict(shared)
        m["xp"] = xp[4 * c:4 * c + 4].reshape(1024, 1024)
        m["xs"] = xs[c]
        m["cak"] = f(inp["cache_attn_k"][c]).reshape(2, 512, 128)
        m["cav"] = f(inp["cache_attn_v"][c]).reshape(2, 512, 128)
        m["cnk"] = f(inp["cache_na_k"][c]).reshape(2, 512, 256)
        m["cnv"] = f(inp["cache_na_v"][c]).reshape(2, 512, 256)
        m["sg"] = f(inp["state_gdn"][c])
        m["sr"] = f(inp["state_ret"][c])
        m["cond"] = np.stack([f(inp["c_ctx"]), f(inp["c"][c])])
        maps.append(m)
    return maps


def kernel(**inputs):
    if "nc" not in _CACHE:
        _CACHE["nc"] = build()[0]
    nc = _CACHE["nc"]
    maps = _in_maps(inputs)
    res = run_bass_kernel_spmd(nc, maps, core_ids=list(range(8)))
    R = res.results
    cat = lambda n: np.concatenate([np.asarray(r[n]) for r in R], axis=0)
    y_prompt = cat("yp").reshape(32, 256, 1024)
    y_sample = np.stack([np.asarray(r["ys"]) for r in R])
    nak = cat("nak").reshape(32, 2, 256, 2, 64)
    nav = cat("nav").reshape(32, 2, 256, 2, 64)
    nnk = cat("nnk").reshape(32, 2, 256, 4, 64)
    nnv = cat("nnv").reshape(32, 2, 256, 4, 64)
    nsg = cat("nsg")
    nsr = cat("nsr")
    return tuple(np.ascontiguousarray(a, dtype=np.float32) for a in (y_prompt, y_sample, nak, nav, nnk, nnv, nsg, nsr))
```

```python
import numpy as np
import concourse.bass as bass
import concourse.mybir as mybir
from concourse.bass_utils import run_bass_kernel_spmd
from contextlib import ExitStack

F32 = mybir.dt.float32
BF16 = mybir.dt.bfloat16
ALU = mybir.AluOpType
AF = mybir.ActivationFunctionType
AX = mybir.AxisListType
EPS = 1e-6
NEG = -30000.0
BIG = 1.0e5


import types
import os
_CL = int(os.environ.get('CLEVEL', '9'))
_BL = int(os.environ.get('BLEVEL', '9'))
_GB = int(os.environ.get('GB', '9'))
_GBQ = int(os.environ.get('GBQ', '99'))
_GBS = int(os.environ.get('GBS', '9'))
_SKIP = os.environ.get('SKIP', '')
_STRICT = bool(int(os.environ.get('STRICT', '0')))


def _freeze(fn, _depth=0):
    if fn is None or fn.__closure__ is None:
        return fn
    cells = []
    for c in fn.__closure__:
        try:
            v = c.cell_contents
        except ValueError:
            cells.append(c)
            continue
        if isinstance(v, types.FunctionType) and v.__closure__ is not None and _depth < 3:
            v = _freeze(v, _depth + 1)
        cells.append(types.CellType(v))
    return types.FunctionType(fn.__code__, fn.__globals__, fn.__name__, fn.__defaults__, tuple(cells))


class _Op:
    __slots__ = ("fn", "waits", "dma", "dma_n")

    def __init__(self, fn, waits, dma, dma_n):
        self.fn, self.waits, self.dma, self.dma_n = fn, waits, dma, dma_n


class Sched:
    KD = 8
    BLK = {"pe": "tensor", "act": "scalar", "dve": "vector", "pool": "gpsimd", "sp": "sync"}

    def __init__(self, nc, es):
        self.nc = nc
        self.ops = {e: [] for e in self.BLK}
        self.state = {}
        self.ndma = {e: 0 for e in self.BLK}
        self.seen_c = {e: {} for e in self.BLK}
        self.seen_d = {e: set() for e in self.BLK}
        self.last = {}
        self.csem = {e: es.enter_context(nc.semaphore("c_" + e)) for e in ("pe", "act", "dve", "pool")}
        self.dsem = {e: [es.enter_context(nc.semaphore("d_%s%d" % (e, i))) for i in range(self.KD)]
                     for e in ("sp", "pool")}

    def _split(self, key):
        if isinstance(key, tuple):
            return key[0], key[1:]
        return key, None

    def _recs(self, key):
        name, sub = self._split(key)
        d = self.state.get(name)
        if not d:
            return []
        if sub is None:
            return list(d.values())
        out = []
        if sub in d:
            out.append(d[sub])
        if None in d:
            out.append(d[None])
        return out

    def _filter(self, eng, raw, other, dma):
        waits = []
        for d in sorted(raw | other):
            if d[0] == "c":
                if d[1] == eng and not dma:
                    if eng == "pe" or (d not in raw and not _STRICT):
                        continue
                if self.seen_c[eng].get(d[1], -1) >= d[2]:
                    continue
                self.seen_c[eng][d[1]] = d[2]
                waits.append(d)
            else:
                if d in self.seen_d[eng]:
                    continue
                self.seen_d[eng].add(d)
                waits.append(d)
        best, fin = {}, []
        for w in waits:
            if w[0] == "c":
                if w[1] not in best or best[w[1]][2] < w[2]:
                    best[w[1]] = w
            else:
                fin.append(w)
        fin.extend(best.values())
        return fin

    mute = False

    def op(self, eng, fn, reads=(), writes=(), dma=False):
        if self.mute:
            return None
        fn = _freeze(fn)
        idx = len(self.ops[eng])
        raw, other = set(), set()
        for key in reads:
            for rec in self._recs(key):
                if rec[0] is not None:
                    raw.add(rec[0])
        for key in writes:
            for rec in self._recs(key):
                if rec[0] is not None:
                    other.add(rec[0])
                other.update(rec[1])
        if dma:
            n = self.ndma[eng]
            self.ndma[eng] += 1
            ev = ("d", eng, n)
            self.last[("d", eng, n % self.KD)] = ev
        else:
            n = None
            ev = ("c", eng, idx)
            self.last[("c", eng)] = ev
        fin = self._filter(eng, raw, other, dma)
        self.ops[eng].append(_Op(fn, fin, dma, n))
        for key in reads:
            name, sub = self._split(key)
            self.state.setdefault(name, {}).setdefault(sub, [None, []])[1].append(ev)
        for key in writes:
            name, sub = self._split(key)
            d = self.state.setdefault(name, {})
            if sub is None:
                d.clear()
            d[sub] = [ev, []]
        return ev

    def barrier(self):
        evs = set(self.last.values())
        for e in self.BLK:
            fin = self._filter(e, set(evs), set(), True)
            if fin:
                self.ops[e].append(_Op(None, fin, False, None))
        self.state = {}

    def emit(self):
        for e in ("sp", "pool"):
            n = self.ndma[e]
            if n:
                self.ops[e].append(_Op(None, [("d", e, i) for i in range(max(0, n - self.KD), n)], False, None))
        waited = {e: set() for e in self.BLK}
        for e, ops in self.ops.items():
            for o in ops:
                for w in o.waits:
                    if w[0] == "c":
                        waited[w[1]].add(w[2])
        val = {}
        for e in self.BLK:
            for rank, idx in enumerate(sorted(waited[e])):
                val[(e, idx)] = rank + 1
        KD = self.KD
        self.maxval = {e: len(waited[e]) for e in self.BLK}
        self.maxdma = {e: 16 * ((self.ndma[e] - 1) // KD + 1) for e in ("sp", "pool")}
        if os.environ.get("SEMDBG"):
            print("SEM max values", self.maxval, self.maxdma, flush=True)
        with self.nc.Block() as block:
            for e in self.BLK:
                def body(engine, e=e):
                    for idx, o in enumerate(self.ops[e]):
                        for w in o.waits:
                            if w[0] == "c":
                                engine.wait_ge(self.csem[w[1]], val[(w[1], w[2])])
                            else:
                                engine.wait_ge(self.dsem[w[1]][w[2] % KD], 16 * (w[2] // KD + 1))
                        if o.fn is None:
                            continue
                        if o.dma:
                            n = o.dma_n
                            if n >= KD:
                                engine.wait_ge(self.dsem[e][n % KD], 16 * (n // KD))
                            o.fn(engine).then_inc(self.dsem[e][n % KD], 16)
                        else:
                            ins = o.fn(engine)
                            if idx in waited[e]:
                                ins.then_inc(self.csem[e], 1)
                getattr(block, self.BLK[e])(body)
        return {e: len(self.ops[e]) for e in self.BLK}


def _na_pairs():
    t = np.arange(1024)
    r, c = t // 64, t % 64
    rs = np.clip(r - 4, 0, 8)
    cs = np.clip(c - 8, 0, 48)
    valid = ((r[None, :] >= rs[:, None]) & (r[None, :] < rs[:, None] + 8) &
             (c[None, :] >= cs[:, None]) & (c[None, :] < cs[:, None] + 16))
    dr = r[None, :] - r[:, None] + 7
    dc = np.clip(c[None, :] - c[:, None] + 15, 0, 30)
    pairs = []
    for qt in range(8):
        for kt in range(8):
            if valid[qt * 128:(qt + 1) * 128, kt * 128:(kt + 1) * 128].any():
                pairs.append((qt, kt))
    return valid, dr, dc, pairs


_NA = _na_pairs()
NPAIR = len(_NA[3])


def _consts(na_bias):
    c = {}
    c["c_ident"] = np.eye(128, dtype=np.float32)
    p = np.arange(128)[:, None]
    f = np.arange(128)[None, :]
    m = np.zeros((6, 128, 128), np.float32)
    m[0] = (p <= f)
    m[1] = (p >= f)
    m[2] = np.where(f < p, 0.0, BIG)
    m[3] = np.where(f > p, 0.0, BIG)
    m[4] = np.where(f >= p, 0.0, -BIG)
    m[5] = np.where(f <= p, 0.0, -BIG)
    c["c_masks"] = m
    h = np.arange(4, dtype=np.float64)
    lgf = np.log1p(-np.exp2(-(5.0 + h)))
    lgb = np.log1p(-np.exp2(-(5.5 + h)))
    j = np.arange(128, dtype=np.float64)
    dct = np.zeros((4, 128, 128), np.float64)
    for hh in range(4):
        d = j[None, :] - j[:, None]
        dct[hh] = np.where(d > 0, np.exp(np.maximum(d, 0) * lgf[hh]), 0.0) + \
            np.where(d < 0, np.exp(np.maximum(-d, 0) * lgb[hh]), 0.0) + np.where(d == 0, 2.0, 0.0)
    c["c_dct"] = (dct * 0.125).astype(np.float32)
    rqt = np.zeros((2, 4, 128), np.float64)
    kdt = np.zeros((128, 4, 2, 64), np.float64)
    cdt = np.zeros((128, 4, 64), np.float64)
    for hh in range(4):
        rqt[0, hh] = np.exp((j + 1.0) * lgf[hh]) * 0.125
        rqt[1, hh] = np.exp((128.0 - j) * lgb[hh]) * 0.125
        kdt[:, hh, 0, :] = np.exp((127.0 - j) * lgf[hh])[:, None]
        kdt[:, hh, 1, :] = np.exp(j * lgb[hh])[:, None]
        hf_, pr_ = hh % 2, hh // 2
        cdt[hf_ * 64:(hf_ + 1) * 64, 0 * 2 + pr_, :] = np.exp(128.0 * lgf[hh])
        cdt[hf_ * 64:(hf_ + 1) * 64, 1 * 2 + pr_, :] = np.exp(128.0 * lgb[hh])
    c["c_rqt"] = rqt.astype(np.float32)
    c["c_kdt"] = kdt.astype(np.float32)
    c["c_cdt"] = cdt.astype(np.float32)
    t = np.arange(1024)
    row = (t // 64).astype(np.float32)
    col = (t % 64).astype(np.float32)
    inv = (10000.0 ** (-np.arange(16, dtype=np.float32) / 16)).astype(np.float32)
    ar = row[:, None] * inv[None, :]
    ac = col[:, None] * inv[None, :]
    cc = np.concatenate([np.cos(ar), np.cos(ar), np.cos(ac), np.cos(ac)], axis=1)
    ss = np.concatenate([-np.sin(ar), np.sin(ar), -np.sin(ac), np.sin(ac)], axis=1)
    c["c_rope"] = np.stack([np.tile(cc, (1, 6)), np.tile(ss, (1, 6))]).astype(np.float32)
    valid, dr, dc, pairs = _NA
    nab = np.empty((2, NPAIR, 4, 128, 128), np.float32)
    for pi, (qt, kt) in enumerate(pairs):
        qs = slice(qt * 128, (qt + 1) * 128)
        ks = slice(kt * 128, (kt + 1) * 128)
        v = valid[qs, ks].T
        g = na_bias[:, :, dr[qs, ks].T, dc[qs, ks].T]
        nab[:, pi] = np.where(v[None, None], g, np.float32(NEG))
    c["c_nab"] = nab
    return c


IN_SHAPES = dict(
    xp=[1024, 1024], xs=[1024, 1024], cak=[2, 512, 128], cav=[2, 512, 128], cnk=[2, 512, 256], cnv=[2, 512, 256],
    sg=[2, 2, 4, 64, 64], sr=[2, 2, 4, 64, 64], cond=[2, 1024],
    w_ada=[2, 1024, 3072], b_ada=[2, 3072], norm_g=[2, 1024], w_in=[2, 1024, 7952], conv_w=[2, 5, 768],
    a_log=[2, 8], dt_bias=[2, 8], gdn_norm=[2, 64], q_norm=[2, 64], k_norm=[2, 64], ret_norm=[2, 64],
    w_branch=[2, 4, 256, 1024], w_out=[2, 1024, 1024], final_norm=[1, 1024],
    c_ident=[128, 128], c_masks=[6, 128, 128], c_dct=[4, 128, 128], c_rqt=[2, 4, 128], c_kdt=[128, 4, 2, 64],
    c_cdt=[128, 4, 64], c_rope=[2, 1024, 384], c_nab=[2, NPAIR, 4, 128, 128])
OUT_SHAPES = dict(yp=[1024, 1024], ys=[1024, 1024], nak=[4, 2, 256, 128], nav=[4, 2, 256, 128],
                  nnk=[4, 2, 256, 256], nnv=[4, 2, 256, 256], nsg=[4, 2, 2, 4, 64, 64], nsr=[4, 2, 2, 4, 64, 64])


class _Stop(Exception):
    pass


def build(taps=None, groups=(0, 1), layers=(0, 1), stop_after=None):
    taps = taps or {}
    nc = bass.Bass("TRN2", target_bir_lowering=False)
    D = {}
    for n, s in IN_SHAPES.items():
        D[n] = nc.dram_tensor(n, list(s), F32, kind="ExternalInput").ap()
    for n, s in OUT_SHAPES.items():
        D[n] = nc.dram_tensor(n, list(s), F32, kind="ExternalOutput").ap()
    for n, s in taps.items():
        D["tap_" + n] = nc.dram_tensor("tap_" + n, list(s), F32, kind="ExternalOutput").ap()

    with ExitStack() as es:
        S = Sched(nc, es)

        uid = [0]

        def sb(name, shape, dt=F32, st=es):
            uid[0] += 1
            return st.enter_context(nc.sbuf_tensor("%s_%d" % (name, uid[0]), list(shape), dt))

        PS = [es.enter_context(nc.psum_tensor("ps%d" % i, [128, 512], F32)) for i in range(8)]
        rr = [0]

        def bank():
            i = 4 + rr[0] % 4
            rr[0] += 1
            return PS[i], ("ps", i)

        def V(fn, r, w): return S.op("dve", fn, r, w)
        def A(fn, r, w): return S.op("act", fn, r, w)
        def G(fn, r, w): return S.op("pool", fn, r, w)
        def P(fn, r, w): return S.op("pe", fn, r, w)
        def DS(fn, r, w): return S.op("sp", fn, r, w, dma=True)
        def DG(fn, r, w): return S.op("pool", fn, r, w, dma=True)

        def mm(out, lhsT, rhs, start, stop, r, w):
            P(lambda e: e.matmul(out, lhsT=lhsT, rhs=rhs, start=start, stop=stop), r, w)

        def tr(out, in_, idt, r, w):
            P(lambda e: e.transpose(out, in_, idt), r, w)

        def act(out, in_, func, r, w, bias=0.0, scale=1.0):
            A(lambda e: e.activation(out=out, in_=in_, func=func, bias=bias, scale=scale), r, w)

        def tap(name, ap, reads):
            if name in taps and name not in os.environ.get("NOTAP", "").split(","):
                DG(lambda e: e.dma_start(out=D["tap_" + name], in_=ap), reads, [])

        ident = sb("ident", [128, 128])
        identb = sb("identb", [128, 128], BF16)
        ones = sb("ones", [128, 128])
        onesblk = sb("onesblk", [128, 128])
        onespad = sb("onespad", [128, 2, 128], BF16)
        masks = sb("masks", [128, 6, 128])
        dct = sb("dct", [128, 4, 128])
        rqt = sb("rqt", [128, 2, 4, 128])
        kdt = sb("kdt", [128, 4, 2, 64])
        cdt = sb("cdt", [128, 4, 64])
        ngT = sb("ngT", [128, 2, 8])
        fnT = sb("fnT", [128, 8])
        baT = sb("baT", [128, 2, 24])
        cwT = sb("cwT", [128, 2, 6, 5])
        gqk = sb("gqk", [128, 2, 6, 64])
        gkn = sb("gkn", [128, 2, 2, 64])
        ggn = sb("ggn", [128, 2, 4, 64])
        grn = sb("grn", [128, 2, 4, 64])
        alb = sb("alb", [128, 2, 8])
        dtb = sb("dtb", [128, 2, 8])
        nega = sb("nega", [128, 2, 8])
        condT = sb("condT", [128, 8, 2])
        scond = sb("scond", [128, 8, 2])
        modT = sb("modT", [128, 2, 24, 2])
        gmul = sb("gmul", [128, 2, 8, 2])

        DS(lambda e: e.dma_start(out=ident[:], in_=D["c_ident"]), [], ["ident"])
        DG(lambda e: e.dma_start(out=identb[:], in_=D["c_ident"]), [], ["identb"])
        V(lambda e: e.memset(ones[:], 1.0), [], ["ones"])
        V(lambda e: e.memset(onesblk[:], 0.0), [], ["onesblk"])
        V(lambda e: e.memset(onesblk[0:64, 0:64], 1.0), [], ["onesblk"])
        V(lambda e: e.memset(onesblk[64:128, 64:128], 1.0), [], ["onesblk"])
        V(lambda e: e.memset(onespad[:], 0.0), [], ["onespad"])
        V(lambda e: e.memset(onespad[:, 0, 0:64], 1.0), [], ["onespad"])
        V(lambda e: e.memset(onespad[:, 1, 64:128], 1.0), [], ["onespad"])
        DS(lambda e: e.dma_start(out=masks[:], in_=D["c_masks"].rearrange("m p f -> p m f")), [], ["masks"])
        DS(lambda e: e.dma_start(out=dct[:], in_=D["c_dct"].rearrange("m p f -> p m f")), [], ["dct"])
        DS(lambda e: e.dma_start(out=rqt[:].rearrange("p a b c -> p (a b c)"),
                                 in_=D["c_rqt"].rearrange("a b c -> (a b c)").partition_broadcast(128)), [], ["rqt"])
        DS(lambda e: e.dma_start(out=kdt[:], in_=D["c_kdt"]), [], ["kdt"])
        DS(lambda e: e.dma_start(out=cdt[:], in_=D["c_cdt"]), [], ["cdt"])
        DS(lambda e: e.dma_start(out=ngT[:], in_=D["norm_g"].rearrange("l (c p) -> p l c", p=128)), [], ["ngT"])
        DS(lambda e: e.dma_start(out=fnT[:], in_=D["final_norm"].rearrange("o (c p) -> p (o c)", p=128)), [], ["fnT"])
        DS(lambda e: e.dma_start(out=baT[:], in_=D["b_ada"].rearrange("l (c p) -> p l c", p=128)), [], ["baT"])
        for l in range(2):
            for c6 in range(6):
                DS(lambda e, l=l, c6=c6: e.dma_start(
                    out=cwT[:, l, c6, :], in_=D["conv_w"][l, :, c6 * 128:(c6 + 1) * 128].rearrange("j p -> p j")),
                   [], ["cwT"])
        for jj in range(2):
            DS(lambda e, jj=jj: e.dma_start(out=condT[:, :, jj], in_=D["cond"][jj].rearrange("(c p) -> p c", p=128)), [], ["condT"])
        for l in range(2):
            for hh in range(6):
                src = "q_norm" if hh < 4 else "k_norm"
                DS(lambda e, l=l, hh=hh, src=src: e.dma_start(out=gqk[:, l, hh, :], in_=D[src][l].partition_broadcast(128)),
                   [], ["gqk"])
            for hh in range(2):
                DS(lambda e, l=l, hh=hh: e.dma_start(out=gkn[:, l, hh, :], in_=D["k_norm"][l].partition_broadcast(128)),
                   [], ["gkn"])
            for hh in range(4):
                DS(lambda e, l=l, hh=hh: e.dma_start(out=ggn[:, l, hh, :], in_=D["gdn_norm"][l].partition_broadcast(128)),
                   [], ["ggn"])
                DS(lambda e, l=l, hh=hh: e.dma_start(out=grn[:, l, hh, :], in_=D["ret_norm"][l].partition_broadcast(128)),
                   [], ["grn"])
            DS(lambda e, l=l: e.dma_start(out=alb[:, l, :], in_=D["a_log"][l].partition_broadcast(128)), [], ["alb"])
            DS(lambda e, l=l: e.dma_start(out=dtb[:, l, :], in_=D["dt_bias"][l].partition_broadcast(128)), [], ["dtb"])
        for l in range(2):
            V(lambda e, l=l: e.tensor_scalar(out=gqk[:, l, 0:4, :], in0=gqk[:, l, 0:4, :], scalar1=0.125, scalar2=None,
                                             op0=ALU.mult), ["gqk"], ["gqk"])
        act(nega[:], alb[:], AF.Exp, ["alb"], ["nega"])
        V(lambda e: e.tensor_scalar(out=nega[:], in0=nega[:], scalar1=-1.0, scalar2=None, op0=ALU.mult), ["nega"], ["nega"])
        act(scond[:], condT[:], AF.Silu, ["condT"], ["scond"])

        with ExitStack() as ph:
            wa = [sb("wa%d" % i, [128, 8, 512], F32, ph) for i in range(2)]
            cnt = 0
            for l in range(2):
                pb, pk = PS[0], ("ps", 0)
                for ch in range(6):
                    w_t, wk = wa[cnt % 2], "wa%d" % (cnt % 2)
                    cnt += 1
                    DS(lambda e, l=l, ch=ch, w_t=w_t: e.dma_start(
                        out=w_t[:], in_=D["w_ada"][l, :, ch * 512:(ch + 1) * 512].rearrange("(k p) n -> p k n", p=128)),
                       [], [wk])
                    for oc in range(4):
                        col = (ch * 4 + oc) * 2
                        for k in range(8):
                            mm(pb[:, col:col + 2], w_t[:, k, oc * 128:(oc + 1) * 128], scond[:, k, :], k == 0, k == 7,
                               [wk, "scond"], [pk])
                for j in range(2):
                    V(lambda e, l=l, j=j, pb=pb: e.tensor_tensor(
                        out=modT[:, l, :, j], in0=pb[:, 0:48].rearrange("p (a b) -> p a b", b=2)[:, :, j],
                        in1=baT[:, l, :], op=ALU.add), [pk, "baT"], ["modT"])
                    V(lambda e, l=l, j=j: e.scalar_tensor_tensor(
                        out=gmul[:, l, :, j], in0=modT[:, l, 8:16, j], scalar=1.0, in1=ngT[:, l, :],
                        op0=ALU.add, op1=ALU.mult), ["modT", "ngT"], ["gmul"])
            tap("modT", modT[:].rearrange("p a b c -> p (a b c)"), ["modT"])
        S.barrier()

        hT = sb("hT", [128, 8, 1024])
        hnT = sb("hnT", [128, 8, 1024], BF16)
        brT = sb("brT", [128, 8, 1024], BF16)
        wbs = [sb("wb%d" % i, [128, 8, 512], BF16) for i in range(3)]
        stg = [sb("stg%d" % i, [128, 1024]) for i in range(2)]
        wcnt = [0]
        scnt = [0]

        pend_w = {}

        def load_w(l, c0, c1):
            if (l, c0, c1) in pend_w:
                return pend_w.pop((l, c0, c1))
            i = wcnt[0] % 3
            wcnt[0] += 1
            t, k = wbs[i], "wb%d" % i
            DG(lambda e: e.dma_start(out=t[:, :, 0:c1 - c0],
                                     in_=D["w_in"][l, :, c0:c1].rearrange("(k p) n -> p k n", p=128)), [], [k])
            return t, k

        def prefetch_w(l, c0, c1):
            if not S.mute:
                pend_w[(l, c0, c1)] = load_w(l, c0, c1)

        def proj_T(wt, wk, a, b, tt, pb, pk):
            for k in range(8):
                mm(pb[:, 0:b - a], hnT[:, k, tt * 128:(tt + 1) * 128], wt[:, k, a:b], k == 0, k == 7, [wk, "hnT"], [pk])

        def proj_F(wt, wk, a, m, tb, pb, pk):
            for k in range(8):
                mm(pb[0:m, :], wt[:, k, a:a + m], hnT[:, k, tb * 512:(tb + 1) * 512], k == 0, k == 7, [wk, "hnT"], [pk])

        def rstd_from(ss_ap, out_ap, n, r, w):
            act(out_ap, ss_ap, AF.Sqrt, r, w, bias=EPS, scale=1.0 / n)
            V(lambda e: e.reciprocal(out=out_ap, in_=out_ap), w, w)

        def rmsnorm_T(st, gcol_fn, scol_fn, out_fn, outkey):
            sq = sb("rn_sq", [128, 8, 512], F32, st)
            rs = sb("rn_rs", [128, 512], F32, st)
            tmp = sb("rn_tmp", [128, 512], F32, st)
            for tb in range(2):
                blk = slice(tb * 512, (tb + 1) * 512)
                act(sq[:], hT[:, :, blk], AF.Square, ["hT"], ["rn_sq"])
                pb, pk = bank()
                for k in range(8):
                    mm(pb[:, :], ones[:, :], sq[:, k, :], k == 0, k == 7, ["ones", "rn_sq"], [pk])
                rstd_from(pb[:, :], rs[:], 1024.0, [pk], ["rn_rs"])
                for k in range(8):
                    V(lambda e, k=k, blk=blk: e.tensor_tensor(out=tmp[:], in0=hT[:, k, blk], in1=rs[:], op=ALU.mult),
                      ["hT", "rn_rs"], ["rn_tmp"])
                    sc = scol_fn(k)
                    if sc is None:
                        V(lambda e, k=k, tb=tb: e.tensor_scalar(out=out_fn(k, tb), in0=tmp[:], scalar1=gcol_fn(k),
                                                                scalar2=None, op0=ALU.mult), ["rn_tmp", "gmul", "fnT"], [outkey])
                    else:
                        V(lambda e, k=k, tb=tb, sc=sc: e.tensor_scalar(out=out_fn(k, tb), in0=tmp[:], scalar1=gcol_fn(k),
                                                                       scalar2=sc, op0=ALU.mult, op1=ALU.add),
                          ["rn_tmp", "gmul", "modT"], [outkey])

        def norm_gate_out(st, tag, o_acc, okey, gtile, gkey, l, zT, zkey, br):
            sq = sb(tag + "_sq", [128, 256], F32, st)
            ss = sb(tag + "_ss", [128, 4], F32, st)
            on = sb(tag + "_on", [128, 256], F32, st)
            for tt in range(8):
                act(sq[:], o_acc[:, tt, :], AF.Square, [(okey, tt)], [tag + "_sq"])
                V(lambda e: e.tensor_reduce(out=ss[:], in_=sq[:].rearrange("p (h d) -> p h d", h=4), axis=AX.X, op=ALU.add),
                  [tag + "_sq"], [tag + "_ss"])
                rstd_from(ss[:], ss[:], 64.0, [tag + "_ss"], [tag + "_ss"])
                V(lambda e, tt=tt: e.tensor_tensor(out=on[:].rearrange("p (h d) -> p h d", h=4),
                                                   in0=o_acc[:, tt, :].rearrange("p (h d) -> p h d", h=4),
                                                   in1=ss[:].unsqueeze(2).to_broadcast([128, 4, 64]), op=ALU.mult),
                  [(okey, tt), tag + "_ss"], [tag + "_on"])
                G(lambda e: e.tensor_tensor(out=on[:].rearrange("p (h d) -> p h d", h=4),
                                            in0=on[:].rearrange("p (h d) -> p h d", h=4), in1=gtile[:, l, :, :], op=ALU.mult),
                  [tag + "_on", gkey], [tag + "_on"])
                pb, pk = bank()
                for c in range(2):
                    tr(pb[:, c * 128:(c + 1) * 128], on[:, c * 128:(c + 1) * 128], ident[:], [tag + "_on", "ident"], [pk])
                V(lambda e, tt=tt, pb=pb: e.tensor_tensor(out=brT[:, br * 2:br * 2 + 2, tt * 128:(tt + 1) * 128],
                                                          in0=pb[:, 0:256].rearrange("p (c t) -> p c t", c=2),
                                                          in1=zT[:, :, tt * 128:(tt + 1) * 128], op=ALU.mult),
                  [pk, zkey], [("brT", br)])

        def attention(st, tag, qT, kT, vpad, kvmap, vslot, qblocks, keys_fn, zT, zkey, br):
            pts = [sb(tag + "_p%d" % i, [128, 512], BF16, st) for i in range(3)]
            rdens = [sb(tag + "_rd%d" % i, [64, 512], F32, st) for i in range(2)]
            osb = sb(tag + "_o", [128, 512], F32, st)
            pc = 0
            it = 0
            for (q0, qn) in qblocks:
                keys = keys_fn(q0)
                nk = len(keys)
                for h in range(4):
                    pr, hf = h // 2, h % 2
                    bo = it % 4
                    it += 1
                    psO, ko = PS[bo], ("ps", bo)
                    for ci, (kidx, bias_fn) in enumerate(keys):
                        pb, pk = bank()
                        mm(pb[:, 0:qn], kT[:, kvmap(h), kidx * 128:(kidx + 1) * 128], qT[:, h, q0:q0 + qn],
                           True, bias_fn is None, [tag + "_kT", tag + "_qT"], [pk])
                        if bias_fn is not None:
                            bap, bkey = bias_fn(h)
                            mm(pb[:, 0:qn], identb[:, :], bap, False, True, ["identb", bkey], [pk])
                        pt, ptk = pts[pc % 3], tag + "_p%d" % (pc % 3)
                        pc += 1
                        act(pt[:, 0:qn], pb[:, 0:qn], AF.Exp, [pk], [ptk])
                        mm(psO[:, 0:qn], vpad[:, kidx, vslot(h), :], pt[:, 0:qn], ci == 0, ci == nk - 1, [tag + "_vp", ptk], [ko])
                    rows = slice(hf * 64, hf * 64 + 64)
                    rden, rdk = rdens[hf], tag + "_rd%d" % hf
                    okey = (tag + "_o", hf)
                    V(lambda e, psO=psO, qn=qn, rden=rden: e.reciprocal(out=rden[:, 0:qn], in_=psO[64:128, 0:qn]), [ko], [rdk])
                    V(lambda e, psO=psO, qn=qn, rden=rden, rows=rows: e.tensor_tensor(
                        out=osb[rows, 0:qn], in0=psO[0:64, 0:qn], in1=rden[:, 0:qn], op=ALU.mult), [ko, rdk], [okey])
                    V(lambda e, pr=pr, q0=q0, qn=qn, rows=rows: e.tensor_tensor(
                        out=brT[rows, br * 2 + pr, q0:q0 + qn], in0=osb[rows, 0:qn], in1=zT[rows, pr, q0:q0 + qn], op=ALU.mult),
                      [okey, zkey], [("brT", br)])

        def zproj(wt, wk, a, zT, zkey):
            for c in range(2):
                for tb in range(2):
                    pb, pk = bank()
                    proj_F(wt, wk, a + c * 128, 128, tb, pb, pk)
                    act(zT[:, c, tb * 512:(tb + 1) * 512], pb[:, :], AF.Silu, [pk], [zkey])

        def stage(name):
            if stop_after == name or (stop_after == "Dproj" and name == "D"):
                raise _Stop()

        def run_groups():
          for grp in (groups if stop_after != "p0" else ()):
            G(lambda e: e.memset(brT[:], 0.0), [], ["brT"])
            xin = D["xp"] if grp == 0 else D["xs"]
            yout = D["yp"] if grp == 0 else D["ys"]
            for tt in range(8):
                s_t, sk = stg[scnt[0] % 2], "stg%d" % (scnt[0] % 2)
                scnt[0] += 1
                DS(lambda e, tt=tt, s_t=s_t: e.dma_start(out=s_t[:], in_=xin[tt * 128:(tt + 1) * 128, :]), [], [sk])
                for half in range(2):
                    pb, pk = bank()
                    for q in range(4):
                        k = half * 4 + q
                        tr(pb[:, q * 128:(q + 1) * 128], s_t[:, k * 128:(k + 1) * 128], ident[:], [sk, "ident"], [pk])
                    eng = A if half == 0 else V
                    if half == 0:
                        A(lambda e, tt=tt, pb=pb: e.copy(out=hT[:, 0:4, tt * 128:(tt + 1) * 128],
                                                         in_=pb[:, :].rearrange("p (a b) -> p a b", a=4)), [pk], ["hT"])
                    else:
                        V(lambda e, tt=tt, pb=pb: e.tensor_copy(out=hT[:, 4:8, tt * 128:(tt + 1) * 128],
                                                                in_=pb[:, :].rearrange("p (a b) -> p a b", a=4)), [pk], ["hT"])
            for l in layers:
                j = grp
                S.mute = False
                with ExitStack() as ph:
                    rmsnorm_T(ph, lambda k: gmul[:, l, k, j:j + 1], lambda k: modT[:, l, k, j:j + 1],
                              lambda k, tb: hnT[:, k, tb * 512:(tb + 1) * 512], "hnT")
                S.barrier()
                stage("norm")
                if grp == groups[0] and l == layers[0]:
                    tap("hnT", hnT[:].rearrange("p a b -> p (a b)"), ["hnT"])

                S.mute = "A" in _SKIP
                with ExitStack() as ph:
                    nkt = 8 if grp == 0 else 12
                    qT = sb("a_qT", [64, 4, 1024], BF16, ph)
                    kT = sb("a_kT", [64, 2, 128 * nkt], BF16, ph)
                    vpad = sb("a_vp", [128, nkt, 4, 128], BF16, ph)
                    zT = sb("a_zT", [128, 2, 1024], BF16, ph)
                    sqs = [sb("a_sq%d" % i, [128, 384], F32, ph) for i in range(2)]
                    sss = [sb("a_ss%d" % i, [128, 6], F32, ph) for i in range(2)]
                    qks = [sb("a_qk%d" % i, [128, 384], F32, ph) for i in range(2)]
                    qk2s = [sb("a_qk2%d" % i, [128, 384], F32, ph) for i in range(2)]
                    kouts = [sb("a_ko%d" % i, [128, 128], F32, ph) for i in range(2)]
                    vouts = [sb("a_vo%d" % i, [128, 128], F32, ph) for i in range(2)]
                    if grp == 1:
                        rp_all = sb("a_rp", [128, 8, 2, 384], F32, ph)
                        for t8 in range(8):
                            DS(lambda e, t8=t8: e.dma_start(out=rp_all[:, t8, :, :], in_=D["c_rope"][:, t8 * 128:(t8 + 1) * 128, :]
                                                            .rearrange("a p f -> p a f")), [], [("a_rp", t8)])
                    G(lambda e: e.memset(vpad[:, :, :, 64:128], 1.0), [], ["a_vp"])
                    w0, w0k = load_w(l, 0, 512)
                    w1, w1k = load_w(l, 512, 768)
                    for tt in range(8):
                        par_ = tt % 2
                        sq, ss, qk, qk2, kout, vout = sqs[par_], sss[par_], qks[par_], qk2s[par_], kouts[par_], vouts[par_]
                        K_sq, K_ss, K_qk, K_qk2, K_ko, K_vo = ["a_%s%d" % (n_, par_) for n_ in ("sq", "ss", "qk", "qk2", "ko", "vo")]
                        K_rp = ("a_rp", tt)
                        if grp == 1:
                            rp = rp_all[:, tt, :, :]
                        pb, pk = bank()
                        proj_T(w0, w0k, 0, 512, tt, pb, pk)
                        act(sq[:], pb[:, 0:384], AF.Square, [pk], [K_sq])
                        V(lambda e: e.tensor_reduce(out=ss[:], in_=sq[:].rearrange("p (h d) -> p h d", h=6), axis=AX.X,
                                                    op=ALU.add), [K_sq], [K_ss])
                        rstd_from(ss[:], ss[:], 64.0, [K_ss], [K_ss])
                        V(lambda e, pb=pb: e.tensor_tensor(out=qk[:].rearrange("p (h d) -> p h d", h=6),
                                                           in0=pb[:, 0:384].rearrange("p (h d) -> p h d", h=6),
                                                           in1=ss[:].unsqueeze(2).to_broadcast([128, 6, 64]), op=ALU.mult),
                          [pk, K_ss], [K_qk])
                        if grp == 0:
                            b_, s0 = tt // 2, (tt % 2) * 128
                            G(lambda e: e.tensor_tensor(out=kout[:].rearrange("p (h d) -> p h d", h=2),
                                                        in0=qk[:, 256:384].rearrange("p (h d) -> p h d", h=2),
                                                        in1=gkn[:, l, :, :], op=ALU.mult), [K_qk, "gkn"], [K_ko])
                            DS(lambda e, b_=b_, s0=s0: e.dma_start(out=D["nak"][b_, l, s0:s0 + 128, :], in_=kout[:]),
                               [K_ko], [])
                            A(lambda e, pb=pb: e.copy(out=vout[:], in_=pb[:, 384:512]), [pk], [K_vo])
                            DS(lambda e, b_=b_, s0=s0: e.dma_start(out=D["nav"][b_, l, s0:s0 + 128, :], in_=vout[:]),
                               [K_vo], [])
                        G(lambda e: e.tensor_tensor(out=qk[:].rearrange("p (h d) -> p h d", h=6),
                                                    in0=qk[:].rearrange("p (h d) -> p h d", h=6), in1=gqk[:, l, :, :],
                                                    op=ALU.mult), [K_qk, "gqk"], [K_qk])
                        src, srck = qk, K_qk
                        if grp == 1:
                            V(lambda e: e.tensor_tensor(out=qk2[:], in0=qk[:], in1=rp[:, 0, :], op=ALU.mult),
                              [K_qk, K_rp], [K_qk2])
                            qv = qk[:].rearrange("p (g s d) -> p g s d", s=2, d=16)
                            sv = rp[:, 1, :].rearrange("p (g s d) -> p g s d", s=2, d=16)
                            G(lambda e, qv=qv, sv=sv: e.tensor_tensor(
                                out=sq[:].rearrange("p (g s d) -> p g s d", s=2, d=16)[:, :, 0, :], in0=qv[:, :, 1, :],
                                in1=sv[:, :, 0, :], op=ALU.mult), [K_qk, K_rp], [K_sq])
                            G(lambda e, qv=qv, sv=sv: e.tensor_tensor(
                                out=sq[:].rearrange("p (g s d) -> p g s d", s=2, d=16)[:, :, 1, :], in0=qv[:, :, 0, :],
                                in1=sv[:, :, 1, :], op=ALU.mult), [K_qk, K_rp], [K_sq])
                            V(lambda e: e.tensor_tensor(out=qk2[:], in0=qk2[:], in1=sq[:], op=ALU.add),
                              [K_qk2, K_sq], [K_qk2])
                            src, srck = qk2, K_qk2
                        pq, pqk = bank()
                        for h in range(4):
                            tr(pq[0:64, h * 128:(h + 1) * 128], src[:, h * 64:(h + 1) * 64], ident[:], [srck, "ident"], [pqk])
                        A(lambda e, tt=tt, pq=pq: e.copy(out=qT[:, :, tt * 128:(tt + 1) * 128],
                                                         in_=pq[0:64, :].rearrange("p (a b) -> p a b", a=4)), [pqk], ["a_qT"])
                        pk2, pk2k = bank()
                        for h in range(2):
                            tr(pk2[0:64, h * 128:(h + 1) * 128], src[:, 256 + h * 64:256 + (h + 1) * 64], ident[:],
                               [srck, "ident"], [pk2k])
                        V(lambda e, tt=tt, pk2=pk2: e.tensor_copy(out=kT[:, :, tt * 128:(tt + 1) * 128],
                                                                  in_=pk2[0:64, 0:256].rearrange("p (a b) -> p a b", a=2)),
                          [pk2k], ["a_kT"])
                        A(lambda e, tt=tt, pb=pb: e.copy(out=vpad[:, tt, 0:2, 0:64],
                                                         in_=pb[:, 384:512].rearrange("p (a b) -> p a b", a=2)), [pk], ["a_vp"])
                    if grp == 1:
                        for ct in range(4):
                            s_t, sk = stg[scnt[0] % 2], "stg%d" % (scnt[0] % 2)
                            scnt[0] += 1
                            DS(lambda e, ct=ct, s_t=s_t: e.dma_start(out=s_t[:, 0:128], in_=D["cak"][l, ct * 128:(ct + 1) * 128, :]),
                               [], [sk])
                            DS(lambda e, ct=ct, s_t=s_t: e.dma_start(out=s_t[:, 128:256], in_=D["cav"][l, ct * 128:(ct + 1) * 128, :]),
                               [], [sk])
                            pk2, pk2k = bank()
                            for h in range(2):
                                tr(pk2[0:64, h * 128:(h + 1) * 128], s_t[:, h * 64:(h + 1) * 64], ident[:], [sk, "ident"], [pk2k])
                            V(lambda e, ct=ct, pk2=pk2: e.tensor_copy(
                                out=kT[:, :, (8 + ct) * 128:(9 + ct) * 128],
                                in_=pk2[0:64, 0:256].rearrange("p (a b) -> p a b", a=2)), [pk2k], ["a_kT"])
                            A(lambda e, ct=ct, s_t=s_t: e.copy(out=vpad[:, 8 + ct, 0:2, 0:64],
                                                               in_=s_t[:, 128:256].rearrange("p (a b) -> p a b", a=2)), [sk], ["a_vp"])
                    zproj(w1, w1k, 0, zT, "a_zT")
                    prefetch_w(l, 2832, 3344)
                    if grp == 0:
                        qblocks = [(b_ * 256, 256) for b_ in range(4)]
                        keys_fn = lambda q0: [((q0 // 128) + i, None) for i in range(2)]
                    else:
                        qblocks = [(0, 512), (512, 512)]
                        keys_fn = lambda q0: [(i, None) for i in range(12)]
                    attention(ph, "a", qT, kT, vpad, lambda h: h // 2, lambda h: h // 2, qblocks, keys_fn,
                              zT, "a_zT", 0)
                S.barrier()
                stage("A")
                if grp == groups[0] and l == layers[0]:
                    tap("brA", brT[:, 0:2, :].rearrange("p a b -> p (a b)"), [("brT", 0)])

                S.mute = "D" in _SKIP
                with ExitStack() as ph:
                    nkt = 8 if grp == 0 else 12
                    qT = sb("d_qT", [64, 4, 1024], BF16, ph)
                    kT = sb("d_kT", [64, 4, 128 * nkt], BF16, ph)
                    vpad = sb("d_vp", [128, nkt, 4, 128], BF16, ph)
                    zT = sb("d_zT", [128, 2, 1024], BF16, ph)
                    kout = sb("d_ko", [128, 256], F32, ph)
                    vout = sb("d_vo", [128, 256], F32, ph)
                    G(lambda e: e.memset(vpad[:, :, :, 64:128], 1.0), [], ["d_vp"])
                    w0, w0k = load_w(l, 2832, 3344)
                    w1, w1k = load_w(l, 3344, 3856)
                    for c in range(2):
                        for tb in range(2):
                            blk = slice(tb * 512, (tb + 1) * 512)
                            pb, pk = bank()
                            proj_F(w0, w0k, c * 128, 128, tb, pb, pk)
                            for hf in range(2):
                                V(lambda e, c=c, hf=hf, blk=blk, pb=pb: e.tensor_scalar(
                                    out=qT[:, 2 * c + hf, blk], in0=pb[hf * 64:(hf + 1) * 64, :], scalar1=0.125, scalar2=None,
                                    op0=ALU.mult), [pk], ["d_qT"])
                            pb, pk = bank()
                            proj_F(w0, w0k, 256 + c * 128, 128, tb, pb, pk)
                            for hf in range(2):
                                A(lambda e, c=c, hf=hf, blk=blk, pb=pb: e.copy(out=kT[:, 2 * c + hf, blk],
                                                                               in_=pb[hf * 64:(hf + 1) * 64, :]), [pk], ["d_kT"])
                    for tt in range(8):
                        pb, pk = bank()
                        proj_T(w1, w1k, 0, 256, tt, pb, pk)
                        A(lambda e, tt=tt, pb=pb: e.copy(out=vpad[:, tt, :, 0:64],
                                                         in_=pb[:, 0:256].rearrange("p (a b) -> p a b", a=4)), [pk], ["d_vp"])
                        if grp == 0:
                            b_, s0 = tt // 2, (tt % 2) * 128
                            A(lambda e, pb=pb: e.copy(out=vout[:], in_=pb[:, 0:256]), [pk], ["d_vo"])
                            DS(lambda e, b_=b_, s0=s0: e.dma_start(out=D["nnv"][b_, l, s0:s0 + 128, :], in_=vout[:]),
                               ["d_vo"], [])
                            pb2, pk2k = bank()
                            proj_T(w0, w0k, 256, 512, tt, pb2, pk2k)
                            V(lambda e, pb2=pb2: e.tensor_copy(out=kout[:], in_=pb2[:, 0:256]), [pk2k], ["d_ko"])
                            DS(lambda e, b_=b_, s0=s0: e.dma_start(out=D["nnk"][b_, l, s0:s0 + 128, :], in_=kout[:]),
                               ["d_ko"], [])
                    if grp == 1:
                        for ct in range(4):
                            s_t, sk = stg[scnt[0] % 2], "stg%d" % (scnt[0] % 2)
                            scnt[0] += 1
                            DS(lambda e, ct=ct, s_t=s_t: e.dma_start(out=s_t[:, 0:256], in_=D["cnk"][l, ct * 128:(ct + 1) * 128, :]),
                               [], [sk])
                            DS(lambda e, ct=ct, s_t=s_t: e.dma_start(out=s_t[:, 256:512], in_=D["cnv"][l, ct * 128:(ct + 1) * 128, :]),
                               [], [sk])
                            pk2, pk2k = bank()
                            for h in range(4):
                                tr(pk2[0:64, h * 128:(h + 1) * 128], s_t[:, h * 64:(h + 1) * 64], ident[:], [sk, "ident"], [pk2k])
                            V(lambda e, ct=ct, pk2=pk2: e.tensor_copy(
                                out=kT[:, :, (8 + ct) * 128:(9 + ct) * 128],
                                in_=pk2[0:64, :].rearrange("p (a b) -> p a b", a=4)), [pk2k], ["d_kT"])
                            A(lambda e, ct=ct, s_t=s_t: e.copy(out=vpad[:, 8 + ct, :, 0:64],
                                                               in_=s_t[:, 256:512].rearrange("p (a b) -> p a b", a=4)), [sk], ["d_vp"])
                    zproj(w1, w1k, 256, zT, "d_zT")
                    prefetch_w(l, 1808, 2320)
                    if stop_after == "Dproj":
                        pass
                    elif grp == 0:
                        qblocks = [(b_ * 256, 256) for b_ in range(4)]
                        keys_fn = lambda q0: [((q0 // 128) + i, None) for i in range(2)]
                        attention(ph, "d", qT, kT, vpad, lambda h: h, lambda h: h, qblocks, keys_fn, zT, "d_zT", 3)
                    else:
                        nbt = [sb("d_nb%d" % i, [128, 6, 4, 128], BF16, ph) for i in range(3)]
                        pairs = _NA[3]
                        qblocks = [(qt * 128, 128) for qt in range(8)]
                        issued = set()

                        def nb_load(qt):
                            if qt in issued or qt > 7:
                                return
                            issued.add(qt)
                            pis_ = [pi for pi, (a, b) in enumerate(pairs) if a == qt]
                            nb_, nbk_ = nbt[qt % 3], "d_nb%d" % (qt % 3)
                            DG(lambda e: e.dma_start(out=nb_[:, 0:len(pis_), :, :],
                                                     in_=D["c_nab"][l, pis_[0]:pis_[0] + len(pis_)].rearrange("a h k q -> k a h q")),
                               [], [nbk_])

                        def keys_fn(q0):
                            qt = q0 // 128
                            pis = [pi for pi, (a, b) in enumerate(pairs) if a == qt]
                            nb, nbk = nbt[qt % 3], "d_nb%d" % (qt % 3)
                            nb_load(qt)
                            nb_load(qt + 1)
                            nb_load(qt + 2)
                            out = []
                            for ii, pi in enumerate(pis):
                                out.append((pairs[pi][1], (lambda h, ii=ii: (nb[:, ii, h, :], nbk))))
                            out += [(8 + i, None) for i in range(4)]
                            return out
                        attention(ph, "d", qT, kT, vpad, lambda h: h, lambda h: h, qblocks, keys_fn, zT, "d_zT", 3)
                S.barrier()
                stage("D")
                if grp == groups[0] and l == layers[0]:
                    tap("brD", brT[:, 6:8, :].rearrange("p a b -> p (a b)"), [("brT", 3)])

                S.mute = "C" in _SKIP
                with ExitStack() as ph:
                    qTm = sb("r_qTm", [128, 2, 2, 1024], BF16, ph)
                    kTr = sb("r_kT", [128, 2, 1024], BF16, ph)
                    kdp = sb("r_kdp", [128, 4, 2, 128], BF16, ph)
                    vtk = sb("r_v", [128, 8, 256], BF16, ph)
                    zT = sb("r_zT", [128, 2, 1024], BF16, ph)
                    U = sb("r_U", [128, 8, 256], F32, ph)
                    Sin = sb("r_Sin", [128, 8, 256], F32, ph)
                    Sfin = sb("r_Sfin", [128, 256], F32, ph)
                    Sinb = sb("r_Sinb", [128, 256], BF16, ph)
                    qkm = sb("r_qkm", [128, 4, 128], BF16, ph)
                    qdm = sb("r_qdm", [128, 4, 2, 128], BF16, ph)
                    oacc = sb("r_oacc", [128, 8, 256], F32, ph)
                    tmpS = sb("r_tmpS", [128, 256], F32, ph)
                    R2 = int(os.environ.get("R2", "511"))
                    if R2 & 1:
                        G(lambda e: e.memset(qTm[:], 0.0), [], ["r_qTm"])
                        G(lambda e: e.memset(kdp[:], 0.0), [], ["r_kdp"])
                    w0, w0k = load_w(l, 1808, 2320)
                    w1, w1k = load_w(l, 2320, 2832)
                    for c in range(2):
                        for tb in range(2):
                            blk = slice(tb * 512, (tb + 1) * 512)
                            pb, pk = bank()
                            if R2 & 2:
                                proj_F(w0, w0k, c * 128, 128, tb, pb, pk)
                            for hf in (range(2) if R2 & 2 else []):
                                rows = slice(hf * 64, (hf + 1) * 64)
                                if hf == 0:
                                    A(lambda e, c=c, hf=hf, blk=blk, rows=rows, pb=pb: e.copy(out=qTm[rows, c, hf, blk], in_=pb[rows, :]),
                                      [pk], ["r_qTm"])
                                else:
                                    V(lambda e, c=c, hf=hf, blk=blk, rows=rows, pb=pb: e.tensor_copy(out=qTm[rows, c, hf, blk], in_=pb[rows, :]),
                                      [pk], ["r_qTm"])
                            pb, pk = bank()
                            if R2 & 4:
                                proj_F(w0, w0k, 256 + c * 128, 128, tb, pb, pk)
                                V(lambda e, c=c, blk=blk, pb=pb: e.tensor_copy(out=kTr[:, c, blk], in_=pb[:, :]), [pk], ["r_kT"])
                    for tt in range(8):
                        pb, pk = bank()
                        if R2 & 8:
                            proj_T(w0, w0k, 256, 512, tt, pb, pk)
                        for d in (range(2) if R2 & 8 else []):
                            for h in range(4):
                                hf = h % 2
                                V(lambda e, d=d, h=h, hf=hf, pb=pb: e.tensor_tensor(
                                    out=kdp[:, h, d, hf * 64:(hf + 1) * 64], in0=pb[:, h * 64:(h + 1) * 64],
                                    in1=kdt[:, h, d, :], op=ALU.mult), [pk, "kdt"], ["r_kdp"])
                        pb, pk = bank()
                        if R2 & 16:
                            proj_T(w1, w1k, 0, 256, tt, pb, pk)
                            A(lambda e, tt=tt, pb=pb: e.copy(out=vtk[:, tt, :], in_=pb[:, 0:256]), [pk], [("r_v", tt)])
                        pb, pk = bank()
                        for d in (range(2) if R2 & 32 else []):
                            for pr in range(2):
                                cs_ = slice((d * 2 + pr) * 64, (d * 2 + pr + 1) * 64)
                                for hf in range(2):
                                    h = 2 * pr + hf
                                    mm(pb[:, cs_], kdp[:, h, d, :], vtk[:, tt, h * 64:(h + 1) * 64], hf == 0, hf == 1,
                                       ["r_kdp", ("r_v", tt)], [pk])
                        if R2 & 32:
                            V(lambda e, tt=tt, pb=pb: e.tensor_copy(out=U[:, tt, :], in_=pb[:, 0:256]), [pk], [("r_U", tt)])
                    if R2 & 64:
                        zproj(w1, w1k, 256, zT, "r_zT")
                    prefetch_w(l, 768, 1280)
                    seqs = [(2 * b_, 2 * b_ + 1) for b_ in range(4)] if grp == 0 else [tuple(range(8))]
                    cdv = cdt[:].rearrange("p a d -> p (a d)")
                    FW, BW = slice(0, 128), slice(128, 256)
                    for si, tiles in enumerate(seqs if R2 & 128 else []):
                        first, lastt = tiles[0], tiles[-1]
                        if grp == 0:
                            G(lambda e, first=first: e.memset(Sin[:, first, FW], 0.0), [], [("r_Sin", "f", first)])
                            G(lambda e, lastt=lastt: e.memset(Sin[:, lastt, BW], 0.0), [], [("r_Sin", "b", lastt)])
                        else:
                            DS(lambda e: e.dma_start(out=Sin[:, 0, FW].rearrange("p (a v) -> p a v", a=2),
                                                     in_=D["sr"][l, 0].rearrange("(a b) k v -> (b k) a v", b=2)), [], [("r_Sin", "f", 0)])
                            DS(lambda e: e.dma_start(out=Sin[:, 7, BW].rearrange("p (a v) -> p a v", a=2),
                                                     in_=D["sr"][l, 1].rearrange("(a b) k v -> (b k) a v", b=2)), [], [("r_Sin", "b", 7)])
                        for tt in tiles:
                            dst = Sin[:, tt + 1, FW] if tt != lastt else Sfin[:, FW]
                            dk = ("r_Sin", "f", tt + 1) if tt != lastt else ("r_Sfin", "f")
                            V(lambda e, tt=tt: e.tensor_tensor(out=tmpS[:, FW], in0=Sin[:, tt, FW], in1=cdv[:, FW], op=ALU.mult),
                              [("r_Sin", "f", tt), "cdt"], [("r_tmpS", "f")])
                            V(lambda e, tt=tt, dst=dst: e.tensor_tensor(out=dst, in0=tmpS[:, FW], in1=U[:, tt, FW], op=ALU.add),
                              [("r_tmpS", "f"), ("r_U", tt)], [dk])
                        for tt in reversed(tiles):
                            dst = Sin[:, tt - 1, BW] if tt != first else Sfin[:, BW]
                            dk = ("r_Sin", "b", tt - 1) if tt != first else ("r_Sfin", "b")
                            G(lambda e, tt=tt: e.tensor_tensor(out=tmpS[:, BW], in0=Sin[:, tt, BW], in1=cdv[:, BW], op=ALU.mult),
                              [("r_Sin", "b", tt), "cdt"], [("r_tmpS", "b")])
                            G(lambda e, tt=tt, dst=dst: e.tensor_tensor(out=dst, in0=tmpS[:, BW], in1=U[:, tt, BW], op=ALU.add),
                              [("r_tmpS", "b"), ("r_U", tt)], [dk])
                        if grp == 0 and R2 & 256:
                            b_ = si
                            DS(lambda e, b_=b_: e.dma_start(out=D["nsr"][b_, l, 0].rearrange("(a b) k v -> (b k) a v", b=2),
                                                            in_=Sfin[:, FW].rearrange("p (a v) -> p a v", a=2)),
                               [("r_Sfin", "f")], [])
                            DS(lambda e, b_=b_: e.dma_start(out=D["nsr"][b_, l, 1].rearrange("(a b) k v -> (b k) a v", b=2),
                                                            in_=Sfin[:, BW].rearrange("p (a v) -> p a v", a=2)),
                               [("r_Sfin", "b")], [])
                    _m = int(os.environ.get("L3", "31"))
                    for tt in range(int(os.environ.get("L3N", "8")) if _CL >= 3 else 0):
                        ts_ = slice(tt * 128, (tt + 1) * 128)
                        pb, pk = bank()
                        for h in (range(4) if _m & 1 else []):
                            mm(pb[:, h * 128:(h + 1) * 128], kTr[:, h // 2, ts_], qTm[:, h // 2, h % 2, ts_], True, True,
                               ["r_kT", "r_qTm"], [pk])
                        if _m & 2:
                            V(lambda e, pb=pb: e.tensor_tensor(out=qkm[:].rearrange("p a b -> p (a b)"), in0=pb[:, :],
                                                               in1=dct[:].rearrange("p a b -> p (a b)"), op=ALU.mult),
                              [pk, "dct"], ["r_qkm"])
                        for d in (range(2) if _m & 4 else []):
                            G(lambda e, d=d, ts_=ts_: e.tensor_tensor(
                                out=qdm[:, :, d, :], in0=qTm[:, :, :, ts_].rearrange("p c f t -> p (c f) t"),
                                in1=rqt[:, d, :, :], op=ALU.mult), ["r_qTm", "rqt"], ["r_qdm"])
                        if _m & 8:
                            A(lambda e, tt=tt: e.copy(out=Sinb[:], in_=Sin[:, tt, :]), [("r_Sin", "f", tt), ("r_Sin", "b", tt)], ["r_Sinb"])
                        pb, pk = bank()
                        for h in (range(4) if _m & 16 else []):
                            pr = h // 2
                            hc = slice(h * 64, (h + 1) * 64)
                            mm(pb[:, hc], qkm[:, h, :], vtk[:, tt, hc], True, False, ["r_qkm", ("r_v", tt)], [pk])
                            mm(pb[:, hc], qdm[:, h, 0, :], Sinb[:, pr * 64:(pr + 1) * 64], False, False, ["r_qdm", "r_Sinb"], [pk])
                            mm(pb[:, hc], qdm[:, h, 1, :], Sinb[:, (2 + pr) * 64:(3 + pr) * 64], False, True, ["r_qdm", "r_Sinb"], [pk])
                        if _m & 16:
                            A(lambda e, tt=tt, pb=pb: e.copy(out=oacc[:, tt, :], in_=pb[:, 0:256]), [pk], [("r_oacc", tt)])
                    if _CL >= 4:
                        norm_gate_out(ph, "r", oacc, "r_oacc", grn, "grn", l, zT, "r_zT", 2)
                S.barrier()
                stage("C")
                if grp == groups[0] and l == layers[0]:
                    tap("brC", brT[:, 4:6, :].rearrange("p a b -> p (a b)"), [("brT", 2)])

                S.mute = "B" in _SKIP
                with ExitStack() as ph:
                  if _BL >= 1:
                      gx = [sb("g_x%d" % i, [128, 1024], F32, ph) for i in range(1)]
                      gy = [sb("g_y%d" % i, [128, 1024], F32, ph) for i in range(1)]
                      qkT = sb("g_qkT", [128, 4, 1024], F32, ph)
                      ktok = sb("g_ktok", [128, 8, 256], F32, ph)
                      vtok = sb("g_vtok", [128, 8, 256], F32, ph)
                      zT = sb("g_zT", [128, 2, 1024], BF16, ph)
                      ab = sb("g_ab", [128, 8, 16], F32, ph)
                      beta = sb("g_beta", [128, 8, 8], F32, ph)
                      nbeta = sb("g_nbeta", [128, 8, 8], F32, ph)
                      la = sb("g_la", [128, 8, 8], F32, ph)
                      gc = sb("g_gc", [128, 8, 8], F32, ph)
                      ngc = sb("g_ngc", [128, 8, 8], F32, ph)
                      egc = sb("g_egc", [128, 8, 8], F32, ph)
                      bege = sb("g_bege", [128, 8, 8], F32, ph)
                      oacc = sb("g_oacc", [128, 8, 256], F32, ph)
                      dg = sb("g_dg", [128, 4, 128], F32, ph)
                      Dm = sb("g_Dm", [128, 4, 128], F32, ph)
                      DT = sb("g_DT", [128, 4, 128], F32, ph)
                      Bm = [sb("g_B%d" % i, [128, 4, 128], F32, ph) for i in range(2)]
                      BTm = [sb("g_BT%d" % i, [128, 4, 128], F32, ph) for i in range(2)]
                      Ym = [sb("g_Y%d" % i, [128, 4, 128], F32, ph) for i in range(2)]
                      qkd = sb("g_qkd", [128, 4, 128], F32, ph)
                      vb = sb("g_vb", [128, 4, 64], F32, ph)
                      kbgp = sb("g_kbgp", [128, 4, 128], F32, ph)
                      kdcp = sb("g_kdcp", [128, 4, 128], F32, ph)
                      kds = sb("g_kds", [128, 4], F32, ph)
                      eg2 = sb("g_eg2", [128, 2], F32, ph)
                      wTm = sb("g_wTm", [128, 4, 128], F32, ph)

                      Sst = sb("g_S", [128, 2, 2, 64], F32, ph)
                      G(lambda e: e.memset(kbgp[:], 0.0), [], ["g_kbgp"])
                      G(lambda e: e.memset(kdcp[:], 0.0), [], ["g_kdcp"])
                      G(lambda e: e.memset(wTm[:], 0.0), [], ["g_wTm"])
                      wA, wAk = load_w(l, 768, 1280)
                      wB, wBk = load_w(l, 1280, 1552)
                      wC, wCk = load_w(l, 1552, 1808)
                      nseq = 4 if grp == 0 else 1
                      L = 1024 // nseq
                      for c6 in range(6):
                          xt, xk = gx[0], "g_x0"
                          yt, yk = gy[0], "g_y0"
                          wt, wk, a = (wA, wAk, c6 * 128) if c6 < 4 else (wB, wBk, (c6 - 4) * 128)
                          for tb in range(2):
                              pb, pk = bank()
                              proj_F(wt, wk, a, 128, tb, pb, pk)
                              A(lambda e, tb=tb, pb=pb, xt=xt: e.copy(out=xt[:, tb * 512:(tb + 1) * 512], in_=pb[:, :]), [pk], [xk])
                          x3 = xt[:].rearrange("p (s t) -> p s t", s=nseq)
                          y3 = yt[:].rearrange("p (s t) -> p s t", s=nseq)
                          V(lambda e, c6=c6, xt=xt, yt=yt: e.tensor_scalar(out=yt[:], in0=xt[:], scalar1=cwT[:, l, c6, 2:3],
                                                                           scalar2=None, op0=ALU.mult), [xk, "cwT"], [yk])
                          for jj in (0, 1, 3, 4):
                              dsh = jj - 2
                              lo, hi = max(0, -dsh), L - max(0, dsh)
                              V(lambda e, c6=c6, jj=jj, x3=x3, y3=y3, lo=lo, hi=hi, dsh=dsh: e.scalar_tensor_tensor(
                                  out=y3[:, :, lo:hi], in0=x3[:, :, lo + dsh:hi + dsh], scalar=cwT[:, l, c6, jj:jj + 1],
                                  in1=y3[:, :, lo:hi], op0=ALU.mult, op1=ALU.add), [xk, yk, "cwT"], [yk])
                          act(yt[:], yt[:], AF.Silu, [yk], [yk])
                          if c6 < 4:
                              for tb in range(2):
                                  blk = slice(tb * 512, (tb + 1) * 512)
                                  sqv = xt[:, 0:512]
                                  rsv = xt[:, 512:1024]
                                  act(sqv, yt[:, blk], AF.Square, [yk], [xk])
                                  pb, pk = bank()
                                  mm(pb[:, :], onesblk[:, :], sqv, True, True, ["onesblk", xk], [pk])
                                  act(rsv, pb[:, :], AF.Sqrt, [pk], [xk], bias=EPS, scale=1.0)
                                  V(lambda e, rsv=rsv: e.reciprocal(out=rsv, in_=rsv), [xk], [xk])
                                  if c6 < 2:
                                      V(lambda e, c6=c6, blk=blk, yt=yt, rsv=rsv: e.scalar_tensor_tensor(
                                          out=qkT[:, c6, blk], in0=yt[:, blk], scalar=0.125, in1=rsv, op0=ALU.mult,
                                          op1=ALU.mult), [yk, xk], [("g_qkT", c6)])
                                  else:
                                      V(lambda e, c6=c6, blk=blk, yt=yt, rsv=rsv: e.tensor_tensor(out=qkT[:, c6, blk], in0=yt[:, blk], in1=rsv,
                                                                                        op=ALU.mult), [yk, xk], [("g_qkT", c6)])
                          if c6 >= 2:
                              srcT = qkT[:, c6, :] if c6 < 4 else yt[:]
                              srck = ("g_qkT", c6) if c6 < 4 else yk
                              dst = ktok if c6 < 4 else vtok
                              dstk = "g_ktok" if c6 < 4 else "g_vtok"
                              cc = c6 % 2
                              for half in range(2):
                                  pb, pk = bank()
                                  for q in range(4):
                                      tt = half * 4 + q
                                      tr(pb[:, q * 128:(q + 1) * 128], srcT[:, tt * 128:(tt + 1) * 128], ident[:], [srck, "ident"], [pk])
                                  A(lambda e, half=half, pb=pb, dst=dst, cc=cc: e.copy(
                                      out=dst[:, half * 4:half * 4 + 4, cc * 128:(cc + 1) * 128],
                                      in_=pb[:, :].rearrange("p (a b) -> p a b", a=4)), [pk], [dstk])
                      zproj(wC, wCk, 0, zT, "g_zT")
                      if _GB >= 2:
                        pb, pk = bank()
                        for tt in range(8):
                            for k in range(8):
                                mm(pb[:, tt * 16:(tt + 1) * 16], hnT[:, k, tt * 128:(tt + 1) * 128], wB[:, k, 256:272], k == 0, k == 7,
                                   [wBk, "hnT"], [pk])
                        V(lambda e, pb=pb: e.tensor_copy(out=ab[:].rearrange("p a b -> p (a b)"), in_=pb[:, 0:128]), [pk], ["g_ab"])
                        act(beta[:], ab[:, :, 0:8], AF.Sigmoid, ["g_ab"], ["g_beta"])
                        V(lambda e: e.tensor_scalar(out=nbeta[:], in0=beta[:], scalar1=-1.0, scalar2=None, op0=ALU.mult),
                          ["g_beta"], ["g_nbeta"])
                        V(lambda e: e.tensor_tensor(out=la[:], in0=ab[:, :, 8:16], in1=dtb[:, l, :].unsqueeze(1).to_broadcast([128, 8, 8]),
                                                    op=ALU.add), ["g_ab", "dtb"], ["g_la"])
                        V(lambda e: e.tensor_scalar(out=la[:], in0=la[:], scalar1=30.0, scalar2=None, op0=ALU.min), ["g_la"], ["g_la"])
                        act(la[:], la[:], AF.Exp, ["g_la"], ["g_la"])
                        act(la[:], la[:], AF.Ln, ["g_la"], ["g_la"], bias=1.0, scale=1.0)
                        V(lambda e: e.tensor_tensor(out=la[:], in0=la[:], in1=nega[:, l, :].unsqueeze(1).to_broadcast([128, 8, 8]),
                                                    op=ALU.mult), ["g_la", "nega"], ["g_la"])
                        pb, pk = bank()
                        for tt in range(8):
                            for d in range(2):
                                mm(pb[:, tt * 8 + d * 4:tt * 8 + d * 4 + 4], masks[:, d, :], la[:, tt, d * 4:(d + 1) * 4], True, True,
                                   ["masks", "g_la"], [pk])
                        V(lambda e, pb=pb: e.tensor_copy(out=gc[:].rearrange("p a b -> p (a b)"), in_=pb[:, 0:64]), [pk], ["g_gc"])
                        V(lambda e: e.tensor_scalar(out=ngc[:], in0=gc[:], scalar1=-1.0, scalar2=None, op0=ALU.mult), ["g_gc"], ["g_ngc"])
                        act(egc[:], gc[:], AF.Exp, ["g_gc"], ["g_egc"])
                        V(lambda e: e.tensor_tensor(out=bege[:], in0=beta[:], in1=egc[:], op=ALU.mult), ["g_beta", "g_egc"], ["g_bege"])
                        tap("g_gc", gc[:].rearrange("p a b -> p (a b)"), ["g_gc"])
                        tap("g_qkT", qkT[:].rearrange("p a b -> p (a b)"), ["g_qkT"])
                      if _GB >= 3:
                        S.barrier()
                        wv = [w_[:].rearrange("p a b -> p (a b)").bitcast(F32) for w_ in wbs]

                        def v4(ap):
                            return ap.rearrange("p (h f) -> p h f", h=4)
                        sets = []
                        sets.append(dict(
                            dg=dg[:], Dm=Dm[:], DT=DT[:], B=[Bm[0][:], Bm[1][:]], BT=[BTm[0][:], BTm[1][:]], Y=[Ym[0][:], Ym[1][:]],
                            qkd=qkd[:], vb=vb[:], kbgp=kbgp[:], kdcp=kdcp[:], kds=kds[:], eg2=eg2[:], wTm=wTm[:],
                            qkz=gx[0][:].rearrange("p (c f t) -> p c f t", c=4, f=2),
                            usb=gy[0][:, 0:256], vnew=gy[0][:, 256:512], o2s=gy[0][:, 512:768], otmp=gy[0][:, 768:1024]))
                        kds1 = sb("g_kds1", [128, 4], F32, ph)
                        eg21 = sb("g_eg21", [128, 2], F32, ph)
                        vb1 = sb("g_vb1", [128, 4, 64], F32, ph)
                        DT1 = sb("g_DT1", [128, 4, 128], F32, ph)
                        sets.append(dict(
                            dg=v4(stg[1][:, 0:512]), Dm=v4(stg[1][:, 512:1024]), DT=DT1[:],
                            B=[v4(wv[0][:, 0:512]), v4(wv[0][:, 512:1024])], BT=[v4(wv[0][:, 1024:1536]), v4(wv[0][:, 1536:2048])],
                            Y=[v4(wv[1][:, 0:512]), v4(wv[1][:, 512:1024])],
                            qkz=wv[1][:, 1024:2048].rearrange("p (c f t) -> p c f t", c=4, f=2),
                            qkd=v4(wv[2][:, 0:512]), kbgp=v4(wv[2][:, 512:1024]), kdcp=v4(wv[2][:, 1024:1536]), wTm=v4(wv[2][:, 1536:2048]),
                            vb=vb1[:], kds=kds1[:], eg2=eg21[:],
                            usb=stg[0][:, 0:256], vnew=stg[0][:, 256:512], o2s=stg[0][:, 512:768], otmp=stg[0][:, 768:1024]))
                        G(lambda e: e.memset(gx[0][:], 0.0), [], ["q0_qkz"])
                        G(lambda e: e.memset(wv[1][:, 1024:2048], 0.0), [], ["q1_qkz"])
                        G(lambda e: e.memset(wv[2][:, 512:2048], 0.0), [], ["q1_kbgp", "q1_kdcp", "q1_wTm"])
                        G(lambda e: e.memset(oacc[:], 0.0), [], ["g_oacc"])
                        seqs = [(2 * b_, 2 * b_ + 1) for b_ in range(4)] if grp == 0 else [tuple(range(8))]

                        def quad(si_, d, tt, init, fin, sidx):
                            T_ = sets[si_]
                            kp = "q%d_" % si_
                            if si_ == 0:
                                kq = dict(kbgp="g_kbgp", kdcp="g_kdcp", wTm="g_wTm")
                            else:
                                kq = dict(kbgp="q1_kbgp", kdcp="q1_kdcp", wTm="q1_wTm")
                            kq = {**{n: kp + n for n in ("dg", "Dm", "DT", "B0", "B1", "BT0", "BT1", "Y0", "Y1", "qkd", "vb", "kds", "eg2",
                                                         "qkz", "usb", "vnew", "o2s", "otmp")}, **kq}
                            qrr = [0]

                            def qbank():
                                i = 4 * si_ + qrr[0] % 4
                                qrr[0] += 1
                                return PS[i], ("ps", i)
                            last = 127 if d == 0 else 0
                            ts_ = slice(tt * 128, (tt + 1) * 128)
                            u0 = d * 4
                            dgt, Dmt, DTt, Bt, BTt, Yt = T_["dg"], T_["Dm"], T_["DT"], T_["B"], T_["BT"], T_["Y"]
                            qkdt, vbt, kbgpt, kdcpt, kdst, eg2t, wTmt, qkzt = (T_["qkd"], T_["vb"], T_["kbgp"], T_["kdcp"], T_["kds"],
                                                                               T_["eg2"], T_["wTm"], T_["qkz"])
                            usbt, vnewt, o2st, otmpt = T_["usb"], T_["vnew"], T_["o2s"], T_["otmp"]
                            if init:
                                if grp == 0:
                                    V(lambda e: e.memset(Sst[:, d, :, :], 0.0), [], [("g_S", d)])
                                else:
                                    DS(lambda e: e.dma_start(out=Sst[:, d, :, :],
                                                             in_=D["sg"][l, d].rearrange("(a b) k v -> (b k) a v", b=2)), [], [("g_S", d)])
                            for h in range(4):
                                A(lambda e, h=h: e.mul(out=dgt[:, h, :], in_=ident[:], mul=gc[:, tt, u0 + h:u0 + h + 1]),
                                  ["ident", "g_gc"], [kq["dg"]])
                            psN, kN = qbank()
                            psP, kP = qbank()
                            for h in range(4):
                                hs = slice(h * 128, (h + 1) * 128)
                                mm(psN[:, hs], ones[:, :], dgt[:, h, :], True, False, ["ones", kq["dg"]], [kN])
                                mm(psN[:, hs], ident[:, :], masks[:, 4 + d, :], False, True, ["ident", "masks"], [kN])
                                mm(psP[:, hs], ones[:, :], dgt[:, h, :], True, False, ["ones", kq["dg"]], [kP])
                                mm(psP[:, hs], ident[:, :], masks[:, 2 + d, :], False, True, ["ident", "masks"], [kP])
                            yield
                            for h in range(4):
                                hs = slice(h * 128, (h + 1) * 128)
                                act(Dmt[:, h, :], psP[:, hs], AF.Exp, [kP, "g_gc"], [kq["Dm"]], bias=gc[:, tt, u0 + h:u0 + h + 1], scale=-1.0)
                                act(DTt[:, h, :], psN[:, hs], AF.Exp, [kN, "g_ngc"], [kq["DT"]], bias=ngc[:, tt, u0 + h:u0 + h + 1], scale=1.0)
                                act(kdst[:, h:h + 1], psN[:, h * 128 + last:h * 128 + last + 1], AF.Exp, [kN, "g_ngc"], [kq["kds"]],
                                    bias=ngc[:, tt, u0 + h:u0 + h + 1], scale=1.0)
                            for pr in range(2):
                                for hf in range(2):
                                    h = 2 * pr + hf
                                    A(lambda e, pr=pr, hf=hf, h=h: e.activation(
                                        out=eg2t[hf * 64:(hf + 1) * 64, pr:pr + 1],
                                        in_=psN[hf * 64:(hf + 1) * 64, h * 128 + last:h * 128 + last + 1], func=AF.Exp), [kN], [kq["eg2"]])
                            for hf in range(2):
                                rows = slice(hf * 64, (hf + 1) * 64)
                                V(lambda e, hf=hf, rows=rows: e.tensor_copy(out=qkzt[rows, :, hf, :], in_=qkT[rows, :, ts_]),
                                  ["g_qkT"], [kq["qkz"]])
                            psK, kK = qbank()
                            psQ, kQ = qbank()
                            for h in range(4):
                                hs = slice(h * 128, (h + 1) * 128)
                                kfull = qkT[:, 2 + h // 2, ts_]
                                mm(psK[:, hs], kfull, qkzt[:, 2 + h // 2, h % 2, :], True, True, ["g_qkT", kq["qkz"]], [kK])
                                mm(psQ[:, hs], kfull, qkzt[:, h // 2, h % 2, :], True, True, ["g_qkT", kq["qkz"]], [kQ])
                            yield
                            for h in range(4):
                                hs = slice(h * 128, (h + 1) * 128)
                                V(lambda e, h=h, hs=hs: e.scalar_tensor_tensor(
                                    out=Bt[0][:, h, :], in0=psK[:, hs], scalar=nbeta[:, tt, u0 + h:u0 + h + 1], in1=Dmt[:, h, :],
                                    op0=ALU.mult, op1=ALU.mult), [kK, "g_nbeta", kq["Dm"]], [kq["B0"]])
                            V(lambda e: e.tensor_tensor(out=qkdt.rearrange("p a b -> p (a b)"), in0=psQ[:, :],
                                                        in1=DTt.rearrange("p a b -> p (a b)"), op=ALU.mult), [kQ, kq["DT"]], [kq["qkd"]])
                            yield
                            pb, pk = qbank()
                            for h in range(4):
                                tr(pb[:, h * 128:(h + 1) * 128], Bt[0][:, h, :], ident[:], [kq["B0"], "ident"], [pk])
                            yield
                            V(lambda e, pb=pb: e.tensor_copy(out=BTt[0].rearrange("p a b -> p (a b)"), in_=pb[:, :]), [pk], [kq["BT0"]])
                            for h in range(4):
                                V(lambda e, pb=pb, h=h: e.tensor_tensor(out=Yt[0][:, h, :], in0=pb[:, h * 128:(h + 1) * 128], in1=ident[:],
                                                                        op=ALU.add), [pk, "ident"], [kq["Y0"]])
                            yield
                            cur = 0
                            for lev in range(1, 7):
                                nxt = 1 - cur
                                pb, pk = qbank()
                                for h in range(4):
                                    mm(pb[:, h * 128:(h + 1) * 128], BTt[cur][:, h, :], Bt[cur][:, h, :], True, True,
                                       [kq["BT%d" % cur], kq["B%d" % cur]], [pk])
                                if lev < 6:
                                    pb2, pk2 = qbank()
                                    for h in range(4):
                                        mm(pb2[:, h * 128:(h + 1) * 128], Bt[cur][:, h, :], BTt[cur][:, h, :], True, True,
                                           [kq["BT%d" % cur], kq["B%d" % cur]], [pk2])
                                yield
                                A(lambda e, pb=pb, nxt=nxt: e.copy(out=Bt[nxt].rearrange("p a b -> p (a b)"), in_=pb[:, :]),
                                  [pk], [kq["B%d" % nxt]])
                                if lev < 6:
                                    V(lambda e, pb2=pb2, nxt=nxt: e.tensor_copy(out=BTt[nxt].rearrange("p a b -> p (a b)"), in_=pb2[:, :]),
                                      [pk2], [kq["BT%d" % nxt]])
                                pb3, pk3 = qbank()
                                for h in range(4):
                                    mm(pb3[:, h * 128:(h + 1) * 128], Bt[nxt][:, h, :], Yt[cur][:, h, :], True, True,
                                       [kq["B%d" % nxt], kq["Y%d" % cur]], [pk3])
                                yield
                                V(lambda e, pb3=pb3, nxt=nxt, cur=cur: e.tensor_tensor(
                                    out=Yt[nxt].rearrange("p a b -> p (a b)"), in0=pb3[:, :],
                                    in1=Yt[cur].rearrange("p a b -> p (a b)"), op=ALU.add), [pk3, kq["Y%d" % cur]], [kq["Y%d" % nxt]])
                                cur = nxt
                                yield
                            Yf, Yk = Yt[cur], kq["Y%d" % cur]
                            k4 = ktok[:, tt, :].rearrange("p (h d) -> p h d", h=4)
                            G(lambda e: e.tensor_tensor(out=vbt, in0=vtok[:, tt, :].rearrange("p (h d) -> p h d", h=4),
                                                        in1=beta[:, tt, u0:u0 + 4].unsqueeze(2).to_broadcast([128, 4, 64]), op=ALU.mult),
                              ["g_vtok", "g_beta"], [kq["vb"]])
                            for hf in range(2):
                                pc_ = slice(hf * 64, hf * 64 + 64)
                                G(lambda e, hf=hf, pc_=pc_: e.tensor_tensor(
                                    out=kbgpt[:, hf::2, pc_], in0=k4[:, hf::2, :],
                                    in1=bege[:, tt, u0 + hf:u0 + 4:2].unsqueeze(2).to_broadcast([128, 2, 64]), op=ALU.mult),
                                  ["g_ktok", "g_bege"], [kq["kbgp"]])
                                V(lambda e, hf=hf, pc_=pc_: e.tensor_tensor(
                                    out=kdcpt[:, hf::2, pc_], in0=k4[:, hf::2, :],
                                    in1=kdst[:, hf::2].unsqueeze(2).to_broadcast([128, 2, 64]), op=ALU.mult),
                                  ["g_ktok", kq["kds"]], [kq["kdcp"]])
                            pbu, pku = qbank()
                            for h in range(4):
                                mm(pbu[:, h * 64:(h + 1) * 64], Yf[:, h, :], vbt[:, h, :], True, True, [Yk, kq["vb"]], [pku])
                            pbw, pkw = qbank()
                            for pr in range(2):
                                for hf in range(2):
                                    h = 2 * pr + hf
                                    mm(pbw[:, pr * 128:(pr + 1) * 128], kbgpt[:, h, :], Yf[:, h, :], hf == 0, hf == 1, [kq["kbgp"], Yk], [pkw])
                            yield
                            A(lambda e: e.copy(out=usbt, in_=pbu[:, 0:256]), [pku], [kq["usb"]])
                            for pr in range(2):
                                for hf in range(2):
                                    rows = slice(hf * 64, (hf + 1) * 64)
                                    V(lambda e, pr=pr, hf=hf, rows=rows: e.tensor_copy(out=wTmt[rows, 2 * pr + hf, :],
                                                                                       in_=pbw[rows, pr * 128:(pr + 1) * 128]), [pkw], [kq["wTm"]])
                            yield
                            pbv, pkv = qbank()
                            pbo, pko = qbank()
                            for h in range(4):
                                hc = slice(h * 64, (h + 1) * 64)
                                mm(pbv[:, hc], wTmt[:, h, :], Sst[:, d, h // 2, :], True, True, [kq["wTm"], ("g_S", d)], [pkv])
                                mm(pbo[:, hc], qkzt[:, h // 2, h % 2, :], Sst[:, d, h // 2, :], True, True, [kq["qkz"], ("g_S", d)], [pko])
                            yield
                            V(lambda e: e.tensor_tensor(out=vnewt, in0=usbt, in1=pbv[:, 0:256], op=ALU.subtract), [kq["usb"], pkv], [kq["vnew"]])
                            pb2, pk2 = qbank()
                            for h in range(4):
                                hc = slice(h * 64, (h + 1) * 64)
                                mm(pb2[:, hc], qkdt[:, h, :], vnewt[:, hc], True, True, [kq["qkd"], kq["vnew"]], [pk2])
                            pbs, pks = qbank()
                            for pr in range(2):
                                for hf in range(2):
                                    h = 2 * pr + hf
                                    mm(pbs[:, pr * 64:(pr + 1) * 64], kdcpt[:, h, :], vnewt[:, h * 64:(h + 1) * 64], hf == 0, hf == 1,
                                       [kq["kdcp"], kq["vnew"]], [pks])
                            yield
                            A(lambda e: e.copy(out=o2st, in_=pb2[:, 0:256]), [pk2], [kq["o2s"]])
                            for h in range(4):
                                hc = slice(h * 64, (h + 1) * 64)
                                V(lambda e, h=h, hc=hc: e.scalar_tensor_tensor(
                                    out=otmpt[:, hc], in0=pbo[:, hc], scalar=egc[:, tt, u0 + h:u0 + h + 1], in1=o2st[:, hc],
                                    op0=ALU.mult, op1=ALU.add), [pko, "g_egc", kq["o2s"]], [kq["otmp"]])
                            G(lambda e: e.tensor_tensor(out=oacc[:, tt, :], in0=oacc[:, tt, :], in1=otmpt, op=ALU.add),
                              [("g_oacc", tt), kq["otmp"]], [("g_oacc", tt)])
                            for pr in range(2):
                                V(lambda e, pr=pr: e.scalar_tensor_tensor(
                                    out=Sst[:, d, pr, :], in0=Sst[:, d, pr, :], scalar=eg2t[:, pr:pr + 1],
                                    in1=pbs[:, pr * 64:(pr + 1) * 64], op0=ALU.mult, op1=ALU.add),
                                  [("g_S", d), kq["eg2"], pks], [("g_S", d)])
                            if fin and grp == 0:
                                DS(lambda e: e.dma_start(out=D["nsg"][sidx, l, d].rearrange("(a b) k v -> (b k) a v", b=2),
                                                         in_=Sst[:, d, :, :]), [("g_S", d)], [])
                            yield

                        sched = [[], []]
                        for d in range(2):
                            for sidx, tiles in enumerate(seqs):
                                order = tiles if d == 0 else tuple(reversed(tiles))
                                for qi, tt in enumerate(order):
                                    sched[d].append((tt, qi == 0, qi == len(order) - 1, sidx))
                        for (f_, b_) in zip(sched[0][:_GBQ], sched[1][:_GBQ]):
                            gens = [quad(0, 0, *f_), quad(1, 1, *b_)]
                            alive = [True, True]
                            while any(alive):
                                for gi in range(2):
                                    if alive[gi]:
                                        try:
                                            next(gens[gi])
                                        except StopIteration:
                                            alive[gi] = False
                      S.mute = False
                      if _GB >= 4:
                        norm_gate_out(ph, "g", oacc, "g_oacc", ggn, "ggn", l, zT, "g_zT", 1)
                S.barrier()
                stage("B")
                if grp == groups[0] and l == layers[0]:
                    tap("brB", brT[:, 2:4, :].rearrange("p a b -> p (a b)"), [("brT", 1)])

                S.mute = "M" in _SKIP
                with ExitStack() as ph:
                    mT = sb("m_T", [128, 8, 1024], BF16, ph)
                    wmg = [sb("m_wg%d" % i, [128, 4, 8, 128], BF16, ph) for i in range(2)]
                    wbr = [sb("m_wb%d" % i, [128, 4, 2, 128], BF16, ph) for i in range(2)]
                    gts = [sb("m_gt%d" % i, [128, 512], BF16, ph) for i in range(2)]
                    acc = sb("m_acc", [128, 512], F32, ph)
                    tmpm = sb("m_tmp", [128, 512], F32, ph)
                    gi = 0
                    for dc in range(8):
                        wg_t, wgk = wmg[dc % 2], "m_wg%d" % (dc % 2)
                        wb_t, wbk = wbr[dc % 2], "m_wb%d" % (dc % 2)
                        for n in range(4):
                            c0 = 3856 + n * 1024 + dc * 128
                            DG(lambda e, n=n, c0=c0, wg_t=wg_t: e.dma_start(
                                out=wg_t[:, n, :, :], in_=D["w_in"][l, :, c0:c0 + 128].rearrange("(k p) c -> p k c", p=128)),
                               [], [wgk])
                            DG(lambda e, n=n, dc=dc, wb_t=wb_t: e.dma_start(
                                out=wb_t[:, n, :, :],
                                in_=D["w_branch"][l, n, :, dc * 128:(dc + 1) * 128].rearrange("(k p) c -> p k c", p=128)),
                               [], [wbk])
                        for tb in range(2):
                            blk = slice(tb * 512, (tb + 1) * 512)
                            for n in range(4):
                                pb, pk = bank()
                                for k in range(8):
                                    mm(pb[:, :], wg_t[:, n, k, :], hnT[:, k, blk], k == 0, k == 7, [wgk, "hnT"], [pk])
                                gt, gtk = gts[gi % 2], "m_gt%d" % (gi % 2)
                                gi += 1
                                act(gt[:], pb[:, :], AF.Sigmoid, [pk], [gtk])
                                pb2, pk2 = bank()
                                for kk in range(2):
                                    mm(pb2[:, :], wb_t[:, n, kk, :], brT[:, n * 2 + kk, blk], kk == 0, kk == 1, [wbk, ("brT", n)], [pk2])
                                if n == 0:
                                    V(lambda e, pb2=pb2, gt=gt: e.tensor_tensor(out=acc[:], in0=pb2[:, :], in1=gt[:], op=ALU.mult),
                                      [pk2, gtk], ["m_acc"])
                                else:
                                    V(lambda e, pb2=pb2, gt=gt: e.tensor_tensor(out=tmpm[:], in0=pb2[:, :], in1=gt[:], op=ALU.mult),
                                      [pk2, gtk], ["m_tmp"])
                                    if n < 3:
                                        V(lambda e: e.tensor_tensor(out=acc[:], in0=acc[:], in1=tmpm[:], op=ALU.add),
                                          ["m_acc", "m_tmp"], ["m_acc"])
                                    else:
                                        V(lambda e, dc=dc, blk=blk: e.tensor_tensor(out=mT[:, dc, blk], in0=acc[:], in1=tmpm[:], op=ALU.add),
                                          ["m_acc", "m_tmp"], [("m_T", dc)])
                    for half in range(2):
                        i = wcnt[0] % 3
                        wcnt[0] += 1
                        wo, wok = wbs[i], "wb%d" % i
                        DG(lambda e, half=half, wo=wo: e.dma_start(
                            out=wo[:], in_=D["w_out"][l, :, half * 512:(half + 1) * 512].rearrange("(k p) n -> p k n", p=128)),
                           [], [wok])
                        for q in range(4):
                            oc = half * 4 + q
                            for tb in range(2):
                                blk = slice(tb * 512, (tb + 1) * 512)
                                pb, pk = bank()
                                for k in range(8):
                                    mm(pb[:, :], wo[:, k, q * 128:(q + 1) * 128], mT[:, k, blk], k == 0, k == 7, [wok, ("m_T", k)], [pk])
                                V(lambda e, oc=oc, blk=blk, pb=pb: e.scalar_tensor_tensor(
                                    out=hT[:, oc, blk], in0=pb[:, :], scalar=modT[:, l, 16 + oc, j:j + 1], in1=hT[:, oc, blk],
                                    op0=ALU.mult, op1=ALU.add), [pk, "modT", "hT"], ["hT"])
                S.barrier()
                stage("merge")
                if grp == groups[0] and l == layers[0]:
                    tap("hT1", hT[:].rearrange("p a b -> p (a b)"), ["hT"])

            S.mute = False
            with ExitStack() as ph:
                ynT = sb("f_yn", [128, 8, 1024], F32, ph)
                rmsnorm_T(ph, lambda k: fnT[:, k:k + 1], lambda k: None,
                          lambda k, tb: ynT[:, k, tb * 512:(tb + 1) * 512], "f_yn")
                for tt in range(8):
                    s_t, sk = stg[scnt[0] % 2], "stg%d" % (scnt[0] % 2)
                    scnt[0] += 1
                    for half in range(2):
                        pb, pk = bank()
                        for q in range(4):
                            k = half * 4 + q
                            tr(pb[:, q * 128:(q + 1) * 128], ynT[:, k, tt * 128:(tt + 1) * 128], ident[:], ["f_yn", "ident"], [pk])
                        if half == 0:
                            A(lambda e, pb=pb, s_t=s_t: e.copy(out=s_t[:, 0:512], in_=pb[:, :]), [pk], [sk])
                        else:
                            V(lambda e, pb=pb, s_t=s_t: e.tensor_copy(out=s_t[:, 512:1024], in_=pb[:, :]), [pk], [sk])
                    DS(lambda e, tt=tt, s_t=s_t: e.dma_start(out=yout[tt * 128:(tt + 1) * 128, :], in_=s_t[:]), [sk], [])
            S.barrier()
        try:
            run_groups()
        except _Stop:
            S.barrier()
        with nc.allow_non_contiguous_dma(reason="small transposed parameter loads"):
            stats = S.emit()
    return nc, stats


_CACHE = {}


def _in_maps(inp):
    f = lambda a: np.ascontiguousarray(np.asarray(a, dtype=np.float32))
    cst = _consts(f(inp["na_bias"]))
    shared = dict(
        w_ada=f(inp["w_ada"]), b_ada=f(inp["b_ada"]), norm_g=f(inp["norm_g"]), w_in=f(inp["w_in"]), conv_w=f(inp["conv_w"]),
        a_log=f(inp["gdn_a_log"]).reshape(2, 8), dt_bias=f(inp["gdn_dt_bias"]).reshape(2, 8), gdn_norm=f(inp["gdn_norm"]),
        q_norm=f(inp["attn_q_norm"]), k_norm=f(inp["attn_k_norm"]), ret_norm=f(inp["ret_norm"]),
        w_branch=f(inp["w_branch"]), w_out=f(inp["w_out"]), final_norm=f(inp["final_norm"]).reshape(1, 1024), **cst)
    xp, xs = f(inp["x_prompt"]), f(inp["x_sample"])
    maps = []
    for c in range(8):
        m = dict(shared)
        m["xp"] = xp[4 * c:4 * c + 4].reshape(1024, 1024)
        m["xs"] = xs[c]
        m["cak"] = f(inp["cache_attn_k"][c]).reshape(2, 512, 128)
        m["cav"] = f(inp["cache_attn_v"][c]).reshape(2, 512, 128)
        m["cnk"] = f(inp["cache_na_k"][c]).reshape(2, 512, 256)
        m["cnv"] = f(inp["cache_na_v"][c]).reshape(2, 512, 256)
        m["sg"] = f(inp["state_gdn"][c])
        m["sr"] = f(inp["state_ret"][c])
        m["cond"] = np.stack([f(inp["c_ctx"]), f(inp["c"][c])])
        maps.append(m)
    return maps


def kernel(**inputs):
    if "nc" not in _CACHE:
        _CACHE["nc"] = build()[0]
    nc = _CACHE["nc"]
    maps = _in_maps(inputs)
    res = run_bass_kernel_spmd(nc, maps, core_ids=list(range(8)))
    R = res.results
    cat = lambda n: np.concatenate([np.asarray(r[n]) for r in R], axis=0)
    y_prompt = cat("yp").reshape(32, 256, 1024)
    y_sample = np.stack([np.asarray(r["ys"]) for r in R])
    nak = cat("nak").reshape(32, 2, 256, 2, 64)
    nav = cat("nav").reshape(32, 2, 256, 2, 64)
    nnk = cat("nnk").reshape(32, 2, 256, 4, 64)
    nnv = cat("nnv").reshape(32, 2, 256, 4, 64)
    nsg = cat("nsg")
    nsr = cat("nsr")
    return tuple(np.ascontiguousarray(a, dtype=np.float32) for a in (y_prompt, y_sample, nak, nav, nnk, nnv, nsg, nsr))
```

```python
import numpy as np
import concourse.bass as bass
import concourse.mybir as mybir
from concourse.bass_utils import run_bass_kernel_spmd
from contextlib import ExitStack

F32 = mybir.dt.float32
BF16 = mybir.dt.bfloat16
ALU = mybir.AluOpType
AF = mybir.ActivationFunctionType
AX = mybir.AxisListType
EPS = 1e-6
NEG = -30000.0
BIG = 1.0e5


import types
import os
_CL = int(os.environ.get('CLEVEL', '9'))
_BL = int(os.environ.get('BLEVEL', '9'))
_GB = int(os.environ.get('GB', '9'))
_GBQ = int(os.environ.get('GBQ', '99'))
_GBS = int(os.environ.get('GBS', '9'))
_SKIP = os.environ.get('SKIP', '')
_STRICT = bool(int(os.environ.get('STRICT', '0')))


def _freeze(fn, _depth=0):
    if fn is None or fn.__closure__ is None:
        return fn
    cells = []
    for c in fn.__closure__:
        try:
            v = c.cell_contents
        except ValueError:
            cells.append(c)
            continue
        if isinstance(v, types.FunctionType) and v.__closure__ is not None and _depth < 3:
            v = _freeze(v, _depth + 1)
        cells.append(types.CellType(v))
    return types.FunctionType(fn.__code__, fn.__globals__, fn.__name__, fn.__defaults__, tuple(cells))


class _Op:
    __slots__ = ("fn", "waits", "dma", "dma_n")

    def __init__(self, fn, waits, dma, dma_n):
        self.fn, self.waits, self.dma, self.dma_n = fn, waits, dma, dma_n


class Sched:
    KD = 8
    BLK = {"pe": "tensor", "act": "scalar", "dve": "vector", "pool": "gpsimd", "sp": "sync"}

    def __init__(self, nc, es):
        self.nc = nc
        self.ops = {e: [] for e in self.BLK}
        self.state = {}
        self.ndma = {e: 0 for e in self.BLK}
        self.seen_c = {e: {} for e in self.BLK}
        self.seen_d = {e: set() for e in self.BLK}
        self.last = {}
        self.csem = {e: es.enter_context(nc.semaphore("c_" + e)) for e in ("pe", "act", "dve", "pool")}
        self.dsem = {e: [es.enter_context(nc.semaphore("d_%s%d" % (e, i))) for i in range(self.KD)]
                     for e in ("sp", "pool")}

    def _split(self, key):
        if isinstance(key, tuple):
            return key[0], key[1:]
        return key, None

    def _recs(self, key):
        name, sub = self._split(key)
        d = self.state.get(name)
        if not d:
            return []
        if sub is None:
            return list(d.values())
        out = []
        if sub in d:
            out.append(d[sub])
        if None in d:
            out.append(d[None])
        return out

    def _filter(self, eng, raw, other, dma):
        waits = []
        for d in sorted(raw | other):
            if d[0] == "c":
                if d[1] == eng and not dma:
                    if eng == "pe" or (d not in raw and not _STRICT):
                        continue
                if self.seen_c[eng].get(d[1], -1) >= d[2]:
                    continue
                self.seen_c[eng][d[1]] = d[2]
                waits.append(d)
            else:
                if d in self.seen_d[eng]:
                    continue
                self.seen_d[eng].add(d)
                waits.append(d)
        best, fin = {}, []
        for w in waits:
            if w[0] == "c":
                if w[1] not in best or best[w[1]][2] < w[2]:
                    best[w[1]] = w
            else:
                fin.append(w)
        fin.extend(best.values())
        return fin

    mute = False

    def op(self, eng, fn, reads=(), writes=(), dma=False):
        if self.mute:
            return None
        fn = _freeze(fn)
        idx = len(self.ops[eng])
        raw, other = set(), set()
        for key in reads:
            for rec in self._recs(key):
                if rec[0] is not None:
                    raw.add(rec[0])
        for key in writes:
            for rec in self._recs(key):
                if rec[0] is not None:
                    other.add(rec[0])
                other.update(rec[1])
        if dma:
            n = self.ndma[eng]
            self.ndma[eng] += 1
            ev = ("d", eng, n)
            self.last[("d", eng, n % self.KD)] = ev
        else:
            n = None
            ev = ("c", eng, idx)
            self.last[("c", eng)] = ev
        fin = self._filter(eng, raw, other, dma)
        self.ops[eng].append(_Op(fn, fin, dma, n))
        for key in reads:
            name, sub = self._split(key)
            self.state.setdefault(name, {}).setdefault(sub, [None, []])[1].append(ev)
        for key in writes:
            name, sub = self._split(key)
            d = self.state.setdefault(name, {})
            if sub is None:
                d.clear()
            d[sub] = [ev, []]
        return ev

    def barrier(self):
        evs = set(self.last.values())
        for e in self.BLK:
            fin = self._filter(e, set(evs), set(), True)
            if fin:
                self.ops[e].append(_Op(None, fin, False, None))
        self.state = {}

    def emit(self):
        for e in ("sp", "pool"):
            n = self.ndma[e]
            if n:
                self.ops[e].append(_Op(None, [("d", e, i) for i in range(max(0, n - self.KD), n)], False, None))
        waited = {e: set() for e in self.BLK}
        for e, ops in self.ops.items():
            for o in ops:
                for w in o.waits:
                    if w[0] == "c":
                        waited[w[1]].add(w[2])
        val = {}
        for e in self.BLK:
            for rank, idx in enumerate(sorted(waited[e])):
                val[(e, idx)] = rank + 1
        KD = self.KD
        self.maxval = {e: len(waited[e]) for e in self.BLK}
        self.maxdma = {e: 16 * ((self.ndma[e] - 1) // KD + 1) for e in ("sp", "pool")}
        if os.environ.get("SEMDBG"):
            print("SEM max values", self.maxval, self.maxdma, flush=True)
        with self.nc.Block() as block:
            for e in self.BLK:
                def body(engine, e=e):
                    for idx, o in enumerate(self.ops[e]):
                        for w in o.waits:
                            if w[0] == "c":
                                engine.wait_ge(self.csem[w[1]], val[(w[1], w[2])])
                            else:
                                engine.wait_ge(self.dsem[w[1]][w[2] % KD], 16 * (w[2] // KD + 1))
                        if o.fn is None:
                            continue
                        if o.dma:
                            n = o.dma_n
                            if n >= KD:
                                engine.wait_ge(self.dsem[e][n % KD], 16 * (n // KD))
                            o.fn(engine).then_inc(self.dsem[e][n % KD], 16)
                        else:
                            ins = o.fn(engine)
                            if idx in waited[e]:
                                ins.then_inc(self.csem[e], 1)
                getattr(block, self.BLK[e])(body)
        return {e: len(self.ops[e]) for e in self.BLK}


def _na_pairs():
    t = np.arange(1024)
    r, c = t // 64, t % 64
    rs = np.clip(r - 4, 0, 8)
    cs = np.clip(c - 8, 0, 48)
    valid = ((r[None, :] >= rs[:, None]) & (r[None, :] < rs[:, None] + 8) &
             (c[None, :] >= cs[:, None]) & (c[None, :] < cs[:, None] + 16))
    dr = r[None, :] - r[:, None] + 7
    dc = np.clip(c[None, :] - c[:, None] + 15, 0, 30)
    pairs = []
    for qt in range(8):
        for kt in range(8):
            if valid[qt * 128:(qt + 1) * 128, kt * 128:(kt + 1) * 128].any():
                pairs.append((qt, kt))
    return valid, dr, dc, pairs


_NA = _na_pairs()
NPAIR = len(_NA[3])


def _consts(na_bias):
    c = {}
    c["c_ident"] = np.eye(128, dtype=np.float32)
    p = np.arange(128)[:, None]
    f = np.arange(128)[None, :]
    m = np.zeros((6, 128, 128), np.float32)
    m[0] = (p <= f)
    m[1] = (p >= f)
    m[2] = np.where(f < p, 0.0, BIG)
    m[3] = np.where(f > p, 0.0, BIG)
    m[4] = np.where(f >= p, 0.0, -BIG)
    m[5] = np.where(f <= p, 0.0, -BIG)
    c["c_masks"] = m
    h = np.arange(4, dtype=np.float64)
    lgf = np.log1p(-np.exp2(-(5.0 + h)))
    lgb = np.log1p(-np.exp2(-(5.5 + h)))
    j = np.arange(128, dtype=np.float64)
    dct = np.zeros((4, 128, 128), np.float64)
    for hh in range(4):
        d = j[None, :] - j[:, None]
        dct[hh] = np.where(d > 0, np.exp(np.maximum(d, 0) * lgf[hh]), 0.0) + \
            np.where(d < 0, np.exp(np.maximum(-d, 0) * lgb[hh]), 0.0) + np.where(d == 0, 2.0, 0.0)
    c["c_dct"] = (dct * 0.125).astype(np.float32)
    rqt = np.zeros((2, 4, 128), np.float64)
    kdt = np.zeros((128, 4, 2, 64), np.float64)
    cdt = np.zeros((128, 4, 64), np.float64)
    for hh in range(4):
        rqt[0, hh] = np.exp((j + 1.0) * lgf[hh]) * 0.125
        rqt[1, hh] = np.exp((128.0 - j) * lgb[hh]) * 0.125
        kdt[:, hh, 0, :] = np.exp((127.0 - j) * lgf[hh])[:, None]
        kdt[:, hh, 1, :] = np.exp(j * lgb[hh])[:, None]
        hf_, pr_ = hh % 2, hh // 2
        cdt[hf_ * 64:(hf_ + 1) * 64, 0 * 2 + pr_, :] = np.exp(128.0 * lgf[hh])
        cdt[hf_ * 64:(hf_ + 1) * 64, 1 * 2 + pr_, :] = np.exp(128.0 * lgb[hh])
    c["c_rqt"] = rqt.astype(np.float32)
    c["c_kdt"] = kdt.astype(np.float32)
    c["c_cdt"] = cdt.astype(np.float32)
    t = np.arange(1024)
    row = (t // 64).astype(np.float32)
    col = (t % 64).astype(np.float32)
    inv = (10000.0 ** (-np.arange(16, dtype=np.float32) / 16)).astype(np.float32)
    ar = row[:, None] * inv[None, :]
    ac = col[:, None] * inv[None, :]
    cc = np.concatenate([np.cos(ar), np.cos(ar), np.cos(ac), np.cos(ac)], axis=1)
    ss = np.concatenate([-np.sin(ar), np.sin(ar), -np.sin(ac), np.sin(ac)], axis=1)
    c["c_rope"] = np.stack([np.tile(cc, (1, 6)), np.tile(ss, (1, 6))]).astype(np.float32)
    valid, dr, dc, pairs = _NA
    nab = np.empty((2, NPAIR, 4, 128, 128), np.float32)
    for pi, (qt, kt) in enumerate(pairs):
        qs = slice(qt * 128, (qt + 1) * 128)
        ks = slice(kt * 128, (kt + 1) * 128)
        v = valid[qs, ks].T
        g = na_bias[:, :, dr[qs, ks].T, dc[qs, ks].T]
        nab[:, pi] = np.where(v[None, None], g, np.float32(NEG))
    c["c_nab"] = nab
    return c


IN_SHAPES = dict(
    xp=[1024, 1024], xs=[1024, 1024], cak=[2, 512, 128], cav=[2, 512, 128], cnk=[2, 512, 256], cnv=[2, 512, 256],
    sg=[2, 2, 4, 64, 64], sr=[2, 2, 4, 64, 64], cond=[2, 1024],
    w_ada=[2, 1024, 3072], b_ada=[2, 3072], norm_g=[2, 1024], w_in=[2, 1024, 7952], conv_w=[2, 5, 768],
    a_log=[2, 8], dt_bias=[2, 8], gdn_norm=[2, 64], q_norm=[2, 64], k_norm=[2, 64], ret_norm=[2, 64],
    w_branch=[2, 4, 256, 1024], w_out=[2, 1024, 1024], final_norm=[1, 1024],
    c_ident=[128, 128], c_masks=[6, 128, 128], c_dct=[4, 128, 128], c_rqt=[2, 4, 128], c_kdt=[128, 4, 2, 64],
    c_cdt=[128, 4, 64], c_rope=[2, 1024, 384], c_nab=[2, NPAIR, 4, 128, 128])
OUT_SHAPES = dict(yp=[1024, 1024], ys=[1024, 1024], nak=[4, 2, 256, 128], nav=[4, 2, 256, 128],
                  nnk=[4, 2, 256, 256], nnv=[4, 2, 256, 256], nsg=[4, 2, 2, 4, 64, 64], nsr=[4, 2, 2, 4, 64, 64])


class _Stop(Exception):
    pass


def build(taps=None, groups=(0, 1), layers=(0, 1), stop_after=None):
    taps = taps or {}
    nc = bass.Bass("TRN2", target_bir_lowering=False)
    D = {}
    for n, s in IN_SHAPES.items():
        D[n] = nc.dram_tensor(n, list(s), F32, kind="ExternalInput").ap()
    for n, s in OUT_SHAPES.items():
        D[n] = nc.dram_tensor(n, list(s), F32, kind="ExternalOutput").ap()
    for n, s in taps.items():
        D["tap_" + n] = nc.dram_tensor("tap_" + n, list(s), F32, kind="ExternalOutput").ap()

    with ExitStack() as es:
        S = Sched(nc, es)

        uid = [0]

        def sb(name, shape, dt=F32, st=es):
            uid[0] += 1
            return st.enter_context(nc.sbuf_tensor("%s_%d" % (name, uid[0]), list(shape), dt))

        PS = [es.enter_context(nc.psum_tensor("ps%d" % i, [128, 512], F32)) for i in range(8)]
        rr = [0]

        def bank():
            i = 4 + rr[0] % 4
            rr[0] += 1
            return PS[i], ("ps", i)

        def V(fn, r, w): return S.op("dve", fn, r, w)
        def A(fn, r, w): return S.op("act", fn, r, w)
        def G(fn, r, w): return S.op("pool", fn, r, w)
        def P(fn, r, w): return S.op("pe", fn, r, w)
        def DS(fn, r, w): return S.op("sp", fn, r, w, dma=True)
        def DG(fn, r, w): return S.op("pool", fn, r, w, dma=True)

        def mm(out, lhsT, rhs, start, stop, r, w):
            P(lambda e: e.matmul(out, lhsT=lhsT, rhs=rhs, start=start, stop=stop), r, w)

        def tr(out, in_, idt, r, w):
            P(lambda e: e.transpose(out, in_, idt), r, w)

        def act(out, in_, func, r, w, bias=0.0, scale=1.0):
            A(lambda e: e.activation(out=out, in_=in_, func=func, bias=bias, scale=scale), r, w)

        def tap(name, ap, reads):
            if name in taps and name not in os.environ.get("NOTAP", "").split(","):
                DG(lambda e: e.dma_start(out=D["tap_" + name], in_=ap), reads, [])

        ident = sb("ident", [128, 128])
        identb = sb("identb", [128, 128], BF16)
        ones = sb("ones", [128, 128])
        onesblk = sb("onesblk", [128, 128])
        onespad = sb("onespad", [128, 2, 128], BF16)
        masks = sb("masks", [128, 6, 128])
        dct = sb("dct", [128, 4, 128])
        rqt = sb("rqt", [128, 2, 4, 128])
        kdt = sb("kdt", [128, 4, 2, 64])
        cdt = sb("cdt", [128, 4, 64])
        ngT = sb("ngT", [128, 2, 8])
        fnT = sb("fnT", [128, 8])
        baT = sb("baT", [128, 2, 24])
        cwT = sb("cwT", [128, 2, 6, 5])
        gqk = sb("gqk", [128, 2, 6, 64])
        gkn = sb("gkn", [128, 2, 2, 64])
        ggn = sb("ggn", [128, 2, 4, 64])
        grn = sb("grn", [128, 2, 4, 64])
        alb = sb("alb", [128, 2, 8])
        dtb = sb("dtb", [128, 2, 8])
        nega = sb("nega", [128, 2, 8])
        condT = sb("condT", [128, 8, 2])
        scond = sb("scond", [128, 8, 2])
        modT = sb("modT", [128, 2, 24, 2])
        gmul = sb("gmul", [128, 2, 8, 2])

        for jj in range(2):
            DS(lambda e, jj=jj: e.dma_start(out=condT[:, :, jj], in_=D["cond"][jj].rearrange("(c p) -> p c", p=128)), [], ["condT"])
        DS(lambda e: e.dma_start(out=baT[:], in_=D["b_ada"].rearrange("l (c p) -> p l c", p=128)), [], ["baT"])
        DS(lambda e: e.dma_start(out=ngT[:], in_=D["norm_g"].rearrange("l (c p) -> p l c", p=128)), [], ["ngT"])
        act(scond[:], condT[:], AF.Silu, ["condT"], ["scond"])
        DS(lambda e: e.dma_start(out=ident[:], in_=D["c_ident"]), [], ["ident"])
        DG(lambda e: e.dma_start(out=identb[:], in_=D["c_ident"]), [], ["identb"])
        V(lambda e: e.memset(ones[:], 1.0), [], ["ones"])
        V(lambda e: e.memset(onesblk[:], 0.0), [], ["onesblk"])
        V(lambda e: e.memset(onesblk[0:64, 0:64], 1.0), [], ["onesblk"])
        V(lambda e: e.memset(onesblk[64:128, 64:128], 1.0), [], ["onesblk"])
        V(lambda e: e.memset(onespad[:], 0.0), [], ["onespad"])
        V(lambda e: e.memset(onespad[:, 0, 0:64], 1.0), [], ["onespad"])
        V(lambda e: e.memset(onespad[:, 1, 64:128], 1.0), [], ["onespad"])
        DS(lambda e: e.dma_start(out=masks[:], in_=D["c_masks"].rearrange("m p f -> p m f")), [], ["masks"])
        DS(lambda e: e.dma_start(out=dct[:], in_=D["c_dct"].rearrange("m p f -> p m f")), [], ["dct"])
        DS(lambda e: e.dma_start(out=rqt[:].rearrange("p a b c -> p (a b c)"),
                                 in_=D["c_rqt"].rearrange("a b c -> (a b c)").partition_broadcast(128)), [], ["rqt"])
        DS(lambda e: e.dma_start(out=kdt[:], in_=D["c_kdt"]), [], ["kdt"])
        DS(lambda e: e.dma_start(out=cdt[:], in_=D["c_cdt"]), [], ["cdt"])
        DS(lambda e: e.dma_start(out=fnT[:], in_=D["final_norm"].rearrange("o (c p) -> p (o c)", p=128)), [], ["fnT"])
        for l in range(2):
            for c6 in range(6):
                DS(lambda e, l=l, c6=c6: e.dma_start(
                    out=cwT[:, l, c6, :], in_=D["conv_w"][l, :, c6 * 128:(c6 + 1) * 128].rearrange("j p -> p j")),
                   [], ["cwT"])
        for l in range(2):
            for hh in range(6):
                src = "q_norm" if hh < 4 else "k_norm"
                DS(lambda e, l=l, hh=hh, src=src: e.dma_start(out=gqk[:, l, hh, :], in_=D[src][l].partition_broadcast(128)),
                   [], ["gqk"])
            for hh in range(2):
                DS(lambda e, l=l, hh=hh: e.dma_start(out=gkn[:, l, hh, :], in_=D["k_norm"][l].partition_broadcast(128)),
                   [], ["gkn"])
            for hh in range(4):
                DS(lambda e, l=l, hh=hh: e.dma_start(out=ggn[:, l, hh, :], in_=D["gdn_norm"][l].partition_broadcast(128)),
                   [], ["ggn"])
                DS(lambda e, l=l, hh=hh: e.dma_start(out=grn[:, l, hh, :], in_=D["ret_norm"][l].partition_broadcast(128)),
                   [], ["grn"])
            DS(lambda e, l=l: e.dma_start(out=alb[:, l, :], in_=D["a_log"][l].partition_broadcast(128)), [], ["alb"])
            DS(lambda e, l=l: e.dma_start(out=dtb[:, l, :], in_=D["dt_bias"][l].partition_broadcast(128)), [], ["dtb"])
        for l in range(2):
            V(lambda e, l=l: e.tensor_scalar(out=gqk[:, l, 0:4, :], in0=gqk[:, l, 0:4, :], scalar1=0.125, scalar2=None,
                                             op0=ALU.mult), ["gqk"], ["gqk"])
        act(nega[:], alb[:], AF.Exp, ["alb"], ["nega"])
        V(lambda e: e.tensor_scalar(out=nega[:], in0=nega[:], scalar1=-1.0, scalar2=None, op0=ALU.mult), ["nega"], ["nega"])

        with ExitStack() as ph:
            wa = [sb("wa%d" % i, [128, 8, 512], F32, ph) for i in range(2)]
            cnt = 0
            for l in range(2):
                pb, pk = PS[0], ("ps", 0)
                for ch in range(6):
                    w_t, wk = wa[cnt % 2], "wa%d" % (cnt % 2)
                    cnt += 1
                    DG(lambda e, l=l, ch=ch, w_t=w_t: e.dma_start(
                        out=w_t[:], in_=D["w_ada"][l, :, ch * 512:(ch + 1) * 512].rearrange("(k p) n -> p k n", p=128)),
                       [], [wk])
                    for oc in range(4):
                        col = (ch * 4 + oc) * 2
                        for k in range(8):
                            mm(pb[:, col:col + 2], w_t[:, k, oc * 128:(oc + 1) * 128], scond[:, k, :], k == 0, k == 7,
                               [wk, "scond"], [pk])
                for j in range(2):
                    V(lambda e, l=l, j=j, pb=pb: e.tensor_tensor(
                        out=modT[:, l, :, j], in0=pb[:, 0:48].rearrange("p (a b) -> p a b", b=2)[:, :, j],
                        in1=baT[:, l, :], op=ALU.add), [pk, "baT"], ["modT"])
                    V(lambda e, l=l, j=j: e.scalar_tensor_tensor(
                        out=gmul[:, l, :, j], in0=modT[:, l, 8:16, j], scalar=1.0, in1=ngT[:, l, :],
                        op0=ALU.add, op1=ALU.mult), ["modT", "ngT"], ["gmul"])
            tap("modT", modT[:].rearrange("p a b c -> p (a b c)"), ["modT"])
        S.barrier()

        hT = sb("hT", [128, 8, 1024])
        hnT = sb("hnT", [128, 8, 1024], BF16)
        brT = sb("brT", [128, 8, 1024], BF16)
        wbs = [sb("wb%d" % i, [128, 8, 512], BF16) for i in range(3)]
        stg = [sb("stg%d" % i, [128, 1024]) for i in range(2)]
        wcnt = [0]
        scnt = [0]

        pend_w = {}

        def load_w(l, c0, c1):
            if (l, c0, c1) in pend_w:
                return pend_w.pop((l, c0, c1))
            i = wcnt[0] % 3
            wcnt[0] += 1
            t, k = wbs[i], "wb%d" % i
            DG(lambda e: e.dma_start(out=t[:, :, 0:c1 - c0],
                                     in_=D["w_in"][l, :, c0:c1].rearrange("(k p) n -> p k n", p=128)), [], [k])
            return t, k

        def prefetch_w(l, c0, c1):
            if not S.mute:
                pend_w[(l, c0, c1)] = load_w(l, c0, c1)

        def proj_T(wt, wk, a, b, tt, pb, pk):
            for k in range(8):
                mm(pb[:, 0:b - a], hnT[:, k, tt * 128:(tt + 1) * 128], wt[:, k, a:b], k == 0, k == 7, [wk, "hnT"], [pk])

        def proj_F(wt, wk, a, m, tb, pb, pk):
            for k in range(8):
                mm(pb[0:m, :], wt[:, k, a:a + m], hnT[:, k, tb * 512:(tb + 1) * 512], k == 0, k == 7, [wk, "hnT"], [pk])

        def rstd_from(ss_ap, out_ap, n, r, w):
            act(out_ap, ss_ap, AF.Sqrt, r, w, bias=EPS, scale=1.0 / n)
            V(lambda e: e.reciprocal(out=out_ap, in_=out_ap), w, w)

        def rmsnorm_T(st, gcol_fn, scol_fn, out_fn, outkey):
            sq = sb("rn_sq", [128, 8, 512], F32, st)
            rs = sb("rn_rs", [128, 512], F32, st)
            tmps = [sb("rn_tmp%d" % i, [128, 512], F32, st) for i in range(2)]
            for tb in range(2):
                blk = slice(tb * 512, (tb + 1) * 512)
                act(sq[:], hT[:, :, blk], AF.Square, ["hT"], ["rn_sq"])
                pb, pk = bank()
                for k in range(8):
                    mm(pb[:, :], ones[:, :], sq[:, k, :], k == 0, k == 7, ["ones", "rn_sq"], [pk])
                rstd_from(pb[:, :], rs[:], 1024.0, [pk], ["rn_rs"])
                for k in range(8):
                    sc = scol_fn(k)
                    if sc is None:
                        V(lambda e, k=k, tb=tb, blk=blk: e.scalar_tensor_tensor(
                            out=out_fn(k, tb), in0=hT[:, k, blk], scalar=gcol_fn(k), in1=rs[:], op0=ALU.mult, op1=ALU.mult),
                          ["hT", "rn_rs", "gmul", "fnT"], [outkey])
                    else:
                        tmpk, tk = tmps[k % 2], "rn_tmp%d" % (k % 2)
                        V(lambda e, k=k, blk=blk, tmpk=tmpk: e.scalar_tensor_tensor(
                            out=tmpk[:], in0=hT[:, k, blk], scalar=gcol_fn(k), in1=rs[:], op0=ALU.mult, op1=ALU.mult),
                          ["hT", "rn_rs", "gmul"], [tk])
                        act(out_fn(k, tb), tmpk[:], AF.Identity, [tk, "modT"], [outkey], bias=sc, scale=1.0)

        def norm_gate_out(st, tag, o_acc, okey, gtile, gkey, l, zT, zkey, br):
            sq = sb(tag + "_sq", [128, 256], F32, st)
            ss = sb(tag + "_ss", [128, 4], F32, st)
            on = sb(tag + "_on", [128, 256], F32, st)
            for tt in range(8):
                act(sq[:], o_acc[:, tt, :], AF.Square, [(okey, tt)], [tag + "_sq"])
                V(lambda e: e.tensor_reduce(out=ss[:], in_=sq[:].rearrange("p (h d) -> p h d", h=4), axis=AX.X, op=ALU.add),
                  [tag + "_sq"], [tag + "_ss"])
                rstd_from(ss[:], ss[:], 64.0, [tag + "_ss"], [tag + "_ss"])
                V(lambda e, tt=tt: e.tensor_tensor(out=on[:].rearrange("p (h d) -> p h d", h=4),
                                                   in0=o_acc[:, tt, :].rearrange("p (h d) -> p h d", h=4),
                                                   in1=ss[:].unsqueeze(2).to_broadcast([128, 4, 64]), op=ALU.mult),
                  [(okey, tt), tag + "_ss"], [tag + "_on"])
                G(lambda e: e.tensor_tensor(out=on[:].rearrange("p (h d) -> p h d", h=4),
                                            in0=on[:].rearrange("p (h d) -> p h d", h=4), in1=gtile[:, l, :, :], op=ALU.mult),
                  [tag + "_on", gkey], [tag + "_on"])
                pb, pk = bank()
                for c in range(2):
                    tr(pb[:, c * 128:(c + 1) * 128], on[:, c * 128:(c + 1) * 128], ident[:], [tag + "_on", "ident"], [pk])
                V(lambda e, tt=tt, pb=pb: e.tensor_tensor(out=brT[:, br * 2:br * 2 + 2, tt * 128:(tt + 1) * 128],
                                                          in0=pb[:, 0:256].rearrange("p (c t) -> p c t", c=2),
                                                          in1=zT[:, :, tt * 128:(tt + 1) * 128], op=ALU.mult),
                  [pk, zkey], [("brT", br)])

        def attention(st, tag, qT, kT, vpad, kvmap, vslot, qblocks, keys_fn, zT, zkey, br):
            pts = [sb(tag + "_p%d" % i, [128, 512], BF16, st) for i in range(3)]
            rdens = [sb(tag + "_rd%d" % i, [64, 512], F32, st) for i in range(2)]
            osb = sb(tag + "_o", [128, 512], F32, st)
            pc = 0
            it = 0
            for (q0, qn) in qblocks:
                keys = keys_fn(q0)
                nk = len(keys)
                for h in range(4):
                    pr, hf = h // 2, h % 2
                    bo = it % 4
                    it += 1
                    psO, ko = PS[bo], ("ps", bo)
                    for ci, (kidx, bias_fn) in enumerate(keys):
                        pb, pk = bank()
                        mm(pb[:, 0:qn], kT[:, kvmap(h), kidx * 128:(kidx + 1) * 128], qT[:, h, q0:q0 + qn],
                           True, bias_fn is None, [tag + "_kT", tag + "_qT"], [pk])
                        if bias_fn is not None:
                            bap, bkey = bias_fn(h)
                            mm(pb[:, 0:qn], identb[:, :], bap, False, True, ["identb", bkey], [pk])
                        pt, ptk = pts[pc % 3], tag + "_p%d" % (pc % 3)
                        pc += 1
                        act(pt[:, 0:qn], pb[:, 0:qn], AF.Exp, [pk], [ptk])
                        mm(psO[:, 0:qn], vpad[:, kidx, vslot(h), :], pt[:, 0:qn], ci == 0, ci == nk - 1, [tag + "_vp", ptk], [ko])
                    rows = slice(hf * 64, hf * 64 + 64)
                    rden, rdk = rdens[hf], tag + "_rd%d" % hf
                    okey = (tag + "_o", hf)
                    V(lambda e, psO=psO, qn=qn, rden=rden: e.reciprocal(out=rden[:, 0:qn], in_=psO[64:128, 0:qn]), [ko], [rdk])
                    V(lambda e, psO=psO, qn=qn, rden=rden, rows=rows: e.tensor_tensor(
                        out=osb[rows, 0:qn], in0=psO[0:64, 0:qn], in1=rden[:, 0:qn], op=ALU.mult), [ko, rdk], [okey])
                    V(lambda e, pr=pr, q0=q0, qn=qn, rows=rows: e.tensor_tensor(
                        out=brT[rows, br * 2 + pr, q0:q0 + qn], in0=osb[rows, 0:qn], in1=zT[rows, pr, q0:q0 + qn], op=ALU.mult),
                      [okey, zkey], [("brT", br)])

        def zproj(wt, wk, a, zT, zkey):
            for c in range(2):
                for tb in range(2):
                    pb, pk = bank()
                    proj_F(wt, wk, a + c * 128, 128, tb, pb, pk)
                    act(zT[:, c, tb * 512:(tb + 1) * 512], pb[:, :], AF.Silu, [pk], [zkey])

        def stage(name):
            if stop_after == name or (stop_after == "Dproj" and name == "D"):
                raise _Stop()

        def run_groups():
          for grp in (groups if stop_after != "p0" else ()):
            G(lambda e: e.memset(brT[:], 0.0), [], ["brT"])
            xin = D["xp"] if grp == 0 else D["xs"]
            yout = D["yp"] if grp == 0 else D["ys"]
            for tt in range(8):
                s_t, sk = stg[scnt[0] % 2], "stg%d" % (scnt[0] % 2)
                scnt[0] += 1
                DS(lambda e, tt=tt, s_t=s_t: e.dma_start(out=s_t[:], in_=xin[tt * 128:(tt + 1) * 128, :]), [], [sk])
                for half in range(2):
                    pb, pk = bank()
                    for q in range(4):
                        k = half * 4 + q
                        tr(pb[:, q * 128:(q + 1) * 128], s_t[:, k * 128:(k + 1) * 128], ident[:], [sk, "ident"], [pk])
                    eng = A if half == 0 else V
                    if half == 0:
                        A(lambda e, tt=tt, pb=pb: e.copy(out=hT[:, 0:4, tt * 128:(tt + 1) * 128],
                                                         in_=pb[:, :].rearrange("p (a b) -> p a b", a=4)), [pk], ["hT"])
                    else:
                        V(lambda e, tt=tt, pb=pb: e.tensor_copy(out=hT[:, 4:8, tt * 128:(tt + 1) * 128],
                                                                in_=pb[:, :].rearrange("p (a b) -> p a b", a=4)), [pk], ["hT"])
            for l in layers:
                j = grp
                S.mute = False
                with ExitStack() as ph:
                    rmsnorm_T(ph, lambda k: gmul[:, l, k, j:j + 1], lambda k: modT[:, l, k, j:j + 1],
                              lambda k, tb: hnT[:, k, tb * 512:(tb + 1) * 512], "hnT")
                S.barrier()
                stage("norm")
                if grp == groups[0] and l == layers[0]:
                    tap("hnT", hnT[:].rearrange("p a b -> p (a b)"), ["hnT"])

                S.mute = "A" in _SKIP
                with ExitStack() as ph:
                    nkt = 8 if grp == 0 else 12
                    qT = sb("a_qT", [64, 4, 1024], BF16, ph)
                    kT = sb("a_kT", [64, 2, 128 * nkt], BF16, ph)
                    vpad = sb("a_vp", [128, nkt, 4, 128], BF16, ph)
                    zT = sb("a_zT", [128, 2, 1024], BF16, ph)
                    sqs = [sb("a_sq%d" % i, [128, 384], F32, ph) for i in range(2)]
                    sss = [sb("a_ss%d" % i, [128, 6], F32, ph) for i in range(2)]
                    qks = [sb("a_qk%d" % i, [128, 384], F32, ph) for i in range(2)]
                    qk2s = [sb("a_qk2%d" % i, [128, 384], F32, ph) for i in range(2)]
                    kouts = [sb("a_ko%d" % i, [128, 128], F32, ph) for i in range(2)]
                    vouts = [sb("a_vo%d" % i, [128, 128], F32, ph) for i in range(2)]
                    if grp == 1:
                        rp_all = sb("a_rp", [128, 8, 2, 384], F32, ph)
                        for t8 in range(8):
                            DS(lambda e, t8=t8: e.dma_start(out=rp_all[:, t8, :, :], in_=D["c_rope"][:, t8 * 128:(t8 + 1) * 128, :]
                                                            .rearrange("a p f -> p a f")), [], [("a_rp", t8)])
                    G(lambda e: e.memset(vpad[:, :, :, 64:128], 1.0), [], ["a_vp"])
                    w0, w0k = load_w(l, 0, 512)
                    w1, w1k = load_w(l, 512, 768)
                    for tt in range(8):
                        par_ = tt % 2
                        sq, ss, qk, qk2, kout, vout = sqs[par_], sss[par_], qks[par_], qk2s[par_], kouts[par_], vouts[par_]
                        K_sq, K_ss, K_qk, K_qk2, K_ko, K_vo = ["a_%s%d" % (n_, par_) for n_ in ("sq", "ss", "qk", "qk2", "ko", "vo")]
                        K_rp = ("a_rp", tt)
                        if grp == 1:
                            rp = rp_all[:, tt, :, :]
                        pb, pk = bank()
                        proj_T(w0, w0k, 0, 512, tt, pb, pk)
                        act(sq[:], pb[:, 0:384], AF.Square, [pk], [K_sq])
                        V(lambda e: e.tensor_reduce(out=ss[:], in_=sq[:].rearrange("p (h d) -> p h d", h=6), axis=AX.X,
                                                    op=ALU.add), [K_sq], [K_ss])
                        rstd_from(ss[:], ss[:], 64.0, [K_ss], [K_ss])
                        V(lambda e, pb=pb: e.tensor_tensor(out=qk[:].rearrange("p (h d) -> p h d", h=6),
                                                           in0=pb[:, 0:384].rearrange("p (h d) -> p h d", h=6),
                                                           in1=ss[:].unsqueeze(2).to_broadcast([128, 6, 64]), op=ALU.mult),
                          [pk, K_ss], [K_qk])
                        if grp == 0:
                            b_, s0 = tt // 2, (tt % 2) * 128
                            G(lambda e: e.tensor_tensor(out=kout[:].rearrange("p (h d) -> p h d", h=2),
                                                        in0=qk[:, 256:384].rearrange("p (h d) -> p h d", h=2),
                                                        in1=gkn[:, l, :, :], op=ALU.mult), [K_qk, "gkn"], [K_ko])
                            DS(lambda e, b_=b_, s0=s0: e.dma_start(out=D["nak"][b_, l, s0:s0 + 128, :], in_=kout[:]),
                               [K_ko], [])
                            A(lambda e, pb=pb: e.copy(out=vout[:], in_=pb[:, 384:512]), [pk], [K_vo])
                            DS(lambda e, b_=b_, s0=s0: e.dma_start(out=D["nav"][b_, l, s0:s0 + 128, :], in_=vout[:]),
                               [K_vo], [])
                        G(lambda e: e.tensor_tensor(out=qk[:].rearrange("p (h d) -> p h d", h=6),
                                                    in0=qk[:].rearrange("p (h d) -> p h d", h=6), in1=gqk[:, l, :, :],
                                                    op=ALU.mult), [K_qk, "gqk"], [K_qk])
                        src, srck = qk, K_qk
                        if grp == 1:
                            V(lambda e: e.tensor_tensor(out=qk2[:], in0=qk[:], in1=rp[:, 0, :], op=ALU.mult),
                              [K_qk, K_rp], [K_qk2])
                            qv = qk[:].rearrange("p (g s d) -> p g s d", s=2, d=16)
                            sv = rp[:, 1, :].rearrange("p (g s d) -> p g s d", s=2, d=16)
                            G(lambda e, qv=qv, sv=sv: e.tensor_tensor(
                                out=sq[:].rearrange("p (g s d) -> p g s d", s=2, d=16)[:, :, 0, :], in0=qv[:, :, 1, :],
                                in1=sv[:, :, 0, :], op=ALU.mult), [K_qk, K_rp], [K_sq])
                            G(lambda e, qv=qv, sv=sv: e.tensor_tensor(
                                out=sq[:].rearrange("p (g s d) -> p g s d", s=2, d=16)[:, :, 1, :], in0=qv[:, :, 0, :],
                                in1=sv[:, :, 1, :], op=ALU.mult), [K_qk, K_rp], [K_sq])
                            V(lambda e: e.tensor_tensor(out=qk2[:], in0=qk2[:], in1=sq[:], op=ALU.add),
                              [K_qk2, K_sq], [K_qk2])
                            src, srck = qk2, K_qk2
                        pq, pqk = bank()
                        for h in range(4):
                            tr(pq[0:64, h * 128:(h + 1) * 128], src[:, h * 64:(h + 1) * 64], ident[:], [srck, "ident"], [pqk])
                        A(lambda e, tt=tt, pq=pq: e.copy(out=qT[:, :, tt * 128:(tt + 1) * 128],
                                                         in_=pq[0:64, :].rearrange("p (a b) -> p a b", a=4)), [pqk], ["a_qT"])
                        pk2, pk2k = bank()
                        for h in range(2):
                            tr(pk2[0:64, h * 128:(h + 1) * 128], src[:, 256 + h * 64:256 + (h + 1) * 64], ident[:],
                               [srck, "ident"], [pk2k])
                        V(lambda e, tt=tt, pk2=pk2: e.tensor_copy(out=kT[:, :, tt * 128:(tt + 1) * 128],
                                                                  in_=pk2[0:64, 0:256].rearrange("p (a b) -> p a b", a=2)),
                          [pk2k], ["a_kT"])
                        A(lambda e, tt=tt, pb=pb: e.copy(out=vpad[:, tt, 0:2, 0:64],
                                                         in_=pb[:, 384:512].rearrange("p (a b) -> p a b", a=2)), [pk], ["a_vp"])
                    if grp == 1:
                        for ct in range(4):
                            s_t, sk = stg[scnt[0] % 2], "stg%d" % (scnt[0] % 2)
                            scnt[0] += 1
                            DS(lambda e, ct=ct, s_t=s_t: e.dma_start(out=s_t[:, 0:128], in_=D["cak"][l, ct * 128:(ct + 1) * 128, :]),
                               [], [sk])
                            DS(lambda e, ct=ct, s_t=s_t: e.dma_start(out=s_t[:, 128:256], in_=D["cav"][l, ct * 128:(ct + 1) * 128, :]),
                               [], [sk])
                            pk2, pk2k = bank()
                            for h in range(2):
                                tr(pk2[0:64, h * 128:(h + 1) * 128], s_t[:, h * 64:(h + 1) * 64], ident[:], [sk, "ident"], [pk2k])
                            V(lambda e, ct=ct, pk2=pk2: e.tensor_copy(
                                out=kT[:, :, (8 + ct) * 128:(9 + ct) * 128],
                                in_=pk2[0:64, 0:256].rearrange("p (a b) -> p a b", a=2)), [pk2k], ["a_kT"])
                            A(lambda e, ct=ct, s_t=s_t: e.copy(out=vpad[:, 8 + ct, 0:2, 0:64],
                                                               in_=s_t[:, 128:256].rearrange("p (a b) -> p a b", a=2)), [sk], ["a_vp"])
                    zproj(w1, w1k, 0, zT, "a_zT")
                    prefetch_w(l, 2832, 3344)
                    if grp == 0:
                        qblocks = [(b_ * 256, 256) for b_ in range(4)]
                        keys_fn = lambda q0: [((q0 // 128) + i, None) for i in range(2)]
                    else:
                        qblocks = [(0, 512), (512, 512)]
                        keys_fn = lambda q0: [(i, None) for i in range(12)]
                    attention(ph, "a", qT, kT, vpad, lambda h: h // 2, lambda h: h // 2, qblocks, keys_fn,
                              zT, "a_zT", 0)
                S.barrier()
                stage("A")
                if grp == groups[0] and l == layers[0]:
                    tap("brA", brT[:, 0:2, :].rearrange("p a b -> p (a b)"), [("brT", 0)])

                S.mute = "D" in _SKIP
                with ExitStack() as ph:
                    nkt = 8 if grp == 0 else 12
                    qT = sb("d_qT", [64, 4, 1024], BF16, ph)
                    kT = sb("d_kT", [64, 4, 128 * nkt], BF16, ph)
                    vpad = sb("d_vp", [128, nkt, 4, 128], BF16, ph)
                    zT = sb("d_zT", [128, 2, 1024], BF16, ph)
                    kout = sb("d_ko", [128, 256], F32, ph)
                    vout = sb("d_vo", [128, 256], F32, ph)
                    G(lambda e: e.memset(vpad[:, :, :, 64:128], 1.0), [], ["d_vp"])
                    w0, w0k = load_w(l, 2832, 3344)
                    w1, w1k = load_w(l, 3344, 3856)
                    for c in range(2):
                        for tb in range(2):
                            blk = slice(tb * 512, (tb + 1) * 512)
                            pb, pk = bank()
                            proj_F(w0, w0k, c * 128, 128, tb, pb, pk)
                            for hf in range(2):
                                V(lambda e, c=c, hf=hf, blk=blk, pb=pb: e.tensor_scalar(
                                    out=qT[:, 2 * c + hf, blk], in0=pb[hf * 64:(hf + 1) * 64, :], scalar1=0.125, scalar2=None,
                                    op0=ALU.mult), [pk], ["d_qT"])
                            pb, pk = bank()
                            proj_F(w0, w0k, 256 + c * 128, 128, tb, pb, pk)
                            for hf in range(2):
                                A(lambda e, c=c, hf=hf, blk=blk, pb=pb: e.copy(out=kT[:, 2 * c + hf, blk],
                                                                               in_=pb[hf * 64:(hf + 1) * 64, :]), [pk], ["d_kT"])
                    for tt in range(8):
                        pb, pk = bank()
                        proj_T(w1, w1k, 0, 256, tt, pb, pk)
                        A(lambda e, tt=tt, pb=pb: e.copy(out=vpad[:, tt, :, 0:64],
                                                         in_=pb[:, 0:256].rearrange("p (a b) -> p a b", a=4)), [pk], ["d_vp"])
                        if grp == 0:
                            b_, s0 = tt // 2, (tt % 2) * 128
                            A(lambda e, pb=pb: e.copy(out=vout[:], in_=pb[:, 0:256]), [pk], ["d_vo"])
                            DS(lambda e, b_=b_, s0=s0: e.dma_start(out=D["nnv"][b_, l, s0:s0 + 128, :], in_=vout[:]),
                               ["d_vo"], [])
                            pb2, pk2k = bank()
                            proj_T(w0, w0k, 256, 512, tt, pb2, pk2k)
                            V(lambda e, pb2=pb2: e.tensor_copy(out=kout[:], in_=pb2[:, 0:256]), [pk2k], ["d_ko"])
                            DS(lambda e, b_=b_, s0=s0: e.dma_start(out=D["nnk"][b_, l, s0:s0 + 128, :], in_=kout[:]),
                               ["d_ko"], [])
                    if grp == 1:
                        for ct in range(4):
                            s_t, sk = stg[scnt[0] % 2], "stg%d" % (scnt[0] % 2)
                            scnt[0] += 1
                            DS(lambda e, ct=ct, s_t=s_t: e.dma_start(out=s_t[:, 0:256], in_=D["cnk"][l, ct * 128:(ct + 1) * 128, :]),
                               [], [sk])
                            DS(lambda e, ct=ct, s_t=s_t: e.dma_start(out=s_t[:, 256:512], in_=D["cnv"][l, ct * 128:(ct + 1) * 128, :]),
                               [], [sk])
                            pk2, pk2k = bank()
                            for h in range(4):
                                tr(pk2[0:64, h * 128:(h + 1) * 128], s_t[:, h * 64:(h + 1) * 64], ident[:], [sk, "ident"], [pk2k])
                            V(lambda e, ct=ct, pk2=pk2: e.tensor_copy(
                                out=kT[:, :, (8 + ct) * 128:(9 + ct) * 128],
                                in_=pk2[0:64, :].rearrange("p (a b) -> p a b", a=4)), [pk2k], ["d_kT"])
                            A(lambda e, ct=ct, s_t=s_t: e.copy(out=vpad[:, 8 + ct, :, 0:64],
                                                               in_=s_t[:, 256:512].rearrange("p (a b) -> p a b", a=4)), [sk], ["d_vp"])
                    zproj(w1, w1k, 256, zT, "d_zT")
                    prefetch_w(l, 1808, 2320)
                    if stop_after == "Dproj":
                        pass
                    elif grp == 0:
                        qblocks = [(b_ * 256, 256) for b_ in range(4)]
                        keys_fn = lambda q0: [((q0 // 128) + i, None) for i in range(2)]
                        attention(ph, "d", qT, kT, vpad, lambda h: h, lambda h: h, qblocks, keys_fn, zT, "d_zT", 3)
                    else:
                        nbt = [sb("d_nb%d" % i, [128, 6, 4, 128], BF16, ph) for i in range(3)]
                        pairs = _NA[3]
                        qblocks = [(qt * 128, 128) for qt in range(8)]
                        issued = set()

                        def nb_load(qt):
                            if qt in issued or qt > 7:
                                return
                            issued.add(qt)
                            pis_ = [pi for pi, (a, b) in enumerate(pairs) if a == qt]
                            nb_, nbk_ = nbt[qt % 3], "d_nb%d" % (qt % 3)
                            DG(lambda e: e.dma_start(out=nb_[:, 0:len(pis_), :, :],
                                                     in_=D["c_nab"][l, pis_[0]:pis_[0] + len(pis_)].rearrange("a h k q -> k a h q")),
                               [], [nbk_])

                        def keys_fn(q0):
                            qt = q0 // 128
                            pis = [pi for pi, (a, b) in enumerate(pairs) if a == qt]
                            nb, nbk = nbt[qt % 3], "d_nb%d" % (qt % 3)
                            nb_load(qt)
                            nb_load(qt + 1)
                            nb_load(qt + 2)
                            out = []
                            for ii, pi in enumerate(pis):
                                out.append((pairs[pi][1], (lambda h, ii=ii: (nb[:, ii, h, :], nbk))))
                            out += [(8 + i, None) for i in range(4)]
                            return out
                        attention(ph, "d", qT, kT, vpad, lambda h: h, lambda h: h, qblocks, keys_fn, zT, "d_zT", 3)
                S.barrier()
                stage("D")
                if grp == groups[0] and l == layers[0]:
                    tap("brD", brT[:, 6:8, :].rearrange("p a b -> p (a b)"), [("brT", 3)])

                S.mute = "C" in _SKIP
                with ExitStack() as ph:
                    qTm = sb("r_qTm", [128, 2, 2, 1024], BF16, ph)
                    kTr = sb("r_kT", [128, 2, 1024], BF16, ph)
                    kdp = sb("r_kdp", [128, 4, 2, 128], BF16, ph)
                    vtk = sb("r_v", [128, 8, 256], BF16, ph)
                    zT = sb("r_zT", [128, 2, 1024], BF16, ph)
                    U = sb("r_U", [128, 8, 256], F32, ph)
                    Sin = sb("r_Sin", [128, 8, 256], F32, ph)
                    Sfin = sb("r_Sfin", [128, 256], F32, ph)
                    Sinb = sb("r_Sinb", [128, 256], BF16, ph)
                    qkm = sb("r_qkm", [128, 4, 128], BF16, ph)
                    qdm = sb("r_qdm", [128, 4, 2, 128], BF16, ph)
                    oacc = sb("r_oacc", [128, 8, 256], F32, ph)
                    tmpS = sb("r_tmpS", [128, 256], F32, ph)
                    R2 = int(os.environ.get("R2", "511"))
                    if R2 & 1:
                        G(lambda e: e.memset(qTm[:], 0.0), [], ["r_qTm"])
                        G(lambda e: e.memset(kdp[:], 0.0), [], ["r_kdp"])
                    w0, w0k = load_w(l, 1808, 2320)
                    w1, w1k = load_w(l, 2320, 2832)
                    for c in range(2):
                        for tb in range(2):
                            blk = slice(tb * 512, (tb + 1) * 512)
                            pb, pk = bank()
                            if R2 & 2:
                                proj_F(w0, w0k, c * 128, 128, tb, pb, pk)
                            for hf in (range(2) if R2 & 2 else []):
                                rows = slice(hf * 64, (hf + 1) * 64)
                                if hf == 0:
                                    A(lambda e, c=c, hf=hf, blk=blk, rows=rows, pb=pb: e.copy(out=qTm[rows, c, hf, blk], in_=pb[rows, :]),
                                      [pk], ["r_qTm"])
                                else:
                                    V(lambda e, c=c, hf=hf, blk=blk, rows=rows, pb=pb: e.tensor_copy(out=qTm[rows, c, hf, blk], in_=pb[rows, :]),
                                      [pk], ["r_qTm"])
                            pb, pk = bank()
                            if R2 & 4:
                                proj_F(w0, w0k, 256 + c * 128, 128, tb, pb, pk)
                                V(lambda e, c=c, blk=blk, pb=pb: e.tensor_copy(out=kTr[:, c, blk], in_=pb[:, :]), [pk], ["r_kT"])
                    for tt in range(8):
                        pb, pk = bank()
                        if R2 & 8:
                            proj_T(w0, w0k, 256, 512, tt, pb, pk)
                        for d in (range(2) if R2 & 8 else []):
                            for h in range(4):
                                hf = h % 2
                                V(lambda e, d=d, h=h, hf=hf, pb=pb: e.tensor_tensor(
                                    out=kdp[:, h, d, hf * 64:(hf + 1) * 64], in0=pb[:, h * 64:(h + 1) * 64],
                                    in1=kdt[:, h, d, :], op=ALU.mult), [pk, "kdt"], ["r_kdp"])
                        pb, pk = bank()
                        if R2 & 16:
                            proj_T(w1, w1k, 0, 256, tt, pb, pk)
                            A(lambda e, tt=tt, pb=pb: e.copy(out=vtk[:, tt, :], in_=pb[:, 0:256]), [pk], [("r_v", tt)])
                        pb, pk = bank()
                        for d in (range(2) if R2 & 32 else []):
                            for pr in range(2):
                                cs_ = slice((d * 2 + pr) * 64, (d * 2 + pr + 1) * 64)
                                for hf in range(2):
                                    h = 2 * pr + hf
                                    mm(pb[:, cs_], kdp[:, h, d, :], vtk[:, tt, h * 64:(h + 1) * 64], hf == 0, hf == 1,
                                       ["r_kdp", ("r_v", tt)], [pk])
                        if R2 & 32:
                            V(lambda e, tt=tt, pb=pb: e.tensor_copy(out=U[:, tt, :], in_=pb[:, 0:256]), [pk], [("r_U", tt)])
                    if R2 & 64:
                        zproj(w1, w1k, 256, zT, "r_zT")
                    prefetch_w(l, 768, 1280)
                    seqs = [(2 * b_, 2 * b_ + 1) for b_ in range(4)] if grp == 0 else [tuple(range(8))]
                    cdv = cdt[:].rearrange("p a d -> p (a d)")
                    FW, BW = slice(0, 128), slice(128, 256)
                    for si, tiles in enumerate(seqs if R2 & 128 else []):
                        first, lastt = tiles[0], tiles[-1]
                        if grp == 0:
                            G(lambda e, first=first: e.memset(Sin[:, first, FW], 0.0), [], [("r_Sin", "f", first)])
                            G(lambda e, lastt=lastt: e.memset(Sin[:, lastt, BW], 0.0), [], [("r_Sin", "b", lastt)])
                        else:
                            DS(lambda e: e.dma_start(out=Sin[:, 0, FW].rearrange("p (a v) -> p a v", a=2),
                                                     in_=D["sr"][l, 0].rearrange("(a b) k v -> (b k) a v", b=2)), [], [("r_Sin", "f", 0)])
                            DS(lambda e: e.dma_start(out=Sin[:, 7, BW].rearrange("p (a v) -> p a v", a=2),
                                                     in_=D["sr"][l, 1].rearrange("(a b) k v -> (b k) a v", b=2)), [], [("r_Sin", "b", 7)])
                        for tt in tiles:
                            dst = Sin[:, tt + 1, FW] if tt != lastt else Sfin[:, FW]
                            dk = ("r_Sin", "f", tt + 1) if tt != lastt else ("r_Sfin", "f")
                            V(lambda e, tt=tt: e.tensor_tensor(out=tmpS[:, FW], in0=Sin[:, tt, FW], in1=cdv[:, FW], op=ALU.mult),
                              [("r_Sin", "f", tt), "cdt"], [("r_tmpS", "f")])
                            V(lambda e, tt=tt, dst=dst: e.tensor_tensor(out=dst, in0=tmpS[:, FW], in1=U[:, tt, FW], op=ALU.add),
                              [("r_tmpS", "f"), ("r_U", tt)], [dk])
                        for tt in reversed(tiles):
                            dst = Sin[:, tt - 1, BW] if tt != first else Sfin[:, BW]
                            dk = ("r_Sin", "b", tt - 1) if tt != first else ("r_Sfin", "b")
                            G(lambda e, tt=tt: e.tensor_tensor(out=tmpS[:, BW], in0=Sin[:, tt, BW], in1=cdv[:, BW], op=ALU.mult),
                              [("r_Sin", "b", tt), "cdt"], [("r_tmpS", "b")])
                            G(lambda e, tt=tt, dst=dst: e.tensor_tensor(out=dst, in0=tmpS[:, BW], in1=U[:, tt, BW], op=ALU.add),
                              [("r_tmpS", "b"), ("r_U", tt)], [dk])
                        if grp == 0 and R2 & 256:
                            b_ = si
                            DS(lambda e, b_=b_: e.dma_start(out=D["nsr"][b_, l, 0].rearrange("(a b) k v -> (b k) a v", b=2),
                                                            in_=Sfin[:, FW].rearrange("p (a v) -> p a v", a=2)),
                               [("r_Sfin", "f")], [])
                            DS(lambda e, b_=b_: e.dma_start(out=D["nsr"][b_, l, 1].rearrange("(a b) k v -> (b k) a v", b=2),
                                                            in_=Sfin[:, BW].rearrange("p (a v) -> p a v", a=2)),
                               [("r_Sfin", "b")], [])
                    _m = int(os.environ.get("L3", "31"))
                    for tt in range(int(os.environ.get("L3N", "8")) if _CL >= 3 else 0):
                        ts_ = slice(tt * 128, (tt + 1) * 128)
                        pb, pk = bank()
                        for h in (range(4) if _m & 1 else []):
                            mm(pb[:, h * 128:(h + 1) * 128], kTr[:, h // 2, ts_], qTm[:, h // 2, h % 2, ts_], True, True,
                               ["r_kT", "r_qTm"], [pk])
                        if _m & 2:
                            V(lambda e, pb=pb: e.tensor_tensor(out=qkm[:].rearrange("p a b -> p (a b)"), in0=pb[:, :],
                                                               in1=dct[:].rearrange("p a b -> p (a b)"), op=ALU.mult),
                              [pk, "dct"], ["r_qkm"])
                        for d in (range(2) if _m & 4 else []):
                            G(lambda e, d=d, ts_=ts_: e.tensor_tensor(
                                out=qdm[:, :, d, :], in0=qTm[:, :, :, ts_].rearrange("p c f t -> p (c f) t"),
                                in1=rqt[:, d, :, :], op=ALU.mult), ["r_qTm", "rqt"], ["r_qdm"])
                        if _m & 8:
                            A(lambda e, tt=tt: e.copy(out=Sinb[:], in_=Sin[:, tt, :]), [("r_Sin", "f", tt), ("r_Sin", "b", tt)], ["r_Sinb"])
                        pb, pk = bank()
                        for h in (range(4) if _m & 16 else []):
                            pr = h // 2
                            hc = slice(h * 64, (h + 1) * 64)
                            mm(pb[:, hc], qkm[:, h, :], vtk[:, tt, hc], True, False, ["r_qkm", ("r_v", tt)], [pk])
                            mm(pb[:, hc], qdm[:, h, 0, :], Sinb[:, pr * 64:(pr + 1) * 64], False, False, ["r_qdm", "r_Sinb"], [pk])
                            mm(pb[:, hc], qdm[:, h, 1, :], Sinb[:, (2 + pr) * 64:(3 + pr) * 64], False, True, ["r_qdm", "r_Sinb"], [pk])
                        if _m & 16:
                            A(lambda e, tt=tt, pb=pb: e.copy(out=oacc[:, tt, :], in_=pb[:, 0:256]), [pk], [("r_oacc", tt)])
                    if _CL >= 4:
                        norm_gate_out(ph, "r", oacc, "r_oacc", grn, "grn", l, zT, "r_zT", 2)
                S.barrier()
                stage("C")
                if grp == groups[0] and l == layers[0]:
                    tap("brC", brT[:, 4:6, :].rearrange("p a b -> p (a b)"), [("brT", 2)])

                S.mute = "B" in _SKIP
                with ExitStack() as ph:
                  if _BL >= 1:
                      gx = [sb("g_x%d" % i, [128, 1024], F32, ph) for i in range(1)]
                      gy = [sb("g_y%d" % i, [128, 1024], F32, ph) for i in range(1)]
                      qkT = sb("g_qkT", [128, 4, 1024], F32, ph)
                      ktok = sb("g_ktok", [128, 8, 256], F32, ph)
                      vtok = sb("g_vtok", [128, 8, 256], F32, ph)
                      zT = sb("g_zT", [128, 2, 1024], BF16, ph)
                      ab = sb("g_ab", [128, 8, 16], F32, ph)
                      beta = sb("g_beta", [128, 8, 8], F32, ph)
                      nbeta = sb("g_nbeta", [128, 8, 8], F32, ph)
                      la = sb("g_la", [128, 8, 8], F32, ph)
                      gc = sb("g_gc", [128, 8, 8], F32, ph)
                      ngc = sb("g_ngc", [128, 8, 8], F32, ph)
                      egc = sb("g_egc", [128, 8, 8], F32, ph)
                      bege = sb("g_bege", [128, 8, 8], F32, ph)
                      oacc = sb("g_oacc", [128, 8, 256], F32, ph)
                      dg = sb("g_dg", [128, 4, 128], F32, ph)
                      Dm = sb("g_Dm", [128, 4, 128], F32, ph)
                      DT = sb("g_DT", [128, 4, 128], F32, ph)
                      Bm = [sb("g_B%d" % i, [128, 4, 128], F32, ph) for i in range(2)]
                      BTm = [sb("g_BT%d" % i, [128, 4, 128], F32, ph) for i in range(2)]
                      Ym = [sb("g_Y%d" % i, [128, 4, 128], F32, ph) for i in range(2)]
                      qkd = sb("g_qkd", [128, 4, 128], F32, ph)
                      vb = sb("g_vb", [128, 4, 64], F32, ph)
                      kbgp = sb("g_kbgp", [128, 4, 128], F32, ph)
                      kdcp = sb("g_kdcp", [128, 4, 128], F32, ph)
                      kds = sb("g_kds", [128, 4], F32, ph)
                      eg2 = sb("g_eg2", [128, 2], F32, ph)
                      wTm = sb("g_wTm", [128, 4, 128], F32, ph)

                      Sst = sb("g_S", [128, 2, 2, 64], F32, ph)
                      G(lambda e: e.memset(kbgp[:], 0.0), [], ["g_kbgp"])
                      G(lambda e: e.memset(kdcp[:], 0.0), [], ["g_kdcp"])
                      G(lambda e: e.memset(wTm[:], 0.0), [], ["g_wTm"])
                      wA, wAk = load_w(l, 768, 1280)
                      wB, wBk = load_w(l, 1280, 1552)
                      wC, wCk = load_w(l, 1552, 1808)
                      nseq = 4 if grp == 0 else 1
                      L = 1024 // nseq
                      for c6 in range(6):
                          xt, xk = gx[0], "g_x0"
                          yt, yk = gy[0], "g_y0"
                          wt, wk, a = (wA, wAk, c6 * 128) if c6 < 4 else (wB, wBk, (c6 - 4) * 128)
                          for tb in range(2):
                              pb, pk = bank()
                              proj_F(wt, wk, a, 128, tb, pb, pk)
                              A(lambda e, tb=tb, pb=pb, xt=xt: e.copy(out=xt[:, tb * 512:(tb + 1) * 512], in_=pb[:, :]), [pk], [xk])
                          x3 = xt[:].rearrange("p (s t) -> p s t", s=nseq)
                          y3 = yt[:].rearrange("p (s t) -> p s t", s=nseq)
                          V(lambda e, c6=c6, xt=xt, yt=yt: e.tensor_scalar(out=yt[:], in0=xt[:], scalar1=cwT[:, l, c6, 2:3],
                                                                           scalar2=None, op0=ALU.mult), [xk, "cwT"], [yk])
                          for jj in (0, 1, 3, 4):
                              dsh = jj - 2
                              lo, hi = max(0, -dsh), L - max(0, dsh)
                              V(lambda e, c6=c6, jj=jj, x3=x3, y3=y3, lo=lo, hi=hi, dsh=dsh: e.scalar_tensor_tensor(
                                  out=y3[:, :, lo:hi], in0=x3[:, :, lo + dsh:hi + dsh], scalar=cwT[:, l, c6, jj:jj + 1],
                                  in1=y3[:, :, lo:hi], op0=ALU.mult, op1=ALU.add), [xk, yk, "cwT"], [yk])
                          act(yt[:], yt[:], AF.Silu, [yk], [yk])
                          if c6 < 4:
                              for tb in range(2):
                                  blk = slice(tb * 512, (tb + 1) * 512)
                                  sqv = xt[:, 0:512]
                                  rsv = xt[:, 512:1024]
                                  act(sqv, yt[:, blk], AF.Square, [yk], [xk])
                                  pb, pk = bank()
                                  mm(pb[:, :], onesblk[:, :], sqv, True, True, ["onesblk", xk], [pk])
                                  act(rsv, pb[:, :], AF.Sqrt, [pk], [xk], bias=EPS, scale=1.0)
                                  V(lambda e, rsv=rsv: e.reciprocal(out=rsv, in_=rsv), [xk], [xk])
                                  if c6 < 2:
                                      V(lambda e, c6=c6, blk=blk, yt=yt, rsv=rsv: e.scalar_tensor_tensor(
                                          out=qkT[:, c6, blk], in0=yt[:, blk], scalar=0.125, in1=rsv, op0=ALU.mult,
                                          op1=ALU.mult), [yk, xk], [("g_qkT", c6)])
                                  else:
                                      V(lambda e, c6=c6, blk=blk, yt=yt, rsv=rsv: e.tensor_tensor(out=qkT[:, c6, blk], in0=yt[:, blk], in1=rsv,
                                                                                        op=ALU.mult), [yk, xk], [("g_qkT", c6)])
                          if c6 >= 2:
                              srcT = qkT[:, c6, :] if c6 < 4 else yt[:]
                              srck = ("g_qkT", c6) if c6 < 4 else yk
                              dst = ktok if c6 < 4 else vtok
                              dstk = "g_ktok" if c6 < 4 else "g_vtok"
                              cc = c6 % 2
                              for half in range(2):
                                  pb, pk = bank()
                                  for q in range(4):
                                      tt = half * 4 + q
                                      tr(pb[:, q * 128:(q + 1) * 128], srcT[:, tt * 128:(tt + 1) * 128], ident[:], [srck, "ident"], [pk])
                                  A(lambda e, half=half, pb=pb, dst=dst, cc=cc: e.copy(
                                      out=dst[:, half * 4:half * 4 + 4, cc * 128:(cc + 1) * 128],
                                      in_=pb[:, :].rearrange("p (a b) -> p a b", a=4)), [pk], [dstk])
                      zproj(wC, wCk, 0, zT, "g_zT")
                      if _GB >= 2:
                        pb, pk = bank()
                        for tt in range(8):
                            for k in range(8):
                                mm(pb[:, tt * 16:(tt + 1) * 16], hnT[:, k, tt * 128:(tt + 1) * 128], wB[:, k, 256:272], k == 0, k == 7,
                                   [wBk, "hnT"], [pk])
                        V(lambda e, pb=pb: e.tensor_copy(out=ab[:].rearrange("p a b -> p (a b)"), in_=pb[:, 0:128]), [pk], ["g_ab"])
                        act(beta[:], ab[:, :, 0:8], AF.Sigmoid, ["g_ab"], ["g_beta"])
                        V(lambda e: e.tensor_scalar(out=nbeta[:], in0=beta[:], scalar1=-1.0, scalar2=None, op0=ALU.mult),
                          ["g_beta"], ["g_nbeta"])
                        V(lambda e: e.tensor_tensor(out=la[:], in0=ab[:, :, 8:16], in1=dtb[:, l, :].unsqueeze(1).to_broadcast([128, 8, 8]),
                                                    op=ALU.add), ["g_ab", "dtb"], ["g_la"])
                        V(lambda e: e.tensor_scalar(out=la[:], in0=la[:], scalar1=30.0, scalar2=None, op0=ALU.min), ["g_la"], ["g_la"])
                        act(la[:], la[:], AF.Exp, ["g_la"], ["g_la"])
                        act(la[:], la[:], AF.Ln, ["g_la"], ["g_la"], bias=1.0, scale=1.0)
                        V(lambda e: e.tensor_tensor(out=la[:], in0=la[:], in1=nega[:, l, :].unsqueeze(1).to_broadcast([128, 8, 8]),
                                                    op=ALU.mult), ["g_la", "nega"], ["g_la"])
                        pb, pk = bank()
                        for tt in range(8):
                            for d in range(2):
                                mm(pb[:, tt * 8 + d * 4:tt * 8 + d * 4 + 4], masks[:, d, :], la[:, tt, d * 4:(d + 1) * 4], True, True,
                                   ["masks", "g_la"], [pk])
                        V(lambda e, pb=pb: e.tensor_copy(out=gc[:].rearrange("p a b -> p (a b)"), in_=pb[:, 0:64]), [pk], ["g_gc"])
                        V(lambda e: e.tensor_scalar(out=ngc[:], in0=gc[:], scalar1=-1.0, scalar2=None, op0=ALU.mult), ["g_gc"], ["g_ngc"])
                        act(egc[:], gc[:], AF.Exp, ["g_gc"], ["g_egc"])
                        V(lambda e: e.tensor_tensor(out=bege[:], in0=beta[:], in1=egc[:], op=ALU.mult), ["g_beta", "g_egc"], ["g_bege"])
                        tap("g_gc", gc[:].rearrange("p a b -> p (a b)"), ["g_gc"])
                        tap("g_qkT", qkT[:].rearrange("p a b -> p (a b)"), ["g_qkT"])
                      if _GB >= 3:
                        S.barrier()
                        wv = [w_[:].rearrange("p a b -> p (a b)").bitcast(F32) for w_ in wbs]

                        def v4(ap):
                            return ap.rearrange("p (h f) -> p h f", h=4)
                        sets = []
                        sets.append(dict(
                            dg=dg[:], Dm=Dm[:], DT=DT[:], B=[Bm[0][:], Bm[1][:]], BT=[BTm[0][:], BTm[1][:]], Y=[Ym[0][:], Ym[1][:]],
                            qkd=qkd[:], vb=vb[:], kbgp=kbgp[:], kdcp=kdcp[:], kds=kds[:], eg2=eg2[:], wTm=wTm[:],
                            qkz=gx[0][:].rearrange("p (c f t) -> p c f t", c=4, f=2),
                            usb=gy[0][:, 0:256], vnew=gy[0][:, 256:512], o2s=gy[0][:, 512:768], otmp=gy[0][:, 768:1024]))
                        kds1 = sb("g_kds1", [128, 4], F32, ph)
                        eg21 = sb("g_eg21", [128, 2], F32, ph)
                        vb1 = sb("g_vb1", [128, 4, 64], F32, ph)
                        DT1 = sb("g_DT1", [128, 4, 128], F32, ph)
                        sets.append(dict(
                            dg=v4(stg[1][:, 0:512]), Dm=v4(stg[1][:, 512:1024]), DT=DT1[:],
                            B=[v4(wv[0][:, 0:512]), v4(wv[0][:, 512:1024])], BT=[v4(wv[0][:, 1024:1536]), v4(wv[0][:, 1536:2048])],
                            Y=[v4(wv[1][:, 0:512]), v4(wv[1][:, 512:1024])],
                            qkz=wv[1][:, 1024:2048].rearrange("p (c f t) -> p c f t", c=4, f=2),
                            qkd=v4(wv[2][:, 0:512]), kbgp=v4(wv[2][:, 512:1024]), kdcp=v4(wv[2][:, 1024:1536]), wTm=v4(wv[2][:, 1536:2048]),
                            vb=vb1[:], kds=kds1[:], eg2=eg21[:],
                            usb=stg[0][:, 0:256], vnew=stg[0][:, 256:512], o2s=stg[0][:, 512:768], otmp=stg[0][:, 768:1024]))
                        G(lambda e: e.memset(gx[0][:], 0.0), [], ["q0_qkz"])
                        G(lambda e: e.memset(wv[1][:, 1024:2048], 0.0), [], ["q1_qkz"])
                        G(lambda e: e.memset(wv[2][:, 512:2048], 0.0), [], ["q1_kbgp", "q1_kdcp", "q1_wTm"])
                        G(lambda e: e.memset(oacc[:], 0.0), [], ["g_oacc"])
                        seqs = [(2 * b_, 2 * b_ + 1) for b_ in range(4)] if grp == 0 else [tuple(range(8))]

                        def quad(si_, d, tt, init, fin, sidx):
                            T_ = sets[si_]
                            kp = "q%d_" % si_
                            if si_ == 0:
                                kq = dict(kbgp="g_kbgp", kdcp="g_kdcp", wTm="g_wTm")
                            else:
                                kq = dict(kbgp="q1_kbgp", kdcp="q1_kdcp", wTm="q1_wTm")
                            kq = {**{n: kp + n for n in ("dg", "Dm", "DT", "B0", "B1", "BT0", "BT1", "Y0", "Y1", "qkd", "vb", "kds", "eg2",
                                                         "qkz", "usb", "vnew", "o2s", "otmp")}, **kq}
                            qrr = [0]

                            def qbank():
                                i = 4 * si_ + qrr[0] % 4
                                qrr[0] += 1
                                return PS[i], ("ps", i)
                            last = 127 if d == 0 else 0
                            ts_ = slice(tt * 128, (tt + 1) * 128)
                            u0 = d * 4
                            dgt, Dmt, DTt, Bt, BTt, Yt = T_["dg"], T_["Dm"], T_["DT"], T_["B"], T_["BT"], T_["Y"]
                            qkdt, vbt, kbgpt, kdcpt, kdst, eg2t, wTmt, qkzt = (T_["qkd"], T_["vb"], T_["kbgp"], T_["kdcp"], T_["kds"],
                                                                               T_["eg2"], T_["wTm"], T_["qkz"])
                            usbt, vnewt, o2st, otmpt = T_["usb"], T_["vnew"], T_["o2s"], T_["otmp"]
                            if init:
                                if grp == 0:
                                    V(lambda e: e.memset(Sst[:, d, :, :], 0.0), [], [("g_S", d)])
                                else:
                                    DS(lambda e: e.dma_start(out=Sst[:, d, :, :],
                                                             in_=D["sg"][l, d].rearrange("(a b) k v -> (b k) a v", b=2)), [], [("g_S", d)])
                            for h in range(4):
                                A(lambda e, h=h: e.mul(out=dgt[:, h, :], in_=ident[:], mul=gc[:, tt, u0 + h:u0 + h + 1]),
                                  ["ident", "g_gc"], [kq["dg"]])
                            psN, kN = qbank()
                            psP, kP = qbank()
                            for h in range(4):
                                hs = slice(h * 128, (h + 1) * 128)
                                mm(psN[:, hs], ones[:, :], dgt[:, h, :], True, False, ["ones", kq["dg"]], [kN])
                                mm(psN[:, hs], ident[:, :], masks[:, 4 + d, :], False, True, ["ident", "masks"], [kN])
                                mm(psP[:, hs], ones[:, :], dgt[:, h, :], True, False, ["ones", kq["dg"]], [kP])
                                mm(psP[:, hs], ident[:, :], masks[:, 2 + d, :], False, True, ["ident", "masks"], [kP])
                            yield
                            for h in range(4):
                                hs = slice(h * 128, (h + 1) * 128)
                                act(Dmt[:, h, :], psP[:, hs], AF.Exp, [kP, "g_gc"], [kq["Dm"]], bias=gc[:, tt, u0 + h:u0 + h + 1], scale=-1.0)
                                act(DTt[:, h, :], psN[:, hs], AF.Exp, [kN, "g_ngc"], [kq["DT"]], bias=ngc[:, tt, u0 + h:u0 + h + 1], scale=1.0)
                                act(kdst[:, h:h + 1], psN[:, h * 128 + last:h * 128 + last + 1], AF.Exp, [kN, "g_ngc"], [kq["kds"]],
                                    bias=ngc[:, tt, u0 + h:u0 + h + 1], scale=1.0)
                            for pr in range(2):
                                for hf in range(2):
                                    h = 2 * pr + hf
                                    A(lambda e, pr=pr, hf=hf, h=h: e.activation(
                                        out=eg2t[hf * 64:(hf + 1) * 64, pr:pr + 1],
                                        in_=psN[hf * 64:(hf + 1) * 64, h * 128 + last:h * 128 + last + 1], func=AF.Exp), [kN], [kq["eg2"]])
                            for hf in range(2):
                                rows = slice(hf * 64, (hf + 1) * 64)
                                V(lambda e, hf=hf, rows=rows: e.tensor_copy(out=qkzt[rows, :, hf, :], in_=qkT[rows, :, ts_]),
                                  ["g_qkT"], [kq["qkz"]])
                            psK, kK = qbank()
                            psQ, kQ = qbank()
                            for h in range(4):
                                hs = slice(h * 128, (h + 1) * 128)
                                kfull = qkT[:, 2 + h // 2, ts_]
                                mm(psK[:, hs], kfull, qkzt[:, 2 + h // 2, h % 2, :], True, True, ["g_qkT", kq["qkz"]], [kK])
                                mm(psQ[:, hs], kfull, qkzt[:, h // 2, h % 2, :], True, True, ["g_qkT", kq["qkz"]], [kQ])
                            yield
                            for h in range(4):
                                hs = slice(h * 128, (h + 1) * 128)
                                V(lambda e, h=h, hs=hs: e.scalar_tensor_tensor(
                                    out=Bt[0][:, h, :], in0=psK[:, hs], scalar=nbeta[:, tt, u0 + h:u0 + h + 1], in1=Dmt[:, h, :],
                                    op0=ALU.mult, op1=ALU.mult), [kK, "g_nbeta", kq["Dm"]], [kq["B0"]])
                            V(lambda e: e.tensor_tensor(out=qkdt.rearrange("p a b -> p (a b)"), in0=psQ[:, :],
                                                        in1=DTt.rearrange("p a b -> p (a b)"), op=ALU.mult), [kQ, kq["DT"]], [kq["qkd"]])
                            yield
                            pb, pk = qbank()
                            for h in range(4):
                                tr(pb[:, h * 128:(h + 1) * 128], Bt[0][:, h, :], ident[:], [kq["B0"], "ident"], [pk])
                            yield
                            V(lambda e, pb=pb: e.tensor_copy(out=BTt[0].rearrange("p a b -> p (a b)"), in_=pb[:, :]), [pk], [kq["BT0"]])
                            for h in range(4):
                                V(lambda e, pb=pb, h=h: e.tensor_tensor(out=Yt[0][:, h, :], in0=pb[:, h * 128:(h + 1) * 128], in1=ident[:],
                                                                        op=ALU.add), [pk, "ident"], [kq["Y0"]])
                            yield
                            cur = 0
                            for lev in range(1, 7):
                                nxt = 1 - cur
                                pb, pk = qbank()
                                for h in range(4):
                                    mm(pb[:, h * 128:(h + 1) * 128], BTt[cur][:, h, :], Bt[cur][:, h, :], True, True,
                                       [kq["BT%d" % cur], kq["B%d" % cur]], [pk])
                                if lev < 6:
                                    pb2, pk2 = qbank()
                                    for h in range(4):
                                        mm(pb2[:, h * 128:(h + 1) * 128], Bt[cur][:, h, :], BTt[cur][:, h, :], True, True,
                                           [kq["BT%d" % cur], kq["B%d" % cur]], [pk2])
                                yield
                                A(lambda e, pb=pb, nxt=nxt: e.copy(out=Bt[nxt].rearrange("p a b -> p (a b)"), in_=pb[:, :]),
                                  [pk], [kq["B%d" % nxt]])
                                if lev < 6:
                                    V(lambda e, pb2=pb2, nxt=nxt: e.tensor_copy(out=BTt[nxt].rearrange("p a b -> p (a b)"), in_=pb2[:, :]),
                                      [pk2], [kq["BT%d" % nxt]])
                                pb3, pk3 = qbank()
                                for h in range(4):
                                    mm(pb3[:, h * 128:(h + 1) * 128], Bt[nxt][:, h, :], Yt[cur][:, h, :], True, True,
                                       [kq["B%d" % nxt], kq["Y%d" % cur]], [pk3])
                                yield
                                V(lambda e, pb3=pb3, nxt=nxt, cur=cur: e.tensor_tensor(
                                    out=Yt[nxt].rearrange("p a b -> p (a b)"), in0=pb3[:, :],
                                    in1=Yt[cur].rearrange("p a b -> p (a b)"), op=ALU.add), [pk3, kq["Y%d" % cur]], [kq["Y%d" % nxt]])
                                cur = nxt
                                yield
                            Yf, Yk = Yt[cur], kq["Y%d" % cur]
                            k4 = ktok[:, tt, :].rearrange("p (h d) -> p h d", h=4)
                            G(lambda e: e.tensor_tensor(out=vbt, in0=vtok[:, tt, :].rearrange("p (h d) -> p h d", h=4),
                                                        in1=beta[:, tt, u0:u0 + 4].unsqueeze(2).to_broadcast([128, 4, 64]), op=ALU.mult),
                              ["g_vtok", "g_beta"], [kq["vb"]])
                            for hf in range(2):
                                pc_ = slice(hf * 64, hf * 64 + 64)
                                G(lambda e, hf=hf, pc_=pc_: e.tensor_tensor(
                                    out=kbgpt[:, hf::2, pc_], in0=k4[:, hf::2, :],
                                    in1=bege[:, tt, u0 + hf:u0 + 4:2].unsqueeze(2).to_broadcast([128, 2, 64]), op=ALU.mult),
                                  ["g_ktok", "g_bege"], [kq["kbgp"]])
                                V(lambda e, hf=hf, pc_=pc_: e.tensor_tensor(
                                    out=kdcpt[:, hf::2, pc_], in0=k4[:, hf::2, :],
                                    in1=kdst[:, hf::2].unsqueeze(2).to_broadcast([128, 2, 64]), op=ALU.mult),
                                  ["g_ktok", kq["kds"]], [kq["kdcp"]])
                            pbu, pku = qbank()
                            for h in range(4):
                                mm(pbu[:, h * 64:(h + 1) * 64], Yf[:, h, :], vbt[:, h, :], True, True, [Yk, kq["vb"]], [pku])
                            pbw, pkw = qbank()
                            for pr in range(2):
                                for hf in range(2):
                                    h = 2 * pr + hf
                                    mm(pbw[:, pr * 128:(pr + 1) * 128], kbgpt[:, h, :], Yf[:, h, :], hf == 0, hf == 1, [kq["kbgp"], Yk], [pkw])
                            yield
                            A(lambda e: e.copy(out=usbt, in_=pbu[:, 0:256]), [pku], [kq["usb"]])
                            for pr in range(2):
                                for hf in range(2):
                                    rows = slice(hf * 64, (hf + 1) * 64)
                                    V(lambda e, pr=pr, hf=hf, rows=rows: e.tensor_copy(out=wTmt[rows, 2 * pr + hf, :],
                                                                                       in_=pbw[rows, pr * 128:(pr + 1) * 128]), [pkw], [kq["wTm"]])
                            yield
                            pbv, pkv = qbank()
                            pbo, pko = qbank()
                            for h in range(4):
                                hc = slice(h * 64, (h + 1) * 64)
                                mm(pbv[:, hc], wTmt[:, h, :], Sst[:, d, h // 2, :], True, True, [kq["wTm"], ("g_S", d)], [pkv])
                                mm(pbo[:, hc], qkzt[:, h // 2, h % 2, :], Sst[:, d, h // 2, :], True, True, [kq["qkz"], ("g_S", d)], [pko])
                            yield
                            V(lambda e: e.tensor_tensor(out=vnewt, in0=usbt, in1=pbv[:, 0:256], op=ALU.subtract), [kq["usb"], pkv], [kq["vnew"]])
                            pb2, pk2 = qbank()
                            for h in range(4):
                                hc = slice(h * 64, (h + 1) * 64)
                                mm(pb2[:, hc], qkdt[:, h, :], vnewt[:, hc], True, True, [kq["qkd"], kq["vnew"]], [pk2])
                            pbs, pks = qbank()
                            for pr in range(2):
                                for hf in range(2):
                                    h = 2 * pr + hf
                                    mm(pbs[:, pr * 64:(pr + 1) * 64], kdcpt[:, h, :], vnewt[:, h * 64:(h + 1) * 64], hf == 0, hf == 1,
                                       [kq["kdcp"], kq["vnew"]], [pks])
                            yield
                            A(lambda e: e.copy(out=o2st, in_=pb2[:, 0:256]), [pk2], [kq["o2s"]])
                            for h in range(4):
                                hc = slice(h * 64, (h + 1) * 64)
                                V(lambda e, h=h, hc=hc: e.scalar_tensor_tensor(
                                    out=otmpt[:, hc], in0=pbo[:, hc], scalar=egc[:, tt, u0 + h:u0 + h + 1], in1=o2st[:, hc],
                                    op0=ALU.mult, op1=ALU.add), [pko, "g_egc", kq["o2s"]], [kq["otmp"]])
                            G(lambda e: e.tensor_tensor(out=oacc[:, tt, :], in0=oacc[:, tt, :], in1=otmpt, op=ALU.add),
                              [("g_oacc", tt), kq["otmp"]], [("g_oacc", tt)])
                            for pr in range(2):
                                V(lambda e, pr=pr: e.scalar_tensor_tensor(
                                    out=Sst[:, d, pr, :], in0=Sst[:, d, pr, :], scalar=eg2t[:, pr:pr + 1],
                                    in1=pbs[:, pr * 64:(pr + 1) * 64], op0=ALU.mult, op1=ALU.add),
                                  [("g_S", d), kq["eg2"], pks], [("g_S", d)])
                            if fin and grp == 0:
                                DS(lambda e: e.dma_start(out=D["nsg"][sidx, l, d].rearrange("(a b) k v -> (b k) a v", b=2),
                                                         in_=Sst[:, d, :, :]), [("g_S", d)], [])
                            yield

                        sched = [[], []]
                        for d in range(2):
                            for sidx, tiles in enumerate(seqs):
                                order = tiles if d == 0 else tuple(reversed(tiles))
                                for qi, tt in enumerate(order):
                                    sched[d].append((tt, qi == 0, qi == len(order) - 1, sidx))
                        for (f_, b_) in zip(sched[0][:_GBQ], sched[1][:_GBQ]):
                            gens = [quad(0, 0, *f_), quad(1, 1, *b_)]
                            alive = [True, True]
                            while any(alive):
                                for gi in range(2):
                                    if alive[gi]:
                                        try:
                                            next(gens[gi])
                                        except StopIteration:
                                            alive[gi] = False
                      S.mute = False
                      if _GB >= 4:
                        norm_gate_out(ph, "g", oacc, "g_oacc", ggn, "ggn", l, zT, "g_zT", 1)
                S.barrier()
                stage("B")
                if grp == groups[0] and l == layers[0]:
                    tap("brB", brT[:, 2:4, :].rearrange("p a b -> p (a b)"), [("brT", 1)])

                S.mute = "M" in _SKIP
                with ExitStack() as ph:
                    mT = sb("m_T", [128, 8, 1024], BF16, ph)
                    wmg = [sb("m_wg%d" % i, [128, 4, 8, 128], BF16, ph) for i in range(2)]
                    wbr = [sb("m_wb%d" % i, [128, 4, 2, 128], BF16, ph) for i in range(2)]
                    gts = [sb("m_gt%d" % i, [128, 512], BF16, ph) for i in range(2)]
                    acc = sb("m_acc", [128, 512], F32, ph)
                    tmpm = sb("m_tmp", [128, 512], F32, ph)
                    gi = 0
                    for dc in range(8):
                        wg_t, wgk = wmg[dc % 2], "m_wg%d" % (dc % 2)
                        wb_t, wbk = wbr[dc % 2], "m_wb%d" % (dc % 2)
                        for n in range(4):
                            c0 = 3856 + n * 1024 + dc * 128
                            DG(lambda e, n=n, c0=c0, wg_t=wg_t: e.dma_start(
                                out=wg_t[:, n, :, :], in_=D["w_in"][l, :, c0:c0 + 128].rearrange("(k p) c -> p k c", p=128)),
                               [], [wgk])
                            DG(lambda e, n=n, dc=dc, wb_t=wb_t: e.dma_start(
                                out=wb_t[:, n, :, :],
                                in_=D["w_branch"][l, n, :, dc * 128:(dc + 1) * 128].rearrange("(k p) c -> p k c", p=128)),
                               [], [wbk])
                        for tb in range(2):
                            blk = slice(tb * 512, (tb + 1) * 512)
                            for n in range(4):
                                pb, pk = bank()
                                for k in range(8):
                                    mm(pb[:, :], wg_t[:, n, k, :], hnT[:, k, blk], k == 0, k == 7, [wgk, "hnT"], [pk])
                                gt, gtk = gts[gi % 2], "m_gt%d" % (gi % 2)
                                gi += 1
                                act(gt[:], pb[:, :], AF.Sigmoid, [pk], [gtk])
                                pb2, pk2 = bank()
                                for kk in range(2):
                                    mm(pb2[:, :], wb_t[:, n, kk, :], brT[:, n * 2 + kk, blk], kk == 0, kk == 1, [wbk, ("brT", n)], [pk2])
                                if n == 0:
                                    V(lambda e, pb2=pb2, gt=gt: e.tensor_tensor(out=acc[:], in0=pb2[:, :], in1=gt[:], op=ALU.mult),
                                      [pk2, gtk], ["m_acc"])
                                else:
                                    V(lambda e, pb2=pb2, gt=gt: e.tensor_tensor(out=tmpm[:], in0=pb2[:, :], in1=gt[:], op=ALU.mult),
                                      [pk2, gtk], ["m_tmp"])
                                    if n < 3:
                                        V(lambda e: e.tensor_tensor(out=acc[:], in0=acc[:], in1=tmpm[:], op=ALU.add),
                                          ["m_acc", "m_tmp"], ["m_acc"])
                                    else:
                                        V(lambda e, dc=dc, blk=blk: e.tensor_tensor(out=mT[:, dc, blk], in0=acc[:], in1=tmpm[:], op=ALU.add),
                                          ["m_acc", "m_tmp"], [("m_T", dc)])
                    for half in range(2):
                        i = wcnt[0] % 3
                        wcnt[0] += 1
                        wo, wok = wbs[i], "wb%d" % i
                        DG(lambda e, half=half, wo=wo: e.dma_start(
                            out=wo[:], in_=D["w_out"][l, :, half * 512:(half + 1) * 512].rearrange("(k p) n -> p k n", p=128)),
                           [], [wok])
                        for q in range(4):
                            oc = half * 4 + q
                            for tb in range(2):
                                blk = slice(tb * 512, (tb + 1) * 512)
                                pb, pk = bank()
                                for k in range(8):
                                    mm(pb[:, :], wo[:, k, q * 128:(q + 1) * 128], mT[:, k, blk], k == 0, k == 7, [wok, ("m_T", k)], [pk])
                                V(lambda e, oc=oc, blk=blk, pb=pb: e.scalar_tensor_tensor(
                                    out=hT[:, oc, blk], in0=pb[:, :], scalar=modT[:, l, 16 + oc, j:j + 1], in1=hT[:, oc, blk],
                                    op0=ALU.mult, op1=ALU.add), [pk, "modT", "hT"], ["hT"])
                S.barrier()
                stage("merge")
                if grp == groups[0] and l == layers[0]:
                    tap("hT1", hT[:].rearrange("p a b -> p (a b)"), ["hT"])

            S.mute = False
            with ExitStack() as ph:
                ynT = sb("f_yn", [128, 8, 1024], F32, ph)
                rmsnorm_T(ph, lambda k: fnT[:, k:k + 1], lambda k: None,
                          lambda k, tb: ynT[:, k, tb * 512:(tb + 1) * 512], "f_yn")
                for tt in range(8):
                    s_t, sk = stg[scnt[0] % 2], "stg%d" % (scnt[0] % 2)
                    scnt[0] += 1
                    for half in range(2):
                        pb, pk = bank()
                        for q in range(4):
                            k = half * 4 + q
                            tr(pb[:, q * 128:(q + 1) * 128], ynT[:, k, tt * 128:(tt + 1) * 128], ident[:], ["f_yn", "ident"], [pk])
                        if half == 0:
                            A(lambda e, pb=pb, s_t=s_t: e.copy(out=s_t[:, 0:512], in_=pb[:, :]), [pk], [sk])
                        else:
                            V(lambda e, pb=pb, s_t=s_t: e.tensor_copy(out=s_t[:, 512:1024], in_=pb[:, :]), [pk], [sk])
                    DS(lambda e, tt=tt, s_t=s_t: e.dma_start(out=yout[tt * 128:(tt + 1) * 128, :], in_=s_t[:]), [sk], [])
            S.barrier()
        try:
            run_groups()
        except _Stop:
            S.barrier()
        with nc.allow_non_contiguous_dma(reason="small transposed parameter loads"):
            stats = S.emit()
    return nc, stats


_CACHE = {}


def _in_maps(inp):
    f = lambda a: np.ascontiguousarray(np.asarray(a, dtype=np.float32))
    cst = _consts(f(inp["na_bias"]))
    shared = dict(
        w_ada=f(inp["w_ada"]), b_ada=f(inp["b_ada"]), norm_g=f(inp["norm_g"]), w_in=f(inp["w_in"]), conv_w=f(inp["conv_w"]),
        a_log=f(inp["gdn_a_log"]).reshape(2, 8), dt_bias=f(inp["gdn_dt_bias"]).reshape(2, 8), gdn_norm=f(inp["gdn_norm"]),
        q_norm=f(inp["attn_q_norm"]), k_norm=f(inp["attn_k_norm"]), ret_norm=f(inp["ret_norm"]),
        w_branch=f(inp["w_branch"]), w_out=f(inp["w_out"]), final_norm=f(inp["final_norm"]).reshape(1, 1024), **cst)
    xp, xs = f(inp["x_prompt"]), f(inp["x_sample"])
    maps = []
    for c in range(8):
        m = dict(shared)
        m["xp"] = xp[4 * c:4 * c + 4].reshape(1024, 1024)
        m["xs"] = xs[c]
        m["cak"] = f(inp["cache_attn_k"][c]).reshape(2, 512, 128)
        m["cav"] = f(inp["cache_attn_v"][c]).reshape(2, 512, 128)
        m["cnk"] = f(inp["cache_na_k"][c]).reshape(2, 512, 256)
        m["cnv"] = f(inp["cache_na_v"][c]).reshape(2, 512, 256)
        m["sg"] = f(inp["state_gdn"][c])
        m["sr"] = f(inp["state_ret"][c])
        m["cond"] = np.stack([f(inp["c_ctx"]), f(inp["c"][c])])
        maps.append(m)
    return maps


def kernel(**inputs):
    if "nc" not in _CACHE:
        _CACHE["nc"] = build()[0]
    nc = _CACHE["nc"]
    maps = _in_maps(inputs)
    res = run_bass_kernel_spmd(nc, maps, core_ids=list(range(8)))
    R = res.results
    cat = lambda n: np.concatenate([np.asarray(r[n]) for r in R], axis=0)
    y_prompt = cat("yp").reshape(32, 256, 1024)
    y_sample = np.stack([np.asarray(r["ys"]) for r in R])
    nak = cat("nak").reshape(32, 2, 256, 2, 64)
    nav = cat("nav").reshape(32, 2, 256, 2, 64)
    nnk = cat("nnk").reshape(32, 2, 256, 4, 64)
    nnv = cat("nnv").reshape(32, 2, 256, 4, 64)
    nsg = cat("nsg")
    nsr = cat("nsr")
    return tuple(np.ascontiguousarray(a, dtype=np.float32) for a in (y_prompt, y_sample, nak, nav, nnk, nnv, nsg, nsr))
```

```python
import numpy as np
import concourse.bass as bass
import concourse.mybir as mybir
from concourse.bass_utils import run_bass_kernel_spmd
from contextlib import ExitStack

F32 = mybir.dt.float32
BF16 = mybir.dt.bfloat16
ALU = mybir.AluOpType
AF = mybir.ActivationFunctionType
AX = mybir.AxisListType
EPS = 1e-6
NEG = -30000.0
BIG = 1.0e5


import types
import os
_CL = int(os.environ.get('CLEVEL', '9'))
_BL = int(os.environ.get('BLEVEL', '9'))
_GB = int(os.environ.get('GB', '9'))
_GBQ = int(os.environ.get('GBQ', '99'))
_GBS = int(os.environ.get('GBS', '9'))
_SKIP = os.environ.get('SKIP', '')
_STRICT = bool(int(os.environ.get('STRICT', '0')))


def _freeze(fn, _depth=0):
    if fn is None or fn.__closure__ is None:
        return fn
    cells = []
    for c in fn.__closure__:
        try:
            v = c.cell_contents
        except ValueError:
            cells.append(c)
            continue
        if isinstance(v, types.FunctionType) and v.__closure__ is not None and _depth < 3:
            v = _freeze(v, _depth + 1)
        cells.append(types.CellType(v))
    return types.FunctionType(fn.__code__, fn.__globals__, fn.__name__, fn.__defaults__, tuple(cells))


class _Op:
    __slots__ = ("fn", "waits", "dma", "dma_n")

    def __init__(self, fn, waits, dma, dma_n):
        self.fn, self.waits, self.dma, self.dma_n = fn, waits, dma, dma_n


class Sched:
    KD = 8
    BLK = {"pe": "tensor", "act": "scalar", "dve": "vector", "pool": "gpsimd", "sp": "sync"}

    def __init__(self, nc, es):
        self.nc = nc
        self.ops = {e: [] for e in self.BLK}
        self.state = {}
        self.ndma = {e: 0 for e in self.BLK}
        self.seen_c = {e: {} for e in self.BLK}
        self.seen_d = {e: set() for e in self.BLK}
        self.last = {}
        self.csem = {e: es.enter_context(nc.semaphore("c_" + e)) for e in ("pe", "act", "dve", "pool")}
        self.dsem = {e: [es.enter_context(nc.semaphore("d_%s%d" % (e, i))) for i in range(self.KD)]
                     for e in ("sp", "pool")}

    def _split(self, key):
        if isinstance(key, tuple):
            return key[0], key[1:]
        return key, None

    def _recs(self, key):
        name, sub = self._split(key)
        d = self.state.get(name)
        if not d:
            return []
        if sub is None:
            return list(d.values())
        out = []
        if sub in d:
            out.append(d[sub])
        if None in d:
            out.append(d[None])
        return out

    def _filter(self, eng, raw, other, dma):
        waits = []
        for d in sorted(raw | other):
            if d[0] == "c":
                if d[1] == eng and not dma:
                    if eng == "pe" or (d not in raw and not _STRICT):
                        continue
                if self.seen_c[eng].get(d[1], -1) >= d[2]:
                    continue
                self.seen_c[eng][d[1]] = d[2]
                waits.append(d)
            else:
                if d in self.seen_d[eng]:
                    continue
                self.seen_d[eng].add(d)
                waits.append(d)
        best, fin = {}, []
        for w in waits:
            if w[0] == "c":
                if w[1] not in best or best[w[1]][2] < w[2]:
                    best[w[1]] = w
            else:
                fin.append(w)
        fin.extend(best.values())
        return fin

    mute = False

    def op(self, eng, fn, reads=(), writes=(), dma=False):
        if self.mute:
            return None
        fn = _freeze(fn)
        idx = len(self.ops[eng])
        raw, other = set(), set()
        for key in reads:
            for rec in self._recs(key):
                if rec[0] is not None:
                    raw.add(rec[0])
        for key in writes:
            for rec in self._recs(key):
                if rec[0] is not None:
                    other.add(rec[0])
                other.update(rec[1])
        if dma:
            n = self.ndma[eng]
            self.ndma[eng] += 1
            ev = ("d", eng, n)
            self.last[("d", eng, n % self.KD)] = ev
        else:
            n = None
            ev = ("c", eng, idx)
            self.last[("c", eng)] = ev
        fin = self._filter(eng, raw, other, dma)
        self.ops[eng].append(_Op(fn, fin, dma, n))
        for key in reads:
            name, sub = self._split(key)
            self.state.setdefault(name, {}).setdefault(sub, [None, []])[1].append(ev)
        for key in writes:
            name, sub = self._split(key)
            d = self.state.setdefault(name, {})
            if sub is None:
                d.clear()
            d[sub] = [ev, []]
        return ev

    def barrier(self):
        evs = set(self.last.values())
        for e in self.BLK:
            fin = self._filter(e, set(evs), set(), True)
            if fin:
                self.ops[e].append(_Op(None, fin, False, None))
        self.state = {}

    def emit(self):
        for e in ("sp", "pool"):
            n = self.ndma[e]
            if n:
                self.ops[e].append(_Op(None, [("d", e, i) for i in range(max(0, n - self.KD), n)], False, None))
        waited = {e: set() for e in self.BLK}
        for e, ops in self.ops.items():
            for o in ops:
                for w in o.waits:
                    if w[0] == "c":
                        waited[w[1]].add(w[2])
        val = {}
        for e in self.BLK:
            for rank, idx in enumerate(sorted(waited[e])):
                val[(e, idx)] = rank + 1
        KD = self.KD
        self.maxval = {e: len(waited[e]) for e in self.BLK}
        self.maxdma = {e: 16 * ((self.ndma[e] - 1) // KD + 1) for e in ("sp", "pool")}
        if os.environ.get("SEMDBG"):
            print("SEM max values", self.maxval, self.maxdma, flush=True)
        with self.nc.Block() as block:
            for e in self.BLK:
                def body(engine, e=e):
                    for idx, o in enumerate(self.ops[e]):
                        for w in o.waits:
                            if w[0] == "c":
                                engine.wait_ge(self.csem[w[1]], val[(w[1], w[2])])
                            else:
                                engine.wait_ge(self.dsem[w[1]][w[2] % KD], 16 * (w[2] // KD + 1))
                        if o.fn is None:
                            continue
                        if o.dma:
                            n = o.dma_n
                            if n >= KD:
                                engine.wait_ge(self.dsem[e][n % KD], 16 * (n // KD))
                            o.fn(engine).then_inc(self.dsem[e][n % KD], 16)
                        else:
                            ins = o.fn(engine)
                            if idx in waited[e]:
                                ins.then_inc(self.csem[e], 1)
                getattr(block, self.BLK[e])(body)
        return {e: len(self.ops[e]) for e in self.BLK}


def _na_pairs():
    t = np.arange(1024)
    r, c = t // 64, t % 64
    rs = np.clip(r - 4, 0, 8)
    cs = np.clip(c - 8, 0, 48)
    valid = ((r[None, :] >= rs[:, None]) & (r[None, :] < rs[:, None] + 8) &
             (c[None, :] >= cs[:, None]) & (c[None, :] < cs[:, None] + 16))
    dr = r[None, :] - r[:, None] + 7
    dc = np.clip(c[None, :] - c[:, None] + 15, 0, 30)
    pairs = []
    for qt in range(8):
        for kt in range(8):
            if valid[qt * 128:(qt + 1) * 128, kt * 128:(kt + 1) * 128].any():
                pairs.append((qt, kt))
    return valid, dr, dc, pairs


_NA = _na_pairs()
NPAIR = len(_NA[3])


def _consts(na_bias):
    c = {}
    c["c_ident"] = np.eye(128, dtype=np.float32)
    p = np.arange(128)[:, None]
    f = np.arange(128)[None, :]
    m = np.zeros((6, 128, 128), np.float32)
    m[0] = (p <= f)
    m[1] = (p >= f)
    m[2] = np.where(f < p, 0.0, BIG)
    m[3] = np.where(f > p, 0.0, BIG)
    m[4] = np.where(f >= p, 0.0, -BIG)
    m[5] = np.where(f <= p, 0.0, -BIG)
    c["c_masks"] = m
    h = np.arange(4, dtype=np.float64)
    lgf = np.log1p(-np.exp2(-(5.0 + h)))
    lgb = np.log1p(-np.exp2(-(5.5 + h)))
    j = np.arange(128, dtype=np.float64)
    dct = np.zeros((4, 128, 128), np.float64)
    for hh in range(4):
        d = j[None, :] - j[:, None]
        dct[hh] = np.where(d > 0, np.exp(np.maximum(d, 0) * lgf[hh]), 0.0) + \
            np.where(d < 0, np.exp(np.maximum(-d, 0) * lgb[hh]), 0.0) + np.where(d == 0, 2.0, 0.0)
    c["c_dct"] = (dct * 0.125).astype(np.float32)
    rqt = np.zeros((2, 4, 128), np.float64)
    kdt = np.zeros((128, 4, 2, 64), np.float64)
    cdt = np.zeros((128, 4, 64), np.float64)
    for hh in range(4):
        rqt[0, hh] = np.exp((j + 1.0) * lgf[hh]) * 0.125
        rqt[1, hh] = np.exp((128.0 - j) * lgb[hh]) * 0.125
        kdt[:, hh, 0, :] = np.exp((127.0 - j) * lgf[hh])[:, None]
        kdt[:, hh, 1, :] = np.exp(j * lgb[hh])[:, None]
        hf_, pr_ = hh % 2, hh // 2
        cdt[hf_ * 64:(hf_ + 1) * 64, 0 * 2 + pr_, :] = np.exp(128.0 * lgf[hh])
        cdt[hf_ * 64:(hf_ + 1) * 64, 1 * 2 + pr_, :] = np.exp(128.0 * lgb[hh])
    c["c_rqt"] = rqt.astype(np.float32)
    c["c_kdt"] = kdt.astype(np.float32)
    c["c_cdt"] = cdt.astype(np.float32)
    t = np.arange(1024)
    row = (t // 64).astype(np.float32)
    col = (t % 64).astype(np.float32)
    inv = (10000.0 ** (-np.arange(16, dtype=np.float32) / 16)).astype(np.float32)
    ar = row[:, None] * inv[None, :]
    ac = col[:, None] * inv[None, :]
    cc = np.concatenate([np.cos(ar), np.cos(ar), np.cos(ac), np.cos(ac)], axis=1)
    ss = np.concatenate([-np.sin(ar), np.sin(ar), -np.sin(ac), np.sin(ac)], axis=1)
    c["c_rope"] = np.stack([np.tile(cc, (1, 6)), np.tile(ss, (1, 6))]).astype(np.float32)
    valid, dr, dc, pairs = _NA
    nab = np.empty((2, NPAIR, 4, 128, 128), np.float32)
    for pi, (qt, kt) in enumerate(pairs):
        qs = slice(qt * 128, (qt + 1) * 128)
        ks = slice(kt * 128, (kt + 1) * 128)
        v = valid[qs, ks].T
        g = na_bias[:, :, dr[qs, ks].T, dc[qs, ks].T]
        nab[:, pi] = np.where(v[None, None], g, np.float32(NEG))
    c["c_nab"] = nab
    return c


IN_SHAPES = dict(
    xp=[1024, 1024], xs=[1024, 1024], cak=[2, 512, 128], cav=[2, 512, 128], cnk=[2, 512, 256], cnv=[2, 512, 256],
    sg=[2, 2, 4, 64, 64], sr=[2, 2, 4, 64, 64], cond=[2, 1024],
    w_ada=[2, 1024, 3072], b_ada=[2, 3072], norm_g=[2, 1024], w_in=[2, 1024, 7952], conv_w=[2, 5, 768],
    a_log=[2, 8], dt_bias=[2, 8], gdn_norm=[2, 64], q_norm=[2, 64], k_norm=[2, 64], ret_norm=[2, 64],
    w_branch=[2, 4, 256, 1024], w_out=[2, 1024, 1024], final_norm=[1, 1024],
    c_ident=[128, 128], c_masks=[6, 128, 128], c_dct=[4, 128, 128], c_rqt=[2, 4, 128], c_kdt=[128, 4, 2, 64],
    c_cdt=[128, 4, 64], c_rope=[2, 1024, 384], c_nab=[2, NPAIR, 4, 128, 128])
OUT_SHAPES = dict(yp=[1024, 1024], ys=[1024, 1024], nak=[4, 2, 256, 128], nav=[4, 2, 256, 128],
                  nnk=[4, 2, 256, 256], nnv=[4, 2, 256, 256], nsg=[4, 2, 2, 4, 64, 64], nsr=[4, 2, 2, 4, 64, 64])


class _Stop(Exception):
    pass


def build(taps=None, groups=(0, 1), layers=(0, 1), stop_after=None):
    taps = taps or {}
    nc = bass.Bass("TRN2", target_bir_lowering=False)
    D = {}
    for n, s in IN_SHAPES.items():
        D[n] = nc.dram_tensor(n, list(s), F32, kind="ExternalInput").ap()
    for n, s in OUT_SHAPES.items():
        D[n] = nc.dram_tensor(n, list(s), F32, kind="ExternalOutput").ap()
    for n, s in taps.items():
        D["tap_" + n] = nc.dram_tensor("tap_" + n, list(s), F32, kind="ExternalOutput").ap()

    with ExitStack() as es:
        S = Sched(nc, es)

        uid = [0]

        def sb(name, shape, dt=F32, st=es):
            uid[0] += 1
            return st.enter_context(nc.sbuf_tensor("%s_%d" % (name, uid[0]), list(shape), dt))

        PS = [es.enter_context(nc.psum_tensor("ps%d" % i, [128, 512], F32)) for i in range(8)]
        rr = [0]

        def bank():
            i = 4 + rr[0] % 4
            rr[0] += 1
            return PS[i], ("ps", i)

        def V(fn, r, w): return S.op("dve", fn, r, w)
        def A(fn, r, w): return S.op("act", fn, r, w)
        def G(fn, r, w): return S.op("pool", fn, r, w)
        def P(fn, r, w): return S.op("pe", fn, r, w)
        def DS(fn, r, w): return S.op("sp", fn, r, w, dma=True)
        def DG(fn, r, w): return S.op("pool", fn, r, w, dma=True)

        def mm(out, lhsT, rhs, start, stop, r, w):
            P(lambda e: e.matmul(out, lhsT=lhsT, rhs=rhs, start=start, stop=stop), r, w)

        def tr(out, in_, idt, r, w):
            P(lambda e: e.transpose(out, in_, idt), r, w)

        def act(out, in_, func, r, w, bias=0.0, scale=1.0):
            A(lambda e: e.activation(out=out, in_=in_, func=func, bias=bias, scale=scale), r, w)

        def tap(name, ap, reads):
            if name in taps and name not in os.environ.get("NOTAP", "").split(","):
                DG(lambda e: e.dma_start(out=D["tap_" + name], in_=ap), reads, [])

        ident = sb("ident", [128, 128])
        identb = sb("identb", [128, 128], BF16)
        ones = sb("ones", [128, 128])
        onesblk = sb("onesblk", [128, 128])
        onespad = sb("onespad", [128, 2, 128], BF16)
        masks = sb("masks", [128, 6, 128])
        dct = sb("dct", [128, 4, 128])
        rqt = sb("rqt", [128, 2, 4, 128])
        kdt = sb("kdt", [128, 4, 2, 64])
        cdt = sb("cdt", [128, 4, 64])
        ngT = sb("ngT", [128, 2, 8])
        fnT = sb("fnT", [128, 8])
        baT = sb("baT", [128, 2, 24])
        cwT = sb("cwT", [128, 2, 6, 5])
        gqk = sb("gqk", [128, 2, 6, 64])
        gkn = sb("gkn", [128, 2, 2, 64])
        ggn = sb("ggn", [128, 2, 4, 64])
        grn = sb("grn", [128, 2, 4, 64])
        alb = sb("alb", [128, 2, 8])
        dtb = sb("dtb", [128, 2, 8])
        nega = sb("nega", [128, 2, 8])
        condT = sb("condT", [128, 8, 2])
        scond = sb("scond", [128, 8, 2])
        modT = sb("modT", [128, 2, 24, 2])
        gmul = sb("gmul", [128, 2, 8, 2])

        for jj in range(2):
            DS(lambda e, jj=jj: e.dma_start(out=condT[:, :, jj], in_=D["cond"][jj].rearrange("(c p) -> p c", p=128)), [], ["condT"])
        DS(lambda e: e.dma_start(out=baT[:], in_=D["b_ada"].rearrange("l (c p) -> p l c", p=128)), [], ["baT"])
        DS(lambda e: e.dma_start(out=ngT[:], in_=D["norm_g"].rearrange("l (c p) -> p l c", p=128)), [], ["ngT"])
        act(scond[:], condT[:], AF.Silu, ["condT"], ["scond"])
        DS(lambda e: e.dma_start(out=ident[:], in_=D["c_ident"]), [], ["ident"])
        DG(lambda e: e.dma_start(out=identb[:], in_=D["c_ident"]), [], ["identb"])
        V(lambda e: e.memset(ones[:], 1.0), [], ["ones"])
        V(lambda e: e.memset(onesblk[:], 0.0), [], ["onesblk"])
        V(lambda e: e.memset(onesblk[0:64, 0:64], 1.0), [], ["onesblk"])
        V(lambda e: e.memset(onesblk[64:128, 64:128], 1.0), [], ["onesblk"])
        V(lambda e: e.memset(onespad[:], 0.0), [], ["onespad"])
        V(lambda e: e.memset(onespad[:, 0, 0:64], 1.0), [], ["onespad"])
        V(lambda e: e.memset(onespad[:, 1, 64:128], 1.0), [], ["onespad"])
        DS(lambda e: e.dma_start(out=masks[:], in_=D["c_masks"].rearrange("m p f -> p m f")), [], ["masks"])
        DS(lambda e: e.dma_start(out=dct[:], in_=D["c_dct"].rearrange("m p f -> p m f")), [], ["dct"])
        DS(lambda e: e.dma_start(out=rqt[:].rearrange("p a b c -> p (a b c)"),
                                 in_=D["c_rqt"].rearrange("a b c -> (a b c)").partition_broadcast(128)), [], ["rqt"])
        DS(lambda e: e.dma_start(out=kdt[:], in_=D["c_kdt"]), [], ["kdt"])
        DS(lambda e: e.dma_start(out=cdt[:], in_=D["c_cdt"]), [], ["cdt"])
        DS(lambda e: e.dma_start(out=fnT[:], in_=D["final_norm"].rearrange("o (c p) -> p (o c)", p=128)), [], ["fnT"])
        for l in range(2):
            for c6 in range(6):
                DS(lambda e, l=l, c6=c6: e.dma_start(
                    out=cwT[:, l, c6, :], in_=D["conv_w"][l, :, c6 * 128:(c6 + 1) * 128].rearrange("j p -> p j")),
                   [], ["cwT"])
        for l in range(2):
            for hh in range(6):
                src = "q_norm" if hh < 4 else "k_norm"
                DS(lambda e, l=l, hh=hh, src=src: e.dma_start(out=gqk[:, l, hh, :], in_=D[src][l].partition_broadcast(128)),
                   [], ["gqk"])
            for hh in range(2):
                DS(lambda e, l=l, hh=hh: e.dma_start(out=gkn[:, l, hh, :], in_=D["k_norm"][l].partition_broadcast(128)),
                   [], ["gkn"])
            for hh in range(4):
                DS(lambda e, l=l, hh=hh: e.dma_start(out=ggn[:, l, hh, :], in_=D["gdn_norm"][l].partition_broadcast(128)),
                   [], ["ggn"])
                DS(lambda e, l=l, hh=hh: e.dma_start(out=grn[:, l, hh, :], in_=D["ret_norm"][l].partition_broadcast(128)),
                   [], ["grn"])
            DS(lambda e, l=l: e.dma_start(out=alb[:, l, :], in_=D["a_log"][l].partition_broadcast(128)), [], ["alb"])
            DS(lambda e, l=l: e.dma_start(out=dtb[:, l, :], in_=D["dt_bias"][l].partition_broadcast(128)), [], ["dtb"])
        for l in range(2):
            V(lambda e, l=l: e.tensor_scalar(out=gqk[:, l, 0:4, :], in0=gqk[:, l, 0:4, :], scalar1=0.125, scalar2=None,
                                             op0=ALU.mult), ["gqk"], ["gqk"])
        act(nega[:], alb[:], AF.Exp, ["alb"], ["nega"])
        V(lambda e: e.tensor_scalar(out=nega[:], in0=nega[:], scalar1=-1.0, scalar2=None, op0=ALU.mult), ["nega"], ["nega"])

        with ExitStack() as ph:
            wa = [sb("wa%d" % i, [128, 8, 512], F32, ph) for i in range(2)]
            cnt = 0
            for l in range(2):
                pb, pk = PS[0], ("ps", 0)
                for ch in range(6):
                    w_t, wk = wa[cnt % 2], "wa%d" % (cnt % 2)
                    cnt += 1
                    DG(lambda e, l=l, ch=ch, w_t=w_t: e.dma_start(
                        out=w_t[:], in_=D["w_ada"][l, :, ch * 512:(ch + 1) * 512].rearrange("(k p) n -> p k n", p=128)),
                       [], [wk])
                    for oc in range(4):
                        col = (ch * 4 + oc) * 2
                        for k in range(8):
                            mm(pb[:, col:col + 2], w_t[:, k, oc * 128:(oc + 1) * 128], scond[:, k, :], k == 0, k == 7,
                               [wk, "scond"], [pk])
                for j in range(2):
                    V(lambda e, l=l, j=j, pb=pb: e.tensor_tensor(
                        out=modT[:, l, :, j], in0=pb[:, 0:48].rearrange("p (a b) -> p a b", b=2)[:, :, j],
                        in1=baT[:, l, :], op=ALU.add), [pk, "baT"], ["modT"])
                    V(lambda e, l=l, j=j: e.scalar_tensor_tensor(
                        out=gmul[:, l, :, j], in0=modT[:, l, 8:16, j], scalar=1.0, in1=ngT[:, l, :],
                        op0=ALU.add, op1=ALU.mult), ["modT", "ngT"], ["gmul"])
            tap("modT", modT[:].rearrange("p a b c -> p (a b c)"), ["modT"])
        S.barrier()

        hT = sb("hT", [128, 8, 1024])
        hnT = sb("hnT", [128, 8, 1024], BF16)
        brT = sb("brT", [128, 8, 1024], BF16)
        wbs = [sb("wb%d" % i, [128, 8, 512], BF16) for i in range(3)]
        stg = [sb("stg%d" % i, [128, 1024]) for i in range(2)]
        wcnt = [0]
        scnt = [0]

        pend_w = {}

        def load_w(l, c0, c1):
            if (l, c0, c1) in pend_w:
                return pend_w.pop((l, c0, c1))
            i = wcnt[0] % 3
            wcnt[0] += 1
            t, k = wbs[i], "wb%d" % i
            DG(lambda e: e.dma_start(out=t[:, :, 0:c1 - c0],
                                     in_=D["w_in"][l, :, c0:c1].rearrange("(k p) n -> p k n", p=128)), [], [k])
            return t, k

        def prefetch_w(l, c0, c1):
            if not S.mute:
                pend_w[(l, c0, c1)] = load_w(l, c0, c1)

        def proj_T(wt, wk, a, b, tt, pb, pk):
            for k in range(8):
                mm(pb[:, 0:b - a], hnT[:, k, tt * 128:(tt + 1) * 128], wt[:, k, a:b], k == 0, k == 7, [wk, "hnT"], [pk])

        def proj_F(wt, wk, a, m, tb, pb, pk):
            for k in range(8):
                mm(pb[0:m, :], wt[:, k, a:a + m], hnT[:, k, tb * 512:(tb + 1) * 512], k == 0, k == 7, [wk, "hnT"], [pk])

        def rstd_from(ss_ap, out_ap, n, r, w):
            act(out_ap, ss_ap, AF.Sqrt, r, w, bias=EPS, scale=1.0 / n)
            V(lambda e: e.reciprocal(out=out_ap, in_=out_ap), w, w)

        def rmsnorm_T(st, gcol_fn, scol_fn, out_fn, outkey):
            sq = sb("rn_sq", [128, 8, 512], F32, st)
            rs = sb("rn_rs", [128, 512], F32, st)
            tmps = [sb("rn_tmp%d" % i, [128, 512], F32, st) for i in range(2)]
            for tb in range(2):
                blk = slice(tb * 512, (tb + 1) * 512)
                act(sq[:], hT[:, :, blk], AF.Square, ["hT"], ["rn_sq"])
                pb, pk = bank()
                for k in range(8):
                    mm(pb[:, :], ones[:, :], sq[:, k, :], k == 0, k == 7, ["ones", "rn_sq"], [pk])
                rstd_from(pb[:, :], rs[:], 1024.0, [pk], ["rn_rs"])
                for k in range(8):
                    sc = scol_fn(k)
                    if sc is None:
                        V(lambda e, k=k, tb=tb, blk=blk: e.scalar_tensor_tensor(
                            out=out_fn(k, tb), in0=hT[:, k, blk], scalar=gcol_fn(k), in1=rs[:], op0=ALU.mult, op1=ALU.mult),
                          ["hT", "rn_rs", "gmul", "fnT"], [outkey])
                    else:
                        tmpk, tk = tmps[k % 2], "rn_tmp%d" % (k % 2)
                        V(lambda e, k=k, blk=blk, tmpk=tmpk: e.scalar_tensor_tensor(
                            out=tmpk[:], in0=hT[:, k, blk], scalar=gcol_fn(k), in1=rs[:], op0=ALU.mult, op1=ALU.mult),
                          ["hT", "rn_rs", "gmul"], [tk])
                        act(out_fn(k, tb), tmpk[:], AF.Identity, [tk, "modT"], [outkey], bias=sc, scale=1.0)

        def norm_gate_out(st, tag, o_acc, okey, gtile, gkey, l, zT, zkey, br):
            sq = sb(tag + "_sq", [128, 256], F32, st)
            ss = sb(tag + "_ss", [128, 4], F32, st)
            on = sb(tag + "_on", [128, 256], F32, st)
            for tt in range(8):
                act(sq[:], o_acc[:, tt, :], AF.Square, [(okey, tt)], [tag + "_sq"])
                V(lambda e: e.tensor_reduce(out=ss[:], in_=sq[:].rearrange("p (h d) -> p h d", h=4), axis=AX.X, op=ALU.add),
                  [tag + "_sq"], [tag + "_ss"])
                rstd_from(ss[:], ss[:], 64.0, [tag + "_ss"], [tag + "_ss"])
                V(lambda e, tt=tt: e.tensor_tensor(out=on[:].rearrange("p (h d) -> p h d", h=4),
                                                   in0=o_acc[:, tt, :].rearrange("p (h d) -> p h d", h=4),
                                                   in1=ss[:].unsqueeze(2).to_broadcast([128, 4, 64]), op=ALU.mult),
                  [(okey, tt), tag + "_ss"], [tag + "_on"])
                G(lambda e: e.tensor_tensor(out=on[:].rearrange("p (h d) -> p h d", h=4),
                                            in0=on[:].rearrange("p (h d) -> p h d", h=4), in1=gtile[:, l, :, :], op=ALU.mult),
                  [tag + "_on", gkey], [tag + "_on"])
                pb, pk = bank()
                for c in range(2):
                    tr(pb[:, c * 128:(c + 1) * 128], on[:, c * 128:(c + 1) * 128], ident[:], [tag + "_on", "ident"], [pk])
                V(lambda e, tt=tt, pb=pb: e.tensor_tensor(out=brT[:, br * 2:br * 2 + 2, tt * 128:(tt + 1) * 128],
                                                          in0=pb[:, 0:256].rearrange("p (c t) -> p c t", c=2),
                                                          in1=zT[:, :, tt * 128:(tt + 1) * 128], op=ALU.mult),
                  [pk, zkey], [("brT", br)])

        def attention(st, tag, qT, kT, vpad, kvmap, vslot, qblocks, keys_fn, zT, zkey, br):
            pts = [sb(tag + "_p%d" % i, [128, 512], BF16, st) for i in range(3)]
            rdens = [sb(tag + "_rd%d" % i, [64, 512], F32, st) for i in range(2)]
            osb = sb(tag + "_o", [128, 512], F32, st)
            pc = 0
            it = 0
            for (q0, qn) in qblocks:
                keys = keys_fn(q0)
                nk = len(keys)
                for h in range(4):
                    pr, hf = h // 2, h % 2
                    bo = it % 4
                    it += 1
                    psO, ko = PS[bo], ("ps", bo)
                    for ci, (kidx, bias_fn) in enumerate(keys):
                        pb, pk = bank()
                        mm(pb[:, 0:qn], kT[:, kvmap(h), kidx * 128:(kidx + 1) * 128], qT[:, h, q0:q0 + qn],
                           True, bias_fn is None, [tag + "_kT", tag + "_qT"], [pk])
                        if bias_fn is not None:
                            bap, bkey = bias_fn(h)
                            mm(pb[:, 0:qn], identb[:, :], bap, False, True, ["identb", bkey], [pk])
                        pt, ptk = pts[pc % 3], tag + "_p%d" % (pc % 3)
                        pc += 1
                        act(pt[:, 0:qn], pb[:, 0:qn], AF.Exp, [pk], [ptk])
                        mm(psO[:, 0:qn], vpad[:, kidx, vslot(h), :], pt[:, 0:qn], ci == 0, ci == nk - 1, [tag + "_vp", ptk], [ko])
                    rows = slice(hf * 64, hf * 64 + 64)
                    rden, rdk = rdens[hf], tag + "_rd%d" % hf
                    okey = (tag + "_o", hf)
                    V(lambda e, psO=psO, qn=qn, rden=rden: e.reciprocal(out=rden[:, 0:qn], in_=psO[64:128, 0:qn]), [ko], [rdk])
                    V(lambda e, psO=psO, qn=qn, rden=rden, rows=rows: e.tensor_tensor(
                        out=osb[rows, 0:qn], in0=psO[0:64, 0:qn], in1=rden[:, 0:qn], op=ALU.mult), [ko, rdk], [okey])
                    V(lambda e, pr=pr, q0=q0, qn=qn, rows=rows: e.tensor_tensor(
                        out=brT[rows, br * 2 + pr, q0:q0 + qn], in0=osb[rows, 0:qn], in1=zT[rows, pr, q0:q0 + qn], op=ALU.mult),
                      [okey, zkey], [("brT", br)])

        def zproj(wt, wk, a, zT, zkey):
            for c in range(2):
                for tb in range(2):
                    pb, pk = bank()
                    proj_F(wt, wk, a + c * 128, 128, tb, pb, pk)
                    act(zT[:, c, tb * 512:(tb + 1) * 512], pb[:, :], AF.Silu, [pk], [zkey])

        def stage(name):
            if stop_after == name or (stop_after == "Dproj" and name == "D"):
                raise _Stop()

        def run_groups():
          for grp in (groups if stop_after != "p0" else ()):
            G(lambda e: e.memset(brT[:], 0.0), [], ["brT"])
            xin = D["xp"] if grp == 0 else D["xs"]
            yout = D["yp"] if grp == 0 else D["ys"]
            for tt in range(8):
                s_t, sk = stg[scnt[0] % 2], "stg%d" % (scnt[0] % 2)
                scnt[0] += 1
                DS(lambda e, tt=tt, s_t=s_t: e.dma_start(out=s_t[:], in_=xin[tt * 128:(tt + 1) * 128, :]), [], [sk])
                for half in range(2):
                    pb, pk = bank()
                    for q in range(4):
                        k = half * 4 + q
                        tr(pb[:, q * 128:(q + 1) * 128], s_t[:, k * 128:(k + 1) * 128], ident[:], [sk, "ident"], [pk])
                    eng = A if half == 0 else V
                    if half == 0:
                        A(lambda e, tt=tt, pb=pb: e.copy(out=hT[:, 0:4, tt * 128:(tt + 1) * 128],
                                                         in_=pb[:, :].rearrange("p (a b) -> p a b", a=4)), [pk], ["hT"])
                    else:
                        V(lambda e, tt=tt, pb=pb: e.tensor_copy(out=hT[:, 4:8, tt * 128:(tt + 1) * 128],
                                                                in_=pb[:, :].rearrange("p (a b) -> p a b", a=4)), [pk], ["hT"])
            for l in layers:
                j = grp
                S.mute = False
                with ExitStack() as ph:
                    rmsnorm_T(ph, lambda k: gmul[:, l, k, j:j + 1], lambda k: modT[:, l, k, j:j + 1],
                              lambda k, tb: hnT[:, k, tb * 512:(tb + 1) * 512], "hnT")
                S.barrier()
                stage("norm")
                if grp == groups[0] and l == layers[0]:
                    tap("hnT", hnT[:].rearrange("p a b -> p (a b)"), ["hnT"])

                S.mute = "A" in _SKIP
                with ExitStack() as ph:
                    nkt = 8 if grp == 0 else 12
                    qT = sb("a_qT", [64, 4, 1024], BF16, ph)
                    kT = sb("a_kT", [64, 2, 128 * nkt], BF16, ph)
                    vpad = sb("a_vp", [128, nkt, 4, 128], BF16, ph)
                    zT = sb("a_zT", [128, 2, 1024], BF16, ph)
                    sqs = [sb("a_sq%d" % i, [128, 384], F32, ph) for i in range(2)]
                    sss = [sb("a_ss%d" % i, [128, 6], F32, ph) for i in range(2)]
                    qks = [sb("a_qk%d" % i, [128, 384], F32, ph) for i in range(2)]
                    qk2s = [sb("a_qk2%d" % i, [128, 384], F32, ph) for i in range(2)]
                    kouts = [sb("a_ko%d" % i, [128, 128], F32, ph) for i in range(2)]
                    vouts = [sb("a_vo%d" % i, [128, 128], F32, ph) for i in range(2)]
                    if grp == 1:
                        rp_all = sb("a_rp", [128, 8, 2, 384], F32, ph)
                        for t8 in range(8):
                            DS(lambda e, t8=t8: e.dma_start(out=rp_all[:, t8, :, :], in_=D["c_rope"][:, t8 * 128:(t8 + 1) * 128, :]
                                                            .rearrange("a p f -> p a f")), [], [("a_rp", t8)])
                    G(lambda e: e.memset(vpad[:, :, :, 64:128], 1.0), [], ["a_vp"])
                    w0, w0k = load_w(l, 0, 512)
                    w1, w1k = load_w(l, 512, 768)
                    for tt in range(8):
                        par_ = tt % 2
                        sq, ss, qk, qk2, kout, vout = sqs[par_], sss[par_], qks[par_], qk2s[par_], kouts[par_], vouts[par_]
                        K_sq, K_ss, K_qk, K_qk2, K_ko, K_vo = ["a_%s%d" % (n_, par_) for n_ in ("sq", "ss", "qk", "qk2", "ko", "vo")]
                        K_rp = ("a_rp", tt)
                        if grp == 1:
                            rp = rp_all[:, tt, :, :]
                        pb, pk = bank()
                        proj_T(w0, w0k, 0, 512, tt, pb, pk)
                        act(sq[:], pb[:, 0:384], AF.Square, [pk], [K_sq])
                        V(lambda e: e.tensor_reduce(out=ss[:], in_=sq[:].rearrange("p (h d) -> p h d", h=6), axis=AX.X,
                                                    op=ALU.add), [K_sq], [K_ss])
                        rstd_from(ss[:], ss[:], 64.0, [K_ss], [K_ss])
                        V(lambda e, pb=pb: e.tensor_tensor(out=qk[:].rearrange("p (h d) -> p h d", h=6),
                                                           in0=pb[:, 0:384].rearrange("p (h d) -> p h d", h=6),
                                                           in1=ss[:].unsqueeze(2).to_broadcast([128, 6, 64]), op=ALU.mult),
                          [pk, K_ss], [K_qk])
                        if grp == 0:
                            b_, s0 = tt // 2, (tt % 2) * 128
                            G(lambda e: e.tensor_tensor(out=kout[:].rearrange("p (h d) -> p h d", h=2),
                                                        in0=qk[:, 256:384].rearrange("p (h d) -> p h d", h=2),
                                                        in1=gkn[:, l, :, :], op=ALU.mult), [K_qk, "gkn"], [K_ko])
                            DS(lambda e, b_=b_, s0=s0: e.dma_start(out=D["nak"][b_, l, s0:s0 + 128, :], in_=kout[:]),
                               [K_ko], [])
                            A(lambda e, pb=pb: e.copy(out=vout[:], in_=pb[:, 384:512]), [pk], [K_vo])
                            DS(lambda e, b_=b_, s0=s0: e.dma_start(out=D["nav"][b_, l, s0:s0 + 128, :], in_=vout[:]),
                               [K_vo], [])
                        G(lambda e: e.tensor_tensor(out=qk[:].rearrange("p (h d) -> p h d", h=6),
                                                    in0=qk[:].rearrange("p (h d) -> p h d", h=6), in1=gqk[:, l, :, :],
                                                    op=ALU.mult), [K_qk, "gqk"], [K_qk])
                        src, srck = qk, K_qk
                        if grp == 1:
                            V(lambda e: e.tensor_tensor(out=qk2[:], in0=qk[:], in1=rp[:, 0, :], op=ALU.mult),
                              [K_qk, K_rp], [K_qk2])
                            qv = qk[:].rearrange("p (g s d) -> p g s d", s=2, d=16)
                            sv = rp[:, 1, :].rearrange("p (g s d) -> p g s d", s=2, d=16)
                            G(lambda e, qv=qv, sv=sv: e.tensor_tensor(
                                out=sq[:].rearrange("p (g s d) -> p g s d", s=2, d=16)[:, :, 0, :], in0=qv[:, :, 1, :],
                                in1=sv[:, :, 0, :], op=ALU.mult), [K_qk, K_rp], [K_sq])
                            G(lambda e, qv=qv, sv=sv: e.tensor_tensor(
                                out=sq[:].rearrange("p (g s d) -> p g s d", s=2, d=16)[:, :, 1, :], in0=qv[:, :, 0, :],
                                in1=sv[:, :, 1, :], op=ALU.mult), [K_qk, K_rp], [K_sq])
                            V(lambda e: e.tensor_tensor(out=qk2[:], in0=qk2[:], in1=sq[:], op=ALU.add),
                              [K_qk2, K_sq], [K_qk2])
                            src, srck = qk2, K_qk2
                        pq, pqk = bank()
                        for h in range(4):
                            tr(pq[0:64, h * 128:(h + 1) * 128], src[:, h * 64:(h + 1) * 64], ident[:], [srck, "ident"], [pqk])
                        A(lambda e, tt=tt, pq=pq: e.copy(out=qT[:, :, tt * 128:(tt + 1) * 128],
                                                         in_=pq[0:64, :].rearrange("p (a b) -> p a b", a=4)), [pqk], ["a_qT"])
                        pk2, pk2k = bank()
                        for h in range(2):
                            tr(pk2[0:64, h * 128:(h + 1) * 128], src[:, 256 + h * 64:256 + (h + 1) * 64], ident[:],
                               [srck, "ident"], [pk2k])
                        V(lambda e, tt=tt, pk2=pk2: e.tensor_copy(out=kT[:, :, tt * 128:(tt + 1) * 128],
                                                                  in_=pk2[0:64, 0:256].rearrange("p (a b) -> p a b", a=2)),
                          [pk2k], ["a_kT"])
                        A(lambda e, tt=tt, pb=pb: e.copy(out=vpad[:, tt, 0:2, 0:64],
                                                         in_=pb[:, 384:512].rearrange("p (a b) -> p a b", a=2)), [pk], ["a_vp"])
                    if grp == 1:
                        for ct in range(4):
                            s_t, sk = stg[scnt[0] % 2], "stg%d" % (scnt[0] % 2)
                            scnt[0] += 1
                            DS(lambda e, ct=ct, s_t=s_t: e.dma_start(out=s_t[:, 0:128], in_=D["cak"][l, ct * 128:(ct + 1) * 128, :]),
                               [], [sk])
                            DS(lambda e, ct=ct, s_t=s_t: e.dma_start(out=s_t[:, 128:256], in_=D["cav"][l, ct * 128:(ct + 1) * 128, :]),
                               [], [sk])
                            pk2, pk2k = bank()
                            for h in range(2):
                                tr(pk2[0:64, h * 128:(h + 1) * 128], s_t[:, h * 64:(h + 1) * 64], ident[:], [sk, "ident"], [pk2k])
                            V(lambda e, ct=ct, pk2=pk2: e.tensor_copy(
                                out=kT[:, :, (8 + ct) * 128:(9 + ct) * 128],
                                in_=pk2[0:64, 0:256].rearrange("p (a b) -> p a b", a=2)), [pk2k], ["a_kT"])
                            A(lambda e, ct=ct, s_t=s_t: e.copy(out=vpad[:, 8 + ct, 0:2, 0:64],
                                                               in_=s_t[:, 128:256].rearrange("p (a b) -> p a b", a=2)), [sk], ["a_vp"])
                    zproj(w1, w1k, 0, zT, "a_zT")
                    prefetch_w(l, 2832, 3344)
                    if grp == 0:
                        qblocks = [(b_ * 256, 256) for b_ in range(4)]
                        keys_fn = lambda q0: [((q0 // 128) + i, None) for i in range(2)]
                    else:
                        qblocks = [(0, 512), (512, 512)]
                        keys_fn = lambda q0: [(i, None) for i in range(12)]
                    attention(ph, "a", qT, kT, vpad, lambda h: h // 2, lambda h: h // 2, qblocks, keys_fn,
                              zT, "a_zT", 0)
                S.barrier()
                stage("A")
                if grp == groups[0] and l == layers[0]:
                    tap("brA", brT[:, 0:2, :].rearrange("p a b -> p (a b)"), [("brT", 0)])

                S.mute = "D" in _SKIP
                with ExitStack() as ph:
                    nkt = 8 if grp == 0 else 12
                    qT = sb("d_qT", [64, 4, 1024], BF16, ph)
                    kT = sb("d_kT", [64, 4, 128 * nkt], BF16, ph)
                    vpad = sb("d_vp", [128, nkt, 4, 128], BF16, ph)
                    zT = sb("d_zT", [128, 2, 1024], BF16, ph)
                    kout = sb("d_ko", [128, 256], F32, ph)
                    vout = sb("d_vo", [128, 256], F32, ph)
                    G(lambda e: e.memset(vpad[:, :, :, 64:128], 1.0), [], ["d_vp"])
                    w0, w0k = load_w(l, 2832, 3344)
                    w1, w1k = load_w(l, 3344, 3856)
                    for c in range(2):
                        for tb in range(2):
                            blk = slice(tb * 512, (tb + 1) * 512)
                            pb, pk = bank()
                            proj_F(w0, w0k, c * 128, 128, tb, pb, pk)
                            for hf in range(2):
                                V(lambda e, c=c, hf=hf, blk=blk, pb=pb: e.tensor_scalar(
                                    out=qT[:, 2 * c + hf, blk], in0=pb[hf * 64:(hf + 1) * 64, :], scalar1=0.125, scalar2=None,
                                    op0=ALU.mult), [pk], ["d_qT"])
                            pb, pk = bank()
                            proj_F(w0, w0k, 256 + c * 128, 128, tb, pb, pk)
                            for hf in range(2):
                                A(lambda e, c=c, hf=hf, blk=blk, pb=pb: e.copy(out=kT[:, 2 * c + hf, blk],
                                                                               in_=pb[hf * 64:(hf + 1) * 64, :]), [pk], ["d_kT"])
                    for tt in range(8):
                        pb, pk = bank()
                        proj_T(w1, w1k, 0, 256, tt, pb, pk)
                        A(lambda e, tt=tt, pb=pb: e.copy(out=vpad[:, tt, :, 0:64],
                                                         in_=pb[:, 0:256].rearrange("p (a b) -> p a b", a=4)), [pk], ["d_vp"])
                        if grp == 0:
                            b_, s0 = tt // 2, (tt % 2) * 128
                            A(lambda e, pb=pb: e.copy(out=vout[:], in_=pb[:, 0:256]), [pk], ["d_vo"])
                            DS(lambda e, b_=b_, s0=s0: e.dma_start(out=D["nnv"][b_, l, s0:s0 + 128, :], in_=vout[:]),
                               ["d_vo"], [])
                            pb2, pk2k = bank()
                            proj_T(w0, w0k, 256, 512, tt, pb2, pk2k)
                            V(lambda e, pb2=pb2: e.tensor_copy(out=kout[:], in_=pb2[:, 0:256]), [pk2k], ["d_ko"])
                            DS(lambda e, b_=b_, s0=s0: e.dma_start(out=D["nnk"][b_, l, s0:s0 + 128, :], in_=kout[:]),
                               ["d_ko"], [])
                    if grp == 1:
                        for ct in range(4):
                            s_t, sk = stg[scnt[0] % 2], "stg%d" % (scnt[0] % 2)
                            scnt[0] += 1
                            DS(lambda e, ct=ct, s_t=s_t: e.dma_start(out=s_t[:, 0:256], in_=D["cnk"][l, ct * 128:(ct + 1) * 128, :]),
                               [], [sk])
                            DS(lambda e, ct=ct, s_t=s_t: e.dma_start(out=s_t[:, 256:512], in_=D["cnv"][l, ct * 128:(ct + 1) * 128, :]),
                               [], [sk])
                            pk2, pk2k = bank()
                            for h in range(4):
                                tr(pk2[0:64, h * 128:(h + 1) * 128], s_t[:, h * 64:(h + 1) * 64], ident[:], [sk, "ident"], [pk2k])
                            V(lambda e, ct=ct, pk2=pk2: e.tensor_copy(
                                out=kT[:, :, (8 + ct) * 128:(9 + ct) * 128],
                                in_=pk2[0:64, :].rearrange("p (a b) -> p a b", a=4)), [pk2k], ["d_kT"])
                            A(lambda e, ct=ct, s_t=s_t: e.copy(out=vpad[:, 8 + ct, :, 0:64],
                                                               in_=s_t[:, 256:512].rearrange("p (a b) -> p a b", a=4)), [sk], ["d_vp"])
                    zproj(w1, w1k, 256, zT, "d_zT")
                    prefetch_w(l, 1808, 2320)
                    if stop_after == "Dproj":
                        pass
                    elif grp == 0:
                        qblocks = [(b_ * 256, 256) for b_ in range(4)]
                        keys_fn = lambda q0: [((q0 // 128) + i, None) for i in range(2)]
                        attention(ph, "d", qT, kT, vpad, lambda h: h, lambda h: h, qblocks, keys_fn, zT, "d_zT", 3)
                    else:
                        nbt = [sb("d_nb%d" % i, [128, 6, 4, 128], BF16, ph) for i in range(3)]
                        pairs = _NA[3]
                        qblocks = [(qt * 128, 128) for qt in range(8)]
                        issued = set()

                        def nb_load(qt):
                            if qt in issued or qt > 7:
                                return
                            issued.add(qt)
                            pis_ = [pi for pi, (a, b) in enumerate(pairs) if a == qt]
                            nb_, nbk_ = nbt[qt % 3], "d_nb%d" % (qt % 3)
                            DG(lambda e: e.dma_start(out=nb_[:, 0:len(pis_), :, :],
                                                     in_=D["c_nab"][l, pis_[0]:pis_[0] + len(pis_)].rearrange("a h k q -> k a h q")),
                               [], [nbk_])

                        def keys_fn(q0):
                            qt = q0 // 128
                            pis = [pi for pi, (a, b) in enumerate(pairs) if a == qt]
                            nb, nbk = nbt[qt % 3], "d_nb%d" % (qt % 3)
                            nb_load(qt)
                            nb_load(qt + 1)
                            nb_load(qt + 2)
                            out = []
                            for ii, pi in enumerate(pis):
                                out.append((pairs[pi][1], (lambda h, ii=ii: (nb[:, ii, h, :], nbk))))
                            out += [(8 + i, None) for i in range(4)]
                            return out
                        attention(ph, "d", qT, kT, vpad, lambda h: h, lambda h: h, qblocks, keys_fn, zT, "d_zT", 3)
                S.barrier()
                stage("D")
                if grp == groups[0] and l == layers[0]:
                    tap("brD", brT[:, 6:8, :].rearrange("p a b -> p (a b)"), [("brT", 3)])

                S.mute = "C" in _SKIP
                with ExitStack() as ph:
                    qTm = sb("r_qTm", [128, 2, 2, 1024], BF16, ph)
                    kTr = sb("r_kT", [128, 2, 1024], BF16, ph)
                    kdp = sb("r_kdp", [128, 4, 2, 128], BF16, ph)
                    vtk = sb("r_v", [128, 8, 256], BF16, ph)
                    zT = sb("r_zT", [128, 2, 1024], BF16, ph)
                    U = sb("r_U", [128, 8, 256], F32, ph)
                    Sin = sb("r_Sin", [128, 8, 256], F32, ph)
                    Sfin = sb("r_Sfin", [128, 256], F32, ph)
                    Sinb = sb("r_Sinb", [128, 256], BF16, ph)
                    qkm = sb("r_qkm", [128, 4, 128], BF16, ph)
                    qdm = sb("r_qdm", [128, 4, 2, 128], BF16, ph)
                    oacc = sb("r_oacc", [128, 8, 256], F32, ph)
                    tmpS = sb("r_tmpS", [128, 256], F32, ph)
                    R2 = int(os.environ.get("R2", "511"))
                    if R2 & 1:
                        G(lambda e: e.memset(qTm[:], 0.0), [], ["r_qTm"])
                        G(lambda e: e.memset(kdp[:], 0.0), [], ["r_kdp"])
                    w0, w0k = load_w(l, 1808, 2320)
                    w1, w1k = load_w(l, 2320, 2832)
                    for c in range(2):
                        for tb in range(2):
                            blk = slice(tb * 512, (tb + 1) * 512)
                            pb, pk = bank()
                            if R2 & 2:
                                proj_F(w0, w0k, c * 128, 128, tb, pb, pk)
                            for hf in (range(2) if R2 & 2 else []):
                                rows = slice(hf * 64, (hf + 1) * 64)
                                if hf == 0:
                                    A(lambda e, c=c, hf=hf, blk=blk, rows=rows, pb=pb: e.copy(out=qTm[rows, c, hf, blk], in_=pb[rows, :]),
                                      [pk], ["r_qTm"])
                                else:
                                    V(lambda e, c=c, hf=hf, blk=blk, rows=rows, pb=pb: e.tensor_copy(out=qTm[rows, c, hf, blk], in_=pb[rows, :]),
                                      [pk], ["r_qTm"])
                            pb, pk = bank()
                            if R2 & 4:
                                proj_F(w0, w0k, 256 + c * 128, 128, tb, pb, pk)
                                V(lambda e, c=c, blk=blk, pb=pb: e.tensor_copy(out=kTr[:, c, blk], in_=pb[:, :]), [pk], ["r_kT"])
                    for tt in range(8):
                        pb, pk = bank()
                        if R2 & 8:
                            proj_T(w0, w0k, 256, 512, tt, pb, pk)
                        for d in (range(2) if R2 & 8 else []):
                            for h in range(4):
                                hf = h % 2
                                V(lambda e, d=d, h=h, hf=hf, pb=pb: e.tensor_tensor(
                                    out=kdp[:, h, d, hf * 64:(hf + 1) * 64], in0=pb[:, h * 64:(h + 1) * 64],
                                    in1=kdt[:, h, d, :], op=ALU.mult), [pk, "kdt"], ["r_kdp"])
                        pb, pk = bank()
                        if R2 & 16:
                            proj_T(w1, w1k, 0, 256, tt, pb, pk)
                            A(lambda e, tt=tt, pb=pb: e.copy(out=vtk[:, tt, :], in_=pb[:, 0:256]), [pk], [("r_v", tt)])
                        pb, pk = bank()
                        for d in (range(2) if R2 & 32 else []):
                            for pr in range(2):
                                cs_ = slice((d * 2 + pr) * 64, (d * 2 + pr + 1) * 64)
                                for hf in range(2):
                                    h = 2 * pr + hf
                                    mm(pb[:, cs_], kdp[:, h, d, :], vtk[:, tt, h * 64:(h + 1) * 64], hf == 0, hf == 1,
                                       ["r_kdp", ("r_v", tt)], [pk])
                        if R2 & 32:
                            V(lambda e, tt=tt, pb=pb: e.tensor_copy(out=U[:, tt, :], in_=pb[:, 0:256]), [pk], [("r_U", tt)])
                    if R2 & 64:
                        zproj(w1, w1k, 256, zT, "r_zT")
                    prefetch_w(l, 768, 1280)
                    seqs = [(2 * b_, 2 * b_ + 1) for b_ in range(4)] if grp == 0 else [tuple(range(8))]
                    cdv = cdt[:].rearrange("p a d -> p (a d)")
                    FW, BW = slice(0, 128), slice(128, 256)
                    for si, tiles in enumerate(seqs if R2 & 128 else []):
                        first, lastt = tiles[0], tiles[-1]
                        if grp == 0:
                            G(lambda e, first=first: e.memset(Sin[:, first, FW], 0.0), [], [("r_Sin", "f", first)])
                            G(lambda e, lastt=lastt: e.memset(Sin[:, lastt, BW], 0.0), [], [("r_Sin", "b", lastt)])
                        else:
                            DS(lambda e: e.dma_start(out=Sin[:, 0, FW].rearrange("p (a v) -> p a v", a=2),
                                                     in_=D["sr"][l, 0].rearrange("(a b) k v -> (b k) a v", b=2)), [], [("r_Sin", "f", 0)])
                            DS(lambda e: e.dma_start(out=Sin[:, 7, BW].rearrange("p (a v) -> p a v", a=2),
                                                     in_=D["sr"][l, 1].rearrange("(a b) k v -> (b k) a v", b=2)), [], [("r_Sin", "b", 7)])
                        for tt in tiles:
                            dst = Sin[:, tt + 1, FW] if tt != lastt else Sfin[:, FW]
                            dk = ("r_Sin", "f", tt + 1) if tt != lastt else ("r_Sfin", "f")
                            V(lambda e, tt=tt: e.tensor_tensor(out=tmpS[:, FW], in0=Sin[:, tt, FW], in1=cdv[:, FW], op=ALU.mult),
                              [("r_Sin", "f", tt), "cdt"], [("r_tmpS", "f")])
                            V(lambda e, tt=tt, dst=dst: e.tensor_tensor(out=dst, in0=tmpS[:, FW], in1=U[:, tt, FW], op=ALU.add),
                              [("r_tmpS", "f"), ("r_U", tt)], [dk])
                        for tt in reversed(tiles):
                            dst = Sin[:, tt - 1, BW] if tt != first else Sfin[:, BW]
                            dk = ("r_Sin", "b", tt - 1) if tt != first else ("r_Sfin", "b")
                            G(lambda e, tt=tt: e.tensor_tensor(out=tmpS[:, BW], in0=Sin[:, tt, BW], in1=cdv[:, BW], op=ALU.mult),
                              [("r_Sin", "b", tt), "cdt"], [("r_tmpS", "b")])
                            G(lambda e, tt=tt, dst=dst: e.tensor_tensor(out=dst, in0=tmpS[:, BW], in1=U[:, tt, BW], op=ALU.add),
                              [("r_tmpS", "b"), ("r_U", tt)], [dk])
                        if grp == 0 and R2 & 256:
                            b_ = si
                            DS(lambda e, b_=b_: e.dma_start(out=D["nsr"][b_, l, 0].rearrange("(a b) k v -> (b k) a v", b=2),
                                                            in_=Sfin[:, FW].rearrange("p (a v) -> p a v", a=2)),
                               [("r_Sfin", "f")], [])
                            DS(lambda e, b_=b_: e.dma_start(out=D["nsr"][b_, l, 1].rearrange("(a b) k v -> (b k) a v", b=2),
                                                            in_=Sfin[:, BW].rearrange("p (a v) -> p a v", a=2)),
                               [("r_Sfin", "b")], [])
                    _m = int(os.environ.get("L3", "31"))
                    for tt in range(int(os.environ.get("L3N", "8")) if _CL >= 3 else 0):
                        ts_ = slice(tt * 128, (tt + 1) * 128)
                        pb, pk = bank()
                        for h in (range(4) if _m & 1 else []):
                            mm(pb[:, h * 128:(h + 1) * 128], kTr[:, h // 2, ts_], qTm[:, h // 2, h % 2, ts_], True, True,
                               ["r_kT", "r_qTm"], [pk])
                        if _m & 2:
                            V(lambda e, pb=pb: e.tensor_tensor(out=qkm[:].rearrange("p a b -> p (a b)"), in0=pb[:, :],
                                                               in1=dct[:].rearrange("p a b -> p (a b)"), op=ALU.mult),
                              [pk, "dct"], ["r_qkm"])
                        for d in (range(2) if _m & 4 else []):
                            G(lambda e, d=d, ts_=ts_: e.tensor_tensor(
                                out=qdm[:, :, d, :], in0=qTm[:, :, :, ts_].rearrange("p c f t -> p (c f) t"),
                                in1=rqt[:, d, :, :], op=ALU.mult), ["r_qTm", "rqt"], ["r_qdm"])
                        if _m & 8:
                            A(lambda e, tt=tt: e.copy(out=Sinb[:], in_=Sin[:, tt, :]), [("r_Sin", "f", tt), ("r_Sin", "b", tt)], ["r_Sinb"])
                        pb, pk = bank()
                        for h in (range(4) if _m & 16 else []):
                            pr = h // 2
                            hc = slice(h * 64, (h + 1) * 64)
                            mm(pb[:, hc], qkm[:, h, :], vtk[:, tt, hc], True, False, ["r_qkm", ("r_v", tt)], [pk])
                            mm(pb[:, hc], qdm[:, h, 0, :], Sinb[:, pr * 64:(pr + 1) * 64], False, False, ["r_qdm", "r_Sinb"], [pk])
                            mm(pb[:, hc], qdm[:, h, 1, :], Sinb[:, (2 + pr) * 64:(3 + pr) * 64], False, True, ["r_qdm", "r_Sinb"], [pk])
                        if _m & 16:
                            A(lambda e, tt=tt, pb=pb: e.copy(out=oacc[:, tt, :], in_=pb[:, 0:256]), [pk], [("r_oacc", tt)])
                    if _CL >= 4:
                        norm_gate_out(ph, "r", oacc, "r_oacc", grn, "grn", l, zT, "r_zT", 2)
                S.barrier()
                stage("C")
                if grp == groups[0] and l == layers[0]:
                    tap("brC", brT[:, 4:6, :].rearrange("p a b -> p (a b)"), [("brT", 2)])

                S.mute = "B" in _SKIP
                with ExitStack() as ph:
                  if _BL >= 1:
                      gx = [sb("g_x%d" % i, [128, 1024], F32, ph) for i in range(1)]
                      gy = [sb("g_y%d" % i, [128, 1024], F32, ph) for i in range(1)]
                      qkT = sb("g_qkT", [128, 4, 1024], F32, ph)
                      ktok = sb("g_ktok", [128, 8, 256], F32, ph)
                      vtok = sb("g_vtok", [128, 8, 256], F32, ph)
                      zT = sb("g_zT", [128, 2, 1024], BF16, ph)
                      ab = sb("g_ab", [128, 8, 16], F32, ph)
                      beta = sb("g_beta", [128, 8, 8], F32, ph)
                      nbeta = sb("g_nbeta", [128, 8, 8], F32, ph)
                      la = sb("g_la", [128, 8, 8], F32, ph)
                      gc = sb("g_gc", [128, 8, 8], F32, ph)
                      ngc = sb("g_ngc", [128, 8, 8], F32, ph)
                      egc = sb("g_egc", [128, 8, 8], F32, ph)
                      bege = sb("g_bege", [128, 8, 8], F32, ph)
                      oacc = sb("g_oacc", [128, 8, 256], F32, ph)
                      dg = sb("g_dg", [128, 4, 128], F32, ph)
                      Dm = sb("g_Dm", [128, 4, 128], F32, ph)
                      DT = sb("g_DT", [128, 4, 128], F32, ph)
                      Bm = [sb("g_B%d" % i, [128, 4, 128], F32, ph) for i in range(2)]
                      BTm = [sb("g_BT%d" % i, [128, 4, 128], F32, ph) for i in range(2)]
                      Ym = [sb("g_Y%d" % i, [128, 4, 128], F32, ph) for i in range(2)]
                      qkd = sb("g_qkd", [128, 4, 128], F32, ph)
                      vb = sb("g_vb", [128, 4, 64], F32, ph)
                      kbgp = sb("g_kbgp", [128, 4, 128], F32, ph)
                      kdcp = sb("g_kdcp", [128, 4, 128], F32, ph)
                      kds = sb("g_kds", [128, 4], F32, ph)
                      eg2 = sb("g_eg2", [128, 2], F32, ph)
                      wTm = sb("g_wTm", [128, 4, 128], F32, ph)

                      Sst = sb("g_S", [128, 2, 2, 64], F32, ph)
                      G(lambda e: e.memset(kbgp[:], 0.0), [], ["g_kbgp"])
                      G(lambda e: e.memset(kdcp[:], 0.0), [], ["g_kdcp"])
                      G(lambda e: e.memset(wTm[:], 0.0), [], ["g_wTm"])
                      wA, wAk = load_w(l, 768, 1280)
                      wB, wBk = load_w(l, 1280, 1552)
                      wC, wCk = load_w(l, 1552, 1808)
                      nseq = 4 if grp == 0 else 1
                      L = 1024 // nseq
                      for c6 in range(6):
                          xt, xk = gx[0], "g_x0"
                          yt, yk = gy[0], "g_y0"
                          wt, wk, a = (wA, wAk, c6 * 128) if c6 < 4 else (wB, wBk, (c6 - 4) * 128)
                          for tb in range(2):
                              pb, pk = bank()
                              proj_F(wt, wk, a, 128, tb, pb, pk)
                              A(lambda e, tb=tb, pb=pb, xt=xt: e.copy(out=xt[:, tb * 512:(tb + 1) * 512], in_=pb[:, :]), [pk], [xk])
                          x3 = xt[:].rearrange("p (s t) -> p s t", s=nseq)
                          y3 = yt[:].rearrange("p (s t) -> p s t", s=nseq)
                          V(lambda e, c6=c6, xt=xt, yt=yt: e.tensor_scalar(out=yt[:], in0=xt[:], scalar1=cwT[:, l, c6, 2:3],
                                                                           scalar2=None, op0=ALU.mult), [xk, "cwT"], [yk])
                          for jj in (0, 1, 3, 4):
                              dsh = jj - 2
                              lo, hi = max(0, -dsh), L - max(0, dsh)
                              V(lambda e, c6=c6, jj=jj, x3=x3, y3=y3, lo=lo, hi=hi, dsh=dsh: e.scalar_tensor_tensor(
                                  out=y3[:, :, lo:hi], in0=x3[:, :, lo + dsh:hi + dsh], scalar=cwT[:, l, c6, jj:jj + 1],
                                  in1=y3[:, :, lo:hi], op0=ALU.mult, op1=ALU.add), [xk, yk, "cwT"], [yk])
                          act(yt[:], yt[:], AF.Silu, [yk], [yk])
                          if c6 < 4:
                              for tb in range(2):
                                  blk = slice(tb * 512, (tb + 1) * 512)
                                  sqv = xt[:, 0:512]
                                  rsv = xt[:, 512:1024]
                                  act(sqv, yt[:, blk], AF.Square, [yk], [xk])
                                  pb, pk = bank()
                                  mm(pb[:, :], onesblk[:, :], sqv, True, True, ["onesblk", xk], [pk])
                                  act(rsv, pb[:, :], AF.Sqrt, [pk], [xk], bias=EPS, scale=1.0)
                                  V(lambda e, rsv=rsv: e.reciprocal(out=rsv, in_=rsv), [xk], [xk])
                                  if c6 < 2:
                                      V(lambda e, c6=c6, blk=blk, yt=yt, rsv=rsv: e.scalar_tensor_tensor(
                                          out=qkT[:, c6, blk], in0=yt[:, blk], scalar=0.125, in1=rsv, op0=ALU.mult,
                                          op1=ALU.mult), [yk, xk], [("g_qkT", c6)])
                                  else:
                                      V(lambda e, c6=c6, blk=blk, yt=yt, rsv=rsv: e.tensor_tensor(out=qkT[:, c6, blk], in0=yt[:, blk], in1=rsv,
                                                                                        op=ALU.mult), [yk, xk], [("g_qkT", c6)])
                          if c6 >= 2:
                              srcT = qkT[:, c6, :] if c6 < 4 else yt[:]
                              srck = ("g_qkT", c6) if c6 < 4 else yk
                              dst = ktok if c6 < 4 else vtok
                              dstk = "g_ktok" if c6 < 4 else "g_vtok"
                              cc = c6 % 2
                              for half in range(2):
                                  pb, pk = bank()
                                  for q in range(4):
                                      tt = half * 4 + q
                                      tr(pb[:, q * 128:(q + 1) * 128], srcT[:, tt * 128:(tt + 1) * 128], ident[:], [srck, "ident"], [pk])
                                  A(lambda e, half=half, pb=pb, dst=dst, cc=cc: e.copy(
                                      out=dst[:, half * 4:half * 4 + 4, cc * 128:(cc + 1) * 128],
                                      in_=pb[:, :].rearrange("p (a b) -> p a b", a=4)), [pk], [dstk])
                      zproj(wC, wCk, 0, zT, "g_zT")
                      if _GB >= 2:
                        pb, pk = bank()
                        for tt in range(8):
                            for k in range(8):
                                mm(pb[:, tt * 16:(tt + 1) * 16], hnT[:, k, tt * 128:(tt + 1) * 128], wB[:, k, 256:272], k == 0, k == 7,
                                   [wBk, "hnT"], [pk])
                        V(lambda e, pb=pb: e.tensor_copy(out=ab[:].rearrange("p a b -> p (a b)"), in_=pb[:, 0:128]), [pk], ["g_ab"])
                        act(beta[:], ab[:, :, 0:8], AF.Sigmoid, ["g_ab"], ["g_beta"])
                        V(lambda e: e.tensor_scalar(out=nbeta[:], in0=beta[:], scalar1=-1.0, scalar2=None, op0=ALU.mult),
                          ["g_beta"], ["g_nbeta"])
                        V(lambda e: e.tensor_tensor(out=la[:], in0=ab[:, :, 8:16], in1=dtb[:, l, :].unsqueeze(1).to_broadcast([128, 8, 8]),
                                                    op=ALU.add), ["g_ab", "dtb"], ["g_la"])
                        V(lambda e: e.tensor_scalar(out=la[:], in0=la[:], scalar1=30.0, scalar2=None, op0=ALU.min), ["g_la"], ["g_la"])
                        act(la[:], la[:], AF.Exp, ["g_la"], ["g_la"])
                        act(la[:], la[:], AF.Ln, ["g_la"], ["g_la"], bias=1.0, scale=1.0)
                        V(lambda e: e.tensor_tensor(out=la[:], in0=la[:], in1=nega[:, l, :].unsqueeze(1).to_broadcast([128, 8, 8]),
                                                    op=ALU.mult), ["g_la", "nega"], ["g_la"])
                        pb, pk = bank()
                        for tt in range(8):
                            for d in range(2):
                                mm(pb[:, tt * 8 + d * 4:tt * 8 + d * 4 + 4], masks[:, d, :], la[:, tt, d * 4:(d + 1) * 4], True, True,
                                   ["masks", "g_la"], [pk])
                        V(lambda e, pb=pb: e.tensor_copy(out=gc[:].rearrange("p a b -> p (a b)"), in_=pb[:, 0:64]), [pk], ["g_gc"])
                        V(lambda e: e.tensor_scalar(out=ngc[:], in0=gc[:], scalar1=-1.0, scalar2=None, op0=ALU.mult), ["g_gc"], ["g_ngc"])
                        act(egc[:], gc[:], AF.Exp, ["g_gc"], ["g_egc"])
                        V(lambda e: e.tensor_tensor(out=bege[:], in0=beta[:], in1=egc[:], op=ALU.mult), ["g_beta", "g_egc"], ["g_bege"])
                        tap("g_gc", gc[:].rearrange("p a b -> p (a b)"), ["g_gc"])
                        tap("g_qkT", qkT[:].rearrange("p a b -> p (a b)"), ["g_qkT"])
                      if _GB >= 3:
                        S.barrier()
                        wv = [w_[:].rearrange("p a b -> p (a b)").bitcast(F32) for w_ in wbs]

                        def v4(ap):
                            return ap.rearrange("p (h f) -> p h f", h=4)
                        sets = []
                        sets.append(dict(
                            dg=dg[:], Dm=Dm[:], DT=DT[:], B=[Bm[0][:], Bm[1][:]], BT=[BTm[0][:], BTm[1][:]], Y=[Ym[0][:], Ym[1][:]],
                            qkd=qkd[:], vb=vb[:], kbgp=kbgp[:], kdcp=kdcp[:], kds=kds[:], eg2=eg2[:], wTm=wTm[:],
                            qkz=gx[0][:].rearrange("p (c f t) -> p c f t", c=4, f=2),
                            usb=gy[0][:, 0:256], vnew=gy[0][:, 256:512], o2s=gy[0][:, 512:768], otmp=gy[0][:, 768:1024]))
                        kds1 = sb("g_kds1", [128, 4], F32, ph)
                        eg21 = sb("g_eg21", [128, 2], F32, ph)
                        vb1 = sb("g_vb1", [128, 4, 64], F32, ph)
                        DT1 = sb("g_DT1", [128, 4, 128], F32, ph)
                        sets.append(dict(
                            dg=v4(stg[1][:, 0:512]), Dm=v4(stg[1][:, 512:1024]), DT=DT1[:],
                            B=[v4(wv[0][:, 0:512]), v4(wv[0][:, 512:1024])], BT=[v4(wv[0][:, 1024:1536]), v4(wv[0][:, 1536:2048])],
                            Y=[v4(wv[1][:, 0:512]), v4(wv[1][:, 512:1024])],
                            qkz=wv[1][:, 1024:2048].rearrange("p (c f t) -> p c f t", c=4, f=2),
                            qkd=v4(wv[2][:, 0:512]), kbgp=v4(wv[2][:, 512:1024]), kdcp=v4(wv[2][:, 1024:1536]), wTm=v4(wv[2][:, 1536:2048]),
                            vb=vb1[:], kds=kds1[:], eg2=eg21[:],
                            usb=stg[0][:, 0:256], vnew=stg[0][:, 256:512], o2s=stg[0][:, 512:768], otmp=stg[0][:, 768:1024]))
                        G(lambda e: e.memset(gx[0][:], 0.0), [], ["q0_qkz"])
                        G(lambda e: e.memset(wv[1][:, 1024:2048], 0.0), [], ["q1_qkz"])
                        G(lambda e: e.memset(wv[2][:, 512:2048], 0.0), [], ["q1_kbgp", "q1_kdcp", "q1_wTm"])
                        G(lambda e: e.memset(oacc[:], 0.0), [], ["g_oacc"])
                        seqs = [(2 * b_, 2 * b_ + 1) for b_ in range(4)] if grp == 0 else [tuple(range(8))]

                        def quad(si_, d, tt, init, fin, sidx):
                            T_ = sets[si_]
                            kp = "q%d_" % si_
                            if si_ == 0:
                                kq = dict(kbgp="g_kbgp", kdcp="g_kdcp", wTm="g_wTm")
                            else:
                                kq = dict(kbgp="q1_kbgp", kdcp="q1_kdcp", wTm="q1_wTm")
                            kq = {**{n: kp + n for n in ("dg", "Dm", "DT", "B0", "B1", "BT0", "BT1", "Y0", "Y1", "qkd", "vb", "kds", "eg2",
                                                         "qkz", "usb", "vnew", "o2s", "otmp")}, **kq}
                            qrr = [0]

                            def qbank():
                                i = 4 * si_ + qrr[0] % 4
                                qrr[0] += 1
                                return PS[i], ("ps", i)
                            last = 127 if d == 0 else 0
                            ts_ = slice(tt * 128, (tt + 1) * 128)
                            u0 = d * 4
                            dgt, Dmt, DTt, Bt, BTt, Yt = T_["dg"], T_["Dm"], T_["DT"], T_["B"], T_["BT"], T_["Y"]
                            qkdt, vbt, kbgpt, kdcpt, kdst, eg2t, wTmt, qkzt = (T_["qkd"], T_["vb"], T_["kbgp"], T_["kdcp"], T_["kds"],
                                                                               T_["eg2"], T_["wTm"], T_["qkz"])
                            usbt, vnewt, o2st, otmpt = T_["usb"], T_["vnew"], T_["o2s"], T_["otmp"]
                            if init:
                                if grp == 0:
                                    V(lambda e: e.memset(Sst[:, d, :, :], 0.0), [], [("g_S", d)])
                                else:
                                    DS(lambda e: e.dma_start(out=Sst[:, d, :, :],
                                                             in_=D["sg"][l, d].rearrange("(a b) k v -> (b k) a v", b=2)), [], [("g_S", d)])
                            for h in range(4):
                                A(lambda e, h=h: e.mul(out=dgt[:, h, :], in_=ident[:], mul=gc[:, tt, u0 + h:u0 + h + 1]),
                                  ["ident", "g_gc"], [kq["dg"]])
                            psN, kN = qbank()
                            psP, kP = qbank()
                            for h in range(4):
                                hs = slice(h * 128, (h + 1) * 128)
                                mm(psN[:, hs], ones[:, :], dgt[:, h, :], True, False, ["ones", kq["dg"]], [kN])
                                mm(psN[:, hs], ident[:, :], masks[:, 4 + d, :], False, True, ["ident", "masks"], [kN])
                                mm(psP[:, hs], ones[:, :], dgt[:, h, :], True, False, ["ones", kq["dg"]], [kP])
                                mm(psP[:, hs], ident[:, :], masks[:, 2 + d, :], False, True, ["ident", "masks"], [kP])
                            yield
                            for h in range(4):
                                hs = slice(h * 128, (h + 1) * 128)
                                act(Dmt[:, h, :], psP[:, hs], AF.Exp, [kP, "g_gc"], [kq["Dm"]], bias=gc[:, tt, u0 + h:u0 + h + 1], scale=-1.0)
                                act(DTt[:, h, :], psN[:, hs], AF.Exp, [kN, "g_ngc"], [kq["DT"]], bias=ngc[:, tt, u0 + h:u0 + h + 1], scale=1.0)
                                act(kdst[:, h:h + 1], psN[:, h * 128 + last:h * 128 + last + 1], AF.Exp, [kN, "g_ngc"], [kq["kds"]],
                                    bias=ngc[:, tt, u0 + h:u0 + h + 1], scale=1.0)
                            for pr in range(2):
                                for hf in range(2):
                                    h = 2 * pr + hf
                                    A(lambda e, pr=pr, hf=hf, h=h: e.activation(
                                        out=eg2t[hf * 64:(hf + 1) * 64, pr:pr + 1],
                                        in_=psN[hf * 64:(hf + 1) * 64, h * 128 + last:h * 128 + last + 1], func=AF.Exp), [kN], [kq["eg2"]])
                            for hf in range(2):
                                rows = slice(hf * 64, (hf + 1) * 64)
                                V(lambda e, hf=hf, rows=rows: e.tensor_copy(out=qkzt[rows, :, hf, :], in_=qkT[rows, :, ts_]),
                                  ["g_qkT"], [kq["qkz"]])
                            psK, kK = qbank()
                            psQ, kQ = qbank()
                            for h in range(4):
                                hs = slice(h * 128, (h + 1) * 128)
                                kfull = qkT[:, 2 + h // 2, ts_]
                                mm(psK[:, hs], kfull, qkzt[:, 2 + h // 2, h % 2, :], True, True, ["g_qkT", kq["qkz"]], [kK])
                                mm(psQ[:, hs], kfull, qkzt[:, h // 2, h % 2, :], True, True, ["g_qkT", kq["qkz"]], [kQ])
                            yield
                            for h in range(4):
                                hs = slice(h * 128, (h + 1) * 128)
                                V(lambda e, h=h, hs=hs: e.scalar_tensor_tensor(
                                    out=Bt[0][:, h, :], in0=psK[:, hs], scalar=nbeta[:, tt, u0 + h:u0 + h + 1], in1=Dmt[:, h, :],
                                    op0=ALU.mult, op1=ALU.mult), [kK, "g_nbeta", kq["Dm"]], [kq["B0"]])
                            V(lambda e: e.tensor_tensor(out=qkdt.rearrange("p a b -> p (a b)"), in0=psQ[:, :],
                                                        in1=DTt.rearrange("p a b -> p (a b)"), op=ALU.mult), [kQ, kq["DT"]], [kq["qkd"]])
                            yield
                            pb, pk = qbank()
                            for h in range(4):
                                tr(pb[:, h * 128:(h + 1) * 128], Bt[0][:, h, :], ident[:], [kq["B0"], "ident"], [pk])
                            yield
                            V(lambda e, pb=pb: e.tensor_copy(out=BTt[0].rearrange("p a b -> p (a b)"), in_=pb[:, :]), [pk], [kq["BT0"]])
                            for h in range(4):
                                V(lambda e, pb=pb, h=h: e.tensor_tensor(out=Yt[0][:, h, :], in0=pb[:, h * 128:(h + 1) * 128], in1=ident[:],
                                                                        op=ALU.add), [pk, "ident"], [kq["Y0"]])
                            yield
                            cur = 0
                            for lev in range(1, 7):
                                nxt = 1 - cur
                                pb, pk = qbank()
                                for h in range(4):
                                    mm(pb[:, h * 128:(h + 1) * 128], BTt[cur][:, h, :], Bt[cur][:, h, :], True, True,
                                       [kq["BT%d" % cur], kq["B%d" % cur]], [pk])
                                if lev < 6:
                                    pb2, pk2 = qbank()
                                    for h in range(4):
                                        mm(pb2[:, h * 128:(h + 1) * 128], Bt[cur][:, h, :], BTt[cur][:, h, :], True, True,
                                           [kq["BT%d" % cur], kq["B%d" % cur]], [pk2])
                                yield
                                A(lambda e, pb=pb, nxt=nxt: e.copy(out=Bt[nxt].rearrange("p a b -> p (a b)"), in_=pb[:, :]),
                                  [pk], [kq["B%d" % nxt]])
                                if lev < 6:
                                    V(lambda e, pb2=pb2, nxt=nxt: e.tensor_copy(out=BTt[nxt].rearrange("p a b -> p (a b)"), in_=pb2[:, :]),
                                      [pk2], [kq["BT%d" % nxt]])
                                pb3, pk3 = qbank()
                                for h in range(4):
                                    mm(pb3[:, h * 128:(h + 1) * 128], Bt[nxt][:, h, :], Yt[cur][:, h, :], True, True,
                                       [kq["B%d" % nxt], kq["Y%d" % cur]], [pk3])
                                yield
                                V(lambda e, pb3=pb3, nxt=nxt, cur=cur: e.tensor_tensor(
                                    out=Yt[nxt].rearrange("p a b -> p (a b)"), in0=pb3[:, :],
                                    in1=Yt[cur].rearrange("p a b -> p (a b)"), op=ALU.add), [pk3, kq["Y%d" % cur]], [kq["Y%d" % nxt]])
                                cur = nxt
                                yield
                            Yf, Yk = Yt[cur], kq["Y%d" % cur]
                            k4 = ktok[:, tt, :].rearrange("p (h d) -> p h d", h=4)
                            G(lambda e: e.tensor_tensor(out=vbt, in0=vtok[:, tt, :].rearrange("p (h d) -> p h d", h=4),
                                                        in1=beta[:, tt, u0:u0 + 4].unsqueeze(2).to_broadcast([128, 4, 64]), op=ALU.mult),
                              ["g_vtok", "g_beta"], [kq["vb"]])
                            for hf in range(2):
                                pc_ = slice(hf * 64, hf * 64 + 64)
                                G(lambda e, hf=hf, pc_=pc_: e.tensor_tensor(
                                    out=kbgpt[:, hf::2, pc_], in0=k4[:, hf::2, :],
                                    in1=bege[:, tt, u0 + hf:u0 + 4:2].unsqueeze(2).to_broadcast([128, 2, 64]), op=ALU.mult),
                                  ["g_ktok", "g_bege"], [kq["kbgp"]])
                                V(lambda e, hf=hf, pc_=pc_: e.tensor_tensor(
                                    out=kdcpt[:, hf::2, pc_], in0=k4[:, hf::2, :],
                                    in1=kdst[:, hf::2].unsqueeze(2).to_broadcast([128, 2, 64]), op=ALU.mult),
                                  ["g_ktok", kq["kds"]], [kq["kdcp"]])
                            pbu, pku = qbank()
                            for h in range(4):
                                mm(pbu[:, h * 64:(h + 1) * 64], Yf[:, h, :], vbt[:, h, :], True, True, [Yk, kq["vb"]], [pku])
                            pbw, pkw = qbank()
                            for pr in range(2):
                                for hf in range(2):
                                    h = 2 * pr + hf
                                    mm(pbw[:, pr * 128:(pr + 1) * 128], kbgpt[:, h, :], Yf[:, h, :], hf == 0, hf == 1, [kq["kbgp"], Yk], [pkw])
                            yield
                            A(lambda e: e.copy(out=usbt, in_=pbu[:, 0:256]), [pku], [kq["usb"]])
                            for pr in range(2):
                                for hf in range(2):
                                    rows = slice(hf * 64, (hf + 1) * 64)
                                    V(lambda e, pr=pr, hf=hf, rows=rows: e.tensor_copy(out=wTmt[rows, 2 * pr + hf, :],
                                                                                       in_=pbw[rows, pr * 128:(pr + 1) * 128]), [pkw], [kq["wTm"]])
                            yield
                            pbv, pkv = qbank()
                            pbo, pko = qbank()
                            for h in range(4):
                                hc = slice(h * 64, (h + 1) * 64)
                                mm(pbv[:, hc], wTmt[:, h, :], Sst[:, d, h // 2, :], True, True, [kq["wTm"], ("g_S", d)], [pkv])
                                mm(pbo[:, hc], qkzt[:, h // 2, h % 2, :], Sst[:, d, h // 2, :], True, True, [kq["qkz"], ("g_S", d)], [pko])
                            yield
                            V(lambda e: e.tensor_tensor(out=vnewt, in0=usbt, in1=pbv[:, 0:256], op=ALU.subtract), [kq["usb"], pkv], [kq["vnew"]])
                            pb2, pk2 = qbank()
                            for h in range(4):
                                hc = slice(h * 64, (h + 1) * 64)
                                mm(pb2[:, hc], qkdt[:, h, :], vnewt[:, hc], True, True, [kq["qkd"], kq["vnew"]], [pk2])
                            pbs, pks = qbank()
                            for pr in range(2):
                                for hf in range(2):
                                    h = 2 * pr + hf
                                    mm(pbs[:, pr * 64:(pr + 1) * 64], kdcpt[:, h, :], vnewt[:, h * 64:(h + 1) * 64], hf == 0, hf == 1,
                                       [kq["kdcp"], kq["vnew"]], [pks])
                            yield
                            A(lambda e: e.copy(out=o2st, in_=pb2[:, 0:256]), [pk2], [kq["o2s"]])
                            for h in range(4):
                                hc = slice(h * 64, (h + 1) * 64)
                                V(lambda e, h=h, hc=hc: e.scalar_tensor_tensor(
                                    out=otmpt[:, hc], in0=pbo[:, hc], scalar=egc[:, tt, u0 + h:u0 + h + 1], in1=o2st[:, hc],
                                    op0=ALU.mult, op1=ALU.add), [pko, "g_egc", kq["o2s"]], [kq["otmp"]])
                            G(lambda e: e.tensor_tensor(out=oacc[:, tt, :], in0=oacc[:, tt, :], in1=otmpt, op=ALU.add),
                              [("g_oacc", tt), kq["otmp"]], [("g_oacc", tt)])
                            for pr in range(2):
                                V(lambda e, pr=pr: e.scalar_tensor_tensor(
                                    out=Sst[:, d, pr, :], in0=Sst[:, d, pr, :], scalar=eg2t[:, pr:pr + 1],
                                    in1=pbs[:, pr * 64:(pr + 1) * 64], op0=ALU.mult, op1=ALU.add),
                                  [("g_S", d), kq["eg2"], pks], [("g_S", d)])
                            if fin and grp == 0:
                                DS(lambda e: e.dma_start(out=D["nsg"][sidx, l, d].rearrange("(a b) k v -> (b k) a v", b=2),
                                                         in_=Sst[:, d, :, :]), [("g_S", d)], [])
                            yield

                        sched = [[], []]
                        for d in range(2):
                            for sidx, tiles in enumerate(seqs):
                                order = tiles if d == 0 else tuple(reversed(tiles))
                                for qi, tt in enumerate(order):
                                    sched[d].append((tt, qi == 0, qi == len(order) - 1, sidx))
                        for (f_, b_) in zip(sched[0][:_GBQ], sched[1][:_GBQ]):
                            gens = [quad(0, 0, *f_), quad(1, 1, *b_)]
                            alive = [True, True]
                            while any(alive):
                                for gi in range(2):
                                    if alive[gi]:
                                        try:
                                            next(gens[gi])
                                        except StopIteration:
                                            alive[gi] = False
                      S.mute = False
                      if _GB >= 4:
                        norm_gate_out(ph, "g", oacc, "g_oacc", ggn, "ggn", l, zT, "g_zT", 1)
                S.barrier()
                stage("B")
                if grp == groups[0] and l == layers[0]:
                    tap("brB", brT[:, 2:4, :].rearrange("p a b -> p (a b)"), [("brT", 1)])

                S.mute = "M" in _SKIP
                with ExitStack() as ph:
                    mT = sb("m_T", [128, 8, 1024], BF16, ph)
                    macc = sb("m_acc", [128, 8, 1024], F32, ph)
                    wgs = [sb("m_wg%d" % i, [128, 8, 512], BF16, ph) for i in range(3)]
                    wbrs = [sb("m_wb%d" % i, [128, 2, 512], BF16, ph) for i in range(3)]
                    gts = [sb("m_gt%d" % i, [128, 512], BF16, ph) for i in range(2)]
                    tmps = [sb("m_tmp%d" % i, [128, 512], F32, ph) for i in range(2)]
                    gi = 0
                    ci_ = 0
                    for n in range(4):
                        for half in range(2):
                            wg_t, wgk = wgs[ci_ % 3], "m_wg%d" % (ci_ % 3)
                            wb_t, wbk = wbrs[ci_ % 3], "m_wb%d" % (ci_ % 3)
                            ci_ += 1
                            c0 = 3856 + n * 1024 + half * 512
                            DG(lambda e, c0=c0, wg_t=wg_t: e.dma_start(
                                out=wg_t[:], in_=D["w_in"][l, :, c0:c0 + 512].rearrange("(k p) c -> p k c", p=128)), [], [wgk])
                            DG(lambda e, n=n, half=half, wb_t=wb_t: e.dma_start(
                                out=wb_t[:], in_=D["w_branch"][l, n, :, half * 512:(half + 1) * 512].rearrange("(k p) c -> p k c", p=128)),
                               [], [wbk])
                            for q in range(4):
                                dc = half * 4 + q
                                qs = slice(q * 128, (q + 1) * 128)
                                for tb in range(2):
                                    blk = slice(tb * 512, (tb + 1) * 512)
                                    akey = ("m_acc", dc, tb)
                                    pb, pk = bank()
                                    for k in range(8):
                                        mm(pb[:, :], wg_t[:, k, qs], hnT[:, k, blk], k == 0, k == 7, [wgk, "hnT"], [pk])
                                    gt, gtk = gts[gi % 2], "m_gt%d" % (gi % 2)
                                    tmpm, tmk = tmps[gi % 2], "m_tmp%d" % (gi % 2)
                                    gi += 1
                                    act(gt[:], pb[:, :], AF.Sigmoid, [pk], [gtk])
                                    pb2, pk2 = bank()
                                    for kk in range(2):
                                        mm(pb2[:, :], wb_t[:, kk, qs], brT[:, n * 2 + kk, blk], kk == 0, kk == 1, [wbk, ("brT", n)], [pk2])
                                    if n == 0:
                                        V(lambda e, pb2=pb2, gt=gt, dc=dc, blk=blk: e.tensor_tensor(out=macc[:, dc, blk], in0=pb2[:, :], in1=gt[:],
                                                                                                   op=ALU.mult), [pk2, gtk], [akey])
                                    else:
                                        V(lambda e, pb2=pb2, gt=gt, tmpm=tmpm: e.tensor_tensor(out=tmpm[:], in0=pb2[:, :], in1=gt[:], op=ALU.mult),
                                          [pk2, gtk], [tmk])
                                        if n < 3:
                                            V(lambda e, dc=dc, blk=blk, tmpm=tmpm: e.tensor_tensor(out=macc[:, dc, blk], in0=macc[:, dc, blk],
                                                                                                   in1=tmpm[:], op=ALU.add), [akey, tmk], [akey])
                                        else:
                                            V(lambda e, dc=dc, blk=blk, tmpm=tmpm: e.tensor_tensor(out=mT[:, dc, blk], in0=macc[:, dc, blk],
                                                                                                   in1=tmpm[:], op=ALU.add), [akey, tmk], [("m_T", dc)])
                    for half in range(2):
                        i = wcnt[0] % 3
                        wcnt[0] += 1
                        wo, wok = wbs[i], "wb%d" % i
                        DG(lambda e, half=half, wo=wo: e.dma_start(
                            out=wo[:], in_=D["w_out"][l, :, half * 512:(half + 1) * 512].rearrange("(k p) n -> p k n", p=128)),
                           [], [wok])
                        for q in range(4):
                            oc = half * 4 + q
                            for tb in range(2):
                                blk = slice(tb * 512, (tb + 1) * 512)
                                pb, pk = bank()
                                for k in range(8):
                                    mm(pb[:, :], wo[:, k, q * 128:(q + 1) * 128], mT[:, k, blk], k == 0, k == 7, [wok, ("m_T", k)], [pk])
                                V(lambda e, oc=oc, blk=blk, pb=pb: e.scalar_tensor_tensor(
                                    out=hT[:, oc, blk], in0=pb[:, :], scalar=modT[:, l, 16 + oc, j:j + 1], in1=hT[:, oc, blk],
                                    op0=ALU.mult, op1=ALU.add), [pk, "modT", "hT"], ["hT"])
                S.barrier()
                stage("merge")
                if grp == groups[0] and l == layers[0]:
                    tap("hT1", hT[:].rearrange("p a b -> p (a b)"), ["hT"])

            S.mute = False
            with ExitStack() as ph:
                ynT = sb("f_yn", [128, 8, 1024], F32, ph)
                rmsnorm_T(ph, lambda k: fnT[:, k:k + 1], lambda k: None,
                          lambda k, tb: ynT[:, k, tb * 512:(tb + 1) * 512], "f_yn")
                for tt in range(8):
                    s_t, sk = stg[scnt[0] % 2], "stg%d" % (scnt[0] % 2)
                    scnt[0] += 1
                    for half in range(2):
                        pb, pk = bank()
                        for q in range(4):
                            k = half * 4 + q
                            tr(pb[:, q * 128:(q + 1) * 128], ynT[:, k, tt * 128:(tt + 1) * 128], ident[:], ["f_yn", "ident"], [pk])
                        if half == 0:
                            A(lambda e, pb=pb, s_t=s_t: e.copy(out=s_t[:, 0:512], in_=pb[:, :]), [pk], [sk])
                        else:
                            V(lambda e, pb=pb, s_t=s_t: e.tensor_copy(out=s_t[:, 512:1024], in_=pb[:, :]), [pk], [sk])
                    DS(lambda e, tt=tt, s_t=s_t: e.dma_start(out=yout[tt * 128:(tt + 1) * 128, :], in_=s_t[:]), [sk], [])
            S.barrier()
        try:
            run_groups()
        except _Stop:
            S.barrier()
        with nc.allow_non_contiguous_dma(reason="small transposed parameter loads"):
            stats = S.emit()
    return nc, stats


_CACHE = {}


def _in_maps(inp):
    f = lambda a: np.ascontiguousarray(np.asarray(a, dtype=np.float32))
    cst = _consts(f(inp["na_bias"]))
    shared = dict(
        w_ada=f(inp["w_ada"]), b_ada=f(inp["b_ada"]), norm_g=f(inp["norm_g"]), w_in=f(inp["w_in"]), conv_w=f(inp["conv_w"]),
        a_log=f(inp["gdn_a_log"]).reshape(2, 8), dt_bias=f(inp["gdn_dt_bias"]).reshape(2, 8), gdn_norm=f(inp["gdn_norm"]),
        q_norm=f(inp["attn_q_norm"]), k_norm=f(inp["attn_k_norm"]), ret_norm=f(inp["ret_norm"]),
        w_branch=f(inp["w_branch"]), w_out=f(inp["w_out"]), final_norm=f(inp["final_norm"]).reshape(1, 1024), **cst)
    xp, xs = f(inp["x_prompt"]), f(inp["x_sample"])
    maps = []
    for c in range(8):
        m = dict(shared)
        m["xp"] = xp[4 * c:4 * c + 4].reshape(1024, 1024)
        m["xs"] = xs[c]
        m["cak"] = f(inp["cache_attn_k"][c]).reshape(2, 512, 128)
        m["cav"] = f(inp["cache_attn_v"][c]).reshape(2, 512, 128)
        m["cnk"] = f(inp["cache_na_k"][c]).reshape(2, 512, 256)
        m["cnv"] = f(inp["cache_na_v"][c]).reshape(2, 512, 256)
        m["sg"] = f(inp["state_gdn"][c])
        m["sr"] = f(inp["state_ret"][c])
        m["cond"] = np.stack([f(inp["c_ctx"]), f(inp["c"][c])])
        maps.append(m)
    return maps


def kernel(**inputs):
    if "nc" not in _CACHE:
        _CACHE["nc"] = build()[0]
    nc = _CACHE["nc"]
    maps = _in_maps(inputs)
    res = run_bass_kernel_spmd(nc, maps, core_ids=list(range(8)))
    R = res.results
    cat = lambda n: np.concatenate([np.asarray(r[n]) for r in R], axis=0)
    y_prompt = cat("yp").reshape(32, 256, 1024)
    y_sample = np.stack([np.asarray(r["ys"]) for r in R])
    nak = cat("nak").reshape(32, 2, 256, 2, 64)
    nav = cat("nav").reshape(32, 2, 256, 2, 64)
    nnk = cat("nnk").reshape(32, 2, 256, 4, 64)
    nnv = cat("nnv").reshape(32, 2, 256, 4, 64)
    nsg = cat("nsg")
    nsr = cat("nsr")
    return tuple(np.ascontiguousarray(a, dtype=np.float32) for a in (y_prompt, y_sample, nak, nav, nnk, nnv, nsg, nsr))
```
